# Optimizing a Trainium2 kernel written in Bass

```python
import math
import jax, jax.numpy as jnp
from jax import lax
import numpy as np

D_MODEL = 2048
BATCH = 16
SEQ = 2048
DEPTH = 1
DEC_BATCH = 128
DEC_SEQ = 1
PAST_LEN = 16384
PAGE_SIZE = 128

SWA_HEADS = 16
SWA_KV_HEADS = 4
SWA_GROUP = SWA_HEADS // SWA_KV_HEADS
SWA_HEAD_DIM = 64
WINDOW = 128
SWA_BLOCK = 128
SWA_Q_DIM = SWA_HEADS * SWA_HEAD_DIM
SWA_KV_DIM = SWA_KV_HEADS * SWA_HEAD_DIM

SSD_D_INNER = D_MODEL
SSD_HEAD_DIM = 64
SSD_HEADS = SSD_D_INNER // SSD_HEAD_DIM
SSD_GROUPS = 4
SSD_HPG = SSD_HEADS // SSD_GROUPS
SSD_D_STATE = 128
SSD_CONV = 4
SSD_CHUNK = 128
SSD_CONV_CH = SSD_D_INNER + 2 * SSD_GROUPS * SSD_D_STATE

MEM_TOKENS = 256
MEM_HEADS = 4
MEM_HEAD_DIM = 256
MEM_DIM = MEM_HEADS * MEM_HEAD_DIM

N_BRANCH = 3
D_FF = ((8 * D_MODEL + 3 * 256 - 1) // (3 * 256)) * 256
EPS = 1e-6

IN_SPLITS = (SWA_Q_DIM, SWA_KV_DIM, SWA_KV_DIM, SSD_D_INNER, SSD_CONV_CH, SSD_HEADS, MEM_DIM, N_BRANCH * D_MODEL)
IN_DIM = sum(IN_SPLITS)

kernel_name = "hybrid_swa_ssd_memxattn_decoder_step"


def _split_points(sizes):
    pts, acc = [], 0
    for s in sizes[:-1]:
        acc += s
        pts.append(acc)
    return pts


def _rmsnorm(x, g):
    xf = x.astype(jnp.float32)
    y = xf * lax.rsqrt(jnp.mean(jnp.square(xf), axis=-1, keepdims=True) + EPS)
    return (y * g.astype(jnp.float32)).astype(x.dtype)


def _alibi_slopes():
    h = jnp.arange(1, SWA_HEADS + 1, dtype=jnp.float32)
    return jnp.exp2(-8.0 * h / SWA_HEADS).reshape(SWA_KV_HEADS, SWA_GROUP)


def _in_proj(x, norm_mix, w_in, q_norm_swa, k_norm_swa, q_norm_mem):
    b, t, _ = x.shape
    h = _rmsnorm(x, norm_mix)
    q, k, v, z, xbc, dt, qm, g = jnp.split(h @ w_in, _split_points(IN_SPLITS), axis=-1)
    q = _rmsnorm(q.reshape(b, t, SWA_KV_HEADS, SWA_GROUP, SWA_HEAD_DIM), q_norm_swa)
    k = _rmsnorm(k.reshape(b, t, SWA_KV_HEADS, SWA_HEAD_DIM), k_norm_swa)
    v = v.reshape(b, t, SWA_KV_HEADS, SWA_HEAD_DIM)
    qm = _rmsnorm(qm.reshape(b, t, MEM_HEADS, MEM_HEAD_DIM), q_norm_mem)
    gates = jax.nn.sigmoid(g.astype(jnp.float32)).astype(x.dtype).reshape(b, t, N_BRANCH, D_MODEL)
    return q, k, v, z, xbc, dt, qm, gates


def _swa_attend(q, k, v, q_pos, k_pos, sinks):
    f32 = jnp.float32
    s = jnp.einsum('...qhgd,...khd->...hgqk', q, k).astype(f32) * (SWA_HEAD_DIM ** -0.5)
    dist = q_pos[..., :, None] - k_pos[..., None, :]
    allowed = (dist >= 0) & (dist <= WINDOW) & (k_pos[..., None, :] >= 0)
    dist = dist[..., None, None, :, :].astype(f32)
    allowed = allowed[..., None, None, :, :]
    s = jnp.where(allowed, s - _alibi_slopes()[:, :, None, None] * dist, -jnp.inf)
    sink = sinks.astype(f32).reshape(SWA_KV_HEADS, SWA_GROUP)[:, :, None]
    m = jnp.maximum(jnp.max(s, axis=-1), sink)
    p = jnp.exp(s - m[..., None])
    denom = jnp.sum(p, axis=-1) + jnp.exp(sink - m)
    p = p / denom[..., None]
    o = jnp.einsum('...hgqk,...khd->...qhgd', p, v.astype(f32))
    return o.astype(q.dtype)


def _swa_prompt(q, k, v, sinks):
    b, t = q.shape[:2]
    nb = t // SWA_BLOCK
    qb = q.reshape(b, nb, SWA_BLOCK, SWA_KV_HEADS, SWA_GROUP, SWA_HEAD_DIM)
    kb = k.reshape(b, nb, SWA_BLOCK, SWA_KV_HEADS, SWA_HEAD_DIM)
    vb = v.reshape(b, nb, SWA_BLOCK, SWA_KV_HEADS, SWA_HEAD_DIM)
    pad = ((0, 0), (1, 0), (0, 0), (0, 0), (0, 0))
    kcat = jnp.concatenate([jnp.pad(kb, pad)[:, :-1], kb], axis=2)
    vcat = jnp.concatenate([jnp.pad(vb, pad)[:, :-1], vb], axis=2)
    pos = jnp.arange(t, dtype=jnp.int32).reshape(nb, SWA_BLOCK)
    kpos = jnp.concatenate([pos - SWA_BLOCK, pos], axis=-1)
    o = _swa_attend(qb, kcat, vcat, pos, kpos, sinks)
    return o.reshape(b, t, SWA_Q_DIM)


def _ssd(xbc_raw, z, dt_raw, conv_buf, h0, conv_w, conv_b, dt_bias, a_log, d_skip, ssd_norm):
    f32 = jnp.float32
    b, L, _ = xbc_raw.shape
    xpad = jnp.concatenate([conv_buf.astype(f32), xbc_raw.astype(f32)], axis=1)
    conv = lax.conv_general_dilated(xpad, conv_w.astype(f32)[:, None, :], (1,), 'VALID',
                                    dimension_numbers=('NWC', 'WIO', 'NWC'),
                                    feature_group_count=SSD_CONV_CH)
    xbc = jax.nn.silu(conv + conv_b.astype(f32))
    new_conv = xpad[:, L:]
    xs, bm, cm = jnp.split(xbc, [SSD_D_INNER, SSD_D_INNER + SSD_GROUPS * SSD_D_STATE], axis=-1)
    l = SSD_CHUNK if L % SSD_CHUNK == 0 else L
    c = L // l
    x = xs.reshape(b, c, l, SSD_GROUPS, SSD_HPG, SSD_HEAD_DIM)
    bm = bm.reshape(b, c, l, SSD_GROUPS, SSD_D_STATE)
    cm = cm.reshape(b, c, l, SSD_GROUPS, SSD_D_STATE)
    dt = jax.nn.softplus(dt_raw.astype(f32) + dt_bias.astype(f32)).reshape(b, c, l, SSD_GROUPS, SSD_HPG)
    a = -jnp.exp(a_log.astype(f32)).reshape(SSD_GROUPS, SSD_HPG)
    acs = jnp.cumsum(dt * a, axis=2)
    acs_t = jnp.moveaxis(acs, 2, -1)
    dt_t = jnp.moveaxis(dt, 2, -1)
    causal = jnp.tril(jnp.ones((l, l), dtype=bool))
    decay = jnp.exp(jnp.where(causal, acs_t[..., :, None] - acs_t[..., None, :], -jnp.inf))
    cb = jnp.einsum('bclgn,bcsgn->bcgls', cm, bm)
    w_intra = cb[:, :, :, None] * decay * dt_t[..., None, :]
    y_diag = jnp.einsum('bcgrls,bcsgrp->bclgrp', w_intra, x)
    xw = x * (jnp.exp(acs[:, :, -1:] - acs) * dt)[..., None]
    states = jnp.einsum('bclgn,bclgrp->bcgrpn', bm, xw)
    chunk_decay = jnp.exp(acs[:, :, -1])

    def step(h, inp):
        dec, st = inp
        return dec[..., None, None] * h + st, h

    h_init = h0.astype(f32).reshape(b, SSD_GROUPS, SSD_HPG, SSD_HEAD_DIM, SSD_D_STATE)
    h_last, h_prev = lax.scan(step, h_init, (jnp.moveaxis(chunk_decay, 1, 0), jnp.moveaxis(states, 1, 0)))
    h_prev = jnp.moveaxis(h_prev, 0, 1)
    y_off = jnp.einsum('bclgn,bcgrpn->bclgrp', cm, h_prev) * jnp.exp(acs)[..., None]
    y = y_diag + y_off + d_skip.astype(f32).reshape(SSD_GROUPS, SSD_HPG, 1) * x
    gsz = SSD_D_INNER // SSD_GROUPS
    y = y.reshape(b, L, SSD_GROUPS, gsz) * jax.nn.silu(z.astype(f32)).reshape(b, L, SSD_GROUPS, gsz)
    y = y * lax.rsqrt(jnp.mean(jnp.square(y), axis=-1, keepdims=True) + EPS)
    y = y.reshape(b, L, SSD_D_INNER) * ssd_norm.astype(f32)
    h_out = h_last.reshape(b, SSD_HEADS, SSD_HEAD_DIM, SSD_D_STATE)
    return y.astype(z.dtype), new_conv.astype(conv_buf.dtype), h_out.astype(h0.dtype)


def _mem_kv(mem, norm_mem, w_mem_kv, k_norm_mem):
    b, m, _ = mem.shape
    k, v = jnp.split(_rmsnorm(mem, norm_mem) @ w_mem_kv, 2, axis=-1)
    k = _rmsnorm(k.reshape(b, m, MEM_HEADS, MEM_HEAD_DIM), k_norm_mem)
    return k, v.reshape(b, m, MEM_HEADS, MEM_HEAD_DIM)


def _mem_attend(q, k, v):
    b, t = q.shape[:2]
    s = jnp.einsum('bthd,bmhd->bhtm', q, k).astype(jnp.float32) * (MEM_HEAD_DIM ** -0.5)
    p = jax.nn.softmax(s, axis=-1)
    o = jnp.einsum('bhtm,bmhd->bthd', p, v.astype(jnp.float32))
    return o.reshape(b, t, MEM_DIM).astype(q.dtype)


def _merge_ffn(x, a_out, s_out, m_out, gates, w_up_swa, w_up_ssd, w_up_mem, w_out, norm_ffn, w_gate, w_up, w_down):
    merged = (gates[:, :, 0] * (a_out @ w_up_swa)
              + gates[:, :, 1] * (s_out @ w_up_ssd)
              + gates[:, :, 2] * (m_out @ w_up_mem))
    x = x + merged @ w_out
    h = _rmsnorm(x, norm_ffn)
    return x + (jax.nn.silu(h @ w_gate) * (h @ w_up)) @ w_down


def setup_inputs(seed: int = 0) -> dict:
    key = jax.random.key(seed)
    ks = iter(jax.random.split(key, 48))
    f32 = jnp.float32

    def nrm(shape, scale=1.0):
        return jax.random.normal(next(ks), shape, f32) * scale

    def gain(n):
        return 1.0 + nrm((DEPTH, n), 0.02)

    w_buf = min(WINDOW, PAST_LEN)
    dt0 = jnp.exp(jax.random.uniform(next(ks), (DEPTH, SSD_HEADS), f32, math.log(1e-3), math.log(1e-1)))
    dt_bias = dt0 + jnp.log(-jnp.expm1(-dt0))
    a_log = jnp.log(jax.random.uniform(next(ks), (DEPTH, SSD_HEADS), f32, 1.0, 16.0))
    return {
        "x_prompt": nrm((BATCH, SEQ, D_MODEL)),
        "x_sample": nrm((DEC_BATCH, DEC_SEQ, D_MODEL)),
        "cache_swa_k": nrm((DEPTH, DEC_BATCH, w_buf, SWA_KV_HEADS, SWA_HEAD_DIM)),
        "cache_swa_v": nrm((DEPTH, DEC_BATCH, w_buf, SWA_KV_HEADS, SWA_HEAD_DIM)),
        "cache_mem_k": nrm((DEPTH, DEC_BATCH, MEM_TOKENS, MEM_HEADS, MEM_HEAD_DIM)),
        "cache_mem_v": nrm((DEPTH, DEC_BATCH, MEM_TOKENS, MEM_HEADS, MEM_HEAD_DIM)),
        "state_ssm": nrm((DEPTH, DEC_BATCH, SSD_HEADS, SSD_HEAD_DIM, SSD_D_STATE), 0.1),
        "state_conv": nrm((DEPTH, DEC_BATCH, SSD_CONV - 1, SSD_CONV_CH)),
        "mem_prompt": nrm((BATCH, MEM_TOKENS, D_MODEL)),
        "norm_mix": gain(D_MODEL),
        "w_in": nrm((DEPTH, D_MODEL, IN_DIM), D_MODEL ** -0.5),
        "q_norm_swa": gain(SWA_HEAD_DIM),
        "k_norm_swa": gain(SWA_HEAD_DIM),
        "swa_sinks": nrm((DEPTH, SWA_HEADS), 0.5),
        "conv_w": nrm((DEPTH, SSD_CONV, SSD_CONV_CH), SSD_CONV ** -0.5),
        "conv_b": nrm((DEPTH, SSD_CONV_CH), 0.01),
        "dt_bias": dt_bias,
        "a_log": a_log,
        "d_skip": 1.0 + nrm((DEPTH, SSD_HEADS), 0.02),
        "ssd_norm": gain(SSD_D_INNER),
        "norm_mem": gain(D_MODEL),
        "w_mem_kv": nrm((DEPTH, D_MODEL, 2 * MEM_DIM), D_MODEL ** -0.5),
        "q_norm_mem": gain(MEM_HEAD_DIM),
        "k_norm_mem": gain(MEM_HEAD_DIM),
        "w_up_swa": nrm((DEPTH, SWA_Q_DIM, D_MODEL), SWA_Q_DIM ** -0.5),
        "w_up_ssd": nrm((DEPTH, SSD_D_INNER, D_MODEL), SSD_D_INNER ** -0.5),
        "w_up_mem": nrm((DEPTH, MEM_DIM, D_MODEL), MEM_DIM ** -0.5),
        "w_out": nrm((DEPTH, D_MODEL, D_MODEL), D_MODEL ** -0.5),
        "norm_ffn": gain(D_MODEL),
        "w_gate": nrm((DEPTH, D_MODEL, D_FF), D_MODEL ** -0.5),
        "w_up": nrm((DEPTH, D_MODEL, D_FF), D_MODEL ** -0.5),
        "w_down": nrm((DEPTH, D_FF, D_MODEL), D_FF ** -0.5),
    }


def reference(x_prompt, x_sample, cache_swa_k, cache_swa_v, cache_mem_k, cache_mem_v, state_ssm, state_conv,
              mem_prompt, norm_mix, w_in, q_norm_swa, k_norm_swa, swa_sinks, conv_w, conv_b, dt_bias, a_log,
              d_skip, ssd_norm, norm_mem, w_mem_kv, q_norm_mem, k_norm_mem, w_up_swa, w_up_ssd, w_up_mem,
              w_out, norm_ffn, w_gate, w_up, w_down):
    bp, tp, _ = x_prompt.shape
    bs, ts, _ = x_sample.shape
    w_buf = cache_swa_k.shape[2]
    w_p = min(WINDOW, tp)
    yp, ys = x_prompt, x_sample
    p_k, p_v, p_mk, p_mv, p_h, p_c = [], [], [], [], [], []
    s_k, s_v, s_h, s_c = [], [], [], []
    for l in range(DEPTH):
        ssd_w = (conv_w[l], conv_b[l], dt_bias[l], a_log[l], d_skip[l], ssd_norm[l])
        out_w = (w_up_swa[l], w_up_ssd[l], w_up_mem[l], w_out[l], norm_ffn[l], w_gate[l], w_up[l], w_down[l])

        q, k, v, z, xbc, dt, qm, gates = _in_proj(yp, norm_mix[l], w_in[l], q_norm_swa[l], k_norm_swa[l], q_norm_mem[l])
        a_out = _swa_prompt(q, k, v, swa_sinks[l])
        conv0 = jnp.zeros((bp, SSD_CONV - 1, SSD_CONV_CH), state_conv.dtype)
        h0 = jnp.zeros((bp, SSD_HEADS, SSD_HEAD_DIM, SSD_D_STATE), state_ssm.dtype)
        s_out, conv_new, h_new = _ssd(xbc, z, dt, conv0, h0, *ssd_w)
        mk, mv = _mem_kv(mem_prompt, norm_mem[l], w_mem_kv[l], k_norm_mem[l])
        m_out = _mem_attend(qm, mk, mv)
        yp = _merge_ffn(yp, a_out, s_out, m_out, gates, *out_w)
        p_k.append(k[:, tp - w_p:])
        p_v.append(v[:, tp - w_p:])
        p_mk.append(mk)
        p_mv.append(mv)
        p_h.append(h_new)
        p_c.append(conv_new)

        q, k, v, z, xbc, dt, qm, gates = _in_proj(ys, norm_mix[l], w_in[l], q_norm_swa[l], k_norm_swa[l], q_norm_mem[l])
        k_all = jnp.concatenate([cache_swa_k[l].astype(k.dtype), k], axis=1)
        v_all = jnp.concatenate([cache_swa_v[l].astype(v.dtype), v], axis=1)
        k_pos = PAST_LEN - w_buf + jnp.arange(w_buf + ts, dtype=jnp.int32)
        q_pos = PAST_LEN + jnp.arange(ts, dtype=jnp.int32)
        a_out = _swa_attend(q, k_all, v_all, q_pos, k_pos, swa_sinks[l]).reshape(bs, ts, SWA_Q_DIM)
        s_out, conv_new, h_new = _ssd(xbc, z, dt, state_conv[l], state_ssm[l], *ssd_w)
        m_out = _mem_attend(qm, cache_mem_k[l], cache_mem_v[l])
        ys = _merge_ffn(ys, a_out, s_out, m_out, gates, *out_w)
        s_k.append(k_all[:, ts:])
        s_v.append(v_all[:, ts:])
        s_h.append(h_new)
        s_c.append(conv_new)

    return (yp, ys,
            jnp.stack(p_k), jnp.stack(p_v), jnp.stack(p_mk), jnp.stack(p_mv), jnp.stack(p_h), jnp.stack(p_c),
            jnp.stack(s_k), jnp.stack(s_v), jnp.stack(s_h), jnp.stack(s_c))
```

```python
import contextlib
import os
import numpy as np
_SKIP = set(os.environ.get('KSKIP', '').split(','))
import concourse.bass as bass
import concourse.mybir as mybir
from concourse.bass_utils import run_bass_kernel_spmd

F32 = mybir.dt.float32
BF16 = mybir.dt.bfloat16
AF = mybir.ActivationFunctionType
ALU = mybir.AluOpType
AX = mybir.AxisListType

NCORES = 8
D = 2048
SEQ = 2048
NSEQ = 2
NSMP = 16
NT = 128
BLK = 128
KC = 16
DFF = 5632
EPS = 1e-6
CW = 256
ENGS = ("pe", "act", "dve", "pool", "sp")

SA_Q, SA_K, SA_V, SA_Z, SA_XBC, SA_QM, SA_G0, SA_G1, SA_G2, SA_MEM, SA_USSD, SA_OUT, SA_GATE, SA_UP = (
    0, 4, 5, 6, 14, 26, 30, 38, 46, 54, 62, 70, 78, 100)
NA = 122
SLOPES = [2.0 ** (-8.0 * (h + 1) / 16.0) for h in range(16)]


class FW:
    def __init__(self, nc, n_dma_sems=48):
        self.nc = nc
        self.es = contextlib.ExitStack()
        self.sem = {e: self.es.enter_context(nc.semaphore("s_" + e)) for e in ENGS}
        self.dsem = [self.es.enter_context(nc.semaphore("d_%d" % i)) for i in range(n_dma_sems)]
        self.dtot = [0] * n_dma_sems
        self.dnext = {}
        self.dpool = {'pool': (0, n_dma_sems // 2), 'sp': (n_dma_sems // 2, n_dma_sems)}
        self.n = {e: 0 for e in ENGS}
        self.waited = {e: {} for e in ENGS}
        self.stream = {e: [] for e in ENGS}
        self.last_w = {}
        self.readers = {}

    def sbuf(self, name, shape, dtype):
        return self.es.enter_context(self.nc.sbuf_tensor(name, list(shape), dtype))

    def psum(self, name, shape, dtype):
        return self.es.enter_context(self.nc.psum_tensor(name, list(shape), dtype))

    def _deps(self, R, W):
        deps = set()
        for r in R:
            w = self.last_w.get(r)
            if w is not None:
                deps.add(w)
            if isinstance(r, tuple) and r[0] == "ps":
                rd = self.readers.get(r)
                if rd:
                    for k, v in rd.items():
                        deps.add((k, v))
        for r in W:
            w = self.last_w.get(r)
            if w is not None:
                deps.add(w)
            rd = self.readers.get(r)
            if rd:
                for k, v in rd.items():
                    deps.add((k, v))
        return deps

    def _waits(self, eng, deps):
        need = {}
        for k, v in deps:
            if k == "pe" and eng == "pe":
                continue
            if v > need.get(k, 0):
                need[k] = v
        out = []
        wd = self.waited[eng]
        for k, v in need.items():
            if wd.get(k, 0) >= v:
                continue
            wd[k] = v
            s = self.sem[k] if isinstance(k, str) else self.dsem[k[1]]
            out.append((s, v))
        return out

    def _record(self, my, R, W):
        for r in R:
            self.readers.setdefault(r, {})[my[0]] = my[1]
        for r in W:
            self.last_w[r] = my
            self.readers[r] = {}

    def op(self, eng, fn, R=(), W=()):
        waits = self._waits(eng, self._deps(R, W))
        self.n[eng] += 1
        my = (eng, self.n[eng])
        self.stream[eng].append((waits, fn, self.sem[eng], 1))
        self._record(my, R, W)

    def dma(self, q, out, in_, R=(), W=(), **kw):
        deps = self._deps(R, W)
        lo, hi = self.dpool[q]
        i = self.dnext.get(q, lo)
        self.dnext[q] = lo + (i + 1 - lo) % (hi - lo)
        if self.dtot[i] > 0:
            deps.add((("d", i), self.dtot[i]))
        waits = self._waits(q, deps)
        self.dtot[i] += 16
        my = (("d", i), self.dtot[i])
        nonctg = kw.pop("nonctg", False)
        nc = self.nc

        def fn(e):
            if nonctg:
                with nc.allow_non_contiguous_dma(reason="tiny strided store"):
                    return e.dma_start(out=out, in_=in_, **kw)
            return e.dma_start(out=out, in_=in_, **kw)
        self.stream[q].append((waits, fn, self.dsem[i], 16))
        self._record(my, R, W)

    def emit(self):
        nc = self.nc
        waits = [(self.dsem[i], t) for i, t in enumerate(self.dtot) if t > 0]
        waits += [(self.sem[e], self.n[e]) for e in ENGS if e != "sp" and self.n[e] > 0]
        self.stream["sp"].append((waits, None, None, 0))
        with nc.Block() as block:
            def replay(name, e):
                for waits, fn, s, inc in self.stream[name]:
                    for (ws, wv) in waits:
                        e.wait_ge(ws, wv)
                    if fn is not None:
                        fn(e).then_inc(s, inc)

            @block.tensor
            def _(e):
                replay("pe", e)

            @block.scalar
            def _(e):
                replay("act", e)

            @block.vector
            def _(e):
                replay("dve", e)

            @block.gpsimd
            def _(e):
                replay("pool", e)

            @block.sync
            def _(e):
                replay("sp", e)
        self.es.close()


def bc(ap, shape):
    return ap.unsqueeze(2).broadcast_to(list(shape))


class Builder:
    def __init__(self, do_samples=True, n_st=None, dbg=False, nseq=NSEQ, stop=None, force_last=False):
        self.force_last = force_last
        self.nseq = nseq
        self.stop = stop
        self.do_samples = do_samples
        self.n_st = n_st
        self.dbg_on = dbg
        self.nc = bass.Bass("TRN2", target_bir_lowering=False)
        self.f = FW(self.nc)
        self.ins = {}
        self.outs = {}
        self.dbg_outs = {}

    def din(self, name, shape, dtype=F32):
        self.ins[name] = self.nc.dram_tensor(name, list(shape), dtype, kind="ExternalInput").ap()
        return self.ins[name]

    def dout(self, name, shape):
        self.outs[name] = self.nc.dram_tensor(name, list(shape), F32, kind="ExternalOutput").ap()
        return self.outs[name]

    def dscr(self, name, shape, dtype):
        return self.nc.dram_tensor(name, list(shape), dtype, kind="Internal").ap()

    def dump(self, name, ap, shape, R):
        if not self.dbg_on:
            return
        o = self.nc.dram_tensor("dbg_" + name, list(shape), F32, kind="ExternalOutput").ap()
        self.dbg_outs[name] = o
        self.f.dma("pool", o, ap, R=R)

    def pb(self):
        i = self.pnext
        self.pnext = (self.pnext + 1) % 8
        return self.ps[i], ("ps", i)

    def slabs(self, keys):
        order = self.worder
        live = set(keys)
        pos = None
        try:
            pos = order.index(keys[-1], self.wpos)
            self.wpos = pos
        except ValueError:
            pass
        upcoming = order[pos + 1: pos + 1 + self.NBUF] if pos is not None else []
        protect = set(live)
        for k in keys:
            if k not in self.wloaded:
                self._issue(k, protect)
        for nk in upcoming:
            if nk in self.wloaded:
                protect.add(nk)
                continue
            if not self._issue(nk, protect):
                break
            protect.add(nk)
        return [(self.wbuf[self.wloaded[k]], ("wbuf", self.wloaded[k])) for k in keys]

    def slab(self, kind, idx):
        return self.slabs([(kind, idx)])[0]

    def _issue(self, key, protect):
        kind, idx = key
        held = {b: k for k, b in self.wloaded.items()}
        b = None
        for i in range(self.NBUF):
            cand = (self.wnext + i) % self.NBUF
            if held.get(cand) not in protect or cand not in held:
                b = cand
                break
        if b is None:
            return False
        self.wnext = (b + 1) % self.NBUF
        if b in held:
            del self.wloaded[held[b]]
        src, n, parts = self.wsrc(kind, idx)
        self.f.dma("sp", self.wbuf[b][0:parts, 0:n], src, R=[("wscr", kind, idx)], W=[("wbuf", b)])
        self.wloaded[key] = b
        return True

    def wsrc(self, kind, idx):
        if kind == "A":
            return self.WAb[idx], 4096, 128
        if kind == "M":
            return self.WMb[idx], 2048, 128
        if kind == "S":
            return self.WSb[idx], 4096, 64
        if kind == "D":
            return self.WDb[idx], 5632, 128
        if kind == "T":
            return self.WTb, 512, 128
        raise ValueError(kind)

    def build(self):
        nc, f = self.nc, self.f
        xp = self.din("xp", [NSEQ, SEQ, D])
        memp = self.din("memp", [NSEQ, 256, D])
        WAf = self.din("WA", [NA, 128, 4096])
        WMf = self.din("WM", [8, 128, 2048])
        WSf = self.din("WS", [8, 64, 4096])
        WDf = self.din("WD", [16, 128, 5632])
        WTf = self.din("WT", [128, 512])
        c128 = self.din("c128", [128, 8 * 128 + 512 + 32])
        pvec = self.din("pvec", [128, 512])
        prow = self.din("prow", [128, 32 * 3 + 256])
        if self.do_samples:
            xs_in = self.din("xs", [NSMP, D])
            csk = self.din("csk", [NSMP, 128, 256])
            csv = self.din("csv", [NSMP, 128, 256])
            cmk = self.din("cmk", [NSMP, 256, 1024])
            cmv = self.din("cmv", [NSMP, 256, 1024])
            sssm = self.din("sssm", [NSMP, 2048, 128])
            sconv = self.din("sconv", [NSMP * 3, 3072])
        yp = self.dout("yp", [NSEQ, SEQ, D])
        pk = self.dout("pk", [NSEQ, 128, 256])
        pv = self.dout("pv", [NSEQ, 128, 256])
        pmk = self.dout("pmk", [NSEQ, 256, 1024])
        pmv = self.dout("pmv", [NSEQ, 256, 1024])
        pssm = self.dout("pssm", [NSEQ, 2048, 128])
        pconv = self.dout("pconv", [NSEQ, 3, 3072])
        ys = self.dout("ys", [NSMP, D])
        sk = self.dout("sk", [NSMP, 128, 256])
        sv = self.dout("sv", [NSMP, 128, 256])
        sssm_o = self.dout("sssm_o", [NSMP, 2048, 128])
        sconv_o = self.dout("sconv_o", [NSMP, 3, 3072])
        self.io = {**self.ins, **self.outs}
        self.WAb = self.dscr("WAb", [NA, 128, 4096], BF16)
        self.WMb = self.dscr("WMb", [8, 128, 2048], BF16)
        self.WSb = self.dscr("WSb", [8, 64, 4096], BF16)
        self.WDb = self.dscr("WDb", [16, 128, 5632], BF16)
        self.WTb = self.dscr("WTb", [128, 512], BF16)

        def cast(dst, src, n, res):
            if n > 2048:
                assert n % 2048 == 0 or n == 5632
                a = n // 2048 if n % 2048 == 0 else 4
                f.dma("pool", dst.rearrange("p (a b) -> p a b", a=a), src.rearrange("p (a b) -> p a b", a=a), W=[res])
            else:
                f.dma("pool", dst, src, W=[res])
        for i in range(NA):
            cast(self.WAb[i], WAf[i], 4096, ("wscr", "A", i))
        for i in range(8):
            cast(self.WMb[i], WMf[i], 2048, ("wscr", "M", i))
            cast(self.WSb[i], WSf[i], 4096, ("wscr", "S", i))
        for i in range(16):
            cast(self.WDb[i], WDf[i], 5632, ("wscr", "D", i))
        cast(self.WTb, WTf, 512, ("wscr", "T", 0))

        self.ps = [f.psum("ps%d" % i, [128, 512], F32) for i in range(8)]
        self.pnext = 0
        self.NBUF = 4
        self.wbuf = [f.sbuf("wbuf%d" % i, [128, 5632], BF16) for i in range(self.NBUF)]
        self.wnext = 0
        self.wloaded = {}
        self.worder = []
        self.wpos = 0

        C = f.sbuf("c128t", [128, 8 * 128 + 512 + 32], F32)
        f.dma("sp", C[:], c128, W=["C"])
        self.identf = C[:, 0:128]
        self.Tm = C[:, 128:256]
        self.Um = C[:, 256:384]
        self.onesf = C[:, 384:512]
        self.Dmd = C[:, 512:640]
        self.Dmp = C[:, 640:768]
        self.ohs = C[:, 1536:1568]
        PV = f.sbuf("pvect", [128, 512], F32)
        f.dma("sp", PV[:], pvec, W=["PV"])
        PR = f.sbuf("prowt", [128, 352], F32)
        f.dma("sp", PR[:], prow, W=["PR"])
        self.PV, self.PR = PV, PR
        self.gmixT = PV[:, 0:16]
        self.gffnT = PV[:, 16:32]
        self.gmemT = PV[:, 32:48]
        self.ssdnT = PV[:, 48:64]
        self.cwT = PV[:, 64:160]
        self.cbT = PV[:, 160:184]
        self.gq = PV[:, 184:185]
        self.gk = PV[:, 185:186]
        self.gqm = PV[:, 186:188]
        self.sink_raw = PV[:, 188:204]
        self.dskT = PV[:, 204:220]
        self.dm16 = PV[:, 220:236]
        self.dtb = PR[:, 0:32]
        self.alog = PR[:, 32:64]
        self.dsk = PR[:, 64:96]
        self.gkm = PR[:, 96:352]
        CB = f.sbuf("cbf", [128, 128 * 3 + 512], BF16)
        f.op("dve", lambda e: e.tensor_copy(out=CB[:, 0:128], in_=C[:, 0:128]), R=["C"], W=["CBc"])
        f.op("dve", lambda e: e.tensor_copy(out=CB[:, 128:256], in_=C[:, 384:512]), R=["C"], W=["CBc"])
        f.op("dve", lambda e: e.tensor_copy(out=CB[:, 256:384], in_=C[:, 768:896]), R=["C"], W=["CBc"])
        f.op("dve", lambda e: e.tensor_copy(out=CB[:, 384:896], in_=C[:, 1024:1536]), R=["C"], W=["CBc"])
        self.identb = CB[:, 0:128]
        self.onesb = CB[:, 128:256]
        self.blockones = CB[:, 256:384]
        self.maskD = CB[:, 384:896]
        SM = f.sbuf("smallp", [128, 64], F32)
        self.SM = SM
        f.op("act", lambda e: e.mul(out=SM[:, 0:1], in_=PV[:, 184:185], mul=0.125), R=["PV"], W=["SM"])
        f.op("act", lambda e: e.activation(out=SM[:, 1:17], in_=PV[:, 188:204], func=AF.Exp), R=["PV"], W=["SM"])
        f.op("act", lambda e: e.activation(out=SM[:, 17:49], in_=PR[:, 32:64], func=AF.Exp), R=["PR"], W=["SM"])
        f.op("act", lambda e: e.mul(out=SM[:, 17:49], in_=SM[:, 17:49], mul=-1.0), R=["SM"], W=["SM"])
        self.gq8 = SM[:, 0:1]
        self.esink = SM[:, 1:17]
        self.a_bc = SM[:, 17:49]

        self.hT = f.sbuf("hT", [128, KC, NT], BF16)
        self.xtok = f.sbuf("xtok", [128, D], F32)
        self.xn = f.sbuf("xn", [128, D], BF16)
        self.st1 = f.sbuf("st1", [128, 8], F32)
        self.qT = f.sbuf("qT", [128, 8, NT], BF16)
        self.kT = f.sbuf("kT", [128, 2, 2 * 128], BF16)
        self.vtok = f.sbuf("vtok", [128, 2, 256], BF16)
        self.zs = f.sbuf("zs", [128, D], BF16)
        self.xbcT = f.sbuf("xbcT", [128, 24, 3 + NT], BF16)
        self.hist = f.sbuf("hist", [128, 24, 3], BF16)
        self.qmT = f.sbuf("qmT", [128, 8, NT], BF16)
        self.aT = f.sbuf("aT", [64, 16, NT], BF16)
        self.sT = f.sbuf("sT", [128, KC, NT], BF16)
        self.mT = f.sbuf("mT", [128, 8, NT], BF16)
        self.actT = f.sbuf("actT", [128, 44, NT], BF16)
        self.hst = f.sbuf("hst", [128, D], F32)
        self.hbf = f.sbuf("hbf", [128, D], BF16)
        self.KTm = f.sbuf("KTm", [128, 8, 256], BF16)
        self.Vm = f.sbuf("Vm", [128, 2, 1024], BF16)
        self.lastst = f.sbuf("lastst", [128, 96 + 256 + 256 + 256], F32)
        self.dtraw = f.sbuf("dtraw", [128, 32], F32)
        self.l3t = f.sbuf("l3t", [128, 224], F32)
        f.op("pool", lambda e: e.memset(self.l3t[:], 0.0), W=["lastst"])
        self.macc = [f.sbuf("macc%d" % i, [128, NT], F32) for i in range(2)]
        self.tmpf = [f.sbuf("tmpf%d" % i, [128, 512], F32) for i in range(6)]
        self.tmpb = [f.sbuf("tmpb%d" % i, [128, 512], BF16) for i in range(4)]
        self.tfn = 0
        self.tbn = 0
        self.ssdbig = f.sbuf("ssdbig", [128, 4 * D], BF16)
        self.xs_tok = self.ssdbig[:, 0:D]
        self.xdt = self.ssdbig[:, D:2 * D]
        self.xw = self.ssdbig[:, 2 * D:3 * D]
        self.s_tok = self.ssdbig[:, 3 * D:4 * D]
        self.stage = self.ssdbig[:].bitcast(F32)
        self.STAGE = ["xs_tok", "xdt", "xw", "s_tok"]
        self.bm_tok = f.sbuf("bm_tok", [128, 512], BF16)
        self.ssd_s = f.sbuf("ssd_s", [128, 256], F32)
        self.HI = f.sbuf("HI", [128, 128], BF16)
        self.LO = f.sbuf("LO", [128, 128], BF16)
        self.lhsD = f.sbuf("lhsD", [128, 128], BF16)
        self.rhsD = f.sbuf("rhsD", [128, 32, 128], BF16)
        self.dA4 = f.sbuf("dA4", [128, 4, 32], F32)
        self.CBt = f.sbuf("CBt", [128, 4, 128], F32)
        self.Wp = [f.sbuf("Wp%d" % i, [128, 4, 128], BF16) for i in range(2)]
        if self.do_samples:
            self.cbuf = f.sbuf("cbuf", [128, 24, 4, NSMP], BF16)
            self.cvs = f.sbuf("cvs", [128, 24, NSMP], F32)
            self.hbf32 = f.sbuf("hbf32", [NSMP, 1024], F32)
            self.cdT = f.sbuf("cdT", [128, 16, NSMP], F32)
            self.dtT = f.sbuf("dtT", [128, 16, NSMP], F32)
            self.xdtT = f.sbuf("xdtT", [128, 16, NSMP], F32)
            self.bcs = f.sbuf("bcs", [NSMP, 1024], F32)
            self.oh16 = f.sbuf("oh16", [NSMP, NSMP, 128], F32)
            self.yT = f.sbuf("yT", [128, 16, NSMP], F32)
            self.y2 = f.sbuf("y2", [128, 16, NSMP], F32)
        f.op("dve", lambda e: e.memset(self.lhsD[64:128, :], 1.0), W=["lhsD"])
        f.op("dve", lambda e: e.tensor_copy(out=self.rhsD[0:64], in_=bc(self.ohs[0:64], [64, 32, 128])), R=["C"], W=["rhsD"])

        n_st = SEQ // NT if self.n_st is None else self.n_st
        for seq in range(self.nseq):
            self.seq_start(seq)
            if self.stop == 'seq_start':
                continue
            for st in range(n_st):
                self.worder = self.super_order()
                self.wpos = 0
                self.super_tile(seq, st, last=(st == SEQ // NT - 1) or (self.force_last and st == n_st - 1))
            if n_st == SEQ // NT or self.force_last:
                self.seq_end(seq)
        if self.do_samples:
            self.sample_group()
        f.emit()
        return nc

    def tf(self):
        i = self.tfn
        self.tfn = (self.tfn + 1) % len(self.tmpf)
        return self.tmpf[i], ("tmpf", i)

    def tb(self):
        i = self.tbn
        self.tbn = (self.tbn + 1) % len(self.tmpb)
        return self.tmpb[i], ("tmpb", i)

    def super_order(self):
        o = []
        o += [("A", SA_Q + i) for i in range(4)] + [("A", SA_K)] + [("A", SA_XBC + i) for i in range(12)]
        o += [("A", SA_QM + i) for i in range(4)] + [("A", SA_V)] + [("A", SA_Z + i) for i in range(8)] + [("T", 0)]
        for s in range(8):
            o += [("A", SA_G0 + s), ("S", s), ("A", SA_G1 + s), ("A", SA_USSD + s), ("A", SA_G2 + s), ("M", s)]
        o += [("A", SA_OUT + i) for i in range(8)]
        for s in range(22):
            o += [("A", SA_GATE + s), ("A", SA_UP + s)]
        o += [("D", i) for i in range(16)]
        return o

    def fm_mm(self, ps_ap, psres, wt, wres, col0, actT, actres, ntok, kcs=KC, M=128):
        wv = wt[:, 0:kcs * CW].rearrange("p (k c) -> p k c", k=kcs)
        for kc in range(kcs):
            self.f.op("pe", lambda e, kc=kc: e.matmul(ps_ap, lhsT=wv[:, kc, col0:col0 + M], rhs=actT[:, kc, 0:ntok],
                                                       start=(kc == 0), stop=(kc == kcs - 1)),
                      R=[wres, actres], W=[psres])

    def tm_mm(self, ps_ap, psres, wt, wres, ncols, actT, actres, t0, bs, kcs=KC, cw=CW, col0=0):
        wv = wt[:, 0:kcs * cw].rearrange("p (k c) -> p k c", k=kcs)
        for kc in range(kcs):
            self.f.op("pe", lambda e, kc=kc: e.matmul(ps_ap, lhsT=actT[:, kc, t0:t0 + bs], rhs=wv[:, kc, col0:col0 + ncols],
                                                       start=(kc == 0), stop=(kc == kcs - 1)),
                      R=[wres, actres], W=[psres])

    def norm_T(self, x_ap, xres, bs, gT, outT, outres, c0):
        f = self.f
        st1, xn = self.st1, self.xn
        f.op("dve", lambda e: e.memset(st1[0:bs, 0:1], 0.0), W=["st1"])
        f.op("act", lambda e: e.activation(out=xn[0:bs, :], in_=x_ap, func=AF.Square, accum_out=st1[0:bs, 0:1]),
             R=[xres], W=["xn", "st1"])
        f.op("act", lambda e: e.activation(out=st1[0:bs, 1:2], in_=st1[0:bs, 0:1], func=AF.Sqrt, scale=1.0 / D, bias=EPS),
             R=["st1"], W=["st1"])
        f.op("dve", lambda e: e.reciprocal(out=st1[0:bs, 2:3], in_=st1[0:bs, 1:2]), R=["st1"], W=["st1"])
        f.op("dve", lambda e: e.tensor_scalar(out=xn[0:bs, :], in0=x_ap, scalar1=st1[0:bs, 2:3], scalar2=None, op0=ALU.mult),
             R=[xres, "st1"], W=["xn"])
        for half in range(2):
            pt, pres = self.pb()
            pbf = pt.bitcast(BF16)
            for k in range(8):
                kc = half * 8 + k
                f.op("pe", lambda e, k=k, kc=kc, pbf=pbf: e.transpose(out=pbf[:, k * bs:(k + 1) * bs], in_=xn[0:bs, kc * 128:(kc + 1) * 128],
                                                                      identity=self.identb[0:bs, 0:bs]),
                     R=["xn", "CBc"], W=[pres])
            f.op("dve", lambda e, half=half, pbf=pbf: e.tensor_tensor(
                out=outT[:, half * 8:half * 8 + 8, c0:c0 + bs],
                in0=pbf[:, 0:8 * bs].rearrange("p (k t) -> p k t", k=8),
                in1=bc(gT[:, half * 8:half * 8 + 8], [128, 8, bs]), op=ALU.mult),
                R=[pres, "PV"], W=[outres])

    def rsqrt_bc(self, ps2, ps2res, n, scale):
        f = self.f
        t1, r1 = self.tf()
        f.op("act", lambda e: e.activation(out=t1[:, 0:n], in_=ps2, func=AF.Sqrt, scale=scale, bias=EPS), R=[ps2res], W=[r1])
        f.op("dve", lambda e: e.reciprocal(out=t1[:, 0:n], in_=t1[:, 0:n]), R=[r1], W=[r1])
        return t1[:, 0:n], r1

    def seq_start(self, seq):
        f = self.f
        io = self.io
        f.op("pool", lambda e: e.memset(self.hst[:], 0.0), W=["hst"])
        f.op("pool", lambda e: e.memset(self.hbf[:], 0.0), W=["hbf"])
        f.op("pool", lambda e: e.memset(self.hist[:], 0.0), W=["hist"])
        self.worder = [("A", SA_MEM + i) for i in range(8)]
        self.wpos = 0
        stage = self.stage
        SR = self.STAGE
        kst = stage[:, 0:2048].rearrange("p (m c) -> p m c", m=2)
        vst = stage[:, 2048:4096].rearrange("p (m c) -> p m c", m=2)
        for mt in range(2):
            xt = self.xtok
            f.dma("pool", xt[:], io["memp"][seq, mt * 128:(mt + 1) * 128, :], W=["x"])
            self.norm_T(xt[:], "x", 128, self.gmemT, self.hT, "hT", 0)
            for s in range(8):
                wt, wres = self.slab("A", SA_MEM + s)
                pt, pres = self.pb()
                self.tm_mm(pt[:, 0:256], pres, wt, wres, 256, self.hT, "hT", 0, 128)
                if s < 4:
                    hm = s
                    st1 = self.st1
                    jk, jkres = self.tb()
                    f.op("dve", lambda e: e.memset(st1[:, 4:5], 0.0), W=["st1b"])
                    f.op("act", lambda e, pt=pt, jk=jk: e.activation(out=jk[:, 0:256], in_=pt[:, 0:256], func=AF.Square,
                                                                     accum_out=st1[:, 4:5]), R=[pres], W=[jkres, "st1b"])
                    f.op("act", lambda e: e.activation(out=st1[:, 5:6], in_=st1[:, 4:5], func=AF.Sqrt, scale=1.0 / 256, bias=EPS),
                         R=["st1b"], W=["st1b"])
                    f.op("dve", lambda e: e.reciprocal(out=st1[:, 6:7], in_=st1[:, 5:6]), R=["st1b"], W=["st1b"])
                    f.op("dve", lambda e, pt=pt, mt=mt, hm=hm: e.scalar_tensor_tensor(
                        out=kst[:, mt, hm * 256:(hm + 1) * 256], in0=pt[:, 0:256], scalar=st1[:, 6:7], in1=self.gkm,
                        op0=ALU.mult, op1=ALU.mult), R=[pres, "st1b", "PR"], W=SR)
                else:
                    hm = s - 4
                    f.op("act", lambda e, pt=pt, mt=mt, hm=hm: e.activation(out=vst[:, mt, hm * 256:(hm + 1) * 256], in_=pt[:, 0:256],
                                                                           func=AF.Copy), R=[pres], W=SR)
            self.worder = [("A", SA_MEM + i) for i in range(8)]
            self.wpos = 0
        f.dma("pool", io["pmk"][seq].rearrange("(m p) c -> p m c", p=128), kst, R=SR)
        f.dma("pool", io["pmv"][seq].rearrange("(m p) c -> p m c", p=128), vst, R=SR)
        f.op("pool", lambda e: e.tensor_copy(out=self.Vm[:], in_=vst), R=SR, W=["Vm"])
        kb = self.xn
        f.op("dve", lambda e: e.tensor_copy(out=kb[:], in_=stage[:, 0:2048]), R=SR, W=["xn"])
        self.kmem_T(kb, "xn", self.KTm, "KTm")

    def kmem_T(self, kb, kres, KTm, ktres):
        f = self.f
        kbv = kb[:, 0:2048].rearrange("p (m c) -> p m c", m=2)
        for mt in range(2):
            pt, pres = self.pb()
            pbf = pt.bitcast(BF16)
            for c in range(8):
                f.op("pe", lambda e, c=c, mt=mt, pbf=pbf: e.transpose(out=pbf[:, c * 128:(c + 1) * 128], in_=kbv[:, mt, c * 128:(c + 1) * 128],
                                                                      identity=self.identb), R=[kres, "CBc"], W=[pres])
            f.op("act", lambda e, mt=mt, pbf=pbf: e.activation(out=KTm[:, :, mt * 128:(mt + 1) * 128],
                                                               in_=pbf[:, 0:1024].rearrange("p (c t) -> p c t", c=8), func=AF.Copy),
                 R=[pres], W=[ktres])

    def qk_evac(self, pt, pres, n, gcol, out_ap, outres, out32=None, out32res=None):
        f = self.f
        sq, sqres = self.tb()
        f.op("act", lambda e: e.activation(out=sq[:, 0:n], in_=pt[:, 0:n], func=AF.Square), R=[pres], W=[sqres])
        p2, p2res = self.pb()
        f.op("pe", lambda e: e.matmul(p2[:, 0:n], lhsT=self.blockones, rhs=sq[:, 0:n], start=True, stop=True),
             R=[sqres, "CBc"], W=[p2res])
        rr, rres = self.rsqrt_bc(p2[:, 0:n], p2res, n, 1.0 / 64)
        f.op("dve", lambda e: e.scalar_tensor_tensor(out=out_ap, in0=pt[:, 0:n], scalar=gcol, in1=rr, op0=ALU.mult, op1=ALU.mult),
             R=[pres, rres, "PV", "SM"], W=[outres])
        if out32 is not None:
            f.op("dve", lambda e: e.scalar_tensor_tensor(out=out32, in0=pt[:, 0:n], scalar=gcol, in1=rr, op0=ALU.mult, op1=ALU.mult),
                 R=[pres, rres, "PV", "SM"], W=[out32res])

    def in_proj_fm(self, hT, ntok, qT, kT_out, xbc_out, qmT, last3=None, k32=None):
        f = self.f
        for s in range(4):
            wt, wres = self.slab("A", SA_Q + s)
            for mm in range(2):
                c = s * 2 + mm
                pt, pres = self.pb()
                self.fm_mm(pt[:, 0:ntok], pres, wt, wres, mm * 128, hT, "hT", ntok)
                self.qk_evac(pt, pres, ntok, self.gq8, qT[:, c, 0:ntok], "qT")
        wt, wres = self.slab("A", SA_K)
        for pair in range(2):
            pt, pres = self.pb()
            self.fm_mm(pt[:, 0:ntok], pres, wt, wres, pair * 128, hT, "hT", ntok)
            if k32 is not None:
                self.qk_evac(pt, pres, ntok, self.gk, kT_out(pair), "kT", out32=k32(pair), out32res="lastst")
            else:
                self.qk_evac(pt, pres, ntok, self.gk, kT_out(pair), "kT")
        for s in range(12):
            wt, wres = self.slab("A", SA_XBC + s)
            if getattr(self, "xbc_hook", None) is not None:
                self.xbc_hook(s, wt, wres)
            for mm in range(2):
                c = s * 2 + mm
                pt, pres = self.pb()
                self.fm_mm(pt[:, 0:ntok], pres, wt, wres, mm * 128, hT, "hT", ntok)
                f.op("act", lambda e, pt=pt, c=c: e.activation(out=xbc_out(c), in_=pt[:, 0:ntok], func=AF.Copy), R=[pres], W=[("xbcT", c)])
                if last3 is not None:
                    f.op("dve", lambda e, pt=pt, c=c: e.tensor_copy(out=last3[:, c, :], in_=pt[:, ntok - 4:ntok]), R=[pres], W=["lastst"])
        for hm in range(4):
            wt, wres = self.slab("A", SA_QM + hm)
            pa, ares = self.pb()
            pbk, bres = self.pb()
            self.fm_mm(pa[:, 0:ntok], ares, wt, wres, 0, hT, "hT", ntok)
            self.fm_mm(pbk[:, 0:ntok], bres, wt, wres, 128, hT, "hT", ntok)
            sq, sqres = self.tb()
            f.op("act", lambda e, pa=pa, sq=sq: e.activation(out=sq[:, 0:ntok], in_=pa[:, 0:ntok], func=AF.Square), R=[ares], W=[sqres])
            f.op("act", lambda e, pbk=pbk, sq=sq: e.activation(out=sq[:, 256:256 + ntok], in_=pbk[:, 0:ntok], func=AF.Square), R=[bres], W=[sqres])
            p2, p2res = self.pb()
            f.op("pe", lambda e, p2=p2, sq=sq: e.matmul(p2[:, 0:ntok], lhsT=self.onesb, rhs=sq[:, 0:ntok], start=True, stop=False),
                 R=[sqres, "CBc"], W=[p2res])
            f.op("pe", lambda e, p2=p2, sq=sq: e.matmul(p2[:, 0:ntok], lhsT=self.onesb, rhs=sq[:, 256:256 + ntok], start=False, stop=True),
                 R=[sqres, "CBc"], W=[p2res])
            rr, rres = self.rsqrt_bc(p2[:, 0:ntok], p2res, ntok, 1.0 / 256)
            for dc, (pp, ppres) in enumerate(((pa, ares), (pbk, bres))):
                f.op("dve", lambda e, pp=pp, dc=dc, hm=hm, rr=rr: e.scalar_tensor_tensor(
                    out=qmT[:, hm * 2 + dc, 0:ntok], in0=pp[:, 0:ntok], scalar=self.gqm[:, dc:dc + 1], in1=rr,
                    op0=ALU.mult, op1=ALU.mult), R=[ppres, rres, "PV"], W=["qmT"])

    def swa_heads(self, q_rhs, kprev, kcur, vprev, vcur, nq, kcn, Dmp, Dmc, out, R_in, W_out):
        f = self.f
        n4 = 4 * nq
        for h in range(4):
            PTs = []
            for which in (0, 1):
                if which == 0 and kprev is None:
                    continue
                kk = kprev(h) if which == 0 else kcur(h)
                kn = 128 if which == 0 else kcn
                Dm = Dmp if which == 0 else Dmc
                pt, pres = self.pb()
                f.op("pe", lambda e, pt=pt, kk=kk, kn=kn, h=h: e.matmul(pt[0:kn, 0:n4], lhsT=kk, rhs=q_rhs(h), start=True, stop=True),
                     R=R_in, W=[pres])
                tt, tres = self.tf()
                for g in range(4):
                    sl = -SLOPES[h * 4 + g]
                    f.op("dve", lambda e, pt=pt, tt=tt, g=g, sl=sl, kn=kn, Dm=Dm: e.scalar_tensor_tensor(
                        out=tt[0:kn, g * nq:(g + 1) * nq], in0=Dm, scalar=sl, in1=pt[0:kn, g * nq:(g + 1) * nq],
                        op0=ALU.mult, op1=ALU.add), R=[pres, "C", "PV"], W=[tres])
                PT, ptres = self.tb()
                f.op("act", lambda e, PT=PT, tt=tt, kn=kn: e.activation(out=PT[0:kn, 0:n4], in_=tt[0:kn, 0:n4], func=AF.Exp),
                     R=[tres], W=[ptres])
                PTs.append((PT, ptres, kn, which))
            po, pores = self.pb()
            pd, pdres = self.pb()
            nP = len(PTs)
            for i, (PT, ptres, kn, which) in enumerate(PTs):
                vv = vprev(h) if which == 0 else vcur(h)
                f.op("pe", lambda e, PT=PT, kn=kn, vv=vv, i=i, po=po: e.matmul(po[0:64, 0:n4], lhsT=vv, rhs=PT[0:kn, 0:n4],
                                                                              start=(i == 0), stop=(i == len(PTs) - 1)),
                     R=R_in + [ptres], W=[pores])
            for i, (PT, ptres, kn, which) in enumerate(PTs):
                f.op("pe", lambda e, PT=PT, kn=kn, i=i, pd=pd: e.matmul(pd[0:64, 0:n4], lhsT=self.onesb[0:kn, 0:64], rhs=PT[0:kn, 0:n4],
                                                                       start=(i == 0), stop=(i == nP - 1)),
                     R=["CBc", ptres], W=[pdres])
            dn, dnres = self.tf()
            f.op("dve", lambda e, h=h, dn=dn, pd=pd: e.tensor_tensor(
                out=dn[0:64, 0:n4].rearrange("p (g q) -> p g q", g=4), in0=pd[0:64, 0:n4].rearrange("p (g q) -> p g q", g=4),
                in1=bc(self.esink[0:64, h * 4:h * 4 + 4], [64, 4, nq]), op=ALU.add), R=[pdres, "SM"], W=[dnres])
            f.op("dve", lambda e, dn=dn: e.reciprocal(out=dn[0:64, 0:n4], in_=dn[0:64, 0:n4]), R=[dnres], W=[dnres])
            f.op("dve", lambda e, h=h, dn=dn, po=po: e.tensor_tensor(
                out=out(h), in0=po[0:64, 0:n4].rearrange("p (g q) -> p g q", g=4),
                in1=dn[0:64, 0:n4].rearrange("p (g q) -> p g q", g=4), op=ALU.mult), R=[pores, dnres], W=W_out)

    def mem_heads(self, q_rhs, KTm, ktres, Vm, vres, nq, out, R_in, W_out):
        f = self.f
        for hm in range(4):
            pt, pres = self.pb()
            for mt in range(2):
                for dc in range(2):
                    f.op("pe", lambda e, mt=mt, dc=dc, hm=hm, pt=pt: e.matmul(
                        pt[:, mt * nq:(mt + 1) * nq], lhsT=KTm[:, hm * 2 + dc, mt * 128:(mt + 1) * 128], rhs=q_rhs(hm * 2 + dc),
                        start=(dc == 0), stop=(dc == 1)), R=R_in + [ktres], W=[pres])
            PT, ptres = self.tb()
            f.op("act", lambda e, PT=PT, pt=pt: e.activation(out=PT[:, 0:2 * nq], in_=pt[:, 0:2 * nq], func=AF.Exp, scale=1.0 / 16),
                 R=[pres], W=[ptres])
            pd, pdres = self.pb()
            for mt in range(2):
                f.op("pe", lambda e, mt=mt, PT=PT, pd=pd: e.matmul(pd[:, 0:nq], lhsT=self.onesb, rhs=PT[:, mt * nq:(mt + 1) * nq],
                                                                   start=(mt == 0), stop=(mt == 1)), R=["CBc", ptres], W=[pdres])
            dn, dnres = self.tf()
            f.op("dve", lambda e, dn=dn, pd=pd: e.reciprocal(out=dn[:, 0:nq], in_=pd[:, 0:nq]), R=[pdres], W=[dnres])
            po, pores = self.pb()
            for dc in range(2):
                for mt in range(2):
                    f.op("pe", lambda e, mt=mt, dc=dc, hm=hm, PT=PT, po=po: e.matmul(
                        po[:, dc * nq:(dc + 1) * nq], lhsT=Vm[:, mt, hm * 256 + dc * 128: hm * 256 + (dc + 1) * 128],
                        rhs=PT[:, mt * nq:(mt + 1) * nq], start=(mt == 0), stop=(mt == 1)), R=[vres, ptres], W=[pores])
            for dc in range(2):
                f.op("dve", lambda e, dc=dc, hm=hm, dn=dn, po=po: e.tensor_tensor(out=out(hm * 2 + dc), in0=po[:, dc * nq:(dc + 1) * nq],
                                                                                 in1=dn[:, 0:nq], op=ALU.mult), R=[pores, dnres], W=W_out)

    def conv_fm(self, tap, ntok, out, res_of, save_hist=None):
        f = self.f
        for c in range(24):
            acc, ares = self.tf()
            f.op("dve", lambda e, c=c, acc=acc: e.tensor_scalar(out=acc[:, 0:ntok], in0=tap(c, 0), scalar1=self.cwT[:, c * 4:c * 4 + 1],
                                                                scalar2=self.cbT[:, c:c + 1], op0=ALU.mult, op1=ALU.add),
                 R=[res_of(c), "PV", "xbch"], W=[ares])
            for j in range(1, 4):
                f.op("dve", lambda e, c=c, j=j, acc=acc: e.scalar_tensor_tensor(
                    out=acc[:, 0:ntok], in0=tap(c, j), scalar=self.cwT[:, c * 4 + j:c * 4 + j + 1], in1=acc[:, 0:ntok],
                    op0=ALU.mult, op1=ALU.add), R=[res_of(c), "PV", ares, "xbch"], W=[ares])
            if save_hist is not None:
                save_hist(c)
            f.op("act", lambda e, c=c, acc=acc: e.activation(out=out(c), in_=acc[:, 0:ntok], func=AF.Silu), R=[ares], W=[res_of(c)])

    def ssd_block(self):
        f = self.f
        S = self.ssd_s
        cv = lambda c: self.xbcT[:, c, 3:3 + 128]
        XS = [("xbcT", c) for c in range(16)]
        BMr = [("xbcT", 16 + g) for g in range(4)]
        CMr = [("xbcT", 20 + g) for g in range(4)]
        for half in range(2):
            pt, pres = self.pb()
            pbf = pt.bitcast(BF16)
            for k in range(8):
                f.op("pe", lambda e, k=k, half=half, pbf=pbf: e.transpose(out=pbf[:, k * 128:(k + 1) * 128], in_=cv(half * 8 + k),
                                                                          identity=self.identb), R=[("xbcT", half * 8 + k), "CBc"], W=[pres])
            f.op("act", lambda e, half=half, pbf=pbf: e.activation(out=self.xs_tok[:, half * 1024:(half + 1) * 1024], in_=pbf[:, 0:1024],
                                                                   func=AF.Copy), R=[pres], W=["xs_tok"])
        pt, pres = self.pb()
        pbf = pt.bitcast(BF16)
        for g in range(4):
            f.op("pe", lambda e, g=g, pbf=pbf: e.transpose(out=pbf[:, g * 128:(g + 1) * 128], in_=cv(16 + g), identity=self.identb),
                 R=[BMr[g], "CBc"], W=[pres])
        f.op("act", lambda e, pbf=pbf: e.activation(out=self.bm_tok[:], in_=pbf[:, 0:512], func=AF.Copy), R=[pres], W=["bm_tok"])
        dt = S[:, 32:64]
        f.op("dve", lambda e: e.tensor_tensor(out=S[:, 0:32], in0=self.dtraw[:], in1=self.dtb, op=ALU.add), R=["dtraw", "PR"], W=["S"])
        f.op("act", lambda e: e.activation(out=S[:, 0:32], in_=S[:, 0:32], func=AF.Exp), R=["S"], W=["S"])
        f.op("act", lambda e: e.activation(out=dt, in_=S[:, 0:32], func=AF.Ln, bias=1.0), R=["S"], W=["S"])
        f.op("dve", lambda e: e.tensor_tensor(out=S[:, 64:96], in0=dt, in1=self.a_bc, op=ALU.mult), R=["S", "SM"], W=["S"])
        f.op("dve", lambda e: e.tensor_copy(out=self.dA4[:], in_=S[:, 64:96].unsqueeze(1).broadcast_to([128, 4, 32])), R=["S"], W=["dA4"])
        pa, pares = self.pb()
        for i, lh in enumerate((self.Tm, self.Um, self.onesf)):
            f.op("pe", lambda e, i=i, lh=lh: e.matmul(pa[:, i * 32:(i + 1) * 32], lhsT=lh, rhs=S[:, 64:96], start=True, stop=True),
                 R=["S", "C"], W=[pares])
        f.op("pe", lambda e: e.matmul(pa[:, 128:256], lhsT=self.dA4[:].rearrange("p a b -> p (a b)"), rhs=self.Tm, start=True, stop=True),
             R=["dA4", "C"], W=[pares])
        f.op("act", lambda e: e.activation(out=S[:, 96:192], in_=pa[:, 0:96], func=AF.Exp), R=[pares], W=["S"])
        eacs, dte, cd = S[:, 96:128], S[:, 128:160], S[:, 160:192]
        f.op("dve", lambda e: e.tensor_tensor(out=S[:, 192:224], in0=dt, in1=dte, op=ALU.mult), R=["S"], W=["S"])
        xs3 = self.xs_tok.rearrange("p (h d) -> p h d", h=32)
        f.op("dve", lambda e: e.tensor_tensor(out=self.xdt.rearrange("p (h d) -> p h d", h=32), in0=xs3, in1=bc(dt, [128, 32, 64]),
                                              op=ALU.mult), R=["xs_tok", "S"], W=["xdt"])
        f.op("dve", lambda e: e.tensor_tensor(out=self.xw.rearrange("p (h d) -> p h d", h=32), in0=xs3,
                                              in1=bc(S[:, 192:224], [128, 32, 64]), op=ALU.mult), R=["xs_tok", "S"], W=["xw"])
        HI, LO = self.HI, self.LO
        f.op("act", lambda e: e.activation(out=HI[:], in_=pa[:, 128:256], func=AF.Copy), R=[pares], W=["HI"])
        f.op("dve", lambda e: e.tensor_tensor(out=LO[:], in0=pa[:, 128:256], in1=HI[:], op=ALU.subtract), R=[pares, "HI"], W=["LO"])
        f.op("act", lambda e: e.mul(out=self.lhsD[0:32, :], in_=HI[0:32, :], mul=-1.0), R=["HI"], W=["lhsD"])
        f.op("act", lambda e: e.mul(out=self.lhsD[32:64, :], in_=LO[32:64, :], mul=-1.0), R=["LO"], W=["lhsD"])
        f.op("dve", lambda e: e.tensor_tensor(out=self.rhsD[64:96], in0=HI[64:96, :].unsqueeze(1).broadcast_to([32, 32, 128]),
                                              in1=bc(self.ohs[64:96], [32, 32, 128]), op=ALU.mult), R=["HI", "C"], W=["rhsD"])
        f.op("dve", lambda e: e.tensor_tensor(out=self.rhsD[96:128], in0=LO[96:128, :].unsqueeze(1).broadcast_to([32, 32, 128]),
                                              in1=bc(self.ohs[96:128], [32, 32, 128]), op=ALU.mult), R=["LO", "C"], W=["rhsD"])
        pc, pcres = self.pb()
        for g in range(4):
            f.op("pe", lambda e, g=g: e.matmul(pc[:, g * 128:(g + 1) * 128], lhsT=cv(16 + g), rhs=cv(20 + g), start=True, stop=True),
                 R=[BMr[g], CMr[g]], W=[pcres])
        f.op("act", lambda e: e.activation(out=self.CBt[:].rearrange("p g l -> p (g l)"), in_=pc[:, 0:512], func=AF.Copy), R=[pcres], W=["CBt"])
        f.op("dve", lambda e: e.memset(S[:, 224:228], 0.0), W=["Sq"])
        for g in range(4):
            pyd, pydres = self.pb()
            pyo, pyores = self.pb()
            f.op("pe", lambda e, g=g, pyo=pyo: e.matmul(pyo[:, 0:512], lhsT=cv(20 + g), rhs=self.hbf[:, g * 512:(g + 1) * 512], start=True, stop=True),
                 R=[CMr[g], "hbf"], W=[pyores])
            for jj in range(2):
                j = g * 2 + jj
                pD, pDres = self.pb()
                f.op("pe", lambda e, j=j, pD=pD: e.matmul(pD[:, 0:512], lhsT=self.lhsD[:], rhs=self.rhsD[:, 4 * j:4 * j + 4, :].rearrange("p a b -> p (a b)"),
                                                          start=True, stop=False), R=["lhsD", "rhsD"], W=[pDres])
                f.op("pe", lambda e, pD=pD: e.matmul(pD[:, 0:512], lhsT=self.identb, rhs=self.maskD, start=False, stop=True),
                     R=["CBc"], W=[pDres])
                E, eres = self.tf()
                f.op("act", lambda e, E=E, pD=pD: e.activation(out=E[:, 0:512], in_=pD[:, 0:512], func=AF.Exp), R=[pDres], W=[eres])
                Wp = self.Wp[jj]
                f.op("dve", lambda e, E=E, Wp=Wp, g=g: e.tensor_tensor(
                    out=Wp[:], in0=E[:, 0:512].rearrange("p (a l) -> p a l", a=4),
                    in1=self.CBt[:, g, :].unsqueeze(1).broadcast_to([128, 4, 128]), op=ALU.mult), R=[eres, "CBt"], W=[("Wp", jj)])
                for h4 in range(4):
                    h = 4 * j + h4
                    f.op("pe", lambda e, Wp=Wp, h4=h4, h=h, pyd=pyd: e.matmul(pyd[:, (h % 8) * 64:(h % 8 + 1) * 64], lhsT=Wp[:, h4, :],
                                                                             rhs=self.xdt[:, h * 64:(h + 1) * 64], start=True, stop=True),
                         R=[("Wp", jj), "xdt"], W=[pydres])
            y1, y1res = self.tf()
            f.op("dve", lambda e, g=g, y1=y1, pyo=pyo: e.tensor_tensor(
                out=y1[:, 0:512].rearrange("p (h d) -> p h d", h=8), in0=pyo[:, 0:512].rearrange("p (h d) -> p h d", h=8),
                in1=bc(eacs[:, g * 8:(g + 1) * 8], [128, 8, 64]), op=ALU.mult), R=[pyores, "S"], W=[y1res])
            f.op("dve", lambda e, y1=y1, pyd=pyd: e.tensor_tensor(out=y1[:, 0:512], in0=y1[:, 0:512], in1=pyd[:, 0:512], op=ALU.add),
                 R=[pydres, y1res], W=[y1res])
            y3, y3res = self.tf()
            f.op("dve", lambda e, g=g, y3=y3: e.tensor_tensor(
                out=y3[:, 0:512].rearrange("p (h d) -> p h d", h=8), in0=self.xs_tok[:, g * 512:(g + 1) * 512].rearrange("p (h d) -> p h d", h=8),
                in1=bc(self.dsk[:, g * 8:(g + 1) * 8], [128, 8, 64]), op=ALU.mult), R=["xs_tok", "PR"], W=[y3res])
            f.op("dve", lambda e, y1=y1, y3=y3: e.tensor_tensor(out=y1[:, 0:512], in0=y1[:, 0:512], in1=y3[:, 0:512], op=ALU.add),
                 R=[y1res, y3res], W=[y1res])
            f.op("dve", lambda e, g=g, y1=y1: e.tensor_tensor(out=y1[:, 0:512], in0=y1[:, 0:512],
                                                              in1=self.zs[:, g * 512:(g + 1) * 512], op=ALU.mult),
                 R=[y1res, "zs"], W=[y1res])
            f.op("act", lambda e, g=g, y1=y1, y3=y3: e.activation(out=y3[:, 0:512], in_=y1[:, 0:512], func=AF.Square,
                                                                  accum_out=S[:, 224 + g:225 + g]), R=[y1res], W=[y3res, "Sq"])
            f.op("act", lambda e, g=g: e.activation(out=S[:, 228 + g:229 + g], in_=S[:, 224 + g:225 + g], func=AF.Sqrt, scale=1.0 / 512, bias=EPS),
                 R=["Sq"], W=["Sq"])
            f.op("dve", lambda e, g=g: e.reciprocal(out=S[:, 232 + g:233 + g], in_=S[:, 228 + g:229 + g]), R=["Sq"], W=["Sq"])
            pst, pstres = self.pb()
            f.op("pe", lambda e, g=g, pst=pst: e.matmul(pst[:, 0:512], lhsT=self.bm_tok[:, g * 128:(g + 1) * 128],
                                                        rhs=self.xw[:, g * 512:(g + 1) * 512], start=True, stop=True),
                 R=["bm_tok", "xw"], W=[pstres])
            f.op("dve", lambda e, g=g, y1=y1: e.tensor_scalar(out=self.s_tok[:, g * 512:(g + 1) * 512], in0=y1[:, 0:512],
                                                              scalar1=S[:, 232 + g:233 + g], scalar2=None, op0=ALU.mult), R=[y1res, "Sq"], W=["s_tok"])
            hg = self.hst[:, g * 512:(g + 1) * 512]
            f.op("dve", lambda e, g=g, hg=hg: e.tensor_tensor(out=hg.rearrange("p (h d) -> p h d", h=8), in0=hg.rearrange("p (h d) -> p h d", h=8),
                                                              in1=bc(cd[:, g * 8:(g + 1) * 8], [128, 8, 64]), op=ALU.mult),
                 R=["S", "hst"], W=["hst"])
            f.op("dve", lambda e, hg=hg, pst=pst: e.tensor_tensor(out=hg, in0=hg, in1=pst[:, 0:512], op=ALU.add), R=[pstres, "hst"], W=["hst"])
            f.op("pool", lambda e, g=g, hg=hg: e.tensor_copy(out=self.hbf[:, g * 512:(g + 1) * 512], in_=hg), R=["hst"], W=["hbf"])
        for half in range(2):
            pt, pres = self.pb()
            pbf = pt.bitcast(BF16)
            for k in range(8):
                kc = half * 8 + k
                f.op("pe", lambda e, k=k, kc=kc, pbf=pbf: e.transpose(out=pbf[:, k * 128:(k + 1) * 128], in_=self.s_tok[:, kc * 128:(kc + 1) * 128],
                                                                      identity=self.identb), R=["s_tok", "CBc"], W=[pres])
            f.op("dve", lambda e, half=half, pbf=pbf: e.tensor_tensor(
                out=self.sT[:, half * 8:half * 8 + 8, 0:128], in0=pbf[:, 0:1024].rearrange("p (k t) -> p k t", k=8),
                in1=bc(self.ssdnT[:, half * 8:half * 8 + 8], [128, 8, 128]), op=ALU.mult), R=[pres, "PV"], W=["sT"])

    def merge_out_ffn(self, ntok, bs, xres, x_ap, aT, sT, mT, hT, mergedT, mres_of):
        f = self.f
        actT = self.actT
        for s in range(8):
            for bi in range(3):
                gk = ("A", (SA_G0, SA_G1, SA_G2)[bi] + s)
                uk = (("S", s), ("A", SA_USSD + s), ("M", s))[bi]
                (gw, gwr), (uw, uwr) = self.slabs([gk, uk])
                for mm in range(2):
                    m = s * 2 + mm
                    col0 = mm * 128
                    acc = self.macc[mm]
                    accres = ("macc", mm)
                    pg, pgres = self.pb()
                    self.fm_mm(pg[:, 0:ntok], pgres, gw, gwr, col0, hT, "hT", ntok)
                    sg, sgres = self.tf()
                    f.op("act", lambda e, sg=sg, pg=pg: e.activation(out=sg[:, 0:ntok], in_=pg[:, 0:ntok], func=AF.Sigmoid), R=[pgres], W=[sgres])
                    pu, pures = self.pb()
                    if bi == 0:
                        uav = uw[0:64, 0:4096].rearrange("p (h c) -> p h c", h=16)
                        for hd in range(16):
                            f.op("pe", lambda e, hd=hd, pu=pu, uav=uav, col0=col0: e.matmul(pu[:, 0:ntok], lhsT=uav[:, hd, col0:col0 + 128], rhs=aT[0:64, hd, 0:ntok],
                                                                                         start=(hd == 0), stop=(hd == 15)), R=[uwr, "aT"], W=[pures])
                        f.op("dve", lambda e, acc=acc, sg=sg, pu=pu: e.tensor_tensor(out=acc[:, 0:ntok], in0=sg[:, 0:ntok], in1=pu[:, 0:ntok], op=ALU.mult),
                             R=[sgres, pures], W=[accres])
                    else:
                        if bi == 1:
                            self.fm_mm(pu[:, 0:ntok], pures, uw, uwr, col0, sT, "sT", ntok)
                        else:
                            self.fm_mm(pu[:, 0:ntok], pures, uw, uwr, col0, mT, "mT", ntok, kcs=8)
                        f.op("dve", lambda e, sg=sg, pu=pu: e.tensor_tensor(out=sg[:, 0:ntok], in0=sg[:, 0:ntok], in1=pu[:, 0:ntok], op=ALU.mult),
                             R=[sgres, pures], W=[sgres])
                        if bi == 1:
                            f.op("dve", lambda e, acc=acc, sg=sg: e.tensor_tensor(out=acc[:, 0:ntok], in0=acc[:, 0:ntok], in1=sg[:, 0:ntok], op=ALU.add),
                                 R=[sgres, accres], W=[accres])
                        else:
                            f.op("dve", lambda e, acc=acc, sg=sg, m=m: e.tensor_tensor(out=mergedT(m), in0=acc[:, 0:ntok], in1=sg[:, 0:ntok], op=ALU.add),
                                 R=[sgres, accres], W=[mres_of(m)])
        MR = [mres_of(m) for m in range(16)]
        for s in range(8):
            wt, wres = self.slab("A", SA_OUT + s)
            wv = wt[:, 0:4096].rearrange("p (k c) -> p k c", k=16)
            pt, pres = self.pb()
            for kc in range(16):
                f.op("pe", lambda e, kc=kc, pt=pt, wv=wv: e.matmul(pt[0:bs, 0:256], lhsT=mergedT(kc)[:, 0:bs], rhs=wv[:, kc, :],
                                                                   start=(kc == 0), stop=(kc == 15)), R=[wres, mres_of(kc)], W=[pres])
            xa = x_ap[:, s * 256:(s + 1) * 256]
            f.op("dve", lambda e, xa=xa, pt=pt: e.tensor_tensor(out=xa, in0=xa, in1=pt[0:bs, 0:256], op=ALU.add), R=[pres, xres], W=[xres])
        self.norm_T(x_ap, xres, bs, self.gffnT, hT, "hT", 0)
        for s in range(22):
            (gw, gwr), (uw, uwr) = self.slabs([("A", SA_GATE + s), ("A", SA_UP + s)])
            for mm in range(2):
                j = s * 2 + mm
                pg, pgres = self.pb()
                pu, pures = self.pb()
                self.fm_mm(pg[:, 0:ntok], pgres, gw, gwr, mm * 128, hT, "hT", ntok)
                self.fm_mm(pu[:, 0:ntok], pures, uw, uwr, mm * 128, hT, "hT", ntok)
                sg, sgres = self.tf()
                f.op("act", lambda e, sg=sg, pg=pg: e.activation(out=sg[:, 0:ntok], in_=pg[:, 0:ntok], func=AF.Silu), R=[pgres], W=[sgres])
                f.op("dve", lambda e, sg=sg, pu=pu, j=j: e.tensor_tensor(out=actT[:, j, 0:ntok], in0=sg[:, 0:ntok], in1=pu[:, 0:ntok], op=ALU.mult),
                     R=[sgres, pures], W=["actT"])
        for cg in range(8):
            pt, pres = self.pb()
            for half in range(2):
                wt, wres = self.slab("D", cg * 2 + half)
                wv = wt[:, 0:5632].rearrange("p (k c) -> p k c", k=22)
                for kc in range(22):
                    f.op("pe", lambda e, kc=kc, half=half, pt=pt, wv=wv: e.matmul(
                        pt[0:bs, 0:256], lhsT=actT[:, half * 22 + kc, 0:bs], rhs=wv[:, kc, :],
                        start=(half == 0 and kc == 0), stop=(half == 1 and kc == 21)), R=[wres, "actT"], W=[pres])
            xa = x_ap[:, cg * 256:(cg + 1) * 256]
            f.op("dve", lambda e, xa=xa, pt=pt: e.tensor_tensor(out=xa, in0=xa, in1=pt[0:bs, 0:256], op=ALU.add), R=[pres, xres], W=[xres])

    def super_tile(self, seq, st, last):
        f = self.f
        io = self.io
        t0 = st * NT
        first = (st == 0)
        f.dma("pool", self.xtok[:], io["xp"][seq, t0:t0 + 128, :], W=["x"])
        self.norm_T(self.xtok[:], "x", 128, self.gmixT, self.hT, "hT", 0)
        last3 = None
        k32 = None
        L = self.lastst
        if last:
            last3 = self.l3t[:, 0:96].rearrange("p (c j) -> p c j", c=24)
            k32t = L[:, 96:96 + 256].rearrange("p (a t) -> p a t", a=2)
            k32 = lambda pair: k32t[:, pair, :]
        f.op("pool", lambda e: e.tensor_copy(out=self.xbcT[:, :, 0:3], in_=self.hist[:]), R=["hist"], W=["xbch"])
        if self.stop == 'norm':
            return
        self.in_proj_fm(self.hT, NT, self.qT, lambda pair: self.kT[:, pair, 128:256],
                        lambda c: self.xbcT[:, c, 3:3 + NT], self.qmT, last3=last3, k32=k32)
        wt, wres = self.slab("A", SA_V)
        pt, pres = self.pb()
        self.tm_mm(pt[:, 0:256], pres, wt, wres, 256, self.hT, "hT", 0, 128)
        f.op("act", lambda e, pt=pt: e.activation(out=self.vtok[:, 1, :], in_=pt[:, 0:256], func=AF.Copy), R=[pres], W=["vtok"])
        if last:
            f.op("dve", lambda e, pt=pt: e.tensor_copy(out=L[:, 352:608], in_=pt[:, 0:256]), R=[pres], W=["lastst"])
        for s in range(8):
            wt, wres = self.slab("A", SA_Z + s)
            pt, pres = self.pb()
            self.tm_mm(pt[:, 0:256], pres, wt, wres, 256, self.hT, "hT", 0, 128)
            f.op("act", lambda e, pt=pt, s=s: e.activation(out=self.zs[:, s * 256:(s + 1) * 256], in_=pt[:, 0:256], func=AF.Silu),
                 R=[pres], W=["zs"])
        wt, wres = self.slab("T", 0)
        pt, pres = self.pb()
        self.tm_mm(pt[:, 0:32], pres, wt, wres, 32, self.hT, "hT", 0, 128, cw=32)
        f.op("act", lambda e, pt=pt: e.activation(out=self.dtraw[:], in_=pt[:, 0:32], func=AF.Copy), R=[pres], W=["dtraw"])
        if self.stop == 'inproj':
            return
        def save_hist(c):
            f.op("pool", lambda e, c=c: e.tensor_copy(out=self.hist[:, c, :], in_=self.xbcT[:, c, NT:NT + 3]), R=[("xbcT", c)], W=["hist"])
        self.conv_fm(lambda c, j: self.xbcT[:, c, j:j + NT], NT, lambda c: self.xbcT[:, c, 3:3 + NT], lambda c: ("xbcT", c), save_hist=save_hist)
        if self.stop == 'conv':
            return
        has_prev = not first
        self.swa_heads(
            q_rhs=lambda h: self.qT[64 * (h % 2):64 * (h % 2) + 64, (h // 2) * 4:(h // 2) * 4 + 4, 0:128],
            kprev=(lambda h: self.kT[64 * (h % 2):64 * (h % 2) + 64, h // 2, 0:128]) if has_prev else None,
            kcur=lambda h: self.kT[64 * (h % 2):64 * (h % 2) + 64, h // 2, 128:256],
            vprev=lambda h: self.vtok[:, 0, h * 64:(h + 1) * 64],
            vcur=lambda h: self.vtok[:, 1, h * 64:(h + 1) * 64],
            nq=128, kcn=128, Dmp=self.Dmp, Dmc=self.Dmd,
            out=lambda h: self.aT[0:64, h * 4:h * 4 + 4, 0:128],
            R_in=["qT", "kT", "vtok"], W_out=["aT"])
        f.op("pool", lambda e: e.tensor_copy(out=self.kT[:, :, 0:128], in_=self.kT[:, :, 128:256]), R=["kT"], W=["kT"])
        f.op("pool", lambda e: e.tensor_copy(out=self.vtok[:, 0, :], in_=self.vtok[:, 1, :]), R=["vtok"], W=["vtok"])
        if self.stop == 'swa':
            return
        self.ssd_block()
        if self.stop == 'ssd':
            return
        self.mem_heads(q_rhs=lambda c: self.qmT[:, c, 0:NT], KTm=self.KTm, ktres="KTm", Vm=self.Vm, vres="Vm", nq=NT,
                       out=lambda c: self.mT[:, c, 0:NT], R_in=["qmT"], W_out=["mT"])
        if self.stop == 'mem':
            return
        self.merge_out_ffn(NT, 128, "x", self.xtok[:], self.aT, self.sT, self.mT, self.hT,
                           lambda m: self.xbcT[:, m, 3:3 + NT], lambda m: ("xbcT", m))
        f.dma("pool", io["yp"][seq, t0:t0 + 128, :], self.xtok[:], R=["x"])
        if last:
            self.seq_last_outputs(seq)

    def seq_last_outputs(self, seq):
        f = self.f
        io = self.io
        L = self.lastst
        k32t = L[:, 96:96 + 256].rearrange("p (a t) -> p a t", a=2)
        if 'pkpv' in _SKIP:
            return
        ptk, pkres = self.pb()
        for pair in range(2):
            f.op("pe", lambda e, pair=pair, ptk=ptk: e.transpose(out=ptk[:, pair * 128:(pair + 1) * 128], in_=k32t[:, pair, :], identity=self.identf),
                 R=["lastst", "C"], W=[pkres])
        f.op("act", lambda e, ptk=ptk: e.activation(out=L[:, 608:864], in_=ptk[:, 0:256], func=AF.Copy), R=[pkres], W=["lastst"])
        f.dma("pool", io["pk"][seq], L[:, 608:864], R=["lastst"])
        f.dma("pool", io["pv"][seq], L[:, 352:608], R=["lastst"])
        if 'pconv' in _SKIP:
            return
        l3t = self.l3t
        pc3 = self.stage[0:4, 0:3072]
        for q6 in range(6):
            pt, pres = self.pb()
            for k in range(4):
                c = q6 * 4 + k
                f.op("pe", lambda e, k=k, c=c, pt=pt: e.transpose(out=pt[:, k * 128:(k + 1) * 128], in_=l3t[:, c * 4:c * 4 + 128], identity=self.identf),
                     R=["lastst", "C"], W=[pres])
            f.op("act", lambda e, q6=q6, pt=pt: e.activation(out=pc3[:, q6 * 512:(q6 + 1) * 512], in_=pt[0:4, 0:512], func=AF.Copy),
                 R=[pres], W=self.STAGE)
        f.dma("pool", io["pconv"][seq], pc3[1:4, :], R=self.STAGE)

    def seq_end(self, seq):
        if "seq_end" in _SKIP:
            return
        f = self.f
        io = self.io
        stage = self.stage
        SR = self.STAGE
        for q4 in range(4):
            pt, pres = self.pb()
            for k in range(4):
                c = q4 * 4 + k
                f.op("pe", lambda e, k=k, c=c, pt=pt: e.transpose(out=pt[:, k * 128:(k + 1) * 128], in_=self.hst[:, c * 128:(c + 1) * 128],
                                                                  identity=self.identf), R=["hst", "C"], W=[pres])
            f.op("act", lambda e, q4=q4, pt=pt: e.activation(out=stage[:, q4 * 512:(q4 + 1) * 512], in_=pt[:, 0:512], func=AF.Copy),
                 R=[pres], W=SR)
        f.dma("pool", io["pssm"][seq].rearrange("(c q) n -> q c n", q=128), stage[:, 0:2048].rearrange("p (c n) -> p c n", c=16), R=SR)

    def sample_group(self):
        f = self.f
        io = self.io
        NS = NSMP
        L = self.lastst
        stage = self.stage
        SR = self.STAGE
        x16 = self.xtok[0:NS, :]
        self.worder = self.super_order()
        self.wpos = 0
        f.dma("pool", io["sk"][:, 0:127, :], io["csk"][:, 1:128, :])
        f.dma("pool", io["sv"][:, 0:127, :], io["csv"][:, 1:128, :])
        f.dma("pool", io["sconv_o"][:, 0:2, :], io["sconv"].rearrange("(b j) c -> b j c", j=3)[:, 1:3, :])
        f.dma("pool", x16, io["xs"], W=["x"])
        self.norm_T(x16, "x", NS, self.gmixT, self.hT, "hT", 0)
        cbuf = self.cbuf
        f.dma("pool", stage[0:48, 0:3072], io["sconv"], W=SR)
        for q6 in range(6):
            pt, pres = self.pb()
            for k in range(4):
                c = q6 * 4 + k
                f.op("pe", lambda e, k=k, c=c, pt=pt: e.transpose(out=pt[:, k * 48:(k + 1) * 48], in_=stage[0:48, c * 128:(c + 1) * 128],
                                                                  identity=self.identf[0:48, 0:48]), R=SR + ["C"], W=[pres])
            for k in range(4):
                c = q6 * 4 + k
                f.op("act", lambda e, k=k, c=c, pt=pt: e.activation(out=cbuf[:, c, 0:3, :], in_=pt[:, k * 48:(k + 1) * 48].rearrange("p (b j) -> p j b", j=3),
                                                                    func=AF.Copy), R=[pres], W=["xbch"])
        k32s = L[:, 96:96 + 256].rearrange("p (a t) -> p a t", a=2)
        f.op("pool", lambda e: e.memset(L[:, 96:352], 0.0), W=["lastst"])
        xbctok = self.hst
        def xbc_hook(s, wt, wres):
            pt, pres = self.pb()
            self.tm_mm(pt[0:NS, 0:256], pres, wt, wres, 256, self.hT, "hT", 0, NS)
            if s < 8:
                dst = self.hst[0:NS, s * 256:(s + 1) * 256]
            else:
                dst = self.hbf32[0:NS, (s - 8) * 256:(s - 7) * 256]
            f.op("act", lambda e, pt=pt, dst=dst: e.activation(out=dst, in_=pt[0:NS, 0:256], func=AF.Copy), R=[pres], W=["hst"])
        self.xbc_hook = xbc_hook
        self.in_proj_fm(self.hT, NS, self.qT, lambda pair: self.kT[:, pair, 128:128 + NS],
                        lambda c: cbuf[:, c, 3, :], self.qmT, last3=None, k32=lambda pair: k32s[:, pair, 0:NS])
        self.xbc_hook = None
        f.dma("pool", io["sconv_o"][:, 2, 0:2048], self.hst[0:NS, :], R=["hst"])
        f.dma("pool", io["sconv_o"][:, 2, 2048:3072], self.hbf32[0:NS, :], R=["hst"])
        pt, pres = self.pb()
        for pair in range(2):
            f.op("pe", lambda e, pair=pair, pt=pt: e.transpose(out=pt[:, pair * 128:(pair + 1) * 128], in_=k32s[:, pair, :], identity=self.identf),
                 R=["lastst", "C"], W=[pres])
        f.op("act", lambda e, pt=pt: e.activation(out=L[0:NS, 608:864], in_=pt[0:NS, 0:256], func=AF.Copy), R=[pres], W=["lastst"])
        f.dma("pool", io["sk"][:, 127, :], L[0:NS, 608:864], R=["lastst"])
        wt, wres = self.slab("A", SA_V)
        pt, pres = self.pb()
        self.tm_mm(pt[0:NS, 0:256], pres, wt, wres, 256, self.hT, "hT", 0, NS)
        f.op("act", lambda e, pt=pt: e.activation(out=self.vtok[0:NS, 1, :], in_=pt[0:NS, 0:256], func=AF.Copy), R=[pres], W=["vtok"])
        f.op("dve", lambda e, pt=pt: e.tensor_copy(out=L[0:NS, 352:608], in_=pt[0:NS, 0:256]), R=[pres], W=["lastst"])
        f.dma("pool", io["sv"][:, 127, :], L[0:NS, 352:608], R=["lastst"])
        zT = self.zs[:, 0:16 * NS].rearrange("p (c t) -> p c t", c=16)
        for s in range(8):
            wt, wres = self.slab("A", SA_Z + s)
            for mm in range(2):
                pt, pres = self.pb()
                self.fm_mm(pt[:, 0:NS], pres, wt, wres, mm * 128, self.hT, "hT", NS)
                f.op("act", lambda e, pt=pt, c=s * 2 + mm: e.activation(out=zT[:, c, :], in_=pt[:, 0:NS], func=AF.Silu), R=[pres], W=["zs"])
        wt, wres = self.slab("T", 0)
        pt, pres = self.pb()
        self.tm_mm(pt[0:NS, 0:32], pres, wt, wres, 32, self.hT, "hT", 0, NS, cw=32)
        f.op("act", lambda e, pt=pt: e.activation(out=self.dtraw[0:NS, :], in_=pt[0:NS, 0:32], func=AF.Copy), R=[pres], W=["dtraw"])
        cvs = self.cvs
        self.conv_fm(lambda c, j: cbuf[:, c, j, :], NS, lambda c: cvs[:, c, :], lambda c: ("xbcT", c))
        CVS = [("xbcT", c) for c in range(24)]
        for b in range(NS):
            ck, ckres = self.tf()
            f.dma("pool", ck[:, 0:256], io["csk"][b], W=[ckres])
            f.dma("pool", ck[:, 256:512], io["csv"][b], W=[ckres])
            cb16, cbres = self.tb()
            f.op("pool", lambda e, ck=ck, cb16=cb16: e.tensor_copy(out=cb16[:, 0:512], in_=ck[:, 0:512]), R=[ckres], W=[cbres])
            pt, pres = self.pb()
            pbf = pt.bitcast(BF16)
            for pair in range(2):
                f.op("pe", lambda e, pair=pair, pbf=pbf, cb16=cb16: e.transpose(out=pbf[:, pair * 128:(pair + 1) * 128], in_=cb16[:, pair * 128:(pair + 1) * 128],
                                                                                identity=self.identb), R=[cbres, "CBc"], W=[pres])
            kTc, kTres = self.tb()
            f.op("act", lambda e, pbf=pbf, kTc=kTc: e.activation(out=kTc[:, 0:256], in_=pbf[:, 0:256], func=AF.Copy), R=[pres], W=[kTres])
            self.swa_heads(
                q_rhs=lambda h, b=b: self.qT[64 * (h % 2):64 * (h % 2) + 64, (h // 2) * 4:(h // 2) * 4 + 4, b:b + 1],
                kprev=lambda h, kTc=kTc: kTc[64 * (h % 2):64 * (h % 2) + 64, (h // 2) * 128:(h // 2 + 1) * 128],
                kcur=lambda h: self.kT[64 * (h % 2):64 * (h % 2) + 64, h // 2, 128:128 + NS],
                vprev=lambda h, cb16=cb16: cb16[:, 256 + h * 64:256 + (h + 1) * 64],
                vcur=lambda h: self.vtok[0:NS, 1, h * 64:(h + 1) * 64],
                nq=1, kcn=NS, Dmp=self.Dmp[:, 0:1], Dmc=self.dm16[0:NS, b:b + 1],
                out=lambda h, b=b: self.aT[0:64, h * 4:h * 4 + 4, b:b + 1],
                R_in=["qT", "kT", "vtok", kTres, cbres], W_out=["aT"])
            kst = stage[:, 0:2048]
            vst = stage[:, 2048:4096]
            f.dma("pool", kst.rearrange("p (m c) -> p m c", m=2), io["cmk"][b].rearrange("(m p) c -> p m c", p=128), W=SR)
            f.dma("pool", vst.rearrange("p (m c) -> p m c", m=2), io["cmv"][b].rearrange("(m p) c -> p m c", p=128), W=SR)
            f.op("pool", lambda e: e.tensor_copy(out=self.Vm[:].rearrange("p m c -> p (m c)"), in_=vst), R=SR, W=["Vm"])
            f.op("dve", lambda e: e.tensor_copy(out=self.xn[:], in_=kst), R=SR, W=["xn"])
            self.kmem_T(self.xn, "xn", self.KTm, "KTm")
            self.mem_heads(q_rhs=lambda c, b=b: self.qmT[:, c, b:b + 1], KTm=self.KTm, ktres="KTm", Vm=self.Vm, vres="Vm", nq=1,
                           out=lambda c, b=b: self.mT[:, c, b:b + 1], R_in=["qmT"], W_out=["mT"])
        S = self.ssd_s
        dt = S[0:NS, 32:64]
        f.op("dve", lambda e: e.tensor_tensor(out=S[0:NS, 0:32], in0=self.dtraw[0:NS, :], in1=self.dtb[0:NS, :], op=ALU.add), R=["dtraw", "PR"], W=["S"])
        f.op("act", lambda e: e.activation(out=S[0:NS, 0:32], in_=S[0:NS, 0:32], func=AF.Exp), R=["S"], W=["S"])
        f.op("act", lambda e: e.activation(out=dt, in_=S[0:NS, 0:32], func=AF.Ln, bias=1.0), R=["S"], W=["S"])
        f.op("dve", lambda e: e.tensor_tensor(out=S[0:NS, 64:96], in0=dt, in1=self.a_bc[0:NS, :], op=ALU.mult), R=["S", "SM"], W=["S"])
        f.op("act", lambda e: e.activation(out=S[0:NS, 96:128], in_=S[0:NS, 64:96], func=AF.Exp), R=["S"], W=["S"])
        ex = stage[0:NS, :]
        f.op("dve", lambda e: e.tensor_copy(out=ex[:, 0:2048].rearrange("p (h d) -> p h d", h=32), in_=bc(S[0:NS, 96:128], [NS, 32, 64])), R=["S"] + SR, W=SR)
        f.op("dve", lambda e: e.tensor_copy(out=ex[:, 2048:4096].rearrange("p (h d) -> p h d", h=32), in_=bc(dt, [NS, 32, 64])), R=["S"] + SR, W=SR)
        cdT, dtT = self.cdT, self.dtT
        for which, dst in ((0, cdT), (1, dtT)):
            pt, pres = self.pb()
            for c in range(16):
                f.op("pe", lambda e, c=c, which=which, pt=pt: e.transpose(out=pt[:, c * NS:(c + 1) * NS], in_=ex[:, which * 2048 + c * 128: which * 2048 + (c + 1) * 128],
                                                                          identity=self.identf[0:NS, 0:NS]), R=SR + ["C"], W=[pres])
            f.op("act", lambda e, pt=pt, dst=dst: e.activation(out=dst[:].rearrange("p c t -> p (c t)"), in_=pt[:, 0:16 * NS], func=AF.Copy), R=[pres], W=["cdT"])
        xdtT = self.xdtT
        f.op("dve", lambda e: e.tensor_tensor(out=xdtT[:], in0=cvs[:, 0:16, :], in1=dtT[:], op=ALU.mult), R=CVS + ["cdT"], W=["xdtT"])
        bcs = self.bcs
        bpad = self.hst[:, 0:1024].rearrange("p (c t) -> p c t", c=8)
        f.op("pool", lambda e: e.memset(self.hst[:, 0:1024], 0.0), W=["hst"])
        f.op("dve", lambda e: e.tensor_copy(out=bpad[:, :, 0:NS], in_=cvs[:, 16:24, :]), R=CVS + ["hst"], W=["hst"])
        for half in range(2):
            pt, pres = self.pb()
            for g in range(4):
                f.op("pe", lambda e, g=g, half=half, pt=pt: e.transpose(out=pt[:, g * 128:(g + 1) * 128], in_=bpad[:, half * 4 + g, :], identity=self.identf),
                     R=["hst", "C"], W=[pres])
            f.op("act", lambda e, half=half, pt=pt: e.activation(out=bcs[0:NS, half * 512:(half + 1) * 512], in_=pt[0:NS, 0:512], func=AF.Copy), R=[pres], W=["bcs"])
        oh16 = self.oh16
        f.op("dve", lambda e: e.tensor_copy(out=oh16[:], in_=bc(self.identf[0:NS, 0:NS], [NS, NS, 128])), R=["C"], W=["oh16"])
        yT = self.yT
        U = self.hst
        for b in range(NS):
            H = stage[:, (b % 2) * 2048:(b % 2 + 1) * 2048]
            Hres = ["xs_tok", "xdt"] if b % 2 == 0 else ["xw", "s_tok"]
            H3 = H.rearrange("p (c n) -> p c n", c=16)
            f.dma("pool", H3, io["sssm"][b].rearrange("(c q) n -> q c n", q=128), W=Hres)
            pbm, pbmres = self.pb()
            pcm, pcmres = self.pb()
            f.op("pe", lambda e, b=b, pbm=pbm: e.matmul(pbm[:, 0:512], lhsT=oh16[0:NS, b, :], rhs=bcs[0:NS, 0:512], start=True, stop=True), R=["oh16", "bcs"], W=[pbmres])
            f.op("pe", lambda e, b=b, pcm=pcm: e.matmul(pcm[:, 0:512], lhsT=oh16[0:NS, b, :], rhs=bcs[0:NS, 512:1024], start=True, stop=True), R=["oh16", "bcs"], W=[pcmres])
            f.op("dve", lambda e, b=b, H3=H3: e.tensor_tensor(out=H3, in0=H3, in1=bc(cdT[:, :, b], [128, 16, 128]), op=ALU.mult), R=Hres + ["cdT"], W=Hres)
            U4 = U[:].rearrange("p (g r n) -> p g r n", g=4, r=4)
            f.op("dve", lambda e, b=b, pbm=pbm, U4=U4: e.tensor_tensor(
                out=U4, in0=pbm[:, 0:512].rearrange("p (g n) -> p g n", g=4).unsqueeze(2).broadcast_to([128, 4, 4, 128]),
                in1=xdtT[:, :, b].rearrange("p (g r) -> p g r", g=4).unsqueeze(3).broadcast_to([128, 4, 4, 128]), op=ALU.mult),
                R=[pbmres, "xdtT", "hst"], W=["hst"])
            f.op("dve", lambda e, H=H: e.tensor_tensor(out=H, in0=H, in1=U[:], op=ALU.add), R=Hres + ["hst"], W=Hres)
            f.dma("pool", io["sssm_o"][b].rearrange("(c q) n -> q c n", q=128), H3, R=Hres)
            f.op("dve", lambda e, pcm=pcm, U4=U4, H=H: e.tensor_tensor(
                out=U4, in0=H.rearrange("p (g r n) -> p g r n", g=4, r=4),
                in1=pcm[:, 0:512].rearrange("p (g n) -> p g n", g=4).unsqueeze(2).broadcast_to([128, 4, 4, 128]), op=ALU.mult),
                R=Hres + [pcmres, "hst"], W=["hst"])
            f.op("dve", lambda e, b=b: e.tensor_reduce(out=yT[:, :, b], in_=U[:].rearrange("p (c n) -> p c n", c=16), op=ALU.add, axis=AX.X),
                 R=["hst"], W=["yT"])
        y2 = self.y2
        f.op("dve", lambda e: e.tensor_tensor(out=y2[:], in0=cvs[:, 0:16, :], in1=bc(self.dskT, [128, 16, NS]), op=ALU.mult), R=CVS + ["PV"], W=["y2"])
        f.op("dve", lambda e: e.tensor_tensor(out=y2[:], in0=y2[:], in1=yT[:], op=ALU.add), R=["y2", "yT"], W=["y2"])
        f.op("dve", lambda e: e.tensor_tensor(out=y2[:], in0=y2[:], in1=zT, op=ALU.mult), R=["y2", "zs"], W=["y2"])
        sq, sqres = self.tb()
        f.op("act", lambda e, sq=sq: e.activation(out=sq[:, 0:16 * NS], in_=y2[:].rearrange("p c t -> p (c t)"), func=AF.Square), R=["y2"], W=[sqres])
        p2, p2res = self.pb()
        for g in range(4):
            for r in range(4):
                c = g * 4 + r
                f.op("pe", lambda e, g=g, r=r, c=c, sq=sq, p2=p2: e.matmul(p2[:, g * NS:(g + 1) * NS], lhsT=self.onesb, rhs=sq[:, c * NS:(c + 1) * NS],
                                                                          start=(r == 0), stop=(r == 3)), R=[sqres, "CBc"], W=[p2res])
        rr, rres = self.rsqrt_bc(p2[:, 0:4 * NS], p2res, 4 * NS, 1.0 / 512)
        f.op("dve", lambda e, rr=rr: e.tensor_tensor(
            out=y2[:].rearrange("p (g r) t -> p g r t", g=4), in0=y2[:].rearrange("p (g r) t -> p g r t", g=4),
            in1=rr.rearrange("p (g t) -> p g t", g=4).unsqueeze(2).broadcast_to([128, 4, 4, NS]), op=ALU.mult), R=["y2", rres], W=["y2"])
        f.op("dve", lambda e: e.tensor_tensor(out=self.sT[:, :, 0:NS], in0=y2[:], in1=bc(self.ssdnT, [128, 16, NS]), op=ALU.mult), R=["y2", "PV"], W=["sT"])
        self.merge_out_ffn(NS, NS, "x", x16, self.aT, self.sT, self.mT, self.hT,
                           lambda m: self.xbcT[:, m, 3:3 + NS], lambda m: ("xbcT", m))
        f.dma("pool", io["ys"], x16, R=["x"])

def slabify(W, cw, kcs):
    K, N = W.shape
    assert K == kcs * 128 and N % cw == 0
    a = W.reshape(kcs, 128, N // cw, cw).transpose(2, 1, 0, 3)
    return np.ascontiguousarray(a).reshape(N // cw, 128, kcs * cw)


def host_prep(inp):
    w_in = inp["w_in"][0]
    q = w_in[:, 0:1024].reshape(2048, 2, 2, 4, 64).transpose(0, 1, 3, 2, 4).reshape(2048, 1024)
    parts = [slabify(q, CW, 16), slabify(w_in[:, 1024:1280], CW, 16), slabify(w_in[:, 1280:1536], CW, 16),
             slabify(w_in[:, 1536:3584], CW, 16), slabify(w_in[:, 3584:6656], CW, 16), slabify(w_in[:, 6688:7712], CW, 16),
             slabify(w_in[:, 7712:13856], CW, 16), slabify(inp["w_mem_kv"][0], CW, 16), slabify(inp["w_up_ssd"][0], CW, 16),
             slabify(inp["w_out"][0], CW, 16), slabify(inp["w_gate"][0], CW, 16), slabify(inp["w_up"][0], CW, 16)]
    WA = np.concatenate(parts, 0)
    assert WA.shape[0] == NA
    WM = slabify(inp["w_up_mem"][0], CW, 8)
    ws = inp["w_up_swa"][0]
    WS = np.ascontiguousarray(ws.reshape(16, 64, 8, CW).transpose(2, 1, 0, 3)).reshape(8, 64, 16 * CW)
    wd = inp["w_down"][0]
    WD = np.ascontiguousarray(wd.reshape(2, 22, 128, 8, CW).transpose(3, 0, 2, 1, 4)).reshape(16, 128, 22 * CW)
    WT = slabify(w_in[:, 6656:6688], 32, 16)[0]
    ar = np.arange(128)
    c128 = np.zeros((128, 8 * 128 + 512 + 32), np.float32)
    c128[:, 0:128] = np.eye(128)
    c128[:, 128:256] = (ar[:, None] <= ar[None, :])
    c128[:, 256:384] = (ar[:, None] > ar[None, :])
    c128[:, 384:512] = 1.0
    k_, q_ = ar[:, None], ar[None, :]
    c128[:, 512:640] = np.where(q_ >= k_, q_ - k_, 20000.0)
    c128[:, 640:768] = np.where(q_ <= k_, q_ + 128 - k_, 20000.0)
    c128[:, 768:896] = (ar[:, None] // 64 == ar[None, :] // 64)
    c128[:, 1024:1536] = np.tile(np.where(ar[None, :] < ar[:, None], -30000.0, 0.0), (1, 4))
    c128[:, 1536:1568] = (ar[:, None] % 32 == np.arange(32)[None, :])
    pvec = np.zeros((128, 512), np.float32)
    T16 = lambda v: np.ascontiguousarray(v.reshape(16, 128).T)
    pvec[:, 0:16] = T16(inp["norm_mix"][0])
    pvec[:, 16:32] = T16(inp["norm_ffn"][0])
    pvec[:, 32:48] = T16(inp["norm_mem"][0])
    pvec[:, 48:64] = T16(inp["ssd_norm"][0])
    pvec[:, 64:160] = inp["conv_w"][0].reshape(4, 24, 128).transpose(2, 1, 0).reshape(128, 96)
    pvec[:, 160:184] = inp["conv_b"][0].reshape(24, 128).T
    pvec[:, 184] = np.tile(inp["q_norm_swa"][0], 2)
    pvec[:, 185] = np.tile(inp["k_norm_swa"][0], 2)
    pvec[:, 186:188] = inp["q_norm_mem"][0].reshape(2, 128).T
    pvec[:, 188:204] = inp["swa_sinks"][0][None, :]
    pvec[:, 204:220] = np.repeat(inp["d_skip"][0], 64).reshape(16, 128).T
    pvec[0:16, 220:236] = np.where(np.eye(16) > 0, 0.0, 20000.0)
    prow = np.zeros((128, 352), np.float32)
    prow[:, 0:32] = inp["dt_bias"][0][None, :]
    prow[:, 32:64] = inp["a_log"][0][None, :]
    prow[:, 64:96] = inp["d_skip"][0][None, :]
    prow[:, 96:352] = inp["k_norm_mem"][0][None, :]
    shared = dict(WA=WA, WM=WM, WS=WS, WD=WD, WT=np.ascontiguousarray(WT), c128=c128, pvec=pvec, prow=prow)
    in_maps = []
    for c in range(NCORES):
        m = dict(shared)
        m["xp"] = np.ascontiguousarray(inp["x_prompt"][2 * c:2 * c + 2])
        m["memp"] = np.ascontiguousarray(inp["mem_prompt"][2 * c:2 * c + 2])
        sl = slice(16 * c, 16 * c + 16)
        m["xs"] = np.ascontiguousarray(inp["x_sample"][sl, 0])
        m["csk"] = np.ascontiguousarray(inp["cache_swa_k"][0, sl]).reshape(16, 128, 256)
        m["csv"] = np.ascontiguousarray(inp["cache_swa_v"][0, sl]).reshape(16, 128, 256)
        m["cmk"] = np.ascontiguousarray(inp["cache_mem_k"][0, sl]).reshape(16, 256, 1024)
        m["cmv"] = np.ascontiguousarray(inp["cache_mem_v"][0, sl]).reshape(16, 256, 1024)
        m["sssm"] = np.ascontiguousarray(inp["state_ssm"][0, sl]).reshape(16, 2048, 128)
        m["sconv"] = np.ascontiguousarray(inp["state_conv"][0, sl]).reshape(48, 3072)
        in_maps.append(m)
    return in_maps


_CACHE = {}


def run(inputs, do_samples=True, n_st=None, dbg=False):
    inp = {k: np.asarray(v) for k, v in inputs.items()}
    b = Builder(do_samples=do_samples, n_st=n_st, dbg=dbg)
    nc = b.build()
    in_maps = host_prep(inp)
    if not do_samples:
        for m in in_maps:
            for k in ("xs", "csk", "csv", "cmk", "cmv", "sssm", "sconv"):
                m.pop(k)
    res = run_bass_kernel_spmd(nc, in_maps, core_ids=list(range(NCORES)))
    R = res.results
    cat = lambda k: np.concatenate([r[k] for r in R], 0)
    yp = cat("yp")
    ys = cat("ys").reshape(128, 1, D)
    outs = (yp, ys,
            cat("pk").reshape(1, 16, 128, 4, 64), cat("pv").reshape(1, 16, 128, 4, 64),
            cat("pmk").reshape(1, 16, 256, 4, 256), cat("pmv").reshape(1, 16, 256, 4, 256),
            cat("pssm").reshape(1, 16, 32, 64, 128), cat("pconv").reshape(1, 16, 3, 3072),
            cat("sk").reshape(1, 128, 128, 4, 64), cat("sv").reshape(1, 128, 128, 4, 64),
            cat("sssm_o").reshape(1, 128, 32, 64, 128), cat("sconv_o").reshape(1, 128, 3, 3072))
    outs = tuple(np.ascontiguousarray(o, dtype=np.float32) for o in outs)
    if dbg:
        return outs, {k: [r["dbg_" + k] for r in R] for k in b.dbg_outs}
    return outs


def kernel(**inputs):
    return run(inputs, do_samples=True)
```

```python
import contextlib
import os
import numpy as np
_SKIP = set(os.environ.get('KSKIP', '').split(','))
import concourse.bass as bass
import concourse.mybir as mybir
from concourse.bass_utils import run_bass_kernel_spmd

F32 = mybir.dt.float32
BF16 = mybir.dt.bfloat16
AF = mybir.ActivationFunctionType
ALU = mybir.AluOpType
AX = mybir.AxisListType

NCORES = 8
D = 2048
SEQ = 2048
NSEQ = 2
NSMP = 16
NT = 128
BLK = 128
KC = 16
DFF = 5632
EPS = 1e-6
CW = 256
ENGS = ("pe", "act", "dve", "pool", "sp")

SA_Q, SA_K, SA_V, SA_Z, SA_XBC, SA_QM, SA_G0, SA_G1, SA_G2, SA_MEM, SA_USSD, SA_OUT, SA_GATE, SA_UP = (
    0, 4, 5, 6, 14, 26, 30, 38, 46, 54, 62, 70, 78, 100)
NA = 122
SLOPES = [2.0 ** (-8.0 * (h + 1) / 16.0) for h in range(16)]


class FW:
    def __init__(self, nc, n_dma_sems=48):
        self.nc = nc
        self.es = contextlib.ExitStack()
        self.sem = {e: self.es.enter_context(nc.semaphore("s_" + e)) for e in ENGS}
        self.dsem = [self.es.enter_context(nc.semaphore("d_%d" % i)) for i in range(n_dma_sems)]
        self.dtot = [0] * n_dma_sems
        self.dnext = {}
        self.dpool = {'pool': (0, n_dma_sems // 2), 'sp': (n_dma_sems // 2, n_dma_sems)}
        self.n = {e: 0 for e in ENGS}
        self.waited = {e: {} for e in ENGS}
        self.stream = {e: [] for e in ENGS}
        self.last_w = {}
        self.readers = {}

    def sbuf(self, name, shape, dtype):
        return self.es.enter_context(self.nc.sbuf_tensor(name, list(shape), dtype))

    def psum(self, name, shape, dtype):
        return self.es.enter_context(self.nc.psum_tensor(name, list(shape), dtype))

    def _deps(self, R, W):
        deps = set()
        for r in R:
            w = self.last_w.get(r)
            if w is not None:
                deps.add(w)
            if isinstance(r, tuple) and r[0] == "ps":
                rd = self.readers.get(r)
                if rd:
                    for k, v in rd.items():
                        deps.add((k, v))
        for r in W:
            w = self.last_w.get(r)
            if w is not None:
                deps.add(w)
            rd = self.readers.get(r)
            if rd:
                for k, v in rd.items():
                    deps.add((k, v))
        return deps

    def _waits(self, eng, deps):
        need = {}
        for k, v in deps:
            if k == "pe" and eng == "pe":
                continue
            if v > need.get(k, 0):
                need[k] = v
        out = []
        wd = self.waited[eng]
        for k, v in need.items():
            if wd.get(k, 0) >= v:
                continue
            wd[k] = v
            s = self.sem[k] if isinstance(k, str) else self.dsem[k[1]]
            out.append((s, v))
        return out

    def _record(self, my, R, W):
        for r in R:
            self.readers.setdefault(r, {})[my[0]] = my[1]
        for r in W:
            self.last_w[r] = my
            self.readers[r] = {}

    def op(self, eng, fn, R=(), W=()):
        waits = self._waits(eng, self._deps(R, W))
        self.n[eng] += 1
        my = (eng, self.n[eng])
        self.stream[eng].append((waits, fn, self.sem[eng], 1))
        self._record(my, R, W)

    def dma(self, q, out, in_, R=(), W=(), **kw):
        deps = self._deps(R, W)
        lo, hi = self.dpool[q]
        i = self.dnext.get(q, lo)
        self.dnext[q] = lo + (i + 1 - lo) % (hi - lo)
        if self.dtot[i] > 0:
            deps.add((("d", i), self.dtot[i]))
        waits = self._waits(q, deps)
        self.dtot[i] += 16
        my = (("d", i), self.dtot[i])
        nonctg = kw.pop("nonctg", False)
        nc = self.nc

        def fn(e):
            if nonctg:
                with nc.allow_non_contiguous_dma(reason="tiny strided store"):
                    return e.dma_start(out=out, in_=in_, **kw)
            return e.dma_start(out=out, in_=in_, **kw)
        self.stream[q].append((waits, fn, self.dsem[i], 16))
        self._record(my, R, W)

    def emit(self):
        nc = self.nc
        waits = [(self.dsem[i], t) for i, t in enumerate(self.dtot) if t > 0]
        waits += [(self.sem[e], self.n[e]) for e in ENGS if e != "sp" and self.n[e] > 0]
        self.stream["sp"].append((waits, None, None, 0))
        with nc.Block() as block:
            def replay(name, e):
                for waits, fn, s, inc in self.stream[name]:
                    for (ws, wv) in waits:
                        e.wait_ge(ws, wv)
                    if fn is not None:
                        fn(e).then_inc(s, inc)

            @block.tensor
            def _(e):
                replay("pe", e)

            @block.scalar
            def _(e):
                replay("act", e)

            @block.vector
            def _(e):
                replay("dve", e)

            @block.gpsimd
            def _(e):
                replay("pool", e)

            @block.sync
            def _(e):
                replay("sp", e)
        self.es.close()


def bc(ap, shape):
    return ap.unsqueeze(2).broadcast_to(list(shape))


class Builder:
    def __init__(self, do_samples=True, n_st=None, dbg=False, nseq=NSEQ, stop=None, force_last=False):
        self.force_last = force_last
        self.pipeline = os.environ.get('KPIPE', '1') == '1'
        self.nseq = nseq
        self.stop = stop
        self.do_samples = do_samples
        self.n_st = n_st
        self.dbg_on = dbg
        self.nc = bass.Bass("TRN2", target_bir_lowering=False)
        self.f = FW(self.nc)
        self.ins = {}
        self.outs = {}
        self.dbg_outs = {}

    def din(self, name, shape, dtype=F32):
        self.ins[name] = self.nc.dram_tensor(name, list(shape), dtype, kind="ExternalInput").ap()
        return self.ins[name]

    def dout(self, name, shape):
        self.outs[name] = self.nc.dram_tensor(name, list(shape), F32, kind="ExternalOutput").ap()
        return self.outs[name]

    def dscr(self, name, shape, dtype):
        return self.nc.dram_tensor(name, list(shape), dtype, kind="Internal").ap()

    def dump(self, name, ap, shape, R):
        if not self.dbg_on:
            return
        o = self.nc.dram_tensor("dbg_" + name, list(shape), F32, kind="ExternalOutput").ap()
        self.dbg_outs[name] = o
        self.f.dma("pool", o, ap, R=R)

    def pb(self):
        i = self.pnext
        self.pnext = (self.pnext + 1) % 8
        return self.ps[i], ("ps", i)

    def slabs(self, keys):
        order = self.worder
        live = set(keys)
        pos = None
        try:
            pos = order.index(keys[-1], self.wpos)
            self.wpos = pos
        except ValueError:
            pass
        upcoming = order[pos + 1: pos + 1 + self.NBUF] if pos is not None else []
        protect = set(live)
        for k in keys:
            if k not in self.wloaded:
                self._issue(k, protect)
        for nk in upcoming:
            if nk in self.wloaded:
                protect.add(nk)
                continue
            if not self._issue(nk, protect):
                break
            protect.add(nk)
        return [(self.wbuf[self.wloaded[k]], ("wbuf", self.wloaded[k])) for k in keys]

    def slab(self, kind, idx):
        return self.slabs([(kind, idx)])[0]

    def _issue(self, key, protect):
        kind, idx = key
        held = {b: k for k, b in self.wloaded.items()}
        b = None
        for i in range(self.NBUF):
            cand = (self.wnext + i) % self.NBUF
            if held.get(cand) not in protect or cand not in held:
                b = cand
                break
        if b is None:
            return False
        self.wnext = (b + 1) % self.NBUF
        if b in held:
            del self.wloaded[held[b]]
        src, n, parts = self.wsrc(kind, idx)
        self.f.dma("sp", self.wbuf[b][0:parts, 0:n], src, R=[("wscr", kind, idx)], W=[("wbuf", b)])
        self.wloaded[key] = b
        return True

    def wsrc(self, kind, idx):
        if kind == "A":
            return self.WAb[idx], 4096, 128
        if kind == "M":
            return self.WMb[idx], 2048, 128
        if kind == "S":
            return self.WSb[idx], 4096, 64
        if kind == "D":
            return self.WDb[idx], 5632, 128
        if kind == "T":
            return self.WTb, 512, 128
        raise ValueError(kind)

    def build(self):
        nc, f = self.nc, self.f
        xp = self.din("xp", [NSEQ, SEQ, D])
        memp = self.din("memp", [NSEQ, 256, D])
        WAf = self.din("WA", [NA, 128, 4096])
        WMf = self.din("WM", [8, 128, 2048])
        WSf = self.din("WS", [8, 64, 4096])
        WDf = self.din("WD", [16, 128, 5632])
        WTf = self.din("WT", [128, 512])
        c128 = self.din("c128", [128, 8 * 128 + 512 + 32])
        pvec = self.din("pvec", [128, 512])
        prow = self.din("prow", [128, 32 * 3 + 256])
        if self.do_samples:
            xs_in = self.din("xs", [NSMP, D])
            csk = self.din("csk", [NSMP, 128, 256])
            csv = self.din("csv", [NSMP, 128, 256])
            cmk = self.din("cmk", [NSMP, 256, 1024])
            cmv = self.din("cmv", [NSMP, 256, 1024])
            sssm = self.din("sssm", [NSMP, 2048, 128])
            sconv = self.din("sconv", [NSMP * 3, 3072])
        yp = self.dout("yp", [NSEQ, SEQ, D])
        pk = self.dout("pk", [NSEQ, 128, 256])
        pv = self.dout("pv", [NSEQ, 128, 256])
        pmk = self.dout("pmk", [NSEQ, 256, 1024])
        pmv = self.dout("pmv", [NSEQ, 256, 1024])
        pssm = self.dout("pssm", [NSEQ, 2048, 128])
        pconv = self.dout("pconv", [NSEQ, 3, 3072])
        ys = self.dout("ys", [NSMP, D])
        sk = self.dout("sk", [NSMP, 128, 256])
        sv = self.dout("sv", [NSMP, 128, 256])
        sssm_o = self.dout("sssm_o", [NSMP, 2048, 128])
        sconv_o = self.dout("sconv_o", [NSMP, 3, 3072])
        self.io = {**self.ins, **self.outs}
        self.WAb = self.dscr("WAb", [NA, 128, 4096], BF16)
        self.WMb = self.dscr("WMb", [8, 128, 2048], BF16)
        self.WSb = self.dscr("WSb", [8, 64, 4096], BF16)
        self.WDb = self.dscr("WDb", [16, 128, 5632], BF16)
        self.WTb = self.dscr("WTb", [128, 512], BF16)

        def cast(dst, src, n, res):
            if n > 2048:
                assert n % 2048 == 0 or n == 5632
                a = n // 2048 if n % 2048 == 0 else 4
                f.dma("pool", dst.rearrange("p (a b) -> p a b", a=a), src.rearrange("p (a b) -> p a b", a=a), W=[res])
            else:
                f.dma("pool", dst, src, W=[res])
        for i in range(NA):
            cast(self.WAb[i], WAf[i], 4096, ("wscr", "A", i))
        for i in range(8):
            cast(self.WMb[i], WMf[i], 2048, ("wscr", "M", i))
            cast(self.WSb[i], WSf[i], 4096, ("wscr", "S", i))
        for i in range(16):
            cast(self.WDb[i], WDf[i], 5632, ("wscr", "D", i))
        cast(self.WTb, WTf, 512, ("wscr", "T", 0))

        self.ps = [f.psum("ps%d" % i, [128, 512], F32) for i in range(8)]
        self.pnext = 0
        self.NBUF = 4
        self.wbuf = [f.sbuf("wbuf%d" % i, [128, 5632], BF16) for i in range(self.NBUF)]
        self.wnext = 0
        self.wloaded = {}
        self.worder = []
        self.wpos = 0

        C = f.sbuf("c128t", [128, 8 * 128 + 512 + 32], F32)
        f.dma("sp", C[:], c128, W=["C"])
        self.identf = C[:, 0:128]
        self.Tm = C[:, 128:256]
        self.Um = C[:, 256:384]
        self.onesf = C[:, 384:512]
        self.Dmd = C[:, 512:640]
        self.Dmp = C[:, 640:768]
        self.ohs = C[:, 1536:1568]
        PV = f.sbuf("pvect", [128, 512], F32)
        f.dma("sp", PV[:], pvec, W=["PV"])
        PR = f.sbuf("prowt", [128, 352], F32)
        f.dma("sp", PR[:], prow, W=["PR"])
        self.PV, self.PR = PV, PR
        self.gmixT = PV[:, 0:16]
        self.gffnT = PV[:, 16:32]
        self.gmemT = PV[:, 32:48]
        self.ssdnT = PV[:, 48:64]
        self.cwT = PV[:, 64:160]
        self.cbT = PV[:, 160:184]
        self.gq = PV[:, 184:185]
        self.gk = PV[:, 185:186]
        self.gqm = PV[:, 186:188]
        self.sink_raw = PV[:, 188:204]
        self.dskT = PV[:, 204:220]
        self.dm16 = PV[:, 220:236]
        self.dtb = PR[:, 0:32]
        self.alog = PR[:, 32:64]
        self.dsk = PR[:, 64:96]
        self.gkm = PR[:, 96:352]
        CB = f.sbuf("cbf", [128, 128 * 3 + 512], BF16)
        f.op("dve", lambda e: e.tensor_copy(out=CB[:, 0:128], in_=C[:, 0:128]), R=["C"], W=["CBc"])
        f.op("dve", lambda e: e.tensor_copy(out=CB[:, 128:256], in_=C[:, 384:512]), R=["C"], W=["CBc"])
        f.op("dve", lambda e: e.tensor_copy(out=CB[:, 256:384], in_=C[:, 768:896]), R=["C"], W=["CBc"])
        f.op("dve", lambda e: e.tensor_copy(out=CB[:, 384:896], in_=C[:, 1024:1536]), R=["C"], W=["CBc"])
        self.identb = CB[:, 0:128]
        self.onesb = CB[:, 128:256]
        self.blockones = CB[:, 256:384]
        self.maskD = CB[:, 384:896]
        SM = f.sbuf("smallp", [128, 64], F32)
        self.SM = SM
        f.op("act", lambda e: e.mul(out=SM[:, 0:1], in_=PV[:, 184:185], mul=0.125), R=["PV"], W=["SM"])
        f.op("act", lambda e: e.activation(out=SM[:, 1:17], in_=PV[:, 188:204], func=AF.Exp), R=["PV"], W=["SM"])
        f.op("act", lambda e: e.activation(out=SM[:, 17:49], in_=PR[:, 32:64], func=AF.Exp), R=["PR"], W=["SM"])
        f.op("act", lambda e: e.mul(out=SM[:, 17:49], in_=SM[:, 17:49], mul=-1.0), R=["SM"], W=["SM"])
        self.gq8 = SM[:, 0:1]
        self.esink = SM[:, 1:17]
        self.a_bc = SM[:, 17:49]

        self.hT = f.sbuf("hT", [128, KC, NT], BF16)
        self.xtoks = [f.sbuf("xtok%d" % i, [128, D], F32) for i in range(2)]
        self.xtok = self.xtoks[0]
        self.h2T = f.sbuf("h2T", [128, KC, NT], BF16)
        self.xn = f.sbuf("xn", [128, D], BF16)
        self.st1 = f.sbuf("st1", [128, 8], F32)
        self.qT = f.sbuf("qT", [128, 8, NT], BF16)
        self.kT = f.sbuf("kT", [128, 2, 2 * 128], BF16)
        self.vtok = f.sbuf("vtok", [128, 2, 256], BF16)
        self.zs = f.sbuf("zs", [128, D], BF16)
        self.xbcT = f.sbuf("xbcT", [128, 24, 3 + NT], BF16)
        self.hist = f.sbuf("hist", [128, 24, 3], BF16)
        self.qmT = f.sbuf("qmT", [128, 8, NT], BF16)
        self.aT = f.sbuf("aT", [64, 16, NT], BF16)
        self.sT = f.sbuf("sT", [128, KC, NT], BF16)
        self.mT = f.sbuf("mT", [128, 8, NT], BF16)
        self.actT = f.sbuf("actT", [128, 44, NT], BF16)
        self.hst = f.sbuf("hst", [128, D], F32)
        self.hbf = f.sbuf("hbf", [128, D], BF16)
        self.KTm = f.sbuf("KTm", [128, 8, 256], BF16)
        self.Vm = f.sbuf("Vm", [128, 2, 1024], BF16)
        self.lastst = f.sbuf("lastst", [128, 96 + 256 + 256 + 256], F32)
        self.dtraw = f.sbuf("dtraw", [128, 32], F32)
        self.l3t = f.sbuf("l3t", [128, 224], F32)
        f.op("pool", lambda e: e.memset(self.l3t[:], 0.0), W=["lastst"])
        self.macc = [f.sbuf("macc%d" % i, [128, NT], F32) for i in range(2)]
        self.tmpf = [f.sbuf("tmpf%d" % i, [128, 512], F32) for i in range(6)]
        self.tmpb = [f.sbuf("tmpb%d" % i, [128, 512], BF16) for i in range(4)]
        self.tfn = 0
        self.tbn = 0
        self.ssdbig = f.sbuf("ssdbig", [128, 4 * D], BF16)
        self.xs_tok = self.ssdbig[:, 0:D]
        self.xdt = self.ssdbig[:, D:2 * D]
        self.xw = self.ssdbig[:, 2 * D:3 * D]
        self.s_tok = self.ssdbig[:, 3 * D:4 * D]
        self.stage = self.ssdbig[:].bitcast(F32)
        self.STAGE = ["xs_tok", "xdt", "xw", "s_tok"]
        self.bm_tok = f.sbuf("bm_tok", [128, 512], BF16)
        self.ssd_s = f.sbuf("ssd_s", [128, 256], F32)
        self.HI = f.sbuf("HI", [128, 128], BF16)
        self.LO = f.sbuf("LO", [128, 128], BF16)
        self.lhsD = f.sbuf("lhsD", [128, 128], BF16)
        self.rhsD = f.sbuf("rhsD", [128, 32, 128], BF16)
        self.dA4 = f.sbuf("dA4", [128, 4, 32], F32)
        self.CBt = f.sbuf("CBt", [128, 4, 128], F32)
        self.Wp = [f.sbuf("Wp%d" % i, [128, 4, 128], BF16) for i in range(2)]
        if self.do_samples:
            self.cbuf = f.sbuf("cbuf", [128, 24, 4, NSMP], BF16)
            self.cvs = f.sbuf("cvs", [128, 24, NSMP], F32)
            self.cdT = f.sbuf("cdT", [128, 16, NSMP], F32)
            self.dtT = f.sbuf("dtT", [128, 16, NSMP], F32)
            self.xdtT = f.sbuf("xdtT", [128, 16, NSMP], F32)
            self.bcs = f.sbuf("bcs", [NSMP, 1024], F32)
            self.hbf32 = self.bcs
            self.yT = f.sbuf("yT", [128, 16, NSMP], F32)
            self.y2 = f.sbuf("y2", [128, 16, NSMP], F32)
        f.op("dve", lambda e: e.memset(self.lhsD[64:128, :], 1.0), W=["lhsD"])
        f.op("dve", lambda e: e.tensor_copy(out=self.rhsD[0:64], in_=bc(self.ohs[0:64], [64, 32, 128])), R=["C"], W=["rhsD"])

        n_st = SEQ // NT if self.n_st is None else self.n_st
        for seq in range(self.nseq):
            self.seq_start(seq)
            if self.stop == 'seq_start' or n_st == 0:
                continue
            self.run_sequence(seq, n_st)
            if n_st == SEQ // NT or self.force_last:
                self.seq_end(seq)
        if self.do_samples:
            self.sample_group()
        f.emit()
        return nc

    def tf(self):
        i = self.tfn
        self.tfn = (self.tfn + 1) % len(self.tmpf)
        return self.tmpf[i], ("tmpf", i)

    def tb(self):
        i = self.tbn
        self.tbn = (self.tbn + 1) % len(self.tmpb)
        return self.tmpb[i], ("tmpb", i)

    def super_order(self):
        o = []
        o += [("A", SA_Q + i) for i in range(4)] + [("A", SA_K)] + [("A", SA_XBC + i) for i in range(12)]
        o += [("A", SA_QM + i) for i in range(4)] + [("A", SA_V)] + [("A", SA_Z + i) for i in range(8)] + [("T", 0)]
        for s in range(8):
            o += [("A", SA_G0 + s), ("S", s), ("A", SA_G1 + s), ("A", SA_USSD + s), ("A", SA_G2 + s), ("M", s)]
        o += [("A", SA_OUT + i) for i in range(8)]
        for s in range(22):
            o += [("A", SA_GATE + s), ("A", SA_UP + s)]
        o += [("D", i) for i in range(16)]
        return o

    def fm_mm(self, ps_ap, psres, wt, wres, col0, actT, actres, ntok, kcs=KC, M=128):
        wv = wt[:, 0:kcs * CW].rearrange("p (k c) -> p k c", k=kcs)
        for kc in range(kcs):
            self.f.op("pe", lambda e, kc=kc: e.matmul(ps_ap, lhsT=wv[:, kc, col0:col0 + M], rhs=actT[:, kc, 0:ntok],
                                                       start=(kc == 0), stop=(kc == kcs - 1)),
                      R=[wres, actres], W=[psres])

    def tm_mm(self, ps_ap, psres, wt, wres, ncols, actT, actres, t0, bs, kcs=KC, cw=CW, col0=0):
        wv = wt[:, 0:kcs * cw].rearrange("p (k c) -> p k c", k=kcs)
        for kc in range(kcs):
            self.f.op("pe", lambda e, kc=kc: e.matmul(ps_ap, lhsT=actT[:, kc, t0:t0 + bs], rhs=wv[:, kc, col0:col0 + ncols],
                                                       start=(kc == 0), stop=(kc == kcs - 1)),
                      R=[wres, actres], W=[psres])

    def norm_T(self, x_ap, xres, bs, gT, outT, outres, c0):
        f = self.f
        st1, xn = self.st1, self.xn
        f.op("dve", lambda e: e.memset(st1[0:bs, 0:1], 0.0), W=["st1"])
        f.op("act", lambda e: e.activation(out=xn[0:bs, :], in_=x_ap, func=AF.Square, accum_out=st1[0:bs, 0:1]),
             R=[xres], W=["xn", "st1"])
        f.op("act", lambda e: e.activation(out=st1[0:bs, 1:2], in_=st1[0:bs, 0:1], func=AF.Sqrt, scale=1.0 / D, bias=EPS),
             R=["st1"], W=["st1"])
        f.op("dve", lambda e: e.reciprocal(out=st1[0:bs, 2:3], in_=st1[0:bs, 1:2]), R=["st1"], W=["st1"])
        f.op("dve", lambda e: e.tensor_scalar(out=xn[0:bs, :], in0=x_ap, scalar1=st1[0:bs, 2:3], scalar2=None, op0=ALU.mult),
             R=[xres, "st1"], W=["xn"])
        for half in range(2):
            pt, pres = self.pb()
            pbf = pt.bitcast(BF16)
            for k in range(8):
                kc = half * 8 + k
                f.op("pe", lambda e, k=k, kc=kc, pbf=pbf: e.transpose(out=pbf[:, k * bs:(k + 1) * bs], in_=xn[0:bs, kc * 128:(kc + 1) * 128],
                                                                      identity=self.identb[0:bs, 0:bs]),
                     R=["xn", "CBc"], W=[pres])
            f.op("dve", lambda e, half=half, pbf=pbf: e.tensor_tensor(
                out=outT[:, half * 8:half * 8 + 8, c0:c0 + bs],
                in0=pbf[:, 0:8 * bs].rearrange("p (k t) -> p k t", k=8),
                in1=bc(gT[:, half * 8:half * 8 + 8], [128, 8, bs]), op=ALU.mult),
                R=[pres, "PV"], W=[outres])

    def rsqrt_bc(self, ps2, ps2res, n, scale):
        f = self.f
        t1, r1 = self.tf()
        f.op("act", lambda e: e.activation(out=t1[:, 0:n], in_=ps2, func=AF.Sqrt, scale=scale, bias=EPS), R=[ps2res], W=[r1])
        f.op("dve", lambda e: e.reciprocal(out=t1[:, 0:n], in_=t1[:, 0:n]), R=[r1], W=[r1])
        return t1[:, 0:n], r1

    def seq_start(self, seq):
        f = self.f
        io = self.io
        f.op("pool", lambda e: e.memset(self.hst[:], 0.0), W=["hst"])
        f.op("pool", lambda e: e.memset(self.hbf[:], 0.0), W=["hbf"])
        f.op("pool", lambda e: e.memset(self.hist[:], 0.0), W=["hist"])
        self.worder = [("A", SA_MEM + i) for i in range(8)]
        self.wpos = 0
        stage = self.stage
        SR = self.STAGE
        kst = stage[:, 0:2048].rearrange("p (m c) -> p m c", m=2)
        vst = stage[:, 2048:4096].rearrange("p (m c) -> p m c", m=2)
        for mt in range(2):
            xt = self.xtok
            f.dma("pool", xt[:], io["memp"][seq, mt * 128:(mt + 1) * 128, :], W=["x0"])
            self.norm_T(xt[:], "x0", 128, self.gmemT, self.hT, "hT", 0)
            for s in range(8):
                wt, wres = self.slab("A", SA_MEM + s)
                pt, pres = self.pb()
                self.tm_mm(pt[:, 0:256], pres, wt, wres, 256, self.hT, "hT", 0, 128)
                if s < 4:
                    hm = s
                    st1 = self.st1
                    jk, jkres = self.tb()
                    f.op("dve", lambda e: e.memset(st1[:, 4:5], 0.0), W=["st1b"])
                    f.op("act", lambda e, pt=pt, jk=jk: e.activation(out=jk[:, 0:256], in_=pt[:, 0:256], func=AF.Square,
                                                                     accum_out=st1[:, 4:5]), R=[pres], W=[jkres, "st1b"])
                    f.op("act", lambda e: e.activation(out=st1[:, 5:6], in_=st1[:, 4:5], func=AF.Sqrt, scale=1.0 / 256, bias=EPS),
                         R=["st1b"], W=["st1b"])
                    f.op("dve", lambda e: e.reciprocal(out=st1[:, 6:7], in_=st1[:, 5:6]), R=["st1b"], W=["st1b"])
                    f.op("dve", lambda e, pt=pt, mt=mt, hm=hm: e.scalar_tensor_tensor(
                        out=kst[:, mt, hm * 256:(hm + 1) * 256], in0=pt[:, 0:256], scalar=st1[:, 6:7], in1=self.gkm,
                        op0=ALU.mult, op1=ALU.mult), R=[pres, "st1b", "PR"], W=SR)
                else:
                    hm = s - 4
                    f.op("act", lambda e, pt=pt, mt=mt, hm=hm: e.activation(out=vst[:, mt, hm * 256:(hm + 1) * 256], in_=pt[:, 0:256],
                                                                           func=AF.Copy), R=[pres], W=SR)
            self.worder = [("A", SA_MEM + i) for i in range(8)]
            self.wpos = 0
        f.dma("pool", io["pmk"][seq].rearrange("(m p) c -> p m c", p=128), kst, R=SR)
        f.dma("pool", io["pmv"][seq].rearrange("(m p) c -> p m c", p=128), vst, R=SR)
        f.op("pool", lambda e: e.tensor_copy(out=self.Vm[:], in_=vst), R=SR, W=["Vm"])
        kb = self.xn
        f.op("dve", lambda e: e.tensor_copy(out=kb[:], in_=stage[:, 0:2048]), R=SR, W=["xn"])
        self.kmem_T(kb, "xn", self.KTm, "KTm")

    def kmem_T(self, kb, kres, KTm, ktres):
        f = self.f
        kbv = kb[:, 0:2048].rearrange("p (m c) -> p m c", m=2)
        for mt in range(2):
            pt, pres = self.pb()
            pbf = pt.bitcast(BF16)
            for c in range(8):
                f.op("pe", lambda e, c=c, mt=mt, pbf=pbf: e.transpose(out=pbf[:, c * 128:(c + 1) * 128], in_=kbv[:, mt, c * 128:(c + 1) * 128],
                                                                      identity=self.identb), R=[kres, "CBc"], W=[pres])
            f.op("act", lambda e, mt=mt, pbf=pbf: e.activation(out=KTm[:, :, mt * 128:(mt + 1) * 128],
                                                               in_=pbf[:, 0:1024].rearrange("p (c t) -> p c t", c=8), func=AF.Copy),
                 R=[pres], W=[ktres])

    def qk_evac(self, pt, pres, n, gcol, out_ap, outres, out32=None, out32res=None):
        f = self.f
        sq, sqres = self.tb()
        f.op("act", lambda e: e.activation(out=sq[:, 0:n], in_=pt[:, 0:n], func=AF.Square), R=[pres], W=[sqres])
        p2, p2res = self.pb()
        f.op("pe", lambda e: e.matmul(p2[:, 0:n], lhsT=self.blockones, rhs=sq[:, 0:n], start=True, stop=True),
             R=[sqres, "CBc"], W=[p2res])
        rr, rres = self.rsqrt_bc(p2[:, 0:n], p2res, n, 1.0 / 64)
        f.op("dve", lambda e: e.scalar_tensor_tensor(out=out_ap, in0=pt[:, 0:n], scalar=gcol, in1=rr, op0=ALU.mult, op1=ALU.mult),
             R=[pres, rres, "PV", "SM"], W=[outres])
        if out32 is not None:
            f.op("dve", lambda e: e.scalar_tensor_tensor(out=out32, in0=pt[:, 0:n], scalar=gcol, in1=rr, op0=ALU.mult, op1=ALU.mult),
                 R=[pres, rres, "PV", "SM"], W=[out32res])

    def in_proj_fm(self, hT, ntok, qT, kT_out, xbc_out, qmT, last3=None, k32=None):
        f = self.f
        for s in range(4):
            wt, wres = self.slab("A", SA_Q + s)
            for mm in range(2):
                c = s * 2 + mm
                pt, pres = self.pb()
                self.fm_mm(pt[:, 0:ntok], pres, wt, wres, mm * 128, hT, "hT", ntok)
                self.qk_evac(pt, pres, ntok, self.gq8, qT[:, c, 0:ntok], "qT")
        wt, wres = self.slab("A", SA_K)
        for pair in range(2):
            pt, pres = self.pb()
            self.fm_mm(pt[:, 0:ntok], pres, wt, wres, pair * 128, hT, "hT", ntok)
            if k32 is not None:
                self.qk_evac(pt, pres, ntok, self.gk, kT_out(pair), "kT", out32=k32(pair), out32res="lastst")
            else:
                self.qk_evac(pt, pres, ntok, self.gk, kT_out(pair), "kT")
        for s in range(12):
            wt, wres = self.slab("A", SA_XBC + s)
            if getattr(self, "xbc_hook", None) is not None:
                self.xbc_hook(s, wt, wres)
            for mm in range(2):
                c = s * 2 + mm
                pt, pres = self.pb()
                self.fm_mm(pt[:, 0:ntok], pres, wt, wres, mm * 128, hT, "hT", ntok)
                f.op("act", lambda e, pt=pt, c=c: e.activation(out=xbc_out(c), in_=pt[:, 0:ntok], func=AF.Copy), R=[pres], W=[("xbcT", c)])
                if last3 is not None:
                    f.op("dve", lambda e, pt=pt, c=c: e.tensor_copy(out=last3[:, c, :], in_=pt[:, ntok - 4:ntok]), R=[pres], W=["lastst"])
        for hm in range(4):
            wt, wres = self.slab("A", SA_QM + hm)
            pa, ares = self.pb()
            pbk, bres = self.pb()
            self.fm_mm(pa[:, 0:ntok], ares, wt, wres, 0, hT, "hT", ntok)
            self.fm_mm(pbk[:, 0:ntok], bres, wt, wres, 128, hT, "hT", ntok)
            sq, sqres = self.tb()
            f.op("act", lambda e, pa=pa, sq=sq: e.activation(out=sq[:, 0:ntok], in_=pa[:, 0:ntok], func=AF.Square), R=[ares], W=[sqres])
            f.op("act", lambda e, pbk=pbk, sq=sq: e.activation(out=sq[:, 256:256 + ntok], in_=pbk[:, 0:ntok], func=AF.Square), R=[bres], W=[sqres])
            p2, p2res = self.pb()
            f.op("pe", lambda e, p2=p2, sq=sq: e.matmul(p2[:, 0:ntok], lhsT=self.onesb, rhs=sq[:, 0:ntok], start=True, stop=False),
                 R=[sqres, "CBc"], W=[p2res])
            f.op("pe", lambda e, p2=p2, sq=sq: e.matmul(p2[:, 0:ntok], lhsT=self.onesb, rhs=sq[:, 256:256 + ntok], start=False, stop=True),
                 R=[sqres, "CBc"], W=[p2res])
            rr, rres = self.rsqrt_bc(p2[:, 0:ntok], p2res, ntok, 1.0 / 256)
            for dc, (pp, ppres) in enumerate(((pa, ares), (pbk, bres))):
                f.op("dve", lambda e, pp=pp, dc=dc, hm=hm, rr=rr: e.scalar_tensor_tensor(
                    out=qmT[:, hm * 2 + dc, 0:ntok], in0=pp[:, 0:ntok], scalar=self.gqm[:, dc:dc + 1], in1=rr,
                    op0=ALU.mult, op1=ALU.mult), R=[ppres, rres, "PV"], W=["qmT"])

    def swa_heads(self, q_rhs, kprev, kcur, vprev, vcur, nq, kcn, Dmp, Dmc, out, R_in, W_out):
        f = self.f
        n4 = 4 * nq
        for h in range(4):
            PTs = []
            for which in (0, 1):
                if which == 0 and kprev is None:
                    continue
                kk = kprev(h) if which == 0 else kcur(h)
                kn = 128 if which == 0 else kcn
                Dm = Dmp if which == 0 else Dmc
                pt, pres = self.pb()
                f.op("pe", lambda e, pt=pt, kk=kk, kn=kn, h=h: e.matmul(pt[0:kn, 0:n4], lhsT=kk, rhs=q_rhs(h), start=True, stop=True),
                     R=R_in, W=[pres])
                tt, tres = self.tf()
                for g in range(4):
                    sl = -SLOPES[h * 4 + g]
                    f.op("dve", lambda e, pt=pt, tt=tt, g=g, sl=sl, kn=kn, Dm=Dm: e.scalar_tensor_tensor(
                        out=tt[0:kn, g * nq:(g + 1) * nq], in0=Dm, scalar=sl, in1=pt[0:kn, g * nq:(g + 1) * nq],
                        op0=ALU.mult, op1=ALU.add), R=[pres, "C", "PV"], W=[tres])
                PT, ptres = self.tb()
                f.op("act", lambda e, PT=PT, tt=tt, kn=kn: e.activation(out=PT[0:kn, 0:n4], in_=tt[0:kn, 0:n4], func=AF.Exp),
                     R=[tres], W=[ptres])
                PTs.append((PT, ptres, kn, which))
            po, pores = self.pb()
            pd, pdres = self.pb()
            nP = len(PTs)
            for i, (PT, ptres, kn, which) in enumerate(PTs):
                vv = vprev(h) if which == 0 else vcur(h)
                f.op("pe", lambda e, PT=PT, kn=kn, vv=vv, i=i, po=po: e.matmul(po[0:64, 0:n4], lhsT=vv, rhs=PT[0:kn, 0:n4],
                                                                              start=(i == 0), stop=(i == len(PTs) - 1)),
                     R=R_in + [ptres], W=[pores])
            for i, (PT, ptres, kn, which) in enumerate(PTs):
                f.op("pe", lambda e, PT=PT, kn=kn, i=i, pd=pd: e.matmul(pd[0:64, 0:n4], lhsT=self.onesb[0:kn, 0:64], rhs=PT[0:kn, 0:n4],
                                                                       start=(i == 0), stop=(i == nP - 1)),
                     R=["CBc", ptres], W=[pdres])
            dn, dnres = self.tf()
            f.op("dve", lambda e, h=h, dn=dn, pd=pd: e.tensor_tensor(
                out=dn[0:64, 0:n4].rearrange("p (g q) -> p g q", g=4), in0=pd[0:64, 0:n4].rearrange("p (g q) -> p g q", g=4),
                in1=bc(self.esink[0:64, h * 4:h * 4 + 4], [64, 4, nq]), op=ALU.add), R=[pdres, "SM"], W=[dnres])
            f.op("dve", lambda e, dn=dn: e.reciprocal(out=dn[0:64, 0:n4], in_=dn[0:64, 0:n4]), R=[dnres], W=[dnres])
            f.op("dve", lambda e, h=h, dn=dn, po=po: e.tensor_tensor(
                out=out(h), in0=po[0:64, 0:n4].rearrange("p (g q) -> p g q", g=4),
                in1=dn[0:64, 0:n4].rearrange("p (g q) -> p g q", g=4), op=ALU.mult), R=[pores, dnres], W=W_out)
            yield

    def mem_heads(self, q_rhs, KTm, ktres, Vm, vres, nq, out, R_in, W_out):
        f = self.f
        for hm in range(4):
            pt, pres = self.pb()
            for mt in range(2):
                for dc in range(2):
                    f.op("pe", lambda e, mt=mt, dc=dc, hm=hm, pt=pt: e.matmul(
                        pt[:, mt * nq:(mt + 1) * nq], lhsT=KTm[:, hm * 2 + dc, mt * 128:(mt + 1) * 128], rhs=q_rhs(hm * 2 + dc),
                        start=(dc == 0), stop=(dc == 1)), R=R_in + [ktres], W=[pres])
            PT, ptres = self.tb()
            f.op("act", lambda e, PT=PT, pt=pt: e.activation(out=PT[:, 0:2 * nq], in_=pt[:, 0:2 * nq], func=AF.Exp, scale=1.0 / 16),
                 R=[pres], W=[ptres])
            pd, pdres = self.pb()
            for mt in range(2):
                f.op("pe", lambda e, mt=mt, PT=PT, pd=pd: e.matmul(pd[:, 0:nq], lhsT=self.onesb, rhs=PT[:, mt * nq:(mt + 1) * nq],
                                                                   start=(mt == 0), stop=(mt == 1)), R=["CBc", ptres], W=[pdres])
            dn, dnres = self.tf()
            f.op("dve", lambda e, dn=dn, pd=pd: e.reciprocal(out=dn[:, 0:nq], in_=pd[:, 0:nq]), R=[pdres], W=[dnres])
            po, pores = self.pb()
            for dc in range(2):
                for mt in range(2):
                    f.op("pe", lambda e, mt=mt, dc=dc, hm=hm, PT=PT, po=po: e.matmul(
                        po[:, dc * nq:(dc + 1) * nq], lhsT=Vm[:, mt, hm * 256 + dc * 128: hm * 256 + (dc + 1) * 128],
                        rhs=PT[:, mt * nq:(mt + 1) * nq], start=(mt == 0), stop=(mt == 1)), R=[vres, ptres], W=[pores])
            for dc in range(2):
                f.op("dve", lambda e, dc=dc, hm=hm, dn=dn, po=po: e.tensor_tensor(out=out(hm * 2 + dc), in0=po[:, dc * nq:(dc + 1) * nq],
                                                                                 in1=dn[:, 0:nq], op=ALU.mult), R=[pores, dnres], W=W_out)
            yield

    def conv_fm(self, tap, ntok, out, res_of, save_hist=None):
        f = self.f
        for c in range(24):
            acc, ares = self.tf()
            f.op("dve", lambda e, c=c, acc=acc: e.tensor_scalar(out=acc[:, 0:ntok], in0=tap(c, 0), scalar1=self.cwT[:, c * 4:c * 4 + 1],
                                                                scalar2=self.cbT[:, c:c + 1], op0=ALU.mult, op1=ALU.add),
                 R=[res_of(c), "PV", "xbch"], W=[ares])
            for j in range(1, 4):
                f.op("dve", lambda e, c=c, j=j, acc=acc: e.scalar_tensor_tensor(
                    out=acc[:, 0:ntok], in0=tap(c, j), scalar=self.cwT[:, c * 4 + j:c * 4 + j + 1], in1=acc[:, 0:ntok],
                    op0=ALU.mult, op1=ALU.add), R=[res_of(c), "PV", ares, "xbch"], W=[ares])
            if save_hist is not None:
                save_hist(c)
            f.op("act", lambda e, c=c, acc=acc: e.activation(out=out(c), in_=acc[:, 0:ntok], func=AF.Silu), R=[ares], W=[res_of(c)])
            if c % 2 == 1:
                yield

    def ssd_block(self):
        f = self.f
        S = self.ssd_s
        cv = lambda c: self.xbcT[:, c, 3:3 + 128]
        XS = [("xbcT", c) for c in range(16)]
        BMr = [("xbcT", 16 + g) for g in range(4)]
        CMr = [("xbcT", 20 + g) for g in range(4)]
        for half in range(2):
            pt, pres = self.pb()
            pbf = pt.bitcast(BF16)
            for k in range(8):
                f.op("pe", lambda e, k=k, half=half, pbf=pbf: e.transpose(out=pbf[:, k * 128:(k + 1) * 128], in_=cv(half * 8 + k),
                                                                          identity=self.identb), R=[("xbcT", half * 8 + k), "CBc"], W=[pres])
            f.op("act", lambda e, half=half, pbf=pbf: e.activation(out=self.xs_tok[:, half * 1024:(half + 1) * 1024], in_=pbf[:, 0:1024],
                                                                   func=AF.Copy), R=[pres], W=["xs_tok"])
        pt, pres = self.pb()
        pbf = pt.bitcast(BF16)
        for g in range(4):
            f.op("pe", lambda e, g=g, pbf=pbf: e.transpose(out=pbf[:, g * 128:(g + 1) * 128], in_=cv(16 + g), identity=self.identb),
                 R=[BMr[g], "CBc"], W=[pres])
        f.op("act", lambda e, pbf=pbf: e.activation(out=self.bm_tok[:], in_=pbf[:, 0:512], func=AF.Copy), R=[pres], W=["bm_tok"])
        yield
        dt = S[:, 32:64]
        f.op("dve", lambda e: e.tensor_tensor(out=S[:, 0:32], in0=self.dtraw[:], in1=self.dtb, op=ALU.add), R=["dtraw", "PR"], W=["S"])
        f.op("act", lambda e: e.activation(out=S[:, 0:32], in_=S[:, 0:32], func=AF.Exp), R=["S"], W=["S"])
        f.op("act", lambda e: e.activation(out=dt, in_=S[:, 0:32], func=AF.Ln, bias=1.0), R=["S"], W=["S"])
        f.op("dve", lambda e: e.tensor_tensor(out=S[:, 64:96], in0=dt, in1=self.a_bc, op=ALU.mult), R=["S", "SM"], W=["S"])
        f.op("dve", lambda e: e.tensor_copy(out=self.dA4[:], in_=S[:, 64:96].unsqueeze(1).broadcast_to([128, 4, 32])), R=["S"], W=["dA4"])
        pa, pares = self.pb()
        for i, lh in enumerate((self.Tm, self.Um, self.onesf)):
            f.op("pe", lambda e, i=i, lh=lh: e.matmul(pa[:, i * 32:(i + 1) * 32], lhsT=lh, rhs=S[:, 64:96], start=True, stop=True),
                 R=["S", "C"], W=[pares])
        f.op("pe", lambda e: e.matmul(pa[:, 128:256], lhsT=self.dA4[:].rearrange("p a b -> p (a b)"), rhs=self.Tm, start=True, stop=True),
             R=["dA4", "C"], W=[pares])
        f.op("act", lambda e: e.activation(out=S[:, 96:192], in_=pa[:, 0:96], func=AF.Exp), R=[pares], W=["S"])
        eacs, dte, cd = S[:, 96:128], S[:, 128:160], S[:, 160:192]
        f.op("dve", lambda e: e.tensor_tensor(out=S[:, 192:224], in0=dt, in1=dte, op=ALU.mult), R=["S"], W=["S"])
        xs3 = self.xs_tok.rearrange("p (h d) -> p h d", h=32)
        f.op("dve", lambda e: e.tensor_tensor(out=self.xdt.rearrange("p (h d) -> p h d", h=32), in0=xs3, in1=bc(dt, [128, 32, 64]),
                                              op=ALU.mult), R=["xs_tok", "S"], W=["xdt"])
        f.op("dve", lambda e: e.tensor_tensor(out=self.xw.rearrange("p (h d) -> p h d", h=32), in0=xs3,
                                              in1=bc(S[:, 192:224], [128, 32, 64]), op=ALU.mult), R=["xs_tok", "S"], W=["xw"])
        HI, LO = self.HI, self.LO
        f.op("act", lambda e: e.activation(out=HI[:], in_=pa[:, 128:256], func=AF.Copy), R=[pares], W=["HI"])
        f.op("dve", lambda e: e.tensor_tensor(out=LO[:], in0=pa[:, 128:256], in1=HI[:], op=ALU.subtract), R=[pares, "HI"], W=["LO"])
        f.op("act", lambda e: e.mul(out=self.lhsD[0:32, :], in_=HI[0:32, :], mul=-1.0), R=["HI"], W=["lhsD"])
        f.op("act", lambda e: e.mul(out=self.lhsD[32:64, :], in_=LO[32:64, :], mul=-1.0), R=["LO"], W=["lhsD"])
        f.op("dve", lambda e: e.tensor_tensor(out=self.rhsD[64:96], in0=HI[64:96, :].unsqueeze(1).broadcast_to([32, 32, 128]),
                                              in1=bc(self.ohs[64:96], [32, 32, 128]), op=ALU.mult), R=["HI", "C"], W=["rhsD"])
        f.op("dve", lambda e: e.tensor_tensor(out=self.rhsD[96:128], in0=LO[96:128, :].unsqueeze(1).broadcast_to([32, 32, 128]),
                                              in1=bc(self.ohs[96:128], [32, 32, 128]), op=ALU.mult), R=["LO", "C"], W=["rhsD"])
        pc, pcres = self.pb()
        for g in range(4):
            f.op("pe", lambda e, g=g: e.matmul(pc[:, g * 128:(g + 1) * 128], lhsT=cv(16 + g), rhs=cv(20 + g), start=True, stop=True),
                 R=[BMr[g], CMr[g]], W=[pcres])
        f.op("act", lambda e: e.activation(out=self.CBt[:].rearrange("p g l -> p (g l)"), in_=pc[:, 0:512], func=AF.Copy), R=[pcres], W=["CBt"])
        f.op("dve", lambda e: e.memset(S[:, 224:228], 0.0), W=["Sq"])
        yield
        for g in range(4):
            pyd, pydres = self.pb()
            pyo, pyores = self.pb()
            f.op("pe", lambda e, g=g, pyo=pyo: e.matmul(pyo[:, 0:512], lhsT=cv(20 + g), rhs=self.hbf[:, g * 512:(g + 1) * 512], start=True, stop=True),
                 R=[CMr[g], "hbf"], W=[pyores])
            for jj in range(2):
                j = g * 2 + jj
                pD, pDres = self.pb()
                f.op("pe", lambda e, j=j, pD=pD: e.matmul(pD[:, 0:512], lhsT=self.lhsD[:], rhs=self.rhsD[:, 4 * j:4 * j + 4, :].rearrange("p a b -> p (a b)"),
                                                          start=True, stop=False), R=["lhsD", "rhsD"], W=[pDres])
                f.op("pe", lambda e, pD=pD: e.matmul(pD[:, 0:512], lhsT=self.identb, rhs=self.maskD, start=False, stop=True),
                     R=["CBc"], W=[pDres])
                E, eres = self.tf()
                f.op("act", lambda e, E=E, pD=pD: e.activation(out=E[:, 0:512], in_=pD[:, 0:512], func=AF.Exp), R=[pDres], W=[eres])
                Wp = self.Wp[jj]
                f.op("dve", lambda e, E=E, Wp=Wp, g=g: e.tensor_tensor(
                    out=Wp[:], in0=E[:, 0:512].rearrange("p (a l) -> p a l", a=4),
                    in1=self.CBt[:, g, :].unsqueeze(1).broadcast_to([128, 4, 128]), op=ALU.mult), R=[eres, "CBt"], W=[("Wp", jj)])
                for h4 in range(4):
                    h = 4 * j + h4
                    f.op("pe", lambda e, Wp=Wp, h4=h4, h=h, pyd=pyd: e.matmul(pyd[:, (h % 8) * 64:(h % 8 + 1) * 64], lhsT=Wp[:, h4, :],
                                                                             rhs=self.xdt[:, h * 64:(h + 1) * 64], start=True, stop=True),
                         R=[("Wp", jj), "xdt"], W=[pydres])
            y1, y1res = self.tf()
            f.op("dve", lambda e, g=g, y1=y1, pyo=pyo: e.tensor_tensor(
                out=y1[:, 0:512].rearrange("p (h d) -> p h d", h=8), in0=pyo[:, 0:512].rearrange("p (h d) -> p h d", h=8),
                in1=bc(eacs[:, g * 8:(g + 1) * 8], [128, 8, 64]), op=ALU.mult), R=[pyores, "S"], W=[y1res])
            f.op("dve", lambda e, y1=y1, pyd=pyd: e.tensor_tensor(out=y1[:, 0:512], in0=y1[:, 0:512], in1=pyd[:, 0:512], op=ALU.add),
                 R=[pydres, y1res], W=[y1res])
            y3, y3res = self.tf()
            f.op("dve", lambda e, g=g, y3=y3: e.tensor_tensor(
                out=y3[:, 0:512].rearrange("p (h d) -> p h d", h=8), in0=self.xs_tok[:, g * 512:(g + 1) * 512].rearrange("p (h d) -> p h d", h=8),
                in1=bc(self.dsk[:, g * 8:(g + 1) * 8], [128, 8, 64]), op=ALU.mult), R=["xs_tok", "PR"], W=[y3res])
            f.op("dve", lambda e, y1=y1, y3=y3: e.tensor_tensor(out=y1[:, 0:512], in0=y1[:, 0:512], in1=y3[:, 0:512], op=ALU.add),
                 R=[y1res, y3res], W=[y1res])
            f.op("dve", lambda e, g=g, y1=y1: e.tensor_tensor(out=y1[:, 0:512], in0=y1[:, 0:512],
                                                              in1=self.zs[:, g * 512:(g + 1) * 512], op=ALU.mult),
                 R=[y1res, "zs"], W=[y1res])
            f.op("act", lambda e, g=g, y1=y1, y3=y3: e.activation(out=y3[:, 0:512], in_=y1[:, 0:512], func=AF.Square,
                                                                  accum_out=S[:, 224 + g:225 + g]), R=[y1res], W=[y3res, "Sq"])
            f.op("act", lambda e, g=g: e.activation(out=S[:, 228 + g:229 + g], in_=S[:, 224 + g:225 + g], func=AF.Sqrt, scale=1.0 / 512, bias=EPS),
                 R=["Sq"], W=["Sq"])
            f.op("dve", lambda e, g=g: e.reciprocal(out=S[:, 232 + g:233 + g], in_=S[:, 228 + g:229 + g]), R=["Sq"], W=["Sq"])
            pst, pstres = self.pb()
            f.op("pe", lambda e, g=g, pst=pst: e.matmul(pst[:, 0:512], lhsT=self.bm_tok[:, g * 128:(g + 1) * 128],
                                                        rhs=self.xw[:, g * 512:(g + 1) * 512], start=True, stop=True),
                 R=["bm_tok", "xw"], W=[pstres])
            f.op("dve", lambda e, g=g, y1=y1: e.tensor_scalar(out=self.s_tok[:, g * 512:(g + 1) * 512], in0=y1[:, 0:512],
                                                              scalar1=S[:, 232 + g:233 + g], scalar2=None, op0=ALU.mult), R=[y1res, "Sq"], W=["s_tok"])
            hg = self.hst[:, g * 512:(g + 1) * 512]
            f.op("dve", lambda e, g=g, hg=hg: e.tensor_tensor(out=hg.rearrange("p (h d) -> p h d", h=8), in0=hg.rearrange("p (h d) -> p h d", h=8),
                                                              in1=bc(cd[:, g * 8:(g + 1) * 8], [128, 8, 64]), op=ALU.mult),
                 R=["S", "hst"], W=["hst"])
            f.op("dve", lambda e, hg=hg, pst=pst: e.tensor_tensor(out=hg, in0=hg, in1=pst[:, 0:512], op=ALU.add), R=[pstres, "hst"], W=["hst"])
            f.op("pool", lambda e, g=g, hg=hg: e.tensor_copy(out=self.hbf[:, g * 512:(g + 1) * 512], in_=hg), R=["hst"], W=["hbf"])
            yield
        for half in range(2):
            pt, pres = self.pb()
            pbf = pt.bitcast(BF16)
            for k in range(8):
                kc = half * 8 + k
                f.op("pe", lambda e, k=k, kc=kc, pbf=pbf: e.transpose(out=pbf[:, k * 128:(k + 1) * 128], in_=self.s_tok[:, kc * 128:(kc + 1) * 128],
                                                                      identity=self.identb), R=["s_tok", "CBc"], W=[pres])
            f.op("dve", lambda e, half=half, pbf=pbf: e.tensor_tensor(
                out=self.sT[:, half * 8:half * 8 + 8, 0:128], in0=pbf[:, 0:1024].rearrange("p (k t) -> p k t", k=8),
                in1=bc(self.ssdnT[:, half * 8:half * 8 + 8], [128, 8, 128]), op=ALU.mult), R=[pres, "PV"], W=["sT"])

    def merge_out(self, ntok, bs, xres, x_ap, aT, sT, mT, hT, mergedT, mres_of, h2T):
        f = self.f
        actT = self.actT
        for s in range(8):
            for bi in range(3):
                gk = ("A", (SA_G0, SA_G1, SA_G2)[bi] + s)
                uk = (("S", s), ("A", SA_USSD + s), ("M", s))[bi]
                (gw, gwr), (uw, uwr) = self.slabs([gk, uk])
                for mm in range(2):
                    m = s * 2 + mm
                    col0 = mm * 128
                    acc = self.macc[mm]
                    accres = ("macc", mm)
                    pg, pgres = self.pb()
                    self.fm_mm(pg[:, 0:ntok], pgres, gw, gwr, col0, hT, "hT", ntok)
                    sg, sgres = self.tf()
                    f.op("act", lambda e, sg=sg, pg=pg: e.activation(out=sg[:, 0:ntok], in_=pg[:, 0:ntok], func=AF.Sigmoid), R=[pgres], W=[sgres])
                    pu, pures = self.pb()
                    if bi == 0:
                        uav = uw[0:64, 0:4096].rearrange("p (h c) -> p h c", h=16)
                        for hd in range(16):
                            f.op("pe", lambda e, hd=hd, pu=pu, uav=uav, col0=col0: e.matmul(pu[:, 0:ntok], lhsT=uav[:, hd, col0:col0 + 128], rhs=aT[0:64, hd, 0:ntok],
                                                                                         start=(hd == 0), stop=(hd == 15)), R=[uwr, "aT"], W=[pures])
                        f.op("dve", lambda e, acc=acc, sg=sg, pu=pu: e.tensor_tensor(out=acc[:, 0:ntok], in0=sg[:, 0:ntok], in1=pu[:, 0:ntok], op=ALU.mult),
                             R=[sgres, pures], W=[accres])
                    else:
                        if bi == 1:
                            self.fm_mm(pu[:, 0:ntok], pures, uw, uwr, col0, sT, "sT", ntok)
                        else:
                            self.fm_mm(pu[:, 0:ntok], pures, uw, uwr, col0, mT, "mT", ntok, kcs=8)
                        f.op("dve", lambda e, sg=sg, pu=pu: e.tensor_tensor(out=sg[:, 0:ntok], in0=sg[:, 0:ntok], in1=pu[:, 0:ntok], op=ALU.mult),
                             R=[sgres, pures], W=[sgres])
                        if bi == 1:
                            f.op("dve", lambda e, acc=acc, sg=sg: e.tensor_tensor(out=acc[:, 0:ntok], in0=acc[:, 0:ntok], in1=sg[:, 0:ntok], op=ALU.add),
                                 R=[sgres, accres], W=[accres])
                        else:
                            f.op("dve", lambda e, acc=acc, sg=sg, m=m: e.tensor_tensor(out=mergedT(m), in0=acc[:, 0:ntok], in1=sg[:, 0:ntok], op=ALU.add),
                                 R=[sgres, accres], W=[mres_of(m)])
        MR = [mres_of(m) for m in range(16)]
        for s in range(8):
            wt, wres = self.slab("A", SA_OUT + s)
            wv = wt[:, 0:4096].rearrange("p (k c) -> p k c", k=16)
            pt, pres = self.pb()
            for kc in range(16):
                f.op("pe", lambda e, kc=kc, pt=pt, wv=wv: e.matmul(pt[0:bs, 0:256], lhsT=mergedT(kc)[:, 0:bs], rhs=wv[:, kc, :],
                                                                   start=(kc == 0), stop=(kc == 15)), R=[wres, mres_of(kc)], W=[pres])
            xa = x_ap[:, s * 256:(s + 1) * 256]
            f.op("dve", lambda e, xa=xa, pt=pt: e.tensor_tensor(out=xa, in0=xa, in1=pt[0:bs, 0:256], op=ALU.add), R=[pres, xres], W=[xres])
        self.norm_T(x_ap, xres, bs, self.gffnT, h2T, "h2T", 0)

    def ffn_gen(self, ntok, bs, xres, x_ap, hT):
        f = self.f
        actT = self.actT
        for s in range(22):
            (gw, gwr), (uw, uwr) = self.slabs([("A", SA_GATE + s), ("A", SA_UP + s)])
            for mm in range(2):
                j = s * 2 + mm
                pg, pgres = self.pb()
                pu, pures = self.pb()
                self.fm_mm(pg[:, 0:ntok], pgres, gw, gwr, mm * 128, hT, "h2T", ntok)
                self.fm_mm(pu[:, 0:ntok], pures, uw, uwr, mm * 128, hT, "h2T", ntok)
                sg, sgres = self.tf()
                f.op("act", lambda e, sg=sg, pg=pg: e.activation(out=sg[:, 0:ntok], in_=pg[:, 0:ntok], func=AF.Silu), R=[pgres], W=[sgres])
                f.op("dve", lambda e, sg=sg, pu=pu, j=j: e.tensor_tensor(out=actT[:, j, 0:ntok], in0=sg[:, 0:ntok], in1=pu[:, 0:ntok], op=ALU.mult),
                     R=[sgres, pures], W=["actT"])
            yield
        for cg in range(8):
            pt, pres = self.pb()
            for half in range(2):
                wt, wres = self.slab("D", cg * 2 + half)
                wv = wt[:, 0:5632].rearrange("p (k c) -> p k c", k=22)
                for kc in range(22):
                    f.op("pe", lambda e, kc=kc, half=half, pt=pt, wv=wv: e.matmul(
                        pt[0:bs, 0:256], lhsT=actT[:, half * 22 + kc, 0:bs], rhs=wv[:, kc, :],
                        start=(half == 0 and kc == 0), stop=(half == 1 and kc == 21)), R=[wres, "actT"], W=[pres])
            xa = x_ap[:, cg * 256:(cg + 1) * 256]
            f.op("dve", lambda e, xa=xa, pt=pt: e.tensor_tensor(out=xa, in0=xa, in1=pt[0:bs, 0:256], op=ALU.add), R=[pres, xres], W=[xres])
            yield

    def order_in(self):
        o = [("A", SA_Q + i) for i in range(4)] + [("A", SA_K)] + [("A", SA_XBC + i) for i in range(12)]
        o += [("A", SA_QM + i) for i in range(4)] + [("A", SA_V)] + [("A", SA_Z + i) for i in range(8)] + [("T", 0)]
        return o

    def order_merge(self):
        o = []
        for s in range(8):
            o += [("A", SA_G0 + s), ("S", s), ("A", SA_G1 + s), ("A", SA_USSD + s), ("A", SA_G2 + s), ("M", s)]
        o += [("A", SA_OUT + i) for i in range(8)]
        return o

    def order_ffn(self):
        o = []
        for s in range(22):
            o += [("A", SA_GATE + s), ("A", SA_UP + s)]
        o += [("D", i) for i in range(16)]
        return o

    def phase_in(self, seq, st, par, last):
        f = self.f
        io = self.io
        t0 = st * NT
        xt = self.xtoks[par]
        xres = "x%d" % par
        f.dma("pool", xt[:], io["xp"][seq, t0:t0 + 128, :], W=[xres])
        self.norm_T(xt[:], xres, 128, self.gmixT, self.hT, "hT", 0)
        last3 = None
        k32 = None
        L = self.lastst
        if last:
            last3 = self.l3t[:, 0:96].rearrange("p (c j) -> p c j", c=24)
            k32t = L[:, 96:96 + 256].rearrange("p (a t) -> p a t", a=2)
            k32 = lambda pair: k32t[:, pair, :]
        f.op("pool", lambda e: e.tensor_copy(out=self.xbcT[:, :, 0:3], in_=self.hist[:]), R=["hist"], W=["xbch"])
        self.in_proj_fm(self.hT, NT, self.qT, lambda pair: self.kT[:, pair, 128:256],
                        lambda c: self.xbcT[:, c, 3:3 + NT], self.qmT, last3=last3, k32=k32)
        wt, wres = self.slab("A", SA_V)
        pt, pres = self.pb()
        self.tm_mm(pt[:, 0:256], pres, wt, wres, 256, self.hT, "hT", 0, 128)
        f.op("act", lambda e, pt=pt: e.activation(out=self.vtok[:, 1, :], in_=pt[:, 0:256], func=AF.Copy), R=[pres], W=["vtok"])
        if last:
            f.op("dve", lambda e, pt=pt: e.tensor_copy(out=L[:, 352:608], in_=pt[:, 0:256]), R=[pres], W=["lastst"])
        for s in range(8):
            wt, wres = self.slab("A", SA_Z + s)
            pt, pres = self.pb()
            self.tm_mm(pt[:, 0:256], pres, wt, wres, 256, self.hT, "hT", 0, 128)
            f.op("act", lambda e, pt=pt, s=s: e.activation(out=self.zs[:, s * 256:(s + 1) * 256], in_=pt[:, 0:256], func=AF.Silu),
                 R=[pres], W=["zs"])
        wt, wres = self.slab("T", 0)
        pt, pres = self.pb()
        self.tm_mm(pt[:, 0:32], pres, wt, wres, 32, self.hT, "hT", 0, 128, cw=32)
        f.op("act", lambda e, pt=pt: e.activation(out=self.dtraw[:], in_=pt[:, 0:32], func=AF.Copy), R=[pres], W=["dtraw"])

    def phase_mix(self, first):
        f = self.f

        def save_hist(c):
            f.op("pool", lambda e, c=c: e.tensor_copy(out=self.hist[:, c, :], in_=self.xbcT[:, c, NT:NT + 3]), R=[("xbcT", c)], W=["hist"])
        yield from self.conv_fm(lambda c, j: self.xbcT[:, c, j:j + NT], NT, lambda c: self.xbcT[:, c, 3:3 + NT], lambda c: ("xbcT", c), save_hist=save_hist)
        has_prev = not first
        yield from self.swa_heads(
            q_rhs=lambda h: self.qT[64 * (h % 2):64 * (h % 2) + 64, (h // 2) * 4:(h // 2) * 4 + 4, 0:128],
            kprev=(lambda h: self.kT[64 * (h % 2):64 * (h % 2) + 64, h // 2, 0:128]) if has_prev else None,
            kcur=lambda h: self.kT[64 * (h % 2):64 * (h % 2) + 64, h // 2, 128:256],
            vprev=lambda h: self.vtok[:, 0, h * 64:(h + 1) * 64],
            vcur=lambda h: self.vtok[:, 1, h * 64:(h + 1) * 64],
            nq=128, kcn=128, Dmp=self.Dmp, Dmc=self.Dmd,
            out=lambda h: self.aT[0:64, h * 4:h * 4 + 4, 0:128],
            R_in=["qT", "kT", "vtok"], W_out=["aT"])
        f.op("pool", lambda e: e.tensor_copy(out=self.kT[:, :, 0:128], in_=self.kT[:, :, 128:256]), R=["kT"], W=["kT"])
        f.op("pool", lambda e: e.tensor_copy(out=self.vtok[:, 0, :], in_=self.vtok[:, 1, :]), R=["vtok"], W=["vtok"])
        yield from self.ssd_block()
        yield from self.mem_heads(q_rhs=lambda c: self.qmT[:, c, 0:NT], KTm=self.KTm, ktres="KTm", Vm=self.Vm, vres="Vm", nq=NT,
                                  out=lambda c: self.mT[:, c, 0:NT], R_in=["qmT"], W_out=["mT"])

    def phase_merge(self, par):
        self.merge_out(NT, 128, "x%d" % par, self.xtoks[par][:], self.aT, self.sT, self.mT, self.hT,
                       lambda m: self.xbcT[:, m, 3:3 + NT], lambda m: ("xbcT", m), self.h2T)

    @staticmethod
    def drain(gen):
        for _ in gen:
            pass

    @staticmethod
    def interleave(ga, gb, na=1, nb=1):
        da = db = False
        while not (da and db):
            for _ in range(na):
                if not da:
                    try:
                        next(ga)
                    except StopIteration:
                        da = True
            for _ in range(nb):
                if not db:
                    try:
                        next(gb)
                    except StopIteration:
                        db = True

    def run_sequence(self, seq, n_st):
        f = self.f
        io = self.io
        nfull = SEQ // NT
        is_last = lambda st: (st == nfull - 1) or (self.force_last and st == n_st - 1)
        self.worder = self.order_in() + self.order_merge()
        self.wpos = 0
        self.phase_in(seq, 0, 0, is_last(0))
        self.drain(self.phase_mix(True))
        for st in range(n_st):
            par = st % 2
            nxt = st + 1 < n_st
            self.worder = self.order_merge() + (self.order_in() if nxt else []) + self.order_ffn() + self.order_merge()
            self.wpos = 0
            self.phase_merge(par)
            ffn = self.ffn_gen(NT, 128, "x%d" % par, self.xtoks[par][:], self.h2T)
            if nxt:
                self.phase_in(seq, st + 1, 1 - par, is_last(st + 1))
                if self.pipeline:
                    self.interleave(ffn, self.phase_mix(False), 1, 1)
                else:
                    self.drain(ffn)
                    self.drain(self.phase_mix(False))
            else:
                self.drain(ffn)
            f.dma("pool", io["yp"][seq, st * NT:st * NT + 128, :], self.xtoks[par][:], R=["x%d" % par])
            if is_last(st):
                self.seq_last_outputs(seq)

    def seq_last_outputs(self, seq):
        f = self.f
        io = self.io
        L = self.lastst
        k32t = L[:, 96:96 + 256].rearrange("p (a t) -> p a t", a=2)
        if 'pkpv' in _SKIP:
            return
        ptk, pkres = self.pb()
        for pair in range(2):
            f.op("pe", lambda e, pair=pair, ptk=ptk: e.transpose(out=ptk[:, pair * 128:(pair + 1) * 128], in_=k32t[:, pair, :], identity=self.identf),
                 R=["lastst", "C"], W=[pkres])
        f.op("act", lambda e, ptk=ptk: e.activation(out=L[:, 608:864], in_=ptk[:, 0:256], func=AF.Copy), R=[pkres], W=["lastst"])
        f.dma("pool", io["pk"][seq], L[:, 608:864], R=["lastst"])
        f.dma("pool", io["pv"][seq], L[:, 352:608], R=["lastst"])
        if 'pconv' in _SKIP:
            return
        l3t = self.l3t
        pc3 = self.stage[0:4, 0:3072]
        for q6 in range(6):
            pt, pres = self.pb()
            for k in range(4):
                c = q6 * 4 + k
                f.op("pe", lambda e, k=k, c=c, pt=pt: e.transpose(out=pt[:, k * 128:(k + 1) * 128], in_=l3t[:, c * 4:c * 4 + 128], identity=self.identf),
                     R=["lastst", "C"], W=[pres])
            f.op("act", lambda e, q6=q6, pt=pt: e.activation(out=pc3[:, q6 * 512:(q6 + 1) * 512], in_=pt[0:4, 0:512], func=AF.Copy),
                 R=[pres], W=self.STAGE)
        f.dma("pool", io["pconv"][seq], pc3[1:4, :], R=self.STAGE)

    def seq_end(self, seq):
        if "seq_end" in _SKIP:
            return
        f = self.f
        io = self.io
        stage = self.stage
        SR = self.STAGE
        for q4 in range(4):
            pt, pres = self.pb()
            for k in range(4):
                c = q4 * 4 + k
                f.op("pe", lambda e, k=k, c=c, pt=pt: e.transpose(out=pt[:, k * 128:(k + 1) * 128], in_=self.hst[:, c * 128:(c + 1) * 128],
                                                                  identity=self.identf), R=["hst", "C"], W=[pres])
            f.op("act", lambda e, q4=q4, pt=pt: e.activation(out=stage[:, q4 * 512:(q4 + 1) * 512], in_=pt[:, 0:512], func=AF.Copy),
                 R=[pres], W=SR)
        f.dma("pool", io["pssm"][seq].rearrange("(c q) n -> q c n", q=128), stage[:, 0:2048].rearrange("p (c n) -> p c n", c=16), R=SR)

    def sample_group(self):
        f = self.f
        io = self.io
        NS = NSMP
        L = self.lastst
        stage = self.stage
        SR = self.STAGE
        x16 = self.xtok[0:NS, :]
        self.worder = self.order_in() + self.order_merge()
        self.wpos = 0
        f.dma("pool", io["sk"][:, 0:127, :], io["csk"][:, 1:128, :])
        f.dma("pool", io["sv"][:, 0:127, :], io["csv"][:, 1:128, :])
        f.dma("pool", io["sconv_o"][:, 0:2, :], io["sconv"].rearrange("(b j) c -> b j c", j=3)[:, 1:3, :])
        f.dma("pool", x16, io["xs"], W=["x0"])
        self.norm_T(x16, "x0", NS, self.gmixT, self.hT, "hT", 0)
        cbuf = self.cbuf
        f.dma("pool", stage[0:48, 0:3072], io["sconv"], W=SR)
        for q6 in range(6):
            pt, pres = self.pb()
            for k in range(4):
                c = q6 * 4 + k
                f.op("pe", lambda e, k=k, c=c, pt=pt: e.transpose(out=pt[:, k * 48:(k + 1) * 48], in_=stage[0:48, c * 128:(c + 1) * 128],
                                                                  identity=self.identf[0:48, 0:48]), R=SR + ["C"], W=[pres])
            for k in range(4):
                c = q6 * 4 + k
                f.op("act", lambda e, k=k, c=c, pt=pt: e.activation(out=cbuf[:, c, 0:3, :], in_=pt[:, k * 48:(k + 1) * 48].rearrange("p (b j) -> p j b", j=3),
                                                                    func=AF.Copy), R=[pres], W=["xbch"])
        k32s = L[:, 96:96 + 256].rearrange("p (a t) -> p a t", a=2)
        f.op("pool", lambda e: e.memset(L[:, 96:352], 0.0), W=["lastst"])
        xbctok = self.hst
        def xbc_hook(s, wt, wres):
            pt, pres = self.pb()
            self.tm_mm(pt[0:NS, 0:256], pres, wt, wres, 256, self.hT, "hT", 0, NS)
            if s < 8:
                dst = self.hst[0:NS, s * 256:(s + 1) * 256]
            else:
                dst = self.hbf32[0:NS, (s - 8) * 256:(s - 7) * 256]
            f.op("act", lambda e, pt=pt, dst=dst: e.activation(out=dst, in_=pt[0:NS, 0:256], func=AF.Copy), R=[pres], W=["hst" if s < 8 else "bcs"])
        self.xbc_hook = xbc_hook
        self.in_proj_fm(self.hT, NS, self.qT, lambda pair: self.kT[:, pair, 128:128 + NS],
                        lambda c: cbuf[:, c, 3, :], self.qmT, last3=None, k32=lambda pair: k32s[:, pair, 0:NS])
        self.xbc_hook = None
        f.dma("pool", io["sconv_o"][:, 2, 0:2048], self.hst[0:NS, :], R=["hst"])
        f.dma("pool", io["sconv_o"][:, 2, 2048:3072], self.hbf32[0:NS, :], R=["bcs"])
        pt, pres = self.pb()
        for pair in range(2):
            f.op("pe", lambda e, pair=pair, pt=pt: e.transpose(out=pt[:, pair * 128:(pair + 1) * 128], in_=k32s[:, pair, :], identity=self.identf),
                 R=["lastst", "C"], W=[pres])
        f.op("act", lambda e, pt=pt: e.activation(out=L[0:NS, 608:864], in_=pt[0:NS, 0:256], func=AF.Copy), R=[pres], W=["lastst"])
        f.dma("pool", io["sk"][:, 127, :], L[0:NS, 608:864], R=["lastst"])
        wt, wres = self.slab("A", SA_V)
        pt, pres = self.pb()
        self.tm_mm(pt[0:NS, 0:256], pres, wt, wres, 256, self.hT, "hT", 0, NS)
        f.op("act", lambda e, pt=pt: e.activation(out=self.vtok[0:NS, 1, :], in_=pt[0:NS, 0:256], func=AF.Copy), R=[pres], W=["vtok"])
        f.op("dve", lambda e, pt=pt: e.tensor_copy(out=L[0:NS, 352:608], in_=pt[0:NS, 0:256]), R=[pres], W=["lastst"])
        f.dma("pool", io["sv"][:, 127, :], L[0:NS, 352:608], R=["lastst"])
        zT = self.zs[:, 0:16 * NS].rearrange("p (c t) -> p c t", c=16)
        for s in range(8):
            wt, wres = self.slab("A", SA_Z + s)
            for mm in range(2):
                pt, pres = self.pb()
                self.fm_mm(pt[:, 0:NS], pres, wt, wres, mm * 128, self.hT, "hT", NS)
                f.op("act", lambda e, pt=pt, c=s * 2 + mm: e.activation(out=zT[:, c, :], in_=pt[:, 0:NS], func=AF.Silu), R=[pres], W=["zs"])
        wt, wres = self.slab("T", 0)
        pt, pres = self.pb()
        self.tm_mm(pt[0:NS, 0:32], pres, wt, wres, 32, self.hT, "hT", 0, NS, cw=32)
        f.op("act", lambda e, pt=pt: e.activation(out=self.dtraw[0:NS, :], in_=pt[0:NS, 0:32], func=AF.Copy), R=[pres], W=["dtraw"])
        cvs = self.cvs
        self.drain(self.conv_fm(lambda c, j: cbuf[:, c, j, :], NS, lambda c: cvs[:, c, :], lambda c: ("xbcT", c)))
        CVS = [("xbcT", c) for c in range(24)]
        for b in range(NS):
            ck, ckres = self.tf()
            f.dma("pool", ck[:, 0:256], io["csk"][b], W=[ckres])
            f.dma("pool", ck[:, 256:512], io["csv"][b], W=[ckres])
            cb16, cbres = self.tb()
            f.op("pool", lambda e, ck=ck, cb16=cb16: e.tensor_copy(out=cb16[:, 0:512], in_=ck[:, 0:512]), R=[ckres], W=[cbres])
            pt, pres = self.pb()
            pbf = pt.bitcast(BF16)
            for pair in range(2):
                f.op("pe", lambda e, pair=pair, pbf=pbf, cb16=cb16: e.transpose(out=pbf[:, pair * 128:(pair + 1) * 128], in_=cb16[:, pair * 128:(pair + 1) * 128],
                                                                                identity=self.identb), R=[cbres, "CBc"], W=[pres])
            kTc, kTres = self.tb()
            f.op("act", lambda e, pbf=pbf, kTc=kTc: e.activation(out=kTc[:, 0:256], in_=pbf[:, 0:256], func=AF.Copy), R=[pres], W=[kTres])
            self.drain(self.swa_heads(
                q_rhs=lambda h, b=b: self.qT[64 * (h % 2):64 * (h % 2) + 64, (h // 2) * 4:(h // 2) * 4 + 4, b:b + 1],
                kprev=lambda h, kTc=kTc: kTc[64 * (h % 2):64 * (h % 2) + 64, (h // 2) * 128:(h // 2 + 1) * 128],
                kcur=lambda h: self.kT[64 * (h % 2):64 * (h % 2) + 64, h // 2, 128:128 + NS],
                vprev=lambda h, cb16=cb16: cb16[:, 256 + h * 64:256 + (h + 1) * 64],
                vcur=lambda h: self.vtok[0:NS, 1, h * 64:(h + 1) * 64],
                nq=1, kcn=NS, Dmp=self.Dmp[:, 0:1], Dmc=self.dm16[0:NS, b:b + 1],
                out=lambda h, b=b: self.aT[0:64, h * 4:h * 4 + 4, b:b + 1],
                R_in=["qT", "kT", "vtok", kTres, cbres], W_out=["aT"]))
            kst = stage[:, 0:2048]
            vst = stage[:, 2048:4096]
            f.dma("pool", kst.rearrange("p (m c) -> p m c", m=2), io["cmk"][b].rearrange("(m p) c -> p m c", p=128), W=SR)
            f.dma("pool", vst.rearrange("p (m c) -> p m c", m=2), io["cmv"][b].rearrange("(m p) c -> p m c", p=128), W=SR)
            f.op("pool", lambda e: e.tensor_copy(out=self.Vm[:].rearrange("p m c -> p (m c)"), in_=vst), R=SR, W=["Vm"])
            f.op("dve", lambda e: e.tensor_copy(out=self.xn[:], in_=kst), R=SR, W=["xn"])
            self.kmem_T(self.xn, "xn", self.KTm, "KTm")
            self.drain(self.mem_heads(q_rhs=lambda c, b=b: self.qmT[:, c, b:b + 1], KTm=self.KTm, ktres="KTm", Vm=self.Vm, vres="Vm", nq=1,
                                      out=lambda c, b=b: self.mT[:, c, b:b + 1], R_in=["qmT"], W_out=["mT"]))
        S = self.ssd_s
        dt = S[0:NS, 32:64]
        f.op("dve", lambda e: e.tensor_tensor(out=S[0:NS, 0:32], in0=self.dtraw[0:NS, :], in1=self.dtb[0:NS, :], op=ALU.add), R=["dtraw", "PR"], W=["S"])
        f.op("act", lambda e: e.activation(out=S[0:NS, 0:32], in_=S[0:NS, 0:32], func=AF.Exp), R=["S"], W=["S"])
        f.op("act", lambda e: e.activation(out=dt, in_=S[0:NS, 0:32], func=AF.Ln, bias=1.0), R=["S"], W=["S"])
        f.op("dve", lambda e: e.tensor_tensor(out=S[0:NS, 64:96], in0=dt, in1=self.a_bc[0:NS, :], op=ALU.mult), R=["S", "SM"], W=["S"])
        f.op("act", lambda e: e.activation(out=S[0:NS, 96:128], in_=S[0:NS, 64:96], func=AF.Exp), R=["S"], W=["S"])
        ex = stage[0:NS, :]
        f.op("dve", lambda e: e.tensor_copy(out=ex[:, 0:2048].rearrange("p (h d) -> p h d", h=32), in_=bc(S[0:NS, 96:128], [NS, 32, 64])), R=["S"] + SR, W=SR)
        f.op("dve", lambda e: e.tensor_copy(out=ex[:, 2048:4096].rearrange("p (h d) -> p h d", h=32), in_=bc(dt, [NS, 32, 64])), R=["S"] + SR, W=SR)
        cdT, dtT = self.cdT, self.dtT
        for which, dst in ((0, cdT), (1, dtT)):
            pt, pres = self.pb()
            for c in range(16):
                f.op("pe", lambda e, c=c, which=which, pt=pt: e.transpose(out=pt[:, c * NS:(c + 1) * NS], in_=ex[:, which * 2048 + c * 128: which * 2048 + (c + 1) * 128],
                                                                          identity=self.identf[0:NS, 0:NS]), R=SR + ["C"], W=[pres])
            f.op("act", lambda e, pt=pt, dst=dst: e.activation(out=dst[:].rearrange("p c t -> p (c t)"), in_=pt[:, 0:16 * NS], func=AF.Copy), R=[pres], W=["cdT"])
        xdtT = self.xdtT
        f.op("dve", lambda e: e.tensor_tensor(out=xdtT[:], in0=cvs[:, 0:16, :], in1=dtT[:], op=ALU.mult), R=CVS + ["cdT"], W=["xdtT"])
        bcs = self.bcs
        bpad = self.hst[:, 0:1024].rearrange("p (c t) -> p c t", c=8)
        f.op("pool", lambda e: e.memset(self.hst[:, 0:1024], 0.0), W=["hst"])
        f.op("dve", lambda e: e.tensor_copy(out=bpad[:, :, 0:NS], in_=cvs[:, 16:24, :]), R=CVS + ["hst"], W=["hst"])
        for half in range(2):
            pt, pres = self.pb()
            for g in range(4):
                f.op("pe", lambda e, g=g, half=half, pt=pt: e.transpose(out=pt[:, g * 128:(g + 1) * 128], in_=bpad[:, half * 4 + g, :], identity=self.identf),
                     R=["hst", "C"], W=[pres])
            f.op("act", lambda e, half=half, pt=pt: e.activation(out=bcs[0:NS, half * 512:(half + 1) * 512], in_=pt[0:NS, 0:512], func=AF.Copy), R=[pres], W=["bcs"])
        oh16 = self.rhsD[:].rearrange("p a b -> p (a b)").bitcast(F32)[0:NS, :].rearrange("p (a b) -> p a b", a=NS)
        f.op("dve", lambda e: e.tensor_copy(out=oh16[:], in_=bc(self.identf[0:NS, 0:NS], [NS, NS, 128])), R=["C"], W=["rhsD"])
        yT = self.yT
        U = self.hst
        for b in range(NS):
            H = stage[:, (b % 2) * 2048:(b % 2 + 1) * 2048]
            Hres = ["xs_tok", "xdt"] if b % 2 == 0 else ["xw", "s_tok"]
            H3 = H.rearrange("p (c n) -> p c n", c=16)
            f.dma("pool", H3, io["sssm"][b].rearrange("(c q) n -> q c n", q=128), W=Hres)
            pbm, pbmres = self.pb()
            pcm, pcmres = self.pb()
            f.op("pe", lambda e, b=b, pbm=pbm: e.matmul(pbm[:, 0:512], lhsT=oh16[0:NS, b, :], rhs=bcs[0:NS, 0:512], start=True, stop=True), R=["rhsD", "bcs"], W=[pbmres])
            f.op("pe", lambda e, b=b, pcm=pcm: e.matmul(pcm[:, 0:512], lhsT=oh16[0:NS, b, :], rhs=bcs[0:NS, 512:1024], start=True, stop=True), R=["rhsD", "bcs"], W=[pcmres])
            f.op("dve", lambda e, b=b, H3=H3: e.tensor_tensor(out=H3, in0=H3, in1=bc(cdT[:, :, b], [128, 16, 128]), op=ALU.mult), R=Hres + ["cdT"], W=Hres)
            U4 = U[:].rearrange("p (g r n) -> p g r n", g=4, r=4)
            f.op("dve", lambda e, b=b, pbm=pbm, U4=U4: e.tensor_tensor(
                out=U4, in0=pbm[:, 0:512].rearrange("p (g n) -> p g n", g=4).unsqueeze(2).broadcast_to([128, 4, 4, 128]),
                in1=xdtT[:, :, b].rearrange("p (g r) -> p g r", g=4).unsqueeze(3).broadcast_to([128, 4, 4, 128]), op=ALU.mult),
                R=[pbmres, "xdtT", "hst"], W=["hst"])
            f.op("dve", lambda e, H=H: e.tensor_tensor(out=H, in0=H, in1=U[:], op=ALU.add), R=Hres + ["hst"], W=Hres)
            f.dma("pool", io["sssm_o"][b].rearrange("(c q) n -> q c n", q=128), H3, R=Hres)
            f.op("dve", lambda e, pcm=pcm, U4=U4, H=H: e.tensor_tensor(
                out=U4, in0=H.rearrange("p (g r n) -> p g r n", g=4, r=4),
                in1=pcm[:, 0:512].rearrange("p (g n) -> p g n", g=4).unsqueeze(2).broadcast_to([128, 4, 4, 128]), op=ALU.mult),
                R=Hres + [pcmres, "hst"], W=["hst"])
            f.op("dve", lambda e, b=b: e.tensor_reduce(out=yT[:, :, b], in_=U[:].rearrange("p (c n) -> p c n", c=16), op=ALU.add, axis=AX.X),
                 R=["hst"], W=["yT"])
        y2 = self.y2
        f.op("dve", lambda e: e.tensor_tensor(out=y2[:], in0=cvs[:, 0:16, :], in1=bc(self.dskT, [128, 16, NS]), op=ALU.mult), R=CVS + ["PV"], W=["y2"])
        f.op("dve", lambda e: e.tensor_tensor(out=y2[:], in0=y2[:], in1=yT[:], op=ALU.add), R=["y2", "yT"], W=["y2"])
        f.op("dve", lambda e: e.tensor_tensor(out=y2[:], in0=y2[:], in1=zT, op=ALU.mult), R=["y2", "zs"], W=["y2"])
        sq, sqres = self.tb()
        f.op("act", lambda e, sq=sq: e.activation(out=sq[:, 0:16 * NS], in_=y2[:].rearrange("p c t -> p (c t)"), func=AF.Square), R=["y2"], W=[sqres])
        p2, p2res = self.pb()
        for g in range(4):
            for r in range(4):
                c = g * 4 + r
                f.op("pe", lambda e, g=g, r=r, c=c, sq=sq, p2=p2: e.matmul(p2[:, g * NS:(g + 1) * NS], lhsT=self.onesb, rhs=sq[:, c * NS:(c + 1) * NS],
                                                                          start=(r == 0), stop=(r == 3)), R=[sqres, "CBc"], W=[p2res])
        rr, rres = self.rsqrt_bc(p2[:, 0:4 * NS], p2res, 4 * NS, 1.0 / 512)
        f.op("dve", lambda e, rr=rr: e.tensor_tensor(
            out=y2[:].rearrange("p (g r) t -> p g r t", g=4), in0=y2[:].rearrange("p (g r) t -> p g r t", g=4),
            in1=rr.rearrange("p (g t) -> p g t", g=4).unsqueeze(2).broadcast_to([128, 4, 4, NS]), op=ALU.mult), R=["y2", rres], W=["y2"])
        f.op("dve", lambda e: e.tensor_tensor(out=self.sT[:, :, 0:NS], in0=y2[:], in1=bc(self.ssdnT, [128, 16, NS]), op=ALU.mult), R=["y2", "PV"], W=["sT"])
        self.worder = self.order_merge() + self.order_ffn()
        self.wpos = 0
        self.merge_out(NS, NS, "x0", x16, self.aT, self.sT, self.mT, self.hT,
                       lambda m: self.xbcT[:, m, 3:3 + NS], lambda m: ("xbcT", m), self.h2T)
        self.drain(self.ffn_gen(NS, NS, "x0", x16, self.h2T))
        f.dma("pool", io["ys"], x16, R=["x0"])


def slabify(W, cw, kcs):
    K, N = W.shape
    assert K == kcs * 128 and N % cw == 0
    a = W.reshape(kcs, 128, N // cw, cw).transpose(2, 1, 0, 3)
    return np.ascontiguousarray(a).reshape(N // cw, 128, kcs * cw)


def host_prep(inp):
    w_in = inp["w_in"][0]
    q = w_in[:, 0:1024].reshape(2048, 2, 2, 4, 64).transpose(0, 1, 3, 2, 4).reshape(2048, 1024)
    parts = [slabify(q, CW, 16), slabify(w_in[:, 1024:1280], CW, 16), slabify(w_in[:, 1280:1536], CW, 16),
             slabify(w_in[:, 1536:3584], CW, 16), slabify(w_in[:, 3584:6656], CW, 16), slabify(w_in[:, 6688:7712], CW, 16),
             slabify(w_in[:, 7712:13856], CW, 16), slabify(inp["w_mem_kv"][0], CW, 16), slabify(inp["w_up_ssd"][0], CW, 16),
             slabify(inp["w_out"][0], CW, 16), slabify(inp["w_gate"][0], CW, 16), slabify(inp["w_up"][0], CW, 16)]
    WA = np.concatenate(parts, 0)
    assert WA.shape[0] == NA
    WM = slabify(inp["w_up_mem"][0], CW, 8)
    ws = inp["w_up_swa"][0]
    WS = np.ascontiguousarray(ws.reshape(16, 64, 8, CW).transpose(2, 1, 0, 3)).reshape(8, 64, 16 * CW)
    wd = inp["w_down"][0]
    WD = np.ascontiguousarray(wd.reshape(2, 22, 128, 8, CW).transpose(3, 0, 2, 1, 4)).reshape(16, 128, 22 * CW)
    WT = slabify(w_in[:, 6656:6688], 32, 16)[0]
    ar = np.arange(128)
    c128 = np.zeros((128, 8 * 128 + 512 + 32), np.float32)
    c128[:, 0:128] = np.eye(128)
    c128[:, 128:256] = (ar[:, None] <= ar[None, :])
    c128[:, 256:384] = (ar[:, None] > ar[None, :])
    c128[:, 384:512] = 1.0
    k_, q_ = ar[:, None], ar[None, :]
    c128[:, 512:640] = np.where(q_ >= k_, q_ - k_, 20000.0)
    c128[:, 640:768] = np.where(q_ <= k_, q_ + 128 - k_, 20000.0)
    c128[:, 768:896] = (ar[:, None] // 64 == ar[None, :] // 64)
    c128[:, 1024:1536] = np.tile(np.where(ar[None, :] < ar[:, None], -30000.0, 0.0), (1, 4))
    c128[:, 1536:1568] = (ar[:, None] % 32 == np.arange(32)[None, :])
    pvec = np.zeros((128, 512), np.float32)
    T16 = lambda v: np.ascontiguousarray(v.reshape(16, 128).T)
    pvec[:, 0:16] = T16(inp["norm_mix"][0])
    pvec[:, 16:32] = T16(inp["norm_ffn"][0])
    pvec[:, 32:48] = T16(inp["norm_mem"][0])
    pvec[:, 48:64] = T16(inp["ssd_norm"][0])
    pvec[:, 64:160] = inp["conv_w"][0].reshape(4, 24, 128).transpose(2, 1, 0).reshape(128, 96)
    pvec[:, 160:184] = inp["conv_b"][0].reshape(24, 128).T
    pvec[:, 184] = np.tile(inp["q_norm_swa"][0], 2)
    pvec[:, 185] = np.tile(inp["k_norm_swa"][0], 2)
    pvec[:, 186:188] = inp["q_norm_mem"][0].reshape(2, 128).T
    pvec[:, 188:204] = inp["swa_sinks"][0][None, :]
    pvec[:, 204:220] = np.repeat(inp["d_skip"][0], 64).reshape(16, 128).T
    pvec[0:16, 220:236] = np.where(np.eye(16) > 0, 0.0, 20000.0)
    prow = np.zeros((128, 352), np.float32)
    prow[:, 0:32] = inp["dt_bias"][0][None, :]
    prow[:, 32:64] = inp["a_log"][0][None, :]
    prow[:, 64:96] = inp["d_skip"][0][None, :]
    prow[:, 96:352] = inp["k_norm_mem"][0][None, :]
    shared = dict(WA=WA, WM=WM, WS=WS, WD=WD, WT=np.ascontiguousarray(WT), c128=c128, pvec=pvec, prow=prow)
    in_maps = []
    for c in range(NCORES):
        m = dict(shared)
        m["xp"] = np.ascontiguousarray(inp["x_prompt"][2 * c:2 * c + 2])
        m["memp"] = np.ascontiguousarray(inp["mem_prompt"][2 * c:2 * c + 2])
        sl = slice(16 * c, 16 * c + 16)
        m["xs"] = np.ascontiguousarray(inp["x_sample"][sl, 0])
        m["csk"] = np.ascontiguousarray(inp["cache_swa_k"][0, sl]).reshape(16, 128, 256)
        m["csv"] = np.ascontiguousarray(inp["cache_swa_v"][0, sl]).reshape(16, 128, 256)
        m["cmk"] = np.ascontiguousarray(inp["cache_mem_k"][0, sl]).reshape(16, 256, 1024)
        m["cmv"] = np.ascontiguousarray(inp["cache_mem_v"][0, sl]).reshape(16, 256, 1024)
        m["sssm"] = np.ascontiguousarray(inp["state_ssm"][0, sl]).reshape(16, 2048, 128)
        m["sconv"] = np.ascontiguousarray(inp["state_conv"][0, sl]).reshape(48, 3072)
        in_maps.append(m)
    return in_maps


_CACHE = {}


def run(inputs, do_samples=True, n_st=None, dbg=False):
    inp = {k: np.asarray(v) for k, v in inputs.items()}
    b = Builder(do_samples=do_samples, n_st=n_st, dbg=dbg)
    nc = b.build()
    in_maps = host_prep(inp)
    if not do_samples:
        for m in in_maps:
            for k in ("xs", "csk", "csv", "cmk", "cmv", "sssm", "sconv"):
                m.pop(k)
    res = run_bass_kernel_spmd(nc, in_maps, core_ids=list(range(NCORES)))
    R = res.results
    cat = lambda k: np.concatenate([r[k] for r in R], 0)
    yp = cat("yp")
    ys = cat("ys").reshape(128, 1, D)
    outs = (yp, ys,
            cat("pk").reshape(1, 16, 128, 4, 64), cat("pv").reshape(1, 16, 128, 4, 64),
            cat("pmk").reshape(1, 16, 256, 4, 256), cat("pmv").reshape(1, 16, 256, 4, 256),
            cat("pssm").reshape(1, 16, 32, 64, 128), cat("pconv").reshape(1, 16, 3, 3072),
            cat("sk").reshape(1, 128, 128, 4, 64), cat("sv").reshape(1, 128, 128, 4, 64),
            cat("sssm_o").reshape(1, 128, 32, 64, 128), cat("sconv_o").reshape(1, 128, 3, 3072))
    outs = tuple(np.ascontiguousarray(o, dtype=np.float32) for o in outs)
    if dbg:
        return outs, {k: [r["dbg_" + k] for r in R] for k in b.dbg_outs}
    return outs


def kernel(**inputs):
    return run(inputs, do_samples=True)
```

```python
import contextlib
import os
import numpy as np
_SKIP = set(os.environ.get('KSKIP', '').split(','))
import concourse.bass as bass
import concourse.mybir as mybir
from concourse.bass_utils import run_bass_kernel_spmd

F32 = mybir.dt.float32
BF16 = mybir.dt.bfloat16
AF = mybir.ActivationFunctionType
ALU = mybir.AluOpType
AX = mybir.AxisListType

NCORES = 8
D = 2048
SEQ = 2048
NSEQ = 2
NSMP = 16
NT = 128
BLK = 128
KC = 16
DFF = 5632
EPS = 1e-6
CW = 256
ENGS = ("pe", "act", "dve", "pool", "sp")

SA_Q, SA_K, SA_V, SA_Z, SA_XBC, SA_QM, SA_G0, SA_G1, SA_G2, SA_MEM, SA_USSD, SA_OUT, SA_GATE, SA_UP = (
    0, 4, 5, 6, 14, 26, 30, 38, 46, 54, 62, 70, 78, 100)
NA = 122
SLOPES = [2.0 ** (-8.0 * (h + 1) / 16.0) for h in range(16)]


class FW:
    def __init__(self, nc, n_dma_sems=48):
        self.nc = nc
        self.es = contextlib.ExitStack()
        self.sem = {e: self.es.enter_context(nc.semaphore("s_" + e)) for e in ENGS}
        self.dsem = [self.es.enter_context(nc.semaphore("d_%d" % i)) for i in range(n_dma_sems)]
        self.dtot = [0] * n_dma_sems
        self.dnext = {}
        self.dpool = {'pool': (0, n_dma_sems // 2), 'sp': (n_dma_sems // 2, n_dma_sems)}
        self.n = {e: 0 for e in ENGS}
        self.waited = {e: {} for e in ENGS}
        self.stream = {e: [] for e in ENGS}
        self.last_w = {}
        self.readers = {}

    def sbuf(self, name, shape, dtype):
        return self.es.enter_context(self.nc.sbuf_tensor(name, list(shape), dtype))

    def psum(self, name, shape, dtype):
        return self.es.enter_context(self.nc.psum_tensor(name, list(shape), dtype))

    def _deps(self, R, W):
        deps = set()
        for r in R:
            w = self.last_w.get(r)
            if w is not None:
                deps.add(w)
            if isinstance(r, tuple) and r[0] == "ps":
                rd = self.readers.get(r)
                if rd:
                    for k, v in rd.items():
                        deps.add((k, v))
        for r in W:
            w = self.last_w.get(r)
            if w is not None:
                deps.add(w)
            rd = self.readers.get(r)
            if rd:
                for k, v in rd.items():
                    deps.add((k, v))
        return deps

    def _waits(self, eng, deps):
        need = {}
        for k, v in deps:
            if k == "pe" and eng == "pe":
                continue
            if v > need.get(k, 0):
                need[k] = v
        out = []
        wd = self.waited[eng]
        for k, v in need.items():
            if wd.get(k, 0) >= v:
                continue
            wd[k] = v
            s = self.sem[k] if isinstance(k, str) else self.dsem[k[1]]
            out.append((s, v))
        return out

    def _record(self, my, R, W):
        for r in R:
            self.readers.setdefault(r, {})[my[0]] = my[1]
        for r in W:
            self.last_w[r] = my
            self.readers[r] = {}

    def op(self, eng, fn, R=(), W=()):
        waits = self._waits(eng, self._deps(R, W))
        self.n[eng] += 1
        my = (eng, self.n[eng])
        self.stream[eng].append((waits, fn, self.sem[eng], 1))
        self._record(my, R, W)

    def dma(self, q, out, in_, R=(), W=(), **kw):
        deps = self._deps(R, W)
        lo, hi = self.dpool[q]
        i = self.dnext.get(q, lo)
        self.dnext[q] = lo + (i + 1 - lo) % (hi - lo)
        if self.dtot[i] > 0:
            deps.add((("d", i), self.dtot[i]))
        waits = self._waits(q, deps)
        self.dtot[i] += 16
        my = (("d", i), self.dtot[i])
        nonctg = kw.pop("nonctg", False)
        nc = self.nc

        def fn(e):
            if nonctg:
                with nc.allow_non_contiguous_dma(reason="tiny strided store"):
                    return e.dma_start(out=out, in_=in_, **kw)
            return e.dma_start(out=out, in_=in_, **kw)
        self.stream[q].append((waits, fn, self.dsem[i], 16))
        self._record(my, R, W)

    def emit(self):
        nc = self.nc
        waits = [(self.dsem[i], t) for i, t in enumerate(self.dtot) if t > 0]
        waits += [(self.sem[e], self.n[e]) for e in ENGS if e != "sp" and self.n[e] > 0]
        self.stream["sp"].append((waits, None, None, 0))
        with nc.Block() as block:
            def replay(name, e):
                for waits, fn, s, inc in self.stream[name]:
                    for (ws, wv) in waits:
                        e.wait_ge(ws, wv)
                    if fn is not None:
                        fn(e).then_inc(s, inc)

            @block.tensor
            def _(e):
                replay("pe", e)

            @block.scalar
            def _(e):
                replay("act", e)

            @block.vector
            def _(e):
                replay("dve", e)

            @block.gpsimd
            def _(e):
                replay("pool", e)

            @block.sync
            def _(e):
                replay("sp", e)
        self.es.close()


def bc(ap, shape):
    return ap.unsqueeze(2).broadcast_to(list(shape))


class Builder:
    def __init__(self, do_samples=True, n_st=None, dbg=False, nseq=NSEQ, stop=None, force_last=False):
        self.force_last = force_last
        self.pipeline = os.environ.get('KPIPE', '1') == '1'
        self.il = tuple(int(v) for v in os.environ.get('KIL', '1,1').split(','))
        self.nseq = nseq
        self.stop = stop
        self.do_samples = do_samples
        self.n_st = n_st
        self.dbg_on = dbg
        self.nc = bass.Bass("TRN2", target_bir_lowering=False)
        self.f = FW(self.nc)
        self.ins = {}
        self.outs = {}
        self.dbg_outs = {}

    def din(self, name, shape, dtype=F32):
        self.ins[name] = self.nc.dram_tensor(name, list(shape), dtype, kind="ExternalInput").ap()
        return self.ins[name]

    def dout(self, name, shape):
        self.outs[name] = self.nc.dram_tensor(name, list(shape), F32, kind="ExternalOutput").ap()
        return self.outs[name]

    def dscr(self, name, shape, dtype):
        return self.nc.dram_tensor(name, list(shape), dtype, kind="Internal").ap()

    def dump(self, name, ap, shape, R):
        if not self.dbg_on:
            return
        o = self.nc.dram_tensor("dbg_" + name, list(shape), F32, kind="ExternalOutput").ap()
        self.dbg_outs[name] = o
        self.f.dma("pool", o, ap, R=R)

    def pb(self):
        if self.ctx == 'F':
            i = self.pnF
            self.pnF = (self.pnF + 1) % 4
        elif self.ctx == 'M':
            i = 4 + self.pnM
            self.pnM = (self.pnM + 1) % 4
        else:
            i = self.pnext
            self.pnext = (self.pnext + 1) % 8
        return self.ps[i], ("ps", i)

    def slabs(self, keys):
        order = self.worder
        live = set(keys)
        pos = None
        try:
            pos = order.index(keys[-1], self.wpos)
            self.wpos = pos
        except ValueError:
            pass
        upcoming = order[pos + 1: pos + 1 + self.NBUF] if pos is not None else []
        protect = set(live)
        for k in keys:
            if k not in self.wloaded:
                self._issue(k, protect)
        for nk in upcoming:
            if nk in self.wloaded:
                protect.add(nk)
                continue
            if not self._issue(nk, protect):
                break
            protect.add(nk)
        return [(self.wbuf[self.wloaded[k]], ("wbuf", self.wloaded[k])) for k in keys]

    def slab(self, kind, idx):
        return self.slabs([(kind, idx)])[0]

    def _issue(self, key, protect):
        kind, idx = key
        held = {b: k for k, b in self.wloaded.items()}
        b = None
        for i in range(self.NBUF):
            cand = (self.wnext + i) % self.NBUF
            if held.get(cand) not in protect or cand not in held:
                b = cand
                break
        if b is None:
            return False
        self.wnext = (b + 1) % self.NBUF
        if b in held:
            del self.wloaded[held[b]]
        src, n, parts = self.wsrc(kind, idx)
        if key not in self.cast_done:
            self.cast_done.add(key)
            fsrc = self._wf32[kind] if kind == "T" else self._wf32[kind][idx]
            self._cast(src, fsrc, n, ("wscr", kind, idx))
        self.f.dma("sp", self.wbuf[b][0:parts, 0:n], src, R=[("wscr", kind, idx)], W=[("wbuf", b)])
        self.wloaded[key] = b
        return True

    def wsrc(self, kind, idx):
        if kind == "A":
            return self.WAb[idx], 4096, 128
        if kind == "M":
            return self.WMb[idx], 2048, 128
        if kind == "S":
            return self.WSb[idx], 4096, 64
        if kind == "D":
            return self.WDb[idx], 5632, 128
        if kind == "T":
            return self.WTb, 512, 128
        raise ValueError(kind)

    def build(self):
        nc, f = self.nc, self.f
        xp = self.din("xp", [NSEQ, SEQ, D])
        memp = self.din("memp", [NSEQ, 256, D])
        WAf = self.din("WA", [NA, 128, 4096])
        WMf = self.din("WM", [8, 128, 2048])
        WSf = self.din("WS", [8, 64, 4096])
        WDf = self.din("WD", [16, 128, 5632])
        WTf = self.din("WT", [128, 512])
        c128 = self.din("c128", [128, 8 * 128 + 512 + 32])
        pvec = self.din("pvec", [128, 512])
        prow = self.din("prow", [128, 32 * 3 + 256])
        if self.do_samples:
            xs_in = self.din("xs", [NSMP, D])
            csk = self.din("csk", [NSMP, 128, 256])
            csv = self.din("csv", [NSMP, 128, 256])
            cmk = self.din("cmk", [NSMP, 256, 1024])
            cmv = self.din("cmv", [NSMP, 256, 1024])
            sssm = self.din("sssm", [NSMP, 2048, 128])
            sconv = self.din("sconv", [NSMP * 3, 3072])
        yp = self.dout("yp", [NSEQ, SEQ, D])
        pk = self.dout("pk", [NSEQ, 128, 256])
        pv = self.dout("pv", [NSEQ, 128, 256])
        pmk = self.dout("pmk", [NSEQ, 256, 1024])
        pmv = self.dout("pmv", [NSEQ, 256, 1024])
        pssm = self.dout("pssm", [NSEQ, 2048, 128])
        pconv = self.dout("pconv", [NSEQ, 3, 3072])
        ys = self.dout("ys", [NSMP, D])
        sk = self.dout("sk", [NSMP, 128, 256])
        sv = self.dout("sv", [NSMP, 128, 256])
        sssm_o = self.dout("sssm_o", [NSMP, 2048, 128])
        sconv_o = self.dout("sconv_o", [NSMP, 3, 3072])
        self.io = {**self.ins, **self.outs}
        self.WAb = self.dscr("WAb", [NA, 128, 4096], BF16)
        self.WMb = self.dscr("WMb", [8, 128, 2048], BF16)
        self.WSb = self.dscr("WSb", [8, 64, 4096], BF16)
        self.WDb = self.dscr("WDb", [16, 128, 5632], BF16)
        self.WTb = self.dscr("WTb", [128, 512], BF16)

        def cast(dst, src, n, res):
            if n > 2048:
                assert n % 2048 == 0 or n == 5632
                a = n // 2048 if n % 2048 == 0 else 4
                f.dma("pool", dst.rearrange("p (a b) -> p a b", a=a), src.rearrange("p (a b) -> p a b", a=a), W=[res])
            else:
                f.dma("pool", dst, src, W=[res])
        self._cast = cast
        self._wf32 = {"A": WAf, "M": WMf, "S": WSf, "D": WDf, "T": WTf}
        self.cast_done = set()

        self.ps = [f.psum("ps%d" % i, [128, 512], F32) for i in range(8)]
        self.pnext = 0
        self.pnF = self.pnM = self.tfF = self.tfM = 0
        self.ctx = None
        self.NBUF = 4
        self.wbuf = [f.sbuf("wbuf%d" % i, [128, 5632], BF16) for i in range(self.NBUF)]
        self.wnext = 0
        self.wloaded = {}
        self.worder = []
        self.wpos = 0

        C = f.sbuf("c128t", [128, 8 * 128 + 512 + 32], F32)
        f.dma("sp", C[:], c128, W=["C"])
        self.identf = C[:, 0:128]
        self.Tm = C[:, 128:256]
        self.Um = C[:, 256:384]
        self.onesf = C[:, 384:512]
        self.Dmd = C[:, 512:640]
        self.Dmp = C[:, 640:768]
        self.ohs = C[:, 1536:1568]
        PV = f.sbuf("pvect", [128, 512], F32)
        f.dma("sp", PV[:], pvec, W=["PV"])
        PR = f.sbuf("prowt", [128, 352], F32)
        f.dma("sp", PR[:], prow, W=["PR"])
        self.PV, self.PR = PV, PR
        self.gmixT = PV[:, 0:16]
        self.gffnT = PV[:, 16:32]
        self.gmemT = PV[:, 32:48]
        self.ssdnT = PV[:, 48:64]
        self.cwT = PV[:, 64:160]
        self.cbT = PV[:, 160:184]
        self.gq = PV[:, 184:185]
        self.gk = PV[:, 185:186]
        self.gqm = PV[:, 186:188]
        self.sink_raw = PV[:, 188:204]
        self.dskT = PV[:, 204:220]
        self.dm16 = PV[:, 220:236]
        self.dtb = PR[:, 0:32]
        self.alog = PR[:, 32:64]
        self.dsk = PR[:, 64:96]
        self.gkm = PR[:, 96:352]
        CB = f.sbuf("cbf", [128, 128 * 3 + 512], BF16)
        f.op("dve", lambda e: e.tensor_copy(out=CB[:, 0:128], in_=C[:, 0:128]), R=["C"], W=["CBc"])
        f.op("dve", lambda e: e.tensor_copy(out=CB[:, 128:256], in_=C[:, 384:512]), R=["C"], W=["CBc"])
        f.op("dve", lambda e: e.tensor_copy(out=CB[:, 256:384], in_=C[:, 768:896]), R=["C"], W=["CBc"])
        f.op("dve", lambda e: e.tensor_copy(out=CB[:, 384:896], in_=C[:, 1024:1536]), R=["C"], W=["CBc"])
        self.identb = CB[:, 0:128]
        self.onesb = CB[:, 128:256]
        self.blockones = CB[:, 256:384]
        self.maskD = CB[:, 384:896]
        SM = f.sbuf("smallp", [128, 64], F32)
        self.SM = SM
        f.op("act", lambda e: e.mul(out=SM[:, 0:1], in_=PV[:, 184:185], mul=0.125), R=["PV"], W=["SM"])
        f.op("act", lambda e: e.activation(out=SM[:, 1:17], in_=PV[:, 188:204], func=AF.Exp), R=["PV"], W=["SM"])
        f.op("act", lambda e: e.activation(out=SM[:, 17:49], in_=PR[:, 32:64], func=AF.Exp), R=["PR"], W=["SM"])
        f.op("act", lambda e: e.mul(out=SM[:, 17:49], in_=SM[:, 17:49], mul=-1.0), R=["SM"], W=["SM"])
        self.gq8 = SM[:, 0:1]
        self.esink = SM[:, 1:17]
        self.a_bc = SM[:, 17:49]

        self.hT = f.sbuf("hT", [128, KC, NT], BF16)
        self.xtoks = [f.sbuf("xtok%d" % i, [128, D], F32) for i in range(2)]
        self.xtok = self.xtoks[0]
        self.h2T = f.sbuf("h2T", [128, KC, NT], BF16)
        self.xn = f.sbuf("xn", [128, D], BF16)
        self.st1 = f.sbuf("st1", [128, 8], F32)
        self.qT = f.sbuf("qT", [128, 8, NT], BF16)
        self.kT = f.sbuf("kT", [128, 2, 2 * 128], BF16)
        self.vtok = f.sbuf("vtok", [128, 2, 256], BF16)
        self.zs = f.sbuf("zs", [128, D], BF16)
        self.xbcT = f.sbuf("xbcT", [128, 24, 3 + NT], BF16)
        self.hist = f.sbuf("hist", [128, 24, 3], BF16)
        self.qmT = f.sbuf("qmT", [128, 8, NT], BF16)
        self.aT = f.sbuf("aT", [64, 16, NT], BF16)
        self.sT = f.sbuf("sT", [128, KC, NT], BF16)
        self.mT = f.sbuf("mT", [128, 8, NT], BF16)
        self.actT = f.sbuf("actT", [128, 44, NT], BF16)
        self.hst = f.sbuf("hst", [128, D], F32)
        self.hbf = f.sbuf("hbf", [128, D], BF16)
        self.KTm = f.sbuf("KTm", [128, 8, 256], BF16)
        self.Vm = f.sbuf("Vm", [128, 2, 1024], BF16)
        self.lastst = f.sbuf("lastst", [128, 96 + 256 + 256 + 256], F32)
        self.dtraw = f.sbuf("dtraw", [128, 32], F32)
        self.l3t = f.sbuf("l3t", [128, 224], F32)
        f.op("pool", lambda e: e.memset(self.l3t[:], 0.0), W=["lastst"])
        self.macc = [f.sbuf("macc%d" % i, [128, NT], F32) for i in range(2)]
        self.tmpf = [f.sbuf("tmpf%d" % i, [128, 512], F32) for i in range(6)]
        self.tmpb = [f.sbuf("tmpb%d" % i, [128, 512], BF16) for i in range(4)]
        self.tfn = 0
        self.tbn = 0
        self.ssdbig = f.sbuf("ssdbig", [128, 4 * D], BF16)
        self.xs_tok = self.ssdbig[:, 0:D]
        self.xdt = self.ssdbig[:, D:2 * D]
        self.xw = self.ssdbig[:, 2 * D:3 * D]
        self.s_tok = self.ssdbig[:, 3 * D:4 * D]
        self.stage = self.ssdbig[:].bitcast(F32)
        self.STAGE = ["xs_tok", "xdt", "xw", "s_tok"]
        self.bm_tok = f.sbuf("bm_tok", [128, 512], BF16)
        self.ssd_s = f.sbuf("ssd_s", [128, 256], F32)
        self.HI = f.sbuf("HI", [128, 128], BF16)
        self.LO = f.sbuf("LO", [128, 128], BF16)
        self.lhsD = f.sbuf("lhsD", [128, 128], BF16)
        self.rhsD = f.sbuf("rhsD", [128, 32, 128], BF16)
        self.dA4 = f.sbuf("dA4", [128, 4, 32], F32)
        self.CBt = f.sbuf("CBt", [128, 4, 128], F32)
        self.Wp = [f.sbuf("Wp%d" % i, [128, 4, 128], BF16) for i in range(2)]
        if self.do_samples:
            self.cbuf = f.sbuf("cbuf", [128, 24, 4, NSMP], BF16)
            self.cvs = f.sbuf("cvs", [128, 24, NSMP], F32)
            self.cdT = f.sbuf("cdT", [128, 16, NSMP], F32)
            self.dtT = f.sbuf("dtT", [128, 16, NSMP], F32)
            self.xdtT = f.sbuf("xdtT", [128, 16, NSMP], F32)
            self.bcs = f.sbuf("bcs", [NSMP, 1024], F32)
            self.hbf32 = self.bcs
            self.yT = f.sbuf("yT", [128, 16, NSMP], F32)
            self.y2 = f.sbuf("y2", [128, 16, NSMP], F32)
        f.op("dve", lambda e: e.memset(self.lhsD[64:128, :], 1.0), W=["lhsD"])
        f.op("dve", lambda e: e.tensor_copy(out=self.rhsD[0:64], in_=bc(self.ohs[0:64], [64, 32, 128])), R=["C"], W=["rhsD"])

        n_st = SEQ // NT if self.n_st is None else self.n_st
        for seq in range(self.nseq):
            self.seq_start(seq)
            if self.stop == 'seq_start' or n_st == 0:
                continue
            self.run_sequence(seq, n_st)
            if n_st == SEQ // NT or self.force_last:
                self.seq_end(seq)
        if self.do_samples:
            self.sample_group()
        f.emit()
        return nc

    def tf(self):
        if self.ctx == 'F':
            i = self.tfF
            self.tfF = (self.tfF + 1) % 2
        elif self.ctx == 'M':
            i = 2 + self.tfM
            self.tfM = (self.tfM + 1) % 4
        else:
            i = self.tfn
            self.tfn = (self.tfn + 1) % len(self.tmpf)
        return self.tmpf[i], ("tmpf", i)

    def tb(self):
        i = self.tbn
        self.tbn = (self.tbn + 1) % len(self.tmpb)
        return self.tmpb[i], ("tmpb", i)

    def super_order(self):
        o = []
        o += [("A", SA_Q + i) for i in range(4)] + [("A", SA_K)] + [("A", SA_XBC + i) for i in range(12)]
        o += [("A", SA_QM + i) for i in range(4)] + [("A", SA_V)] + [("A", SA_Z + i) for i in range(8)] + [("T", 0)]
        for s in range(8):
            o += [("A", SA_G0 + s), ("S", s), ("A", SA_G1 + s), ("A", SA_USSD + s), ("A", SA_G2 + s), ("M", s)]
        o += [("A", SA_OUT + i) for i in range(8)]
        for s in range(22):
            o += [("A", SA_GATE + s), ("A", SA_UP + s)]
        o += [("D", i) for i in range(16)]
        return o

    def fm_mm(self, ps_ap, psres, wt, wres, col0, actT, actres, ntok, kcs=KC, M=128):
        wv = wt[:, 0:kcs * CW].rearrange("p (k c) -> p k c", k=kcs)
        for kc in range(kcs):
            self.f.op("pe", lambda e, kc=kc: e.matmul(ps_ap, lhsT=wv[:, kc, col0:col0 + M], rhs=actT[:, kc, 0:ntok],
                                                       start=(kc == 0), stop=(kc == kcs - 1)),
                      R=[wres, actres], W=[psres])

    def tm_mm(self, ps_ap, psres, wt, wres, ncols, actT, actres, t0, bs, kcs=KC, cw=CW, col0=0):
        wv = wt[:, 0:kcs * cw].rearrange("p (k c) -> p k c", k=kcs)
        for kc in range(kcs):
            self.f.op("pe", lambda e, kc=kc: e.matmul(ps_ap, lhsT=actT[:, kc, t0:t0 + bs], rhs=wv[:, kc, col0:col0 + ncols],
                                                       start=(kc == 0), stop=(kc == kcs - 1)),
                      R=[wres, actres], W=[psres])

    def norm_T(self, x_ap, xres, bs, gT, outT, outres, c0):
        f = self.f
        st1, xn = self.st1, self.xn
        f.op("dve", lambda e: e.memset(st1[0:bs, 0:1], 0.0), W=["st1"])
        f.op("act", lambda e: e.activation(out=xn[0:bs, :], in_=x_ap, func=AF.Square, accum_out=st1[0:bs, 0:1]),
             R=[xres], W=["xn", "st1"])
        f.op("act", lambda e: e.activation(out=st1[0:bs, 1:2], in_=st1[0:bs, 0:1], func=AF.Sqrt, scale=1.0 / D, bias=EPS),
             R=["st1"], W=["st1"])
        f.op("dve", lambda e: e.reciprocal(out=st1[0:bs, 2:3], in_=st1[0:bs, 1:2]), R=["st1"], W=["st1"])
        f.op("dve", lambda e: e.tensor_scalar(out=xn[0:bs, :], in0=x_ap, scalar1=st1[0:bs, 2:3], scalar2=None, op0=ALU.mult),
             R=[xres, "st1"], W=["xn"])
        for half in range(2):
            pt, pres = self.pb()
            pbf = pt.bitcast(BF16)
            for k in range(8):
                kc = half * 8 + k
                f.op("pe", lambda e, k=k, kc=kc, pbf=pbf: e.transpose(out=pbf[:, k * bs:(k + 1) * bs], in_=xn[0:bs, kc * 128:(kc + 1) * 128],
                                                                      identity=self.identb[0:bs, 0:bs]),
                     R=["xn", "CBc"], W=[pres])
            f.op("dve", lambda e, half=half, pbf=pbf: e.tensor_tensor(
                out=outT[:, half * 8:half * 8 + 8, c0:c0 + bs],
                in0=pbf[:, 0:8 * bs].rearrange("p (k t) -> p k t", k=8),
                in1=bc(gT[:, half * 8:half * 8 + 8], [128, 8, bs]), op=ALU.mult),
                R=[pres, "PV"], W=[outres])

    def rsqrt_bc(self, ps2, ps2res, n, scale):
        f = self.f
        t1, r1 = self.tf()
        f.op("act", lambda e: e.activation(out=t1[:, 0:n], in_=ps2, func=AF.Sqrt, scale=scale, bias=EPS), R=[ps2res], W=[r1])
        f.op("dve", lambda e: e.reciprocal(out=t1[:, 0:n], in_=t1[:, 0:n]), R=[r1], W=[r1])
        return t1[:, 0:n], r1

    def seq_start(self, seq):
        f = self.f
        io = self.io
        f.op("pool", lambda e: e.memset(self.hst[:], 0.0), W=["hst"])
        f.op("pool", lambda e: e.memset(self.hbf[:], 0.0), W=["hbf"])
        f.op("pool", lambda e: e.memset(self.hist[:], 0.0), W=["hist"])
        self.worder = [("A", SA_MEM + i) for i in range(8)]
        self.wpos = 0
        stage = self.stage
        SR = self.STAGE
        kst = stage[:, 0:2048].rearrange("p (m c) -> p m c", m=2)
        vst = stage[:, 2048:4096].rearrange("p (m c) -> p m c", m=2)
        for mt in range(2):
            xt = self.xtok
            f.dma("pool", xt[:], io["memp"][seq, mt * 128:(mt + 1) * 128, :], W=["x0"])
            self.norm_T(xt[:], "x0", 128, self.gmemT, self.hT, "hT", 0)
            for s in range(8):
                wt, wres = self.slab("A", SA_MEM + s)
                pt, pres = self.pb()
                self.tm_mm(pt[:, 0:256], pres, wt, wres, 256, self.hT, "hT", 0, 128)
                if s < 4:
                    hm = s
                    st1 = self.st1
                    jk, jkres = self.tb()
                    f.op("dve", lambda e: e.memset(st1[:, 4:5], 0.0), W=["st1b"])
                    f.op("act", lambda e, pt=pt, jk=jk: e.activation(out=jk[:, 0:256], in_=pt[:, 0:256], func=AF.Square,
                                                                     accum_out=st1[:, 4:5]), R=[pres], W=[jkres, "st1b"])
                    f.op("act", lambda e: e.activation(out=st1[:, 5:6], in_=st1[:, 4:5], func=AF.Sqrt, scale=1.0 / 256, bias=EPS),
                         R=["st1b"], W=["st1b"])
                    f.op("dve", lambda e: e.reciprocal(out=st1[:, 6:7], in_=st1[:, 5:6]), R=["st1b"], W=["st1b"])
                    f.op("dve", lambda e, pt=pt, mt=mt, hm=hm: e.scalar_tensor_tensor(
                        out=kst[:, mt, hm * 256:(hm + 1) * 256], in0=pt[:, 0:256], scalar=st1[:, 6:7], in1=self.gkm,
                        op0=ALU.mult, op1=ALU.mult), R=[pres, "st1b", "PR"], W=SR)
                else:
                    hm = s - 4
                    f.op("act", lambda e, pt=pt, mt=mt, hm=hm: e.activation(out=vst[:, mt, hm * 256:(hm + 1) * 256], in_=pt[:, 0:256],
                                                                           func=AF.Copy), R=[pres], W=SR)
            self.worder = [("A", SA_MEM + i) for i in range(8)]
            self.wpos = 0
        f.dma("pool", io["pmk"][seq].rearrange("(m p) c -> p m c", p=128), kst, R=SR)
        f.dma("pool", io["pmv"][seq].rearrange("(m p) c -> p m c", p=128), vst, R=SR)
        f.op("pool", lambda e: e.tensor_copy(out=self.Vm[:], in_=vst), R=SR, W=["Vm"])
        kb = self.xn
        f.op("dve", lambda e: e.tensor_copy(out=kb[:], in_=stage[:, 0:2048]), R=SR, W=["xn"])
        self.kmem_T(kb, "xn", self.KTm, "KTm")

    def kmem_T(self, kb, kres, KTm, ktres):
        f = self.f
        kbv = kb[:, 0:2048].rearrange("p (m c) -> p m c", m=2)
        for mt in range(2):
            pt, pres = self.pb()
            pbf = pt.bitcast(BF16)
            for c in range(8):
                f.op("pe", lambda e, c=c, mt=mt, pbf=pbf: e.transpose(out=pbf[:, c * 128:(c + 1) * 128], in_=kbv[:, mt, c * 128:(c + 1) * 128],
                                                                      identity=self.identb), R=[kres, "CBc"], W=[pres])
            f.op("act", lambda e, mt=mt, pbf=pbf: e.activation(out=KTm[:, :, mt * 128:(mt + 1) * 128],
                                                               in_=pbf[:, 0:1024].rearrange("p (c t) -> p c t", c=8), func=AF.Copy),
                 R=[pres], W=[ktres])

    def qk_evac(self, pt, pres, n, gcol, out_ap, outres, out32=None, out32res=None):
        f = self.f
        sq, sqres = self.tb()
        f.op("act", lambda e: e.activation(out=sq[:, 0:n], in_=pt[:, 0:n], func=AF.Square), R=[pres], W=[sqres])
        p2, p2res = self.pb()
        f.op("pe", lambda e: e.matmul(p2[:, 0:n], lhsT=self.blockones, rhs=sq[:, 0:n], start=True, stop=True),
             R=[sqres, "CBc"], W=[p2res])
        rr, rres = self.rsqrt_bc(p2[:, 0:n], p2res, n, 1.0 / 64)
        f.op("dve", lambda e: e.scalar_tensor_tensor(out=out_ap, in0=pt[:, 0:n], scalar=gcol, in1=rr, op0=ALU.mult, op1=ALU.mult),
             R=[pres, rres, "PV", "SM"], W=[outres])
        if out32 is not None:
            f.op("dve", lambda e: e.scalar_tensor_tensor(out=out32, in0=pt[:, 0:n], scalar=gcol, in1=rr, op0=ALU.mult, op1=ALU.mult),
                 R=[pres, rres, "PV", "SM"], W=[out32res])

    def in_proj_fm(self, hT, ntok, qT, kT_out, xbc_out, qmT, last3=None, k32=None, parts=("q", "k", "xbc", "qm")):
        f = self.f
        pend = []

        def flush(keep=0):
            while len(pend) > keep:
                pend.pop(0)()
        if "q" in parts:
            for s in range(4):
                wt, wres = self.slab("A", SA_Q + s)
                for mm in range(2):
                    c = s * 2 + mm
                    pt, pres = self.pb()
                    self.fm_mm(pt[:, 0:ntok], pres, wt, wres, mm * 128, hT, "hT", ntok)
                    flush(0)
                    pend.append(lambda pt=pt, pres=pres, c=c: self.qk_evac(pt, pres, ntok, self.gq8, qT[:, c, 0:ntok], "qT"))
                yield
        if "k" in parts:
            wt, wres = self.slab("A", SA_K)
            for pair in range(2):
                pt, pres = self.pb()
                self.fm_mm(pt[:, 0:ntok], pres, wt, wres, pair * 128, hT, "hT", ntok)
                flush(0)
                if k32 is not None:
                    pend.append(lambda pt=pt, pres=pres, pair=pair: self.qk_evac(pt, pres, ntok, self.gk, kT_out(pair), "kT", out32=k32(pair), out32res="lastst"))
                else:
                    pend.append(lambda pt=pt, pres=pres, pair=pair: self.qk_evac(pt, pres, ntok, self.gk, kT_out(pair), "kT"))
            yield
        if "xbc" in parts:
            for s in range(12):
                wt, wres = self.slab("A", SA_XBC + s)
                if getattr(self, "xbc_hook", None) is not None:
                    self.xbc_hook(s, wt, wres)
                for mm in range(2):
                    c = s * 2 + mm
                    pt, pres = self.pb()
                    self.fm_mm(pt[:, 0:ntok], pres, wt, wres, mm * 128, hT, "hT", ntok)
                    flush(0)
                    f.op("act", lambda e, pt=pt, c=c: e.activation(out=xbc_out(c), in_=pt[:, 0:ntok], func=AF.Copy), R=[pres], W=[("xbcT", c)])
                    if last3 is not None:
                        f.op("dve", lambda e, pt=pt, c=c: e.tensor_copy(out=last3[:, c, :], in_=pt[:, ntok - 4:ntok]), R=[pres], W=["lastst"])
                yield
        flush(0)
        if "qm" in parts:
            for hm in range(4):
                wt, wres = self.slab("A", SA_QM + hm)
                pa, ares = self.pb()
                pbk, bres = self.pb()
                self.fm_mm(pa[:, 0:ntok], ares, wt, wres, 0, hT, "hT", ntok)
                self.fm_mm(pbk[:, 0:ntok], bres, wt, wres, 128, hT, "hT", ntok)
                flush(0)

                def qm_evac(pa=pa, ares=ares, pbk=pbk, bres=bres, hm=hm):
                    sq, sqres = self.tb()
                    f.op("act", lambda e: e.activation(out=sq[:, 0:ntok], in_=pa[:, 0:ntok], func=AF.Square), R=[ares], W=[sqres])
                    f.op("act", lambda e: e.activation(out=sq[:, 256:256 + ntok], in_=pbk[:, 0:ntok], func=AF.Square), R=[bres], W=[sqres])
                    p2, p2res = self.pb()
                    f.op("pe", lambda e: e.matmul(p2[:, 0:ntok], lhsT=self.onesb, rhs=sq[:, 0:ntok], start=True, stop=False),
                         R=[sqres, "CBc"], W=[p2res])
                    f.op("pe", lambda e: e.matmul(p2[:, 0:ntok], lhsT=self.onesb, rhs=sq[:, 256:256 + ntok], start=False, stop=True),
                         R=[sqres, "CBc"], W=[p2res])
                    rr, rres = self.rsqrt_bc(p2[:, 0:ntok], p2res, ntok, 1.0 / 256)
                    for dc, (pp, ppres) in enumerate(((pa, ares), (pbk, bres))):
                        f.op("dve", lambda e, pp=pp, dc=dc: e.scalar_tensor_tensor(
                            out=qmT[:, hm * 2 + dc, 0:ntok], in0=pp[:, 0:ntok], scalar=self.gqm[:, dc:dc + 1], in1=rr,
                            op0=ALU.mult, op1=ALU.mult), R=[ppres, rres, "PV"], W=["qmT"])
                pend.append(qm_evac)
                yield
        flush(0)

    def swa_heads(self, q_rhs, kprev, kcur, vprev, vcur, nq, kcn, Dmp, Dmc, out, R_in, W_out):
        f = self.f
        n4 = 4 * nq
        for h in range(4):
            PTs = []
            for which in (0, 1):
                if which == 0 and kprev is None:
                    continue
                kk = kprev(h) if which == 0 else kcur(h)
                kn = 128 if which == 0 else kcn
                Dm = Dmp if which == 0 else Dmc
                pt, pres = self.pb()
                f.op("pe", lambda e, pt=pt, kk=kk, kn=kn, h=h: e.matmul(pt[0:kn, 0:n4], lhsT=kk, rhs=q_rhs(h), start=True, stop=True),
                     R=R_in, W=[pres])
                tt, tres = self.tf()
                for g in range(4):
                    sl = -SLOPES[h * 4 + g]
                    f.op("dve", lambda e, pt=pt, tt=tt, g=g, sl=sl, kn=kn, Dm=Dm: e.scalar_tensor_tensor(
                        out=tt[0:kn, g * nq:(g + 1) * nq], in0=Dm, scalar=sl, in1=pt[0:kn, g * nq:(g + 1) * nq],
                        op0=ALU.mult, op1=ALU.add), R=[pres, "C", "PV"], W=[tres])
                PT, ptres = self.tb()
                f.op("act", lambda e, PT=PT, tt=tt, kn=kn: e.activation(out=PT[0:kn, 0:n4], in_=tt[0:kn, 0:n4], func=AF.Exp),
                     R=[tres], W=[ptres])
                PTs.append((PT, ptres, kn, which))
            yield
            po, pores = self.pb()
            pd, pdres = self.pb()
            nP = len(PTs)
            for i, (PT, ptres, kn, which) in enumerate(PTs):
                vv = vprev(h) if which == 0 else vcur(h)
                f.op("pe", lambda e, PT=PT, kn=kn, vv=vv, i=i, po=po: e.matmul(po[0:64, 0:n4], lhsT=vv, rhs=PT[0:kn, 0:n4],
                                                                              start=(i == 0), stop=(i == len(PTs) - 1)),
                     R=R_in + [ptres], W=[pores])
            for i, (PT, ptres, kn, which) in enumerate(PTs):
                f.op("pe", lambda e, PT=PT, kn=kn, i=i, pd=pd: e.matmul(pd[0:64, 0:n4], lhsT=self.onesb[0:kn, 0:64], rhs=PT[0:kn, 0:n4],
                                                                       start=(i == 0), stop=(i == nP - 1)),
                     R=["CBc", ptres], W=[pdres])
            dn, dnres = self.tf()
            f.op("dve", lambda e, h=h, dn=dn, pd=pd: e.tensor_tensor(
                out=dn[0:64, 0:n4].rearrange("p (g q) -> p g q", g=4), in0=pd[0:64, 0:n4].rearrange("p (g q) -> p g q", g=4),
                in1=bc(self.esink[0:64, h * 4:h * 4 + 4], [64, 4, nq]), op=ALU.add), R=[pdres, "SM"], W=[dnres])
            f.op("dve", lambda e, dn=dn: e.reciprocal(out=dn[0:64, 0:n4], in_=dn[0:64, 0:n4]), R=[dnres], W=[dnres])
            f.op("dve", lambda e, h=h, dn=dn, po=po: e.tensor_tensor(
                out=out(h), in0=po[0:64, 0:n4].rearrange("p (g q) -> p g q", g=4),
                in1=dn[0:64, 0:n4].rearrange("p (g q) -> p g q", g=4), op=ALU.mult), R=[pores, dnres], W=W_out)
            yield

    def mem_heads(self, q_rhs, KTm, ktres, Vm, vres, nq, out, R_in, W_out):
        f = self.f
        for hm in range(4):
            pt, pres = self.pb()
            for mt in range(2):
                for dc in range(2):
                    f.op("pe", lambda e, mt=mt, dc=dc, hm=hm, pt=pt: e.matmul(
                        pt[:, mt * nq:(mt + 1) * nq], lhsT=KTm[:, hm * 2 + dc, mt * 128:(mt + 1) * 128], rhs=q_rhs(hm * 2 + dc),
                        start=(dc == 0), stop=(dc == 1)), R=R_in + [ktres], W=[pres])
            PT, ptres = self.tb()
            f.op("act", lambda e, PT=PT, pt=pt: e.activation(out=PT[:, 0:2 * nq], in_=pt[:, 0:2 * nq], func=AF.Exp, scale=1.0 / 16),
                 R=[pres], W=[ptres])
            yield
            pd, pdres = self.pb()
            for mt in range(2):
                f.op("pe", lambda e, mt=mt, PT=PT, pd=pd: e.matmul(pd[:, 0:nq], lhsT=self.onesb, rhs=PT[:, mt * nq:(mt + 1) * nq],
                                                                   start=(mt == 0), stop=(mt == 1)), R=["CBc", ptres], W=[pdres])
            dn, dnres = self.tf()
            f.op("dve", lambda e, dn=dn, pd=pd: e.reciprocal(out=dn[:, 0:nq], in_=pd[:, 0:nq]), R=[pdres], W=[dnres])
            po, pores = self.pb()
            for dc in range(2):
                for mt in range(2):
                    f.op("pe", lambda e, mt=mt, dc=dc, hm=hm, PT=PT, po=po: e.matmul(
                        po[:, dc * nq:(dc + 1) * nq], lhsT=Vm[:, mt, hm * 256 + dc * 128: hm * 256 + (dc + 1) * 128],
                        rhs=PT[:, mt * nq:(mt + 1) * nq], start=(mt == 0), stop=(mt == 1)), R=[vres, ptres], W=[pores])
            for dc in range(2):
                f.op("dve", lambda e, dc=dc, hm=hm, dn=dn, po=po: e.tensor_tensor(out=out(hm * 2 + dc), in0=po[:, dc * nq:(dc + 1) * nq],
                                                                                 in1=dn[:, 0:nq], op=ALU.mult), R=[pores, dnres], W=W_out)
            yield

    def conv_fm(self, tap, ntok, out, res_of, save_hist=None):
        f = self.f
        for c in range(24):
            acc, ares = self.tf()
            f.op("dve", lambda e, c=c, acc=acc: e.tensor_scalar(out=acc[:, 0:ntok], in0=tap(c, 0), scalar1=self.cwT[:, c * 4:c * 4 + 1],
                                                                scalar2=self.cbT[:, c:c + 1], op0=ALU.mult, op1=ALU.add),
                 R=[res_of(c), "PV", "xbch"], W=[ares])
            for j in range(1, 4):
                f.op("dve", lambda e, c=c, j=j, acc=acc: e.scalar_tensor_tensor(
                    out=acc[:, 0:ntok], in0=tap(c, j), scalar=self.cwT[:, c * 4 + j:c * 4 + j + 1], in1=acc[:, 0:ntok],
                    op0=ALU.mult, op1=ALU.add), R=[res_of(c), "PV", ares, "xbch"], W=[ares])
            if save_hist is not None:
                save_hist(c)
            f.op("act", lambda e, c=c, acc=acc: e.activation(out=out(c), in_=acc[:, 0:ntok], func=AF.Silu), R=[ares], W=[res_of(c)])
            if c % 2 == 1:
                yield

    def ssd_block(self):
        f = self.f
        S = self.ssd_s
        cv = lambda c: self.xbcT[:, c, 3:3 + 128]
        XS = [("xbcT", c) for c in range(16)]
        BMr = [("xbcT", 16 + g) for g in range(4)]
        CMr = [("xbcT", 20 + g) for g in range(4)]
        for half in range(2):
            pt, pres = self.pb()
            pbf = pt.bitcast(BF16)
            for k in range(8):
                f.op("pe", lambda e, k=k, half=half, pbf=pbf: e.transpose(out=pbf[:, k * 128:(k + 1) * 128], in_=cv(half * 8 + k),
                                                                          identity=self.identb), R=[("xbcT", half * 8 + k), "CBc"], W=[pres])
            f.op("act", lambda e, half=half, pbf=pbf: e.activation(out=self.xs_tok[:, half * 1024:(half + 1) * 1024], in_=pbf[:, 0:1024],
                                                                   func=AF.Copy), R=[pres], W=["xs_tok"])
        pt, pres = self.pb()
        pbf = pt.bitcast(BF16)
        for g in range(4):
            f.op("pe", lambda e, g=g, pbf=pbf: e.transpose(out=pbf[:, g * 128:(g + 1) * 128], in_=cv(16 + g), identity=self.identb),
                 R=[BMr[g], "CBc"], W=[pres])
        f.op("act", lambda e, pbf=pbf: e.activation(out=self.bm_tok[:], in_=pbf[:, 0:512], func=AF.Copy), R=[pres], W=["bm_tok"])
        yield
        dt = S[:, 32:64]
        f.op("dve", lambda e: e.tensor_tensor(out=S[:, 0:32], in0=self.dtraw[:], in1=self.dtb, op=ALU.add), R=["dtraw", "PR"], W=["S"])
        f.op("act", lambda e: e.activation(out=S[:, 0:32], in_=S[:, 0:32], func=AF.Exp), R=["S"], W=["S"])
        f.op("act", lambda e: e.activation(out=dt, in_=S[:, 0:32], func=AF.Ln, bias=1.0), R=["S"], W=["S"])
        f.op("dve", lambda e: e.tensor_tensor(out=S[:, 64:96], in0=dt, in1=self.a_bc, op=ALU.mult), R=["S", "SM"], W=["S"])
        f.op("dve", lambda e: e.tensor_copy(out=self.dA4[:], in_=S[:, 64:96].unsqueeze(1).broadcast_to([128, 4, 32])), R=["S"], W=["dA4"])
        yield
        pa, pares = self.pb()
        for i, lh in enumerate((self.Tm, self.Um, self.onesf)):
            f.op("pe", lambda e, i=i, lh=lh: e.matmul(pa[:, i * 32:(i + 1) * 32], lhsT=lh, rhs=S[:, 64:96], start=True, stop=True),
                 R=["S", "C"], W=[pares])
        f.op("pe", lambda e: e.matmul(pa[:, 128:256], lhsT=self.dA4[:].rearrange("p a b -> p (a b)"), rhs=self.Tm, start=True, stop=True),
             R=["dA4", "C"], W=[pares])
        f.op("act", lambda e: e.activation(out=S[:, 96:192], in_=pa[:, 0:96], func=AF.Exp), R=[pares], W=["S"])
        eacs, dte, cd = S[:, 96:128], S[:, 128:160], S[:, 160:192]
        f.op("dve", lambda e: e.tensor_tensor(out=S[:, 192:224], in0=dt, in1=dte, op=ALU.mult), R=["S"], W=["S"])
        xs3 = self.xs_tok.rearrange("p (h d) -> p h d", h=32)
        f.op("dve", lambda e: e.tensor_tensor(out=self.xdt.rearrange("p (h d) -> p h d", h=32), in0=xs3, in1=bc(dt, [128, 32, 64]),
                                              op=ALU.mult), R=["xs_tok", "S"], W=["xdt"])
        f.op("dve", lambda e: e.tensor_tensor(out=self.xw.rearrange("p (h d) -> p h d", h=32), in0=xs3,
                                              in1=bc(S[:, 192:224], [128, 32, 64]), op=ALU.mult), R=["xs_tok", "S"], W=["xw"])
        HI, LO = self.HI, self.LO
        f.op("act", lambda e: e.activation(out=HI[:], in_=pa[:, 128:256], func=AF.Copy), R=[pares], W=["HI"])
        f.op("dve", lambda e: e.tensor_tensor(out=LO[:], in0=pa[:, 128:256], in1=HI[:], op=ALU.subtract), R=[pares, "HI"], W=["LO"])
        f.op("act", lambda e: e.mul(out=self.lhsD[0:32, :], in_=HI[0:32, :], mul=-1.0), R=["HI"], W=["lhsD"])
        f.op("act", lambda e: e.mul(out=self.lhsD[32:64, :], in_=LO[32:64, :], mul=-1.0), R=["LO"], W=["lhsD"])
        f.op("dve", lambda e: e.tensor_tensor(out=self.rhsD[64:96], in0=HI[64:96, :].unsqueeze(1).broadcast_to([32, 32, 128]),
                                              in1=bc(self.ohs[64:96], [32, 32, 128]), op=ALU.mult), R=["HI", "C"], W=["rhsD"])
        f.op("dve", lambda e: e.tensor_tensor(out=self.rhsD[96:128], in0=LO[96:128, :].unsqueeze(1).broadcast_to([32, 32, 128]),
                                              in1=bc(self.ohs[96:128], [32, 32, 128]), op=ALU.mult), R=["LO", "C"], W=["rhsD"])
        yield
        pc, pcres = self.pb()
        for g in range(4):
            f.op("pe", lambda e, g=g: e.matmul(pc[:, g * 128:(g + 1) * 128], lhsT=cv(16 + g), rhs=cv(20 + g), start=True, stop=True),
                 R=[BMr[g], CMr[g]], W=[pcres])
        f.op("act", lambda e: e.activation(out=self.CBt[:].rearrange("p g l -> p (g l)"), in_=pc[:, 0:512], func=AF.Copy), R=[pcres], W=["CBt"])
        f.op("dve", lambda e: e.memset(S[:, 224:228], 0.0), W=["Sq"])
        yield
        for g in range(4):
            pyd, pydres = self.pb()
            pyo, pyores = self.pb()
            f.op("pe", lambda e, g=g, pyo=pyo: e.matmul(pyo[:, 0:512], lhsT=cv(20 + g), rhs=self.hbf[:, g * 512:(g + 1) * 512], start=True, stop=True),
                 R=[CMr[g], "hbf"], W=[pyores])
            for jj in range(2):
                j = g * 2 + jj
                pD, pDres = self.pb()
                f.op("pe", lambda e, j=j, pD=pD: e.matmul(pD[:, 0:512], lhsT=self.lhsD[:], rhs=self.rhsD[:, 4 * j:4 * j + 4, :].rearrange("p a b -> p (a b)"),
                                                          start=True, stop=False), R=["lhsD", "rhsD"], W=[pDres])
                f.op("pe", lambda e, pD=pD: e.matmul(pD[:, 0:512], lhsT=self.identb, rhs=self.maskD, start=False, stop=True),
                     R=["CBc"], W=[pDres])
                E, eres = self.tf()
                f.op("act", lambda e, E=E, pD=pD: e.activation(out=E[:, 0:512], in_=pD[:, 0:512], func=AF.Exp), R=[pDres], W=[eres])
                Wp = self.Wp[jj]
                f.op("dve", lambda e, E=E, Wp=Wp, g=g: e.tensor_tensor(
                    out=Wp[:], in0=E[:, 0:512].rearrange("p (a l) -> p a l", a=4),
                    in1=self.CBt[:, g, :].unsqueeze(1).broadcast_to([128, 4, 128]), op=ALU.mult), R=[eres, "CBt"], W=[("Wp", jj)])
            yield
            for jj in range(2):
                j = g * 2 + jj
                Wp = self.Wp[jj]
                for h4 in range(4):
                    h = 4 * j + h4
                    f.op("pe", lambda e, Wp=Wp, h4=h4, h=h, pyd=pyd: e.matmul(pyd[:, (h % 8) * 64:(h % 8 + 1) * 64], lhsT=Wp[:, h4, :],
                                                                             rhs=self.xdt[:, h * 64:(h + 1) * 64], start=True, stop=True),
                         R=[("Wp", jj), "xdt"], W=[pydres])
            y1, y1res = self.tf()
            f.op("dve", lambda e, g=g, y1=y1, pyo=pyo: e.tensor_tensor(
                out=y1[:, 0:512].rearrange("p (h d) -> p h d", h=8), in0=pyo[:, 0:512].rearrange("p (h d) -> p h d", h=8),
                in1=bc(eacs[:, g * 8:(g + 1) * 8], [128, 8, 64]), op=ALU.mult), R=[pyores, "S"], W=[y1res])
            f.op("dve", lambda e, y1=y1, pyd=pyd: e.tensor_tensor(out=y1[:, 0:512], in0=y1[:, 0:512], in1=pyd[:, 0:512], op=ALU.add),
                 R=[pydres, y1res], W=[y1res])
            y3, y3res = self.tf()
            f.op("dve", lambda e, g=g, y3=y3: e.tensor_tensor(
                out=y3[:, 0:512].rearrange("p (h d) -> p h d", h=8), in0=self.xs_tok[:, g * 512:(g + 1) * 512].rearrange("p (h d) -> p h d", h=8),
                in1=bc(self.dsk[:, g * 8:(g + 1) * 8], [128, 8, 64]), op=ALU.mult), R=["xs_tok", "PR"], W=[y3res])
            f.op("dve", lambda e, y1=y1, y3=y3: e.tensor_tensor(out=y1[:, 0:512], in0=y1[:, 0:512], in1=y3[:, 0:512], op=ALU.add),
                 R=[y1res, y3res], W=[y1res])
            f.op("dve", lambda e, g=g, y1=y1: e.tensor_tensor(out=y1[:, 0:512], in0=y1[:, 0:512],
                                                              in1=self.zs[:, g * 512:(g + 1) * 512], op=ALU.mult),
                 R=[y1res, "zs"], W=[y1res])
            f.op("act", lambda e, g=g, y1=y1, y3=y3: e.activation(out=y3[:, 0:512], in_=y1[:, 0:512], func=AF.Square,
                                                                  accum_out=S[:, 224 + g:225 + g]), R=[y1res], W=[y3res, "Sq"])
            f.op("act", lambda e, g=g: e.activation(out=S[:, 228 + g:229 + g], in_=S[:, 224 + g:225 + g], func=AF.Sqrt, scale=1.0 / 512, bias=EPS),
                 R=["Sq"], W=["Sq"])
            f.op("dve", lambda e, g=g: e.reciprocal(out=S[:, 232 + g:233 + g], in_=S[:, 228 + g:229 + g]), R=["Sq"], W=["Sq"])
            pst, pstres = self.pb()
            f.op("pe", lambda e, g=g, pst=pst: e.matmul(pst[:, 0:512], lhsT=self.bm_tok[:, g * 128:(g + 1) * 128],
                                                        rhs=self.xw[:, g * 512:(g + 1) * 512], start=True, stop=True),
                 R=["bm_tok", "xw"], W=[pstres])
            f.op("dve", lambda e, g=g, y1=y1: e.tensor_scalar(out=self.s_tok[:, g * 512:(g + 1) * 512], in0=y1[:, 0:512],
                                                              scalar1=S[:, 232 + g:233 + g], scalar2=None, op0=ALU.mult), R=[y1res, "Sq"], W=["s_tok"])
            hg = self.hst[:, g * 512:(g + 1) * 512]
            f.op("dve", lambda e, g=g, hg=hg: e.tensor_tensor(out=hg.rearrange("p (h d) -> p h d", h=8), in0=hg.rearrange("p (h d) -> p h d", h=8),
                                                              in1=bc(cd[:, g * 8:(g + 1) * 8], [128, 8, 64]), op=ALU.mult),
                 R=["S", "hst"], W=["hst"])
            f.op("dve", lambda e, hg=hg, pst=pst: e.tensor_tensor(out=hg, in0=hg, in1=pst[:, 0:512], op=ALU.add), R=[pstres, "hst"], W=["hst"])
            f.op("pool", lambda e, g=g, hg=hg: e.tensor_copy(out=self.hbf[:, g * 512:(g + 1) * 512], in_=hg), R=["hst"], W=["hbf"])
            yield
        for half in range(2):
            yield
            pt, pres = self.pb()
            pbf = pt.bitcast(BF16)
            for k in range(8):
                kc = half * 8 + k
                f.op("pe", lambda e, k=k, kc=kc, pbf=pbf: e.transpose(out=pbf[:, k * 128:(k + 1) * 128], in_=self.s_tok[:, kc * 128:(kc + 1) * 128],
                                                                      identity=self.identb), R=["s_tok", "CBc"], W=[pres])
            f.op("dve", lambda e, half=half, pbf=pbf: e.tensor_tensor(
                out=self.sT[:, half * 8:half * 8 + 8, 0:128], in0=pbf[:, 0:1024].rearrange("p (k t) -> p k t", k=8),
                in1=bc(self.ssdnT[:, half * 8:half * 8 + 8], [128, 8, 128]), op=ALU.mult), R=[pres, "PV"], W=["sT"])

    def merge_out(self, ntok, bs, xres, x_ap, aT, sT, mT, hT, mergedT, mres_of, h2T):
        f = self.f
        actT = self.actT
        for s in range(8):
            for bi in range(3):
                gk = ("A", (SA_G0, SA_G1, SA_G2)[bi] + s)
                uk = (("S", s), ("A", SA_USSD + s), ("M", s))[bi]
                (gw, gwr), (uw, uwr) = self.slabs([gk, uk])
                for mm in range(2):
                    m = s * 2 + mm
                    col0 = mm * 128
                    acc = self.macc[mm]
                    accres = ("macc", mm)
                    pg, pgres = self.pb()
                    self.fm_mm(pg[:, 0:ntok], pgres, gw, gwr, col0, hT, "hT", ntok)
                    sg, sgres = self.tf()
                    f.op("act", lambda e, sg=sg, pg=pg: e.activation(out=sg[:, 0:ntok], in_=pg[:, 0:ntok], func=AF.Sigmoid), R=[pgres], W=[sgres])
                    pu, pures = self.pb()
                    if bi == 0:
                        uav = uw[0:64, 0:4096].rearrange("p (h c) -> p h c", h=16)
                        for hd in range(16):
                            f.op("pe", lambda e, hd=hd, pu=pu, uav=uav, col0=col0: e.matmul(pu[:, 0:ntok], lhsT=uav[:, hd, col0:col0 + 128], rhs=aT[0:64, hd, 0:ntok],
                                                                                         start=(hd == 0), stop=(hd == 15)), R=[uwr, "aT"], W=[pures])
                        f.op("dve", lambda e, acc=acc, sg=sg, pu=pu: e.tensor_tensor(out=acc[:, 0:ntok], in0=sg[:, 0:ntok], in1=pu[:, 0:ntok], op=ALU.mult),
                             R=[sgres, pures], W=[accres])
                    else:
                        if bi == 1:
                            self.fm_mm(pu[:, 0:ntok], pures, uw, uwr, col0, sT, "sT", ntok)
                        else:
                            self.fm_mm(pu[:, 0:ntok], pures, uw, uwr, col0, mT, "mT", ntok, kcs=8)
                        f.op("dve", lambda e, sg=sg, pu=pu: e.tensor_tensor(out=sg[:, 0:ntok], in0=sg[:, 0:ntok], in1=pu[:, 0:ntok], op=ALU.mult),
                             R=[sgres, pures], W=[sgres])
                        if bi == 1:
                            f.op("dve", lambda e, acc=acc, sg=sg: e.tensor_tensor(out=acc[:, 0:ntok], in0=acc[:, 0:ntok], in1=sg[:, 0:ntok], op=ALU.add),
                                 R=[sgres, accres], W=[accres])
                        else:
                            f.op("dve", lambda e, acc=acc, sg=sg, m=m: e.tensor_tensor(out=mergedT(m), in0=acc[:, 0:ntok], in1=sg[:, 0:ntok], op=ALU.add),
                                 R=[sgres, accres], W=[mres_of(m)])
        MR = [mres_of(m) for m in range(16)]
        for s in range(8):
            wt, wres = self.slab("A", SA_OUT + s)
            wv = wt[:, 0:4096].rearrange("p (k c) -> p k c", k=16)
            pt, pres = self.pb()
            for kc in range(16):
                f.op("pe", lambda e, kc=kc, pt=pt, wv=wv: e.matmul(pt[0:bs, 0:256], lhsT=mergedT(kc)[:, 0:bs], rhs=wv[:, kc, :],
                                                                   start=(kc == 0), stop=(kc == 15)), R=[wres, mres_of(kc)], W=[pres])
            xa = x_ap[:, s * 256:(s + 1) * 256]
            f.op("dve", lambda e, xa=xa, pt=pt: e.tensor_tensor(out=xa, in0=xa, in1=pt[0:bs, 0:256], op=ALU.add), R=[pres, xres], W=[xres])
        self.norm_T(x_ap, xres, bs, self.gffnT, h2T, "h2T", 0)

    def ffn_gen(self, ntok, bs, xres, x_ap, hT):
        f = self.f
        actT = self.actT
        for s in range(22):
            (gw, gwr), (uw, uwr) = self.slabs([("A", SA_GATE + s), ("A", SA_UP + s)])
            for mm in range(2):
                j = s * 2 + mm
                pg, pgres = self.pb()
                pu, pures = self.pb()
                self.fm_mm(pg[:, 0:ntok], pgres, gw, gwr, mm * 128, hT, "h2T", ntok)
                self.fm_mm(pu[:, 0:ntok], pures, uw, uwr, mm * 128, hT, "h2T", ntok)
                sg, sgres = self.tf()
                f.op("act", lambda e, sg=sg, pg=pg: e.activation(out=sg[:, 0:ntok], in_=pg[:, 0:ntok], func=AF.Silu), R=[pgres], W=[sgres])
                f.op("dve", lambda e, sg=sg, pu=pu, j=j: e.tensor_tensor(out=actT[:, j, 0:ntok], in0=sg[:, 0:ntok], in1=pu[:, 0:ntok], op=ALU.mult),
                     R=[sgres, pures], W=["actT"])
            yield
        for cg in range(8):
            pt, pres = self.pb()
            for half in range(2):
                wt, wres = self.slab("D", cg * 2 + half)
                wv = wt[:, 0:5632].rearrange("p (k c) -> p k c", k=22)
                for kc in range(22):
                    f.op("pe", lambda e, kc=kc, half=half, pt=pt, wv=wv: e.matmul(
                        pt[0:bs, 0:256], lhsT=actT[:, half * 22 + kc, 0:bs], rhs=wv[:, kc, :],
                        start=(half == 0 and kc == 0), stop=(half == 1 and kc == 21)), R=[wres, "actT"], W=[pres])
            xa = x_ap[:, cg * 256:(cg + 1) * 256]
            f.op("dve", lambda e, xa=xa, pt=pt: e.tensor_tensor(out=xa, in0=xa, in1=pt[0:bs, 0:256], op=ALU.add), R=[pres, xres], W=[xres])
            yield

    def order_in(self):
        o = [("A", SA_Q + i) for i in range(4)] + [("A", SA_K)] + [("A", SA_XBC + i) for i in range(12)]
        o += [("A", SA_QM + i) for i in range(4)] + [("A", SA_V)] + [("A", SA_Z + i) for i in range(8)] + [("T", 0)]
        return o

    def order_merge(self):
        o = []
        for s in range(8):
            o += [("A", SA_G0 + s), ("S", s), ("A", SA_G1 + s), ("A", SA_USSD + s), ("A", SA_G2 + s), ("M", s)]
        o += [("A", SA_OUT + i) for i in range(8)]
        return o

    def order_ffn(self):
        o = []
        for s in range(22):
            o += [("A", SA_GATE + s), ("A", SA_UP + s)]
        o += [("D", i) for i in range(16)]
        return o

    def phase_in(self, seq, st, par, last):
        f = self.f
        io = self.io
        t0 = st * NT
        xt = self.xtoks[par]
        xres = "x%d" % par
        f.dma("pool", xt[:], io["xp"][seq, t0:t0 + 128, :], W=[xres])
        self.norm_T(xt[:], xres, 128, self.gmixT, self.hT, "hT", 0)
        last3 = None
        k32 = None
        L = self.lastst
        if last:
            last3 = self.l3t[:, 0:96].rearrange("p (c j) -> p c j", c=24)
            k32t = L[:, 96:96 + 256].rearrange("p (a t) -> p a t", a=2)
            k32 = lambda pair: k32t[:, pair, :]
        f.op("pool", lambda e: e.tensor_copy(out=self.xbcT[:, :, 0:3], in_=self.hist[:]), R=["hist"], W=["xbch"])
        self.drain(self.in_proj_fm(self.hT, NT, self.qT, lambda pair: self.kT[:, pair, 128:256],
                                   lambda c: self.xbcT[:, c, 3:3 + NT], self.qmT, last3=last3, k32=k32, parts=("q", "k", "xbc")))
        self.interleave(self.phase_in_b(last), self.conv_gen(), 1, 1, None, None)

    def phase_in_b(self, last):
        f = self.f
        L = self.lastst
        yield from self.in_proj_fm(self.hT, NT, self.qT, None, None, self.qmT, parts=("qm",))
        wt, wres = self.slab("A", SA_V)
        pt, pres = self.pb()
        self.tm_mm(pt[:, 0:256], pres, wt, wres, 256, self.hT, "hT", 0, 128)
        f.op("act", lambda e, pt=pt: e.activation(out=self.vtok[:, 1, :], in_=pt[:, 0:256], func=AF.Copy), R=[pres], W=["vtok"])
        if last:
            f.op("dve", lambda e, pt=pt: e.tensor_copy(out=L[:, 352:608], in_=pt[:, 0:256]), R=[pres], W=["lastst"])
        yield
        for s in range(8):
            wt, wres = self.slab("A", SA_Z + s)
            pt, pres = self.pb()
            self.tm_mm(pt[:, 0:256], pres, wt, wres, 256, self.hT, "hT", 0, 128)
            f.op("act", lambda e, pt=pt, s=s: e.activation(out=self.zs[:, s * 256:(s + 1) * 256], in_=pt[:, 0:256], func=AF.Silu),
                 R=[pres], W=["zs"])
            yield
        wt, wres = self.slab("T", 0)
        pt, pres = self.pb()
        self.tm_mm(pt[:, 0:32], pres, wt, wres, 32, self.hT, "hT", 0, 128, cw=32)
        f.op("act", lambda e, pt=pt: e.activation(out=self.dtraw[:], in_=pt[:, 0:32], func=AF.Copy), R=[pres], W=["dtraw"])

    def conv_gen(self):
        f = self.f

        def save_hist(c):
            f.op("pool", lambda e, c=c: e.tensor_copy(out=self.hist[:, c, :], in_=self.xbcT[:, c, NT:NT + 3]), R=[("xbcT", c)], W=["hist"])
        yield from self.conv_fm(lambda c, j: self.xbcT[:, c, j:j + NT], NT, lambda c: self.xbcT[:, c, 3:3 + NT], lambda c: ("xbcT", c), save_hist=save_hist)

    def phase_mix(self, first):
        f = self.f

        has_prev = not first
        yield from self.swa_heads(
            q_rhs=lambda h: self.qT[64 * (h % 2):64 * (h % 2) + 64, (h // 2) * 4:(h // 2) * 4 + 4, 0:128],
            kprev=(lambda h: self.kT[64 * (h % 2):64 * (h % 2) + 64, h // 2, 0:128]) if has_prev else None,
            kcur=lambda h: self.kT[64 * (h % 2):64 * (h % 2) + 64, h // 2, 128:256],
            vprev=lambda h: self.vtok[:, 0, h * 64:(h + 1) * 64],
            vcur=lambda h: self.vtok[:, 1, h * 64:(h + 1) * 64],
            nq=128, kcn=128, Dmp=self.Dmp, Dmc=self.Dmd,
            out=lambda h: self.aT[0:64, h * 4:h * 4 + 4, 0:128],
            R_in=["qT", "kT", "vtok"], W_out=["aT"])
        f.op("pool", lambda e: e.tensor_copy(out=self.kT[:, :, 0:128], in_=self.kT[:, :, 128:256]), R=["kT"], W=["kT"])
        f.op("pool", lambda e: e.tensor_copy(out=self.vtok[:, 0, :], in_=self.vtok[:, 1, :]), R=["vtok"], W=["vtok"])
        yield from self.ssd_block()
        yield from self.mem_heads(q_rhs=lambda c: self.qmT[:, c, 0:NT], KTm=self.KTm, ktres="KTm", Vm=self.Vm, vres="Vm", nq=NT,
                                  out=lambda c: self.mT[:, c, 0:NT], R_in=["qmT"], W_out=["mT"])

    def phase_merge(self, par):
        self.merge_out(NT, 128, "x%d" % par, self.xtoks[par][:], self.aT, self.sT, self.mT, self.hT,
                       lambda m: self.xbcT[:, m, 3:3 + NT], lambda m: ("xbcT", m), self.h2T)

    @staticmethod
    def drain(gen):
        for _ in gen:
            pass

    def interleave(self, ga, gb, na=1, nb=1, ca="F", cb="M"):
        da = db = False
        while not (da and db):
            for _ in range(na):
                if not da:
                    self.ctx = ca
                    try:
                        next(ga)
                    except StopIteration:
                        da = True
            for _ in range(nb):
                if not db:
                    self.ctx = cb
                    try:
                        next(gb)
                    except StopIteration:
                        db = True
        self.ctx = None

    def run_sequence(self, seq, n_st):
        f = self.f
        io = self.io
        nfull = SEQ // NT
        is_last = lambda st: (st == nfull - 1) or (self.force_last and st == n_st - 1)
        self.worder = self.order_in() + self.order_merge()
        self.wpos = 0
        self.phase_in(seq, 0, 0, is_last(0))
        self.drain(self.phase_mix(True))
        for st in range(n_st):
            par = st % 2
            nxt = st + 1 < n_st
            self.worder = self.order_merge() + (self.order_in() if nxt else []) + self.order_ffn() + self.order_merge()
            self.wpos = 0
            self.phase_merge(par)
            ffn = self.ffn_gen(NT, 128, "x%d" % par, self.xtoks[par][:], self.h2T)
            if nxt:
                self.phase_in(seq, st + 1, 1 - par, is_last(st + 1))
                if self.pipeline:
                    self.interleave(ffn, self.phase_mix(False), self.il[0], self.il[1])
                else:
                    self.drain(ffn)
                    self.drain(self.phase_mix(False))
            else:
                self.drain(ffn)
            f.dma("pool", io["yp"][seq, st * NT:st * NT + 128, :], self.xtoks[par][:], R=["x%d" % par])
            if is_last(st):
                self.seq_last_outputs(seq)

    def seq_last_outputs(self, seq):
        f = self.f
        io = self.io
        L = self.lastst
        k32t = L[:, 96:96 + 256].rearrange("p (a t) -> p a t", a=2)
        if 'pkpv' in _SKIP:
            return
        ptk, pkres = self.pb()
        for pair in range(2):
            f.op("pe", lambda e, pair=pair, ptk=ptk: e.transpose(out=ptk[:, pair * 128:(pair + 1) * 128], in_=k32t[:, pair, :], identity=self.identf),
                 R=["lastst", "C"], W=[pkres])
        f.op("act", lambda e, ptk=ptk: e.activation(out=L[:, 608:864], in_=ptk[:, 0:256], func=AF.Copy), R=[pkres], W=["lastst"])
        f.dma("pool", io["pk"][seq], L[:, 608:864], R=["lastst"])
        f.dma("pool", io["pv"][seq], L[:, 352:608], R=["lastst"])
        if 'pconv' in _SKIP:
            return
        l3t = self.l3t
        pc3 = self.stage[0:4, 0:3072]
        for q6 in range(6):
            pt, pres = self.pb()
            for k in range(4):
                c = q6 * 4 + k
                f.op("pe", lambda e, k=k, c=c, pt=pt: e.transpose(out=pt[:, k * 128:(k + 1) * 128], in_=l3t[:, c * 4:c * 4 + 128], identity=self.identf),
                     R=["lastst", "C"], W=[pres])
            f.op("act", lambda e, q6=q6, pt=pt: e.activation(out=pc3[:, q6 * 512:(q6 + 1) * 512], in_=pt[0:4, 0:512], func=AF.Copy),
                 R=[pres], W=self.STAGE)
        f.dma("pool", io["pconv"][seq], pc3[1:4, :], R=self.STAGE)

    def seq_end(self, seq):
        if "seq_end" in _SKIP:
            return
        f = self.f
        io = self.io
        stage = self.stage
        SR = self.STAGE
        for q4 in range(4):
            pt, pres = self.pb()
            for k in range(4):
                c = q4 * 4 + k
                f.op("pe", lambda e, k=k, c=c, pt=pt: e.transpose(out=pt[:, k * 128:(k + 1) * 128], in_=self.hst[:, c * 128:(c + 1) * 128],
                                                                  identity=self.identf), R=["hst", "C"], W=[pres])
            f.op("act", lambda e, q4=q4, pt=pt: e.activation(out=stage[:, q4 * 512:(q4 + 1) * 512], in_=pt[:, 0:512], func=AF.Copy),
                 R=[pres], W=SR)
        f.dma("pool", io["pssm"][seq].rearrange("(c q) n -> q c n", q=128), stage[:, 0:2048].rearrange("p (c n) -> p c n", c=16), R=SR)

    def sample_group(self):
        f = self.f
        io = self.io
        NS = NSMP
        L = self.lastst
        stage = self.stage
        SR = self.STAGE
        x16 = self.xtok[0:NS, :]
        self.worder = self.order_in() + self.order_merge()
        self.wpos = 0
        f.dma("pool", io["sk"][:, 0:127, :], io["csk"][:, 1:128, :])
        f.dma("pool", io["sv"][:, 0:127, :], io["csv"][:, 1:128, :])
        f.dma("pool", io["sconv_o"][:, 0:2, :], io["sconv"].rearrange("(b j) c -> b j c", j=3)[:, 1:3, :])
        f.dma("pool", x16, io["xs"], W=["x0"])
        self.norm_T(x16, "x0", NS, self.gmixT, self.hT, "hT", 0)
        cbuf = self.cbuf
        f.dma("pool", stage[0:48, 0:3072], io["sconv"], W=SR)
        for q6 in range(6):
            pt, pres = self.pb()
            for k in range(4):
                c = q6 * 4 + k
                f.op("pe", lambda e, k=k, c=c, pt=pt: e.transpose(out=pt[:, k * 48:(k + 1) * 48], in_=stage[0:48, c * 128:(c + 1) * 128],
                                                                  identity=self.identf[0:48, 0:48]), R=SR + ["C"], W=[pres])
            for k in range(4):
                c = q6 * 4 + k
                f.op("act", lambda e, k=k, c=c, pt=pt: e.activation(out=cbuf[:, c, 0:3, :], in_=pt[:, k * 48:(k + 1) * 48].rearrange("p (b j) -> p j b", j=3),
                                                                    func=AF.Copy), R=[pres], W=["xbch"])
        k32s = L[:, 96:96 + 256].rearrange("p (a t) -> p a t", a=2)
        f.op("pool", lambda e: e.memset(L[:, 96:352], 0.0), W=["lastst"])
        xbctok = self.hst
        def xbc_hook(s, wt, wres):
            pt, pres = self.pb()
            self.tm_mm(pt[0:NS, 0:256], pres, wt, wres, 256, self.hT, "hT", 0, NS)
            if s < 8:
                dst = self.hst[0:NS, s * 256:(s + 1) * 256]
            else:
                dst = self.hbf32[0:NS, (s - 8) * 256:(s - 7) * 256]
            f.op("act", lambda e, pt=pt, dst=dst: e.activation(out=dst, in_=pt[0:NS, 0:256], func=AF.Copy), R=[pres], W=["hst" if s < 8 else "bcs"])
        self.xbc_hook = xbc_hook
        self.drain(self.in_proj_fm(self.hT, NS, self.qT, lambda pair: self.kT[:, pair, 128:128 + NS],
                                   lambda c: cbuf[:, c, 3, :], self.qmT, last3=None, k32=lambda pair: k32s[:, pair, 0:NS]))
        self.xbc_hook = None
        f.dma("pool", io["sconv_o"][:, 2, 0:2048], self.hst[0:NS, :], R=["hst"])
        f.dma("pool", io["sconv_o"][:, 2, 2048:3072], self.hbf32[0:NS, :], R=["bcs"])
        pt, pres = self.pb()
        for pair in range(2):
            f.op("pe", lambda e, pair=pair, pt=pt: e.transpose(out=pt[:, pair * 128:(pair + 1) * 128], in_=k32s[:, pair, :], identity=self.identf),
                 R=["lastst", "C"], W=[pres])
        f.op("act", lambda e, pt=pt: e.activation(out=L[0:NS, 608:864], in_=pt[0:NS, 0:256], func=AF.Copy), R=[pres], W=["lastst"])
        f.dma("pool", io["sk"][:, 127, :], L[0:NS, 608:864], R=["lastst"])
        wt, wres = self.slab("A", SA_V)
        pt, pres = self.pb()
        self.tm_mm(pt[0:NS, 0:256], pres, wt, wres, 256, self.hT, "hT", 0, NS)
        f.op("act", lambda e, pt=pt: e.activation(out=self.vtok[0:NS, 1, :], in_=pt[0:NS, 0:256], func=AF.Copy), R=[pres], W=["vtok"])
        f.op("dve", lambda e, pt=pt: e.tensor_copy(out=L[0:NS, 352:608], in_=pt[0:NS, 0:256]), R=[pres], W=["lastst"])
        f.dma("pool", io["sv"][:, 127, :], L[0:NS, 352:608], R=["lastst"])
        zT = self.zs[:, 0:16 * NS].rearrange("p (c t) -> p c t", c=16)
        for s in range(8):
            wt, wres = self.slab("A", SA_Z + s)
            for mm in range(2):
                pt, pres = self.pb()
                self.fm_mm(pt[:, 0:NS], pres, wt, wres, mm * 128, self.hT, "hT", NS)
                f.op("act", lambda e, pt=pt, c=s * 2 + mm: e.activation(out=zT[:, c, :], in_=pt[:, 0:NS], func=AF.Silu), R=[pres], W=["zs"])
        wt, wres = self.slab("T", 0)
        pt, pres = self.pb()
        self.tm_mm(pt[0:NS, 0:32], pres, wt, wres, 32, self.hT, "hT", 0, NS, cw=32)
        f.op("act", lambda e, pt=pt: e.activation(out=self.dtraw[0:NS, :], in_=pt[0:NS, 0:32], func=AF.Copy), R=[pres], W=["dtraw"])
        cvs = self.cvs
        self.drain(self.conv_fm(lambda c, j: cbuf[:, c, j, :], NS, lambda c: cvs[:, c, :], lambda c: ("xbcT", c)))
        CVS = [("xbcT", c) for c in range(24)]
        for b in range(NS):
            ck, ckres = self.tf()
            f.dma("pool", ck[:, 0:256], io["csk"][b], W=[ckres])
            f.dma("pool", ck[:, 256:512], io["csv"][b], W=[ckres])
            cb16, cbres = self.tb()
            f.op("pool", lambda e, ck=ck, cb16=cb16: e.tensor_copy(out=cb16[:, 0:512], in_=ck[:, 0:512]), R=[ckres], W=[cbres])
            pt, pres = self.pb()
            pbf = pt.bitcast(BF16)
            for pair in range(2):
                f.op("pe", lambda e, pair=pair, pbf=pbf, cb16=cb16: e.transpose(out=pbf[:, pair * 128:(pair + 1) * 128], in_=cb16[:, pair * 128:(pair + 1) * 128],
                                                                                identity=self.identb), R=[cbres, "CBc"], W=[pres])
            kTc, kTres = self.tb()
            f.op("act", lambda e, pbf=pbf, kTc=kTc: e.activation(out=kTc[:, 0:256], in_=pbf[:, 0:256], func=AF.Copy), R=[pres], W=[kTres])
            self.drain(self.swa_heads(
                q_rhs=lambda h, b=b: self.qT[64 * (h % 2):64 * (h % 2) + 64, (h // 2) * 4:(h // 2) * 4 + 4, b:b + 1],
                kprev=lambda h, kTc=kTc: kTc[64 * (h % 2):64 * (h % 2) + 64, (h // 2) * 128:(h // 2 + 1) * 128],
                kcur=lambda h: self.kT[64 * (h % 2):64 * (h % 2) + 64, h // 2, 128:128 + NS],
                vprev=lambda h, cb16=cb16: cb16[:, 256 + h * 64:256 + (h + 1) * 64],
                vcur=lambda h: self.vtok[0:NS, 1, h * 64:(h + 1) * 64],
                nq=1, kcn=NS, Dmp=self.Dmp[:, 0:1], Dmc=self.dm16[0:NS, b:b + 1],
                out=lambda h, b=b: self.aT[0:64, h * 4:h * 4 + 4, b:b + 1],
                R_in=["qT", "kT", "vtok", kTres, cbres], W_out=["aT"]))
            kst = stage[:, 0:2048]
            vst = stage[:, 2048:4096]
            f.dma("pool", kst.rearrange("p (m c) -> p m c", m=2), io["cmk"][b].rearrange("(m p) c -> p m c", p=128), W=SR)
            f.dma("pool", vst.rearrange("p (m c) -> p m c", m=2), io["cmv"][b].rearrange("(m p) c -> p m c", p=128), W=SR)
            f.op("pool", lambda e: e.tensor_copy(out=self.Vm[:].rearrange("p m c -> p (m c)"), in_=vst), R=SR, W=["Vm"])
            f.op("dve", lambda e: e.tensor_copy(out=self.xn[:], in_=kst), R=SR, W=["xn"])
            self.kmem_T(self.xn, "xn", self.KTm, "KTm")
            self.drain(self.mem_heads(q_rhs=lambda c, b=b: self.qmT[:, c, b:b + 1], KTm=self.KTm, ktres="KTm", Vm=self.Vm, vres="Vm", nq=1,
                                      out=lambda c, b=b: self.mT[:, c, b:b + 1], R_in=["qmT"], W_out=["mT"]))
        S = self.ssd_s
        dt = S[0:NS, 32:64]
        f.op("dve", lambda e: e.tensor_tensor(out=S[0:NS, 0:32], in0=self.dtraw[0:NS, :], in1=self.dtb[0:NS, :], op=ALU.add), R=["dtraw", "PR"], W=["S"])
        f.op("act", lambda e: e.activation(out=S[0:NS, 0:32], in_=S[0:NS, 0:32], func=AF.Exp), R=["S"], W=["S"])
        f.op("act", lambda e: e.activation(out=dt, in_=S[0:NS, 0:32], func=AF.Ln, bias=1.0), R=["S"], W=["S"])
        f.op("dve", lambda e: e.tensor_tensor(out=S[0:NS, 64:96], in0=dt, in1=self.a_bc[0:NS, :], op=ALU.mult), R=["S", "SM"], W=["S"])
        f.op("act", lambda e: e.activation(out=S[0:NS, 96:128], in_=S[0:NS, 64:96], func=AF.Exp), R=["S"], W=["S"])
        ex = stage[0:NS, :]
        f.op("dve", lambda e: e.tensor_copy(out=ex[:, 0:2048].rearrange("p (h d) -> p h d", h=32), in_=bc(S[0:NS, 96:128], [NS, 32, 64])), R=["S"] + SR, W=SR)
        f.op("dve", lambda e: e.tensor_copy(out=ex[:, 2048:4096].rearrange("p (h d) -> p h d", h=32), in_=bc(dt, [NS, 32, 64])), R=["S"] + SR, W=SR)
        cdT, dtT = self.cdT, self.dtT
        for which, dst in ((0, cdT), (1, dtT)):
            pt, pres = self.pb()
            for c in range(16):
                f.op("pe", lambda e, c=c, which=which, pt=pt: e.transpose(out=pt[:, c * NS:(c + 1) * NS], in_=ex[:, which * 2048 + c * 128: which * 2048 + (c + 1) * 128],
                                                                          identity=self.identf[0:NS, 0:NS]), R=SR + ["C"], W=[pres])
            f.op("act", lambda e, pt=pt, dst=dst: e.activation(out=dst[:].rearrange("p c t -> p (c t)"), in_=pt[:, 0:16 * NS], func=AF.Copy), R=[pres], W=["cdT"])
        xdtT = self.xdtT
        f.op("dve", lambda e: e.tensor_tensor(out=xdtT[:], in0=cvs[:, 0:16, :], in1=dtT[:], op=ALU.mult), R=CVS + ["cdT"], W=["xdtT"])
        bcs = self.bcs
        bpad = self.hst[:, 0:1024].rearrange("p (c t) -> p c t", c=8)
        f.op("pool", lambda e: e.memset(self.hst[:, 0:1024], 0.0), W=["hst"])
        f.op("dve", lambda e: e.tensor_copy(out=bpad[:, :, 0:NS], in_=cvs[:, 16:24, :]), R=CVS + ["hst"], W=["hst"])
        for half in range(2):
            pt, pres = self.pb()
            for g in range(4):
                f.op("pe", lambda e, g=g, half=half, pt=pt: e.transpose(out=pt[:, g * 128:(g + 1) * 128], in_=bpad[:, half * 4 + g, :], identity=self.identf),
                     R=["hst", "C"], W=[pres])
            f.op("act", lambda e, half=half, pt=pt: e.activation(out=bcs[0:NS, half * 512:(half + 1) * 512], in_=pt[0:NS, 0:512], func=AF.Copy), R=[pres], W=["bcs"])
        oh16 = self.rhsD[:].rearrange("p a b -> p (a b)").bitcast(F32)[0:NS, :].rearrange("p (a b) -> p a b", a=NS)
        f.op("dve", lambda e: e.tensor_copy(out=oh16[:], in_=bc(self.identf[0:NS, 0:NS], [NS, NS, 128])), R=["C"], W=["rhsD"])
        yT = self.yT
        U = self.hst
        for b in range(NS):
            H = stage[:, (b % 2) * 2048:(b % 2 + 1) * 2048]
            Hres = ["xs_tok", "xdt"] if b % 2 == 0 else ["xw", "s_tok"]
            H3 = H.rearrange("p (c n) -> p c n", c=16)
            f.dma("pool", H3, io["sssm"][b].rearrange("(c q) n -> q c n", q=128), W=Hres)
            pbm, pbmres = self.pb()
            pcm, pcmres = self.pb()
            f.op("pe", lambda e, b=b, pbm=pbm: e.matmul(pbm[:, 0:512], lhsT=oh16[0:NS, b, :], rhs=bcs[0:NS, 0:512], start=True, stop=True), R=["rhsD", "bcs"], W=[pbmres])
            f.op("pe", lambda e, b=b, pcm=pcm: e.matmul(pcm[:, 0:512], lhsT=oh16[0:NS, b, :], rhs=bcs[0:NS, 512:1024], start=True, stop=True), R=["rhsD", "bcs"], W=[pcmres])
            f.op("dve", lambda e, b=b, H3=H3: e.tensor_tensor(out=H3, in0=H3, in1=bc(cdT[:, :, b], [128, 16, 128]), op=ALU.mult), R=Hres + ["cdT"], W=Hres)
            U4 = U[:].rearrange("p (g r n) -> p g r n", g=4, r=4)
            f.op("dve", lambda e, b=b, pbm=pbm, U4=U4: e.tensor_tensor(
                out=U4, in0=pbm[:, 0:512].rearrange("p (g n) -> p g n", g=4).unsqueeze(2).broadcast_to([128, 4, 4, 128]),
                in1=xdtT[:, :, b].rearrange("p (g r) -> p g r", g=4).unsqueeze(3).broadcast_to([128, 4, 4, 128]), op=ALU.mult),
                R=[pbmres, "xdtT", "hst"], W=["hst"])
            f.op("dve", lambda e, H=H: e.tensor_tensor(out=H, in0=H, in1=U[:], op=ALU.add), R=Hres + ["hst"], W=Hres)
            f.dma("pool", io["sssm_o"][b].rearrange("(c q) n -> q c n", q=128), H3, R=Hres)
            f.op("dve", lambda e, pcm=pcm, U4=U4, H=H: e.tensor_tensor(
                out=U4, in0=H.rearrange("p (g r n) -> p g r n", g=4, r=4),
                in1=pcm[:, 0:512].rearrange("p (g n) -> p g n", g=4).unsqueeze(2).broadcast_to([128, 4, 4, 128]), op=ALU.mult),
                R=Hres + [pcmres, "hst"], W=["hst"])
            f.op("dve", lambda e, b=b: e.tensor_reduce(out=yT[:, :, b], in_=U[:].rearrange("p (c n) -> p c n", c=16), op=ALU.add, axis=AX.X),
                 R=["hst"], W=["yT"])
        y2 = self.y2
        f.op("dve", lambda e: e.tensor_tensor(out=y2[:], in0=cvs[:, 0:16, :], in1=bc(self.dskT, [128, 16, NS]), op=ALU.mult), R=CVS + ["PV"], W=["y2"])
        f.op("dve", lambda e: e.tensor_tensor(out=y2[:], in0=y2[:], in1=yT[:], op=ALU.add), R=["y2", "yT"], W=["y2"])
        f.op("dve", lambda e: e.tensor_tensor(out=y2[:], in0=y2[:], in1=zT, op=ALU.mult), R=["y2", "zs"], W=["y2"])
        sq, sqres = self.tb()
        f.op("act", lambda e, sq=sq: e.activation(out=sq[:, 0:16 * NS], in_=y2[:].rearrange("p c t -> p (c t)"), func=AF.Square), R=["y2"], W=[sqres])
        p2, p2res = self.pb()
        for g in range(4):
            for r in range(4):
                c = g * 4 + r
                f.op("pe", lambda e, g=g, r=r, c=c, sq=sq, p2=p2: e.matmul(p2[:, g * NS:(g + 1) * NS], lhsT=self.onesb, rhs=sq[:, c * NS:(c + 1) * NS],
                                                                          start=(r == 0), stop=(r == 3)), R=[sqres, "CBc"], W=[p2res])
        rr, rres = self.rsqrt_bc(p2[:, 0:4 * NS], p2res, 4 * NS, 1.0 / 512)
        f.op("dve", lambda e, rr=rr: e.tensor_tensor(
            out=y2[:].rearrange("p (g r) t -> p g r t", g=4), in0=y2[:].rearrange("p (g r) t -> p g r t", g=4),
            in1=rr.rearrange("p (g t) -> p g t", g=4).unsqueeze(2).broadcast_to([128, 4, 4, NS]), op=ALU.mult), R=["y2", rres], W=["y2"])
        f.op("dve", lambda e: e.tensor_tensor(out=self.sT[:, :, 0:NS], in0=y2[:], in1=bc(self.ssdnT, [128, 16, NS]), op=ALU.mult), R=["y2", "PV"], W=["sT"])
        self.worder = self.order_merge() + self.order_ffn()
        self.wpos = 0
        self.merge_out(NS, NS, "x0", x16, self.aT, self.sT, self.mT, self.hT,
                       lambda m: self.xbcT[:, m, 3:3 + NS], lambda m: ("xbcT", m), self.h2T)
        self.drain(self.ffn_gen(NS, NS, "x0", x16, self.h2T))
        f.dma("pool", io["ys"], x16, R=["x0"])


def slabify(W, cw, kcs):
    K, N = W.shape
    assert K == kcs * 128 and N % cw == 0
    a = W.reshape(kcs, 128, N // cw, cw).transpose(2, 1, 0, 3)
    return np.ascontiguousarray(a).reshape(N // cw, 128, kcs * cw)


def host_prep(inp):
    w_in = inp["w_in"][0]
    q = w_in[:, 0:1024].reshape(2048, 2, 2, 4, 64).transpose(0, 1, 3, 2, 4).reshape(2048, 1024)
    parts = [slabify(q, CW, 16), slabify(w_in[:, 1024:1280], CW, 16), slabify(w_in[:, 1280:1536], CW, 16),
             slabify(w_in[:, 1536:3584], CW, 16), slabify(w_in[:, 3584:6656], CW, 16), slabify(w_in[:, 6688:7712], CW, 16),
             slabify(w_in[:, 7712:13856], CW, 16), slabify(inp["w_mem_kv"][0], CW, 16), slabify(inp["w_up_ssd"][0], CW, 16),
             slabify(inp["w_out"][0], CW, 16), slabify(inp["w_gate"][0], CW, 16), slabify(inp["w_up"][0], CW, 16)]
    WA = np.concatenate(parts, 0)
    assert WA.shape[0] == NA
    WM = slabify(inp["w_up_mem"][0], CW, 8)
    ws = inp["w_up_swa"][0]
    WS = np.ascontiguousarray(ws.reshape(16, 64, 8, CW).transpose(2, 1, 0, 3)).reshape(8, 64, 16 * CW)
    wd = inp["w_down"][0]
    WD = np.ascontiguousarray(wd.reshape(2, 22, 128, 8, CW).transpose(3, 0, 2, 1, 4)).reshape(16, 128, 22 * CW)
    WT = slabify(w_in[:, 6656:6688], 32, 16)[0]
    ar = np.arange(128)
    c128 = np.zeros((128, 8 * 128 + 512 + 32), np.float32)
    c128[:, 0:128] = np.eye(128)
    c128[:, 128:256] = (ar[:, None] <= ar[None, :])
    c128[:, 256:384] = (ar[:, None] > ar[None, :])
    c128[:, 384:512] = 1.0
    k_, q_ = ar[:, None], ar[None, :]
    c128[:, 512:640] = np.where(q_ >= k_, q_ - k_, 20000.0)
    c128[:, 640:768] = np.where(q_ <= k_, q_ + 128 - k_, 20000.0)
    c128[:, 768:896] = (ar[:, None] // 64 == ar[None, :] // 64)
    c128[:, 1024:1536] = np.tile(np.where(ar[None, :] < ar[:, None], -30000.0, 0.0), (1, 4))
    c128[:, 1536:1568] = (ar[:, None] % 32 == np.arange(32)[None, :])
    pvec = np.zeros((128, 512), np.float32)
    T16 = lambda v: np.ascontiguousarray(v.reshape(16, 128).T)
    pvec[:, 0:16] = T16(inp["norm_mix"][0])
    pvec[:, 16:32] = T16(inp["norm_ffn"][0])
    pvec[:, 32:48] = T16(inp["norm_mem"][0])
    pvec[:, 48:64] = T16(inp["ssd_norm"][0])
    pvec[:, 64:160] = inp["conv_w"][0].reshape(4, 24, 128).transpose(2, 1, 0).reshape(128, 96)
    pvec[:, 160:184] = inp["conv_b"][0].reshape(24, 128).T
    pvec[:, 184] = np.tile(inp["q_norm_swa"][0], 2)
    pvec[:, 185] = np.tile(inp["k_norm_swa"][0], 2)
    pvec[:, 186:188] = inp["q_norm_mem"][0].reshape(2, 128).T
    pvec[:, 188:204] = inp["swa_sinks"][0][None, :]
    pvec[:, 204:220] = np.repeat(inp["d_skip"][0], 64).reshape(16, 128).T
    pvec[0:16, 220:236] = np.where(np.eye(16) > 0, 0.0, 20000.0)
    prow = np.zeros((128, 352), np.float32)
    prow[:, 0:32] = inp["dt_bias"][0][None, :]
    prow[:, 32:64] = inp["a_log"][0][None, :]
    prow[:, 64:96] = inp["d_skip"][0][None, :]
    prow[:, 96:352] = inp["k_norm_mem"][0][None, :]
    shared = dict(WA=WA, WM=WM, WS=WS, WD=WD, WT=np.ascontiguousarray(WT), c128=c128, pvec=pvec, prow=prow)
    in_maps = []
    for c in range(NCORES):
        m = dict(shared)
        m["xp"] = np.ascontiguousarray(inp["x_prompt"][2 * c:2 * c + 2])
        m["memp"] = np.ascontiguousarray(inp["mem_prompt"][2 * c:2 * c + 2])
        sl = slice(16 * c, 16 * c + 16)
        m["xs"] = np.ascontiguousarray(inp["x_sample"][sl, 0])
        m["csk"] = np.ascontiguousarray(inp["cache_swa_k"][0, sl]).reshape(16, 128, 256)
        m["csv"] = np.ascontiguousarray(inp["cache_swa_v"][0, sl]).reshape(16, 128, 256)
        m["cmk"] = np.ascontiguousarray(inp["cache_mem_k"][0, sl]).reshape(16, 256, 1024)
        m["cmv"] = np.ascontiguousarray(inp["cache_mem_v"][0, sl]).reshape(16, 256, 1024)
        m["sssm"] = np.ascontiguousarray(inp["state_ssm"][0, sl]).reshape(16, 2048, 128)
        m["sconv"] = np.ascontiguousarray(inp["state_conv"][0, sl]).reshape(48, 3072)
        in_maps.append(m)
    return in_maps


_CACHE = {}


def run(inputs, do_samples=True, n_st=None, dbg=False):
    inp = {k: np.asarray(v) for k, v in inputs.items()}
    b = Builder(do_samples=do_samples, n_st=n_st, dbg=dbg)
    nc = b.build()
    in_maps = host_prep(inp)
    if not do_samples:
        for m in in_maps:
            for k in ("xs", "csk", "csv", "cmk", "cmv", "sssm", "sconv"):
                m.pop(k)
    res = run_bass_kernel_spmd(nc, in_maps, core_ids=list(range(NCORES)))
    R = res.results
    cat = lambda k: np.concatenate([r[k] for r in R], 0)
    yp = cat("yp")
    ys = cat("ys").reshape(128, 1, D)
    outs = (yp, ys,
            cat("pk").reshape(1, 16, 128, 4, 64), cat("pv").reshape(1, 16, 128, 4, 64),
            cat("pmk").reshape(1, 16, 256, 4, 256), cat("pmv").reshape(1, 16, 256, 4, 256),
            cat("pssm").reshape(1, 16, 32, 64, 128), cat("pconv").reshape(1, 16, 3, 3072),
            cat("sk").reshape(1, 128, 128, 4, 64), cat("sv").reshape(1, 128, 128, 4, 64),
            cat("sssm_o").reshape(1, 128, 32, 64, 128), cat("sconv_o").reshape(1, 128, 3, 3072))
    outs = tuple(np.ascontiguousarray(o, dtype=np.float32) for o in outs)
    if dbg:
        return outs, {k: [r["dbg_" + k] for r in R] for k in b.dbg_outs}
    return outs


def kernel(**inputs):
    return run(inputs, do_samples=True)
```

```python
import contextlib
import os
import numpy as np
_SKIP = set(os.environ.get('KSKIP', '').split(','))
import concourse.bass as bass
import concourse.mybir as mybir
from concourse.bass_utils import run_bass_kernel_spmd

F32 = mybir.dt.float32
BF16 = mybir.dt.bfloat16
AF = mybir.ActivationFunctionType
ALU = mybir.AluOpType
AX = mybir.AxisListType

NCORES = 8
D = 2048
SEQ = 2048
NSEQ = 2
NSMP = 16
NT = 128
BLK = 128
KC = 16
DFF = 5632
EPS = 1e-6
CW = 256
ENGS = ("pe", "act", "dve", "pool", "sp")

SA_Q, SA_K, SA_V, SA_Z, SA_XBC, SA_QM, SA_G0, SA_G1, SA_G2, SA_MEM, SA_USSD, SA_OUT, SA_GATE, SA_UP = (
    0, 4, 5, 6, 14, 26, 30, 38, 46, 54, 62, 70, 78, 100)
NA = 122
SLOPES = [2.0 ** (-8.0 * (h + 1) / 16.0) for h in range(16)]


class FW:
    def __init__(self, nc, n_dma_sems=48):
        self.nc = nc
        self.es = contextlib.ExitStack()
        self.sem = {e: self.es.enter_context(nc.semaphore("s_" + e)) for e in ENGS}
        self.dsem = [self.es.enter_context(nc.semaphore("d_%d" % i)) for i in range(n_dma_sems)]
        self.dtot = [0] * n_dma_sems
        self.dnext = {}
        self.dpool = {'pool': (0, n_dma_sems // 2), 'sp': (n_dma_sems // 2, n_dma_sems)}
        self.n = {e: 0 for e in ENGS}
        self.waited = {e: {} for e in ENGS}
        self.stream = {e: [] for e in ENGS}
        self.last_w = {}
        self.readers = {}

    def sbuf(self, name, shape, dtype):
        return self.es.enter_context(self.nc.sbuf_tensor(name, list(shape), dtype))

    def psum(self, name, shape, dtype):
        return self.es.enter_context(self.nc.psum_tensor(name, list(shape), dtype))

    def _deps(self, R, W):
        deps = set()
        for r in R:
            w = self.last_w.get(r)
            if w is not None:
                deps.add(w)
            if isinstance(r, tuple) and r[0] == "ps":
                rd = self.readers.get(r)
                if rd:
                    for k, v in rd.items():
                        deps.add((k, v))
        for r in W:
            w = self.last_w.get(r)
            if w is not None:
                deps.add(w)
            rd = self.readers.get(r)
            if rd:
                for k, v in rd.items():
                    deps.add((k, v))
        return deps

    def _waits(self, eng, deps):
        need = {}
        for k, v in deps:
            if k == "pe" and eng == "pe":
                continue
            if v > need.get(k, 0):
                need[k] = v
        out = []
        wd = self.waited[eng]
        for k, v in need.items():
            if wd.get(k, 0) >= v:
                continue
            wd[k] = v
            s = self.sem[k] if isinstance(k, str) else self.dsem[k[1]]
            out.append((s, v))
        return out

    def _record(self, my, R, W):
        for r in R:
            self.readers.setdefault(r, {})[my[0]] = my[1]
        for r in W:
            self.last_w[r] = my
            self.readers[r] = {}

    def op(self, eng, fn, R=(), W=()):
        waits = self._waits(eng, self._deps(R, W))
        self.n[eng] += 1
        my = (eng, self.n[eng])
        self.stream[eng].append((waits, fn, self.sem[eng], 1))
        self._record(my, R, W)

    def dma(self, q, out, in_, R=(), W=(), **kw):
        deps = self._deps(R, W)
        lo, hi = self.dpool[q]
        i = self.dnext.get(q, lo)
        self.dnext[q] = lo + (i + 1 - lo) % (hi - lo)
        if self.dtot[i] > 0:
            deps.add((("d", i), self.dtot[i]))
        waits = self._waits(q, deps)
        self.dtot[i] += 16
        my = (("d", i), self.dtot[i])
        nonctg = kw.pop("nonctg", False)
        nc = self.nc

        def fn(e):
            if nonctg:
                with nc.allow_non_contiguous_dma(reason="tiny strided store"):
                    return e.dma_start(out=out, in_=in_, **kw)
            return e.dma_start(out=out, in_=in_, **kw)
        self.stream[q].append((waits, fn, self.dsem[i], 16))
        self._record(my, R, W)

    def emit(self):
        nc = self.nc
        waits = [(self.dsem[i], t) for i, t in enumerate(self.dtot) if t > 0]
        waits += [(self.sem[e], self.n[e]) for e in ENGS if e != "sp" and self.n[e] > 0]
        self.stream["sp"].append((waits, None, None, 0))
        with nc.Block() as block:
            def replay(name, e):
                for waits, fn, s, inc in self.stream[name]:
                    for (ws, wv) in waits:
                        e.wait_ge(ws, wv)
                    if fn is not None:
                        fn(e).then_inc(s, inc)

            @block.tensor
            def _(e):
                replay("pe", e)

            @block.scalar
            def _(e):
                replay("act", e)

            @block.vector
            def _(e):
                replay("dve", e)

            @block.gpsimd
            def _(e):
                replay("pool", e)

            @block.sync
            def _(e):
                replay("sp", e)
        self.es.close()


def bc(ap, shape):
    return ap.unsqueeze(2).broadcast_to(list(shape))


class Builder:
    def __init__(self, do_samples=True, n_st=None, dbg=False, nseq=NSEQ, stop=None, force_last=False):
        self.force_last = force_last
        self.pipeline = os.environ.get('KPIPE', '1') == '1'
        self.il = tuple(int(v) for v in os.environ.get('KIL', '1,1').split(','))
        self.nseq = nseq
        self.stop = stop
        self.do_samples = do_samples
        self.n_st = n_st
        self.dbg_on = dbg
        self.nc = bass.Bass("TRN2", target_bir_lowering=False)
        self.f = FW(self.nc)
        self.ins = {}
        self.outs = {}
        self.dbg_outs = {}

    def din(self, name, shape, dtype=F32):
        self.ins[name] = self.nc.dram_tensor(name, list(shape), dtype, kind="ExternalInput").ap()
        return self.ins[name]

    def dout(self, name, shape):
        self.outs[name] = self.nc.dram_tensor(name, list(shape), F32, kind="ExternalOutput").ap()
        return self.outs[name]

    def dscr(self, name, shape, dtype):
        return self.nc.dram_tensor(name, list(shape), dtype, kind="Internal").ap()

    def dump(self, name, ap, shape, R):
        if not self.dbg_on:
            return
        o = self.nc.dram_tensor("dbg_" + name, list(shape), F32, kind="ExternalOutput").ap()
        self.dbg_outs[name] = o
        self.f.dma("pool", o, ap, R=R)

    def pb(self):
        if self.ctx == 'F':
            i = self.pnF
            self.pnF = (self.pnF + 1) % 4
        elif self.ctx == 'M':
            i = 4 + self.pnM
            self.pnM = (self.pnM + 1) % 4
        else:
            i = self.pnext
            self.pnext = (self.pnext + 1) % 8
        return self.ps[i], ("ps", i)

    def slabs(self, keys):
        order = self.worder
        live = set(keys)
        pos = None
        try:
            pos = order.index(keys[-1], self.wpos)
            self.wpos = pos
        except ValueError:
            pass
        upcoming = order[pos + 1: pos + 1 + self.NBUF] if pos is not None else []
        protect = set(live)
        for k in keys:
            if k not in self.wloaded:
                self._issue(k, protect)
        for nk in upcoming:
            if nk in self.wloaded:
                protect.add(nk)
                continue
            if not self._issue(nk, protect):
                break
            protect.add(nk)
        return [(self.wbuf[self.wloaded[k]], ("wbuf", self.wloaded[k])) for k in keys]

    def slab(self, kind, idx):
        return self.slabs([(kind, idx)])[0]

    def _issue(self, key, protect):
        kind, idx = key
        held = {b: k for k, b in self.wloaded.items()}
        b = None
        for i in range(self.NBUF):
            cand = (self.wnext + i) % self.NBUF
            if held.get(cand) not in protect or cand not in held:
                b = cand
                break
        if b is None:
            return False
        self.wnext = (b + 1) % self.NBUF
        if b in held:
            del self.wloaded[held[b]]
        src, n, parts = self.wsrc(kind, idx)
        if key not in self.cast_done:
            self.cast_done.add(key)
            fsrc = self._wf32[kind] if kind == "T" else self._wf32[kind][idx]
            self._cast(src, fsrc, n, ("wscr", kind, idx))
        self.f.dma("sp", self.wbuf[b][0:parts, 0:n], src, R=[("wscr", kind, idx)], W=[("wbuf", b)])
        self.wloaded[key] = b
        return True

    def wsrc(self, kind, idx):
        if kind == "A":
            return self.WAb[idx], 4096, 128
        if kind == "M":
            return self.WMb[idx], 2048, 128
        if kind == "S":
            return self.WSb[idx], 4096, 64
        if kind == "D":
            return self.WDb[idx], 2816, 128
        if kind == "T":
            return self.WTb, 512, 128
        raise ValueError(kind)

    def build(self):
        nc, f = self.nc, self.f
        xp = self.din("xp", [NSEQ, SEQ, D])
        memp = self.din("memp", [NSEQ, 256, D])
        WAf = self.din("WA", [NA, 128, 4096])
        WMf = self.din("WM", [8, 128, 2048])
        WSf = self.din("WS", [8, 64, 4096])
        WDf = self.din("WD", [32, 128, 2816])
        WTf = self.din("WT", [128, 512])
        c128 = self.din("c128", [128, 8 * 128 + 512 + 32])
        pvec = self.din("pvec", [128, 512])
        prow = self.din("prow", [128, 32 * 3 + 256])
        if self.do_samples:
            xs_in = self.din("xs", [NSMP, D])
            csk = self.din("csk", [NSMP, 128, 256])
            csv = self.din("csv", [NSMP, 128, 256])
            cmk = self.din("cmk", [NSMP, 256, 1024])
            cmv = self.din("cmv", [NSMP, 256, 1024])
            sssm = self.din("sssm", [NSMP, 2048, 128])
            sconv = self.din("sconv", [NSMP * 3, 3072])
        yp = self.dout("yp", [NSEQ, SEQ, D])
        pk = self.dout("pk", [NSEQ, 128, 256])
        pv = self.dout("pv", [NSEQ, 128, 256])
        pmk = self.dout("pmk", [NSEQ, 256, 1024])
        pmv = self.dout("pmv", [NSEQ, 256, 1024])
        pssm = self.dout("pssm", [NSEQ, 2048, 128])
        pconv = self.dout("pconv", [NSEQ, 3, 3072])
        ys = self.dout("ys", [NSMP, D])
        sk = self.dout("sk", [NSMP, 128, 256])
        sv = self.dout("sv", [NSMP, 128, 256])
        sssm_o = self.dout("sssm_o", [NSMP, 2048, 128])
        sconv_o = self.dout("sconv_o", [NSMP, 3, 3072])
        self.io = {**self.ins, **self.outs}
        self.WAb = self.dscr("WAb", [NA, 128, 4096], BF16)
        self.WMb = self.dscr("WMb", [8, 128, 2048], BF16)
        self.WSb = self.dscr("WSb", [8, 64, 4096], BF16)
        self.WDb = self.dscr("WDb", [32, 128, 2816], BF16)
        self.WTb = self.dscr("WTb", [128, 512], BF16)

        def cast(dst, src, n, res):
            if n > 2048:
                assert n % 2048 == 0 or n == 2816
                a = n // 2048 if n % 2048 == 0 else 2
                f.dma("pool", dst.rearrange("p (a b) -> p a b", a=a), src.rearrange("p (a b) -> p a b", a=a), W=[res])
            else:
                f.dma("pool", dst, src, W=[res])
        self._cast = cast
        self._wf32 = {"A": WAf, "M": WMf, "S": WSf, "D": WDf, "T": WTf}
        self.cast_done = set()

        self.ps = [f.psum("ps%d" % i, [128, 512], F32) for i in range(8)]
        self.pnext = 0
        self.pnF = self.pnM = self.tfF = self.tfM = 0
        self.ctx = None
        self.NBUF = int(os.environ.get("KNBUF", "6"))
        self.wbuf = [f.sbuf("wbuf%d" % i, [128, 4096], BF16) for i in range(self.NBUF)]
        self.wnext = 0
        self.wloaded = {}
        self.worder = []
        self.wpos = 0

        C = f.sbuf("c128t", [128, 8 * 128 + 512 + 32], F32)
        f.dma("sp", C[:], c128, W=["C"])
        self.identf = C[:, 0:128]
        self.Tm = C[:, 128:256]
        self.Um = C[:, 256:384]
        self.onesf = C[:, 384:512]
        self.Dmd = C[:, 512:640]
        self.Dmp = C[:, 640:768]
        self.ohs = C[:, 1536:1568]
        PV = f.sbuf("pvect", [128, 512], F32)
        f.dma("sp", PV[:], pvec, W=["PV"])
        PR = f.sbuf("prowt", [128, 352], F32)
        f.dma("sp", PR[:], prow, W=["PR"])
        self.PV, self.PR = PV, PR
        self.gmixT = PV[:, 0:16]
        self.gffnT = PV[:, 16:32]
        self.gmemT = PV[:, 32:48]
        self.ssdnT = PV[:, 48:64]
        self.cwT = PV[:, 64:160]
        self.cbT = PV[:, 160:184]
        self.gq = PV[:, 184:185]
        self.gk = PV[:, 185:186]
        self.gqm = PV[:, 186:188]
        self.sink_raw = PV[:, 188:204]
        self.dskT = PV[:, 204:220]
        self.dm16 = PV[:, 220:236]
        self.dtb = PR[:, 0:32]
        self.alog = PR[:, 32:64]
        self.dsk = PR[:, 64:96]
        self.gkm = PR[:, 96:352]
        CB = f.sbuf("cbf", [128, 128 * 3 + 512], BF16)
        f.op("dve", lambda e: e.tensor_copy(out=CB[:, 0:128], in_=C[:, 0:128]), R=["C"], W=["CBc"])
        f.op("dve", lambda e: e.tensor_copy(out=CB[:, 128:256], in_=C[:, 384:512]), R=["C"], W=["CBc"])
        f.op("dve", lambda e: e.tensor_copy(out=CB[:, 256:384], in_=C[:, 768:896]), R=["C"], W=["CBc"])
        f.op("dve", lambda e: e.tensor_copy(out=CB[:, 384:896], in_=C[:, 1024:1536]), R=["C"], W=["CBc"])
        self.identb = CB[:, 0:128]
        self.onesb = CB[:, 128:256]
        self.blockones = CB[:, 256:384]
        self.maskD = CB[:, 384:896]
        SM = f.sbuf("smallp", [128, 64], F32)
        self.SM = SM
        f.op("act", lambda e: e.mul(out=SM[:, 0:1], in_=PV[:, 184:185], mul=0.125), R=["PV"], W=["SM"])
        f.op("act", lambda e: e.activation(out=SM[:, 1:17], in_=PV[:, 188:204], func=AF.Exp), R=["PV"], W=["SM"])
        f.op("act", lambda e: e.activation(out=SM[:, 17:49], in_=PR[:, 32:64], func=AF.Exp), R=["PR"], W=["SM"])
        f.op("act", lambda e: e.mul(out=SM[:, 17:49], in_=SM[:, 17:49], mul=-1.0), R=["SM"], W=["SM"])
        self.gq8 = SM[:, 0:1]
        self.esink = SM[:, 1:17]
        self.a_bc = SM[:, 17:49]

        self.hT = f.sbuf("hT", [128, KC, NT], BF16)
        self.xtoks = [f.sbuf("xtok%d" % i, [128, D], F32) for i in range(2)]
        self.xtok = self.xtoks[0]
        self.h2T = f.sbuf("h2T", [128, KC, NT], BF16)
        self.xn = f.sbuf("xn", [128, D], BF16)
        self.st1 = f.sbuf("st1", [128, 8], F32)
        self.qT = f.sbuf("qT", [128, 8, NT], BF16)
        self.kT = f.sbuf("kT", [128, 2, 2 * 128], BF16)
        self.vtok = f.sbuf("vtok", [128, 2, 256], BF16)
        self.zs = f.sbuf("zs", [128, D], BF16)
        self.xbcT = f.sbuf("xbcT", [128, 24, 3 + NT], BF16)
        self.hist = f.sbuf("hist", [128, 24, 3], BF16)
        self.qmT = f.sbuf("qmT", [128, 8, NT], BF16)
        self.aT = f.sbuf("aT", [64, 16, NT], BF16)
        self.sT = f.sbuf("sT", [128, KC, NT], BF16)
        self.mT = f.sbuf("mT", [128, 8, NT], BF16)
        self.actT = f.sbuf("actT", [128, 44, NT], BF16)
        self.hst = f.sbuf("hst", [128, D], F32)
        self.hbf = f.sbuf("hbf", [128, D], BF16)
        self.KTm = f.sbuf("KTm", [128, 8, 256], BF16)
        self.Vm = f.sbuf("Vm", [128, 2, 1024], BF16)
        self.lastst = f.sbuf("lastst", [128, 96 + 256 + 256 + 256], F32)
        self.dtraw = f.sbuf("dtraw", [128, 32], F32)
        self.l3t = f.sbuf("l3t", [128, 224], F32)
        f.op("pool", lambda e: e.memset(self.l3t[:], 0.0), W=["lastst"])
        self.macc = [f.sbuf("macc%d" % i, [128, NT], F32) for i in range(2)]
        self.tmpf = [f.sbuf("tmpf%d" % i, [128, 512], F32) for i in range(5)]
        self.tmpb = [f.sbuf("tmpb%d" % i, [128, 512], BF16) for i in range(4)]
        self.tfn = 0
        self.tbn = 0
        self.ssdbig = f.sbuf("ssdbig", [128, 4 * D], BF16)
        self.xs_tok = self.ssdbig[:, 0:D]
        self.xdt = self.ssdbig[:, D:2 * D]
        self.xw = self.ssdbig[:, 2 * D:3 * D]
        self.s_tok = self.ssdbig[:, 3 * D:4 * D]
        self.stage = self.ssdbig[:].bitcast(F32)
        self.STAGE = ["xs_tok", "xdt", "xw", "s_tok"]
        self.bm_tok = f.sbuf("bm_tok", [128, 512], BF16)
        self.ssd_s = f.sbuf("ssd_s", [128, 256], F32)
        self.HI = f.sbuf("HI", [128, 128], BF16)
        self.LO = f.sbuf("LO", [128, 128], BF16)
        self.lhsD = f.sbuf("lhsD", [128, 128], BF16)
        self.rhsD = f.sbuf("rhsD", [128, 32, 128], BF16)
        self.dA4 = f.sbuf("dA4", [128, 4, 32], F32)
        self.CBt = f.sbuf("CBt", [128, 4, 128], F32)
        self.Wp = [f.sbuf("Wp%d" % i, [128, 4, 128], BF16) for i in range(2)]
        if self.do_samples:
            self.cbuf = f.sbuf("cbuf", [128, 24, 4, NSMP], BF16)
            self.cvs = f.sbuf("cvs", [128, 24, NSMP], F32)
            self.cdT = f.sbuf("cdT", [128, 16, NSMP], F32)
            self.dtT = f.sbuf("dtT", [128, 16, NSMP], F32)
            self.xdtT = f.sbuf("xdtT", [128, 16, NSMP], F32)
            self.bcs = f.sbuf("bcs", [NSMP, 1024], F32)
            self.hbf32 = self.bcs
            self.yT = self.dtT
            self.y2 = self.cdT
        f.op("dve", lambda e: e.memset(self.lhsD[64:128, :], 1.0), W=["lhsD"])
        f.op("dve", lambda e: e.tensor_copy(out=self.rhsD[0:64], in_=bc(self.ohs[0:64], [64, 32, 128])), R=["C"], W=["rhsD"])

        n_st = SEQ // NT if self.n_st is None else self.n_st
        for seq in range(self.nseq):
            self.seq_start(seq)
            if self.stop == 'seq_start' or n_st == 0:
                continue
            self.run_sequence(seq, n_st)
            if n_st == SEQ // NT or self.force_last:
                self.seq_end(seq)
        if self.do_samples:
            self.sample_group()
        f.emit()
        return nc

    def tf(self):
        if self.ctx == 'F':
            i = 0
        elif self.ctx == 'M':
            i = 1 + self.tfM
            self.tfM = (self.tfM + 1) % 4
        else:
            i = self.tfn
            self.tfn = (self.tfn + 1) % len(self.tmpf)
        return self.tmpf[i], ("tmpf", i)

    def tb(self):
        i = self.tbn
        self.tbn = (self.tbn + 1) % len(self.tmpb)
        return self.tmpb[i], ("tmpb", i)

    def super_order(self):
        o = []
        o += [("A", SA_Q + i) for i in range(4)] + [("A", SA_K)] + [("A", SA_XBC + i) for i in range(12)]
        o += [("A", SA_QM + i) for i in range(4)] + [("A", SA_V)] + [("A", SA_Z + i) for i in range(8)] + [("T", 0)]
        for s in range(8):
            o += [("A", SA_G0 + s), ("S", s), ("A", SA_G1 + s), ("A", SA_USSD + s), ("A", SA_G2 + s), ("M", s)]
        o += [("A", SA_OUT + i) for i in range(8)]
        for s in range(22):
            o += [("A", SA_GATE + s), ("A", SA_UP + s)]
        o += [("D", i) for i in range(16)]
        return o

    def fm_mm(self, ps_ap, psres, wt, wres, col0, actT, actres, ntok, kcs=KC, M=128):
        wv = wt[:, 0:kcs * CW].rearrange("p (k c) -> p k c", k=kcs)
        for kc in range(kcs):
            self.f.op("pe", lambda e, kc=kc: e.matmul(ps_ap, lhsT=wv[:, kc, col0:col0 + M], rhs=actT[:, kc, 0:ntok],
                                                       start=(kc == 0), stop=(kc == kcs - 1)),
                      R=[wres, actres], W=[psres])

    def tm_mm(self, ps_ap, psres, wt, wres, ncols, actT, actres, t0, bs, kcs=KC, cw=CW, col0=0):
        wv = wt[:, 0:kcs * cw].rearrange("p (k c) -> p k c", k=kcs)
        for kc in range(kcs):
            self.f.op("pe", lambda e, kc=kc: e.matmul(ps_ap, lhsT=actT[:, kc, t0:t0 + bs], rhs=wv[:, kc, col0:col0 + ncols],
                                                       start=(kc == 0), stop=(kc == kcs - 1)),
                      R=[wres, actres], W=[psres])

    def norm_T(self, x_ap, xres, bs, gT, outT, outres, c0):
        f = self.f
        st1, xn = self.st1, self.xn
        f.op("dve", lambda e: e.memset(st1[0:bs, 0:1], 0.0), W=["st1"])
        f.op("act", lambda e: e.activation(out=xn[0:bs, :], in_=x_ap, func=AF.Square, accum_out=st1[0:bs, 0:1]),
             R=[xres], W=["xn", "st1"])
        f.op("act", lambda e: e.activation(out=st1[0:bs, 1:2], in_=st1[0:bs, 0:1], func=AF.Sqrt, scale=1.0 / D, bias=EPS),
             R=["st1"], W=["st1"])
        f.op("dve", lambda e: e.reciprocal(out=st1[0:bs, 2:3], in_=st1[0:bs, 1:2]), R=["st1"], W=["st1"])
        f.op("dve", lambda e: e.tensor_scalar(out=xn[0:bs, :], in0=x_ap, scalar1=st1[0:bs, 2:3], scalar2=None, op0=ALU.mult),
             R=[xres, "st1"], W=["xn"])
        for half in range(2):
            pt, pres = self.pb()
            pbf = pt.bitcast(BF16)
            for k in range(8):
                kc = half * 8 + k
                f.op("pe", lambda e, k=k, kc=kc, pbf=pbf: e.transpose(out=pbf[:, k * bs:(k + 1) * bs], in_=xn[0:bs, kc * 128:(kc + 1) * 128],
                                                                      identity=self.identb[0:bs, 0:bs]),
                     R=["xn", "CBc"], W=[pres])
            f.op("dve", lambda e, half=half, pbf=pbf: e.tensor_tensor(
                out=outT[:, half * 8:half * 8 + 8, c0:c0 + bs],
                in0=pbf[:, 0:8 * bs].rearrange("p (k t) -> p k t", k=8),
                in1=bc(gT[:, half * 8:half * 8 + 8], [128, 8, bs]), op=ALU.mult),
                R=[pres, "PV"], W=[outres])

    def rsqrt_bc(self, ps2, ps2res, n, scale):
        f = self.f
        t1, r1 = self.tf()
        f.op("act", lambda e: e.activation(out=t1[:, 0:n], in_=ps2, func=AF.Sqrt, scale=scale, bias=EPS), R=[ps2res], W=[r1])
        f.op("dve", lambda e: e.reciprocal(out=t1[:, 0:n], in_=t1[:, 0:n]), R=[r1], W=[r1])
        return t1[:, 0:n], r1

    def seq_start(self, seq):
        f = self.f
        io = self.io
        f.op("pool", lambda e: e.memset(self.hst[:], 0.0), W=["hst"])
        f.op("pool", lambda e: e.memset(self.hbf[:], 0.0), W=["hbf"])
        f.op("pool", lambda e: e.memset(self.hist[:], 0.0), W=["hist"])
        self.worder = [("A", SA_MEM + i) for i in range(8)]
        self.wpos = 0
        stage = self.stage
        SR = self.STAGE
        kst = stage[:, 0:2048].rearrange("p (m c) -> p m c", m=2)
        vst = stage[:, 2048:4096].rearrange("p (m c) -> p m c", m=2)
        for mt in range(2):
            xt = self.xtok
            f.dma("pool", xt[:], io["memp"][seq, mt * 128:(mt + 1) * 128, :], W=["x0"])
            self.norm_T(xt[:], "x0", 128, self.gmemT, self.hT, "hT", 0)
            for s in range(8):
                wt, wres = self.slab("A", SA_MEM + s)
                pt, pres = self.pb()
                self.tm_mm(pt[:, 0:256], pres, wt, wres, 256, self.hT, "hT", 0, 128)
                if s < 4:
                    hm = s
                    st1 = self.st1
                    jk, jkres = self.tb()
                    f.op("dve", lambda e: e.memset(st1[:, 4:5], 0.0), W=["st1b"])
                    f.op("act", lambda e, pt=pt, jk=jk: e.activation(out=jk[:, 0:256], in_=pt[:, 0:256], func=AF.Square,
                                                                     accum_out=st1[:, 4:5]), R=[pres], W=[jkres, "st1b"])
                    f.op("act", lambda e: e.activation(out=st1[:, 5:6], in_=st1[:, 4:5], func=AF.Sqrt, scale=1.0 / 256, bias=EPS),
                         R=["st1b"], W=["st1b"])
                    f.op("dve", lambda e: e.reciprocal(out=st1[:, 6:7], in_=st1[:, 5:6]), R=["st1b"], W=["st1b"])
                    f.op("dve", lambda e, pt=pt, mt=mt, hm=hm: e.scalar_tensor_tensor(
                        out=kst[:, mt, hm * 256:(hm + 1) * 256], in0=pt[:, 0:256], scalar=st1[:, 6:7], in1=self.gkm,
                        op0=ALU.mult, op1=ALU.mult), R=[pres, "st1b", "PR"], W=SR)
                else:
                    hm = s - 4
                    f.op("act", lambda e, pt=pt, mt=mt, hm=hm: e.activation(out=vst[:, mt, hm * 256:(hm + 1) * 256], in_=pt[:, 0:256],
                                                                           func=AF.Copy), R=[pres], W=SR)
            self.worder = [("A", SA_MEM + i) for i in range(8)]
            self.wpos = 0
        f.dma("pool", io["pmk"][seq].rearrange("(m p) c -> p m c", p=128), kst, R=SR)
        f.dma("pool", io["pmv"][seq].rearrange("(m p) c -> p m c", p=128), vst, R=SR)
        f.op("pool", lambda e: e.tensor_copy(out=self.Vm[:], in_=vst), R=SR, W=["Vm"])
        kb = self.xn
        f.op("dve", lambda e: e.tensor_copy(out=kb[:], in_=stage[:, 0:2048]), R=SR, W=["xn"])
        self.kmem_T(kb, "xn", self.KTm, "KTm")

    def kmem_T(self, kb, kres, KTm, ktres):
        f = self.f
        kbv = kb[:, 0:2048].rearrange("p (m c) -> p m c", m=2)
        for mt in range(2):
            pt, pres = self.pb()
            pbf = pt.bitcast(BF16)
            for c in range(8):
                f.op("pe", lambda e, c=c, mt=mt, pbf=pbf: e.transpose(out=pbf[:, c * 128:(c + 1) * 128], in_=kbv[:, mt, c * 128:(c + 1) * 128],
                                                                      identity=self.identb), R=[kres, "CBc"], W=[pres])
            f.op("act", lambda e, mt=mt, pbf=pbf: e.activation(out=KTm[:, :, mt * 128:(mt + 1) * 128],
                                                               in_=pbf[:, 0:1024].rearrange("p (c t) -> p c t", c=8), func=AF.Copy),
                 R=[pres], W=[ktres])

    def qk_evac(self, pt, pres, n, gcol, out_ap, outres, out32=None, out32res=None):
        f = self.f
        sq, sqres = self.tb()
        f.op("act", lambda e: e.activation(out=sq[:, 0:n], in_=pt[:, 0:n], func=AF.Square), R=[pres], W=[sqres])
        p2, p2res = self.pb()
        f.op("pe", lambda e: e.matmul(p2[:, 0:n], lhsT=self.blockones, rhs=sq[:, 0:n], start=True, stop=True),
             R=[sqres, "CBc"], W=[p2res])
        rr, rres = self.rsqrt_bc(p2[:, 0:n], p2res, n, 1.0 / 64)
        f.op("dve", lambda e: e.scalar_tensor_tensor(out=out_ap, in0=pt[:, 0:n], scalar=gcol, in1=rr, op0=ALU.mult, op1=ALU.mult),
             R=[pres, rres, "PV", "SM"], W=[outres])
        if out32 is not None:
            f.op("dve", lambda e: e.scalar_tensor_tensor(out=out32, in0=pt[:, 0:n], scalar=gcol, in1=rr, op0=ALU.mult, op1=ALU.mult),
                 R=[pres, rres, "PV", "SM"], W=[out32res])

    def in_proj_fm(self, hT, ntok, qT, kT_out, xbc_out, qmT, last3=None, k32=None, parts=("q", "k", "xbc", "qm")):
        f = self.f
        pend = []

        def flush(keep=0):
            while len(pend) > keep:
                pend.pop(0)()
        if "q" in parts:
            for s in range(4):
                wt, wres = self.slab("A", SA_Q + s)
                for mm in range(2):
                    c = s * 2 + mm
                    pt, pres = self.pb()
                    self.fm_mm(pt[:, 0:ntok], pres, wt, wres, mm * 128, hT, "hT", ntok)
                    flush(0)
                    pend.append(lambda pt=pt, pres=pres, c=c: self.qk_evac(pt, pres, ntok, self.gq8, qT[:, c, 0:ntok], "qT"))
                yield
        if "k" in parts:
            wt, wres = self.slab("A", SA_K)
            for pair in range(2):
                pt, pres = self.pb()
                self.fm_mm(pt[:, 0:ntok], pres, wt, wres, pair * 128, hT, "hT", ntok)
                flush(0)
                if k32 is not None:
                    pend.append(lambda pt=pt, pres=pres, pair=pair: self.qk_evac(pt, pres, ntok, self.gk, kT_out(pair), "kT", out32=k32(pair), out32res="lastst"))
                else:
                    pend.append(lambda pt=pt, pres=pres, pair=pair: self.qk_evac(pt, pres, ntok, self.gk, kT_out(pair), "kT"))
            yield
        if "xbc" in parts:
            for s in range(12):
                wt, wres = self.slab("A", SA_XBC + s)
                if getattr(self, "xbc_hook", None) is not None:
                    self.xbc_hook(s, wt, wres)
                for mm in range(2):
                    c = s * 2 + mm
                    pt, pres = self.pb()
                    self.fm_mm(pt[:, 0:ntok], pres, wt, wres, mm * 128, hT, "hT", ntok)
                    flush(0)
                    f.op("act", lambda e, pt=pt, c=c: e.activation(out=xbc_out(c), in_=pt[:, 0:ntok], func=AF.Copy), R=[pres], W=[("xbcT", c)])
                    if last3 is not None:
                        f.op("dve", lambda e, pt=pt, c=c: e.tensor_copy(out=last3[:, c, :], in_=pt[:, ntok - 4:ntok]), R=[pres], W=["lastst"])
                yield
        flush(0)
        if "qm" in parts:
            for hm in range(4):
                wt, wres = self.slab("A", SA_QM + hm)
                pa, ares = self.pb()
                pbk, bres = self.pb()
                self.fm_mm(pa[:, 0:ntok], ares, wt, wres, 0, hT, "hT", ntok)
                self.fm_mm(pbk[:, 0:ntok], bres, wt, wres, 128, hT, "hT", ntok)
                flush(0)

                def qm_evac(pa=pa, ares=ares, pbk=pbk, bres=bres, hm=hm):
                    sq, sqres = self.tb()
                    f.op("act", lambda e: e.activation(out=sq[:, 0:ntok], in_=pa[:, 0:ntok], func=AF.Square), R=[ares], W=[sqres])
                    f.op("act", lambda e: e.activation(out=sq[:, 256:256 + ntok], in_=pbk[:, 0:ntok], func=AF.Square), R=[bres], W=[sqres])
                    p2, p2res = self.pb()
                    f.op("pe", lambda e: e.matmul(p2[:, 0:ntok], lhsT=self.onesb, rhs=sq[:, 0:ntok], start=True, stop=False),
                         R=[sqres, "CBc"], W=[p2res])
                    f.op("pe", lambda e: e.matmul(p2[:, 0:ntok], lhsT=self.onesb, rhs=sq[:, 256:256 + ntok], start=False, stop=True),
                         R=[sqres, "CBc"], W=[p2res])
                    rr, rres = self.rsqrt_bc(p2[:, 0:ntok], p2res, ntok, 1.0 / 256)
                    for dc, (pp, ppres) in enumerate(((pa, ares), (pbk, bres))):
                        f.op("dve", lambda e, pp=pp, dc=dc: e.scalar_tensor_tensor(
                            out=qmT[:, hm * 2 + dc, 0:ntok], in0=pp[:, 0:ntok], scalar=self.gqm[:, dc:dc + 1], in1=rr,
                            op0=ALU.mult, op1=ALU.mult), R=[ppres, rres, "PV"], W=["qmT"])
                pend.append(qm_evac)
                yield
        flush(0)

    def swa_heads(self, q_rhs, kprev, kcur, vprev, vcur, nq, kcn, Dmp, Dmc, out, R_in, W_out):
        f = self.f
        n4 = 4 * nq
        for h in range(4):
            PTs = []
            for which in (0, 1):
                if which == 0 and kprev is None:
                    continue
                kk = kprev(h) if which == 0 else kcur(h)
                kn = 128 if which == 0 else kcn
                Dm = Dmp if which == 0 else Dmc
                pt, pres = self.pb()
                f.op("pe", lambda e, pt=pt, kk=kk, kn=kn, h=h: e.matmul(pt[0:kn, 0:n4], lhsT=kk, rhs=q_rhs(h), start=True, stop=True),
                     R=R_in, W=[pres])
                tt, tres = self.tf()
                for g in range(4):
                    sl = -SLOPES[h * 4 + g]
                    f.op("dve", lambda e, pt=pt, tt=tt, g=g, sl=sl, kn=kn, Dm=Dm: e.scalar_tensor_tensor(
                        out=tt[0:kn, g * nq:(g + 1) * nq], in0=Dm, scalar=sl, in1=pt[0:kn, g * nq:(g + 1) * nq],
                        op0=ALU.mult, op1=ALU.add), R=[pres, "C", "PV"], W=[tres])
                PT, ptres = self.tb()
                f.op("act", lambda e, PT=PT, tt=tt, kn=kn: e.activation(out=PT[0:kn, 0:n4], in_=tt[0:kn, 0:n4], func=AF.Exp),
                     R=[tres], W=[ptres])
                PTs.append((PT, ptres, kn, which))
            yield
            po, pores = self.pb()
            pd, pdres = self.pb()
            nP = len(PTs)
            for i, (PT, ptres, kn, which) in enumerate(PTs):
                vv = vprev(h) if which == 0 else vcur(h)
                f.op("pe", lambda e, PT=PT, kn=kn, vv=vv, i=i, po=po: e.matmul(po[0:64, 0:n4], lhsT=vv, rhs=PT[0:kn, 0:n4],
                                                                              start=(i == 0), stop=(i == len(PTs) - 1)),
                     R=R_in + [ptres], W=[pores])
            for i, (PT, ptres, kn, which) in enumerate(PTs):
                f.op("pe", lambda e, PT=PT, kn=kn, i=i, pd=pd: e.matmul(pd[0:64, 0:n4], lhsT=self.onesb[0:kn, 0:64], rhs=PT[0:kn, 0:n4],
                                                                       start=(i == 0), stop=(i == nP - 1)),
                     R=["CBc", ptres], W=[pdres])
            dn, dnres = self.tf()
            f.op("dve", lambda e, h=h, dn=dn, pd=pd: e.tensor_tensor(
                out=dn[0:64, 0:n4].rearrange("p (g q) -> p g q", g=4), in0=pd[0:64, 0:n4].rearrange("p (g q) -> p g q", g=4),
                in1=bc(self.esink[0:64, h * 4:h * 4 + 4], [64, 4, nq]), op=ALU.add), R=[pdres, "SM"], W=[dnres])
            f.op("dve", lambda e, dn=dn: e.reciprocal(out=dn[0:64, 0:n4], in_=dn[0:64, 0:n4]), R=[dnres], W=[dnres])
            f.op("dve", lambda e, h=h, dn=dn, po=po: e.tensor_tensor(
                out=out(h), in0=po[0:64, 0:n4].rearrange("p (g q) -> p g q", g=4),
                in1=dn[0:64, 0:n4].rearrange("p (g q) -> p g q", g=4), op=ALU.mult), R=[pores, dnres], W=W_out)
            yield

    def mem_heads(self, q_rhs, KTm, ktres, Vm, vres, nq, out, R_in, W_out):
        f = self.f
        for hm in range(4):
            pt, pres = self.pb()
            for mt in range(2):
                for dc in range(2):
                    f.op("pe", lambda e, mt=mt, dc=dc, hm=hm, pt=pt: e.matmul(
                        pt[:, mt * nq:(mt + 1) * nq], lhsT=KTm[:, hm * 2 + dc, mt * 128:(mt + 1) * 128], rhs=q_rhs(hm * 2 + dc),
                        start=(dc == 0), stop=(dc == 1)), R=R_in + [ktres], W=[pres])
            PT, ptres = self.tb()
            f.op("act", lambda e, PT=PT, pt=pt: e.activation(out=PT[:, 0:2 * nq], in_=pt[:, 0:2 * nq], func=AF.Exp, scale=1.0 / 16),
                 R=[pres], W=[ptres])
            yield
            pd, pdres = self.pb()
            for mt in range(2):
                f.op("pe", lambda e, mt=mt, PT=PT, pd=pd: e.matmul(pd[:, 0:nq], lhsT=self.onesb, rhs=PT[:, mt * nq:(mt + 1) * nq],
                                                                   start=(mt == 0), stop=(mt == 1)), R=["CBc", ptres], W=[pdres])
            dn, dnres = self.tf()
            f.op("dve", lambda e, dn=dn, pd=pd: e.reciprocal(out=dn[:, 0:nq], in_=pd[:, 0:nq]), R=[pdres], W=[dnres])
            po, pores = self.pb()
            for dc in range(2):
                for mt in range(2):
                    f.op("pe", lambda e, mt=mt, dc=dc, hm=hm, PT=PT, po=po: e.matmul(
                        po[:, dc * nq:(dc + 1) * nq], lhsT=Vm[:, mt, hm * 256 + dc * 128: hm * 256 + (dc + 1) * 128],
                        rhs=PT[:, mt * nq:(mt + 1) * nq], start=(mt == 0), stop=(mt == 1)), R=[vres, ptres], W=[pores])
            for dc in range(2):
                f.op("dve", lambda e, dc=dc, hm=hm, dn=dn, po=po: e.tensor_tensor(out=out(hm * 2 + dc), in0=po[:, dc * nq:(dc + 1) * nq],
                                                                                 in1=dn[:, 0:nq], op=ALU.mult), R=[pores, dnres], W=W_out)
            yield

    def conv_fm(self, tap, ntok, out, res_of, save_hist=None):
        f = self.f
        for c in range(24):
            acc, ares = self.tf()
            f.op("dve", lambda e, c=c, acc=acc: e.tensor_scalar(out=acc[:, 0:ntok], in0=tap(c, 0), scalar1=self.cwT[:, c * 4:c * 4 + 1],
                                                                scalar2=self.cbT[:, c:c + 1], op0=ALU.mult, op1=ALU.add),
                 R=[res_of(c), "PV", "xbch"], W=[ares])
            for j in range(1, 4):
                f.op("dve", lambda e, c=c, j=j, acc=acc: e.scalar_tensor_tensor(
                    out=acc[:, 0:ntok], in0=tap(c, j), scalar=self.cwT[:, c * 4 + j:c * 4 + j + 1], in1=acc[:, 0:ntok],
                    op0=ALU.mult, op1=ALU.add), R=[res_of(c), "PV", ares, "xbch"], W=[ares])
            if save_hist is not None:
                save_hist(c)
            f.op("act", lambda e, c=c, acc=acc: e.activation(out=out(c), in_=acc[:, 0:ntok], func=AF.Silu), R=[ares], W=[res_of(c)])
            if c % 2 == 1:
                yield

    def ssd_block(self):
        f = self.f
        S = self.ssd_s
        cv = lambda c: self.xbcT[:, c, 3:3 + 128]
        XS = [("xbcT", c) for c in range(16)]
        BMr = [("xbcT", 16 + g) for g in range(4)]
        CMr = [("xbcT", 20 + g) for g in range(4)]
        for half in range(2):
            pt, pres = self.pb()
            pbf = pt.bitcast(BF16)
            for k in range(8):
                f.op("pe", lambda e, k=k, half=half, pbf=pbf: e.transpose(out=pbf[:, k * 128:(k + 1) * 128], in_=cv(half * 8 + k),
                                                                          identity=self.identb), R=[("xbcT", half * 8 + k), "CBc"], W=[pres])
            f.op("act", lambda e, half=half, pbf=pbf: e.activation(out=self.xs_tok[:, half * 1024:(half + 1) * 1024], in_=pbf[:, 0:1024],
                                                                   func=AF.Copy), R=[pres], W=["xs_tok"])
        pt, pres = self.pb()
        pbf = pt.bitcast(BF16)
        for g in range(4):
            f.op("pe", lambda e, g=g, pbf=pbf: e.transpose(out=pbf[:, g * 128:(g + 1) * 128], in_=cv(16 + g), identity=self.identb),
                 R=[BMr[g], "CBc"], W=[pres])
        f.op("act", lambda e, pbf=pbf: e.activation(out=self.bm_tok[:], in_=pbf[:, 0:512], func=AF.Copy), R=[pres], W=["bm_tok"])
        yield
        dt = S[:, 32:64]
        f.op("dve", lambda e: e.tensor_tensor(out=S[:, 0:32], in0=self.dtraw[:], in1=self.dtb, op=ALU.add), R=["dtraw", "PR"], W=["S"])
        f.op("act", lambda e: e.activation(out=S[:, 0:32], in_=S[:, 0:32], func=AF.Exp), R=["S"], W=["S"])
        f.op("act", lambda e: e.activation(out=dt, in_=S[:, 0:32], func=AF.Ln, bias=1.0), R=["S"], W=["S"])
        f.op("dve", lambda e: e.tensor_tensor(out=S[:, 64:96], in0=dt, in1=self.a_bc, op=ALU.mult), R=["S", "SM"], W=["S"])
        f.op("dve", lambda e: e.tensor_copy(out=self.dA4[:], in_=S[:, 64:96].unsqueeze(1).broadcast_to([128, 4, 32])), R=["S"], W=["dA4"])
        yield
        pa, pares = self.pb()
        for i, lh in enumerate((self.Tm, self.Um, self.onesf)):
            f.op("pe", lambda e, i=i, lh=lh: e.matmul(pa[:, i * 32:(i + 1) * 32], lhsT=lh, rhs=S[:, 64:96], start=True, stop=True),
                 R=["S", "C"], W=[pares])
        f.op("pe", lambda e: e.matmul(pa[:, 128:256], lhsT=self.dA4[:].rearrange("p a b -> p (a b)"), rhs=self.Tm, start=True, stop=True),
             R=["dA4", "C"], W=[pares])
        f.op("act", lambda e: e.activation(out=S[:, 96:192], in_=pa[:, 0:96], func=AF.Exp), R=[pares], W=["S"])
        eacs, dte, cd = S[:, 96:128], S[:, 128:160], S[:, 160:192]
        f.op("dve", lambda e: e.tensor_tensor(out=S[:, 192:224], in0=dt, in1=dte, op=ALU.mult), R=["S"], W=["S"])
        xs3 = self.xs_tok.rearrange("p (h d) -> p h d", h=32)
        f.op("dve", lambda e: e.tensor_tensor(out=self.xdt.rearrange("p (h d) -> p h d", h=32), in0=xs3, in1=bc(dt, [128, 32, 64]),
                                              op=ALU.mult), R=["xs_tok", "S"], W=["xdt"])
        f.op("dve", lambda e: e.tensor_tensor(out=self.xw.rearrange("p (h d) -> p h d", h=32), in0=xs3,
                                              in1=bc(S[:, 192:224], [128, 32, 64]), op=ALU.mult), R=["xs_tok", "S"], W=["xw"])
        HI, LO = self.HI, self.LO
        f.op("act", lambda e: e.activation(out=HI[:], in_=pa[:, 128:256], func=AF.Copy), R=[pares], W=["HI"])
        f.op("dve", lambda e: e.tensor_tensor(out=LO[:], in0=pa[:, 128:256], in1=HI[:], op=ALU.subtract), R=[pares, "HI"], W=["LO"])
        f.op("act", lambda e: e.mul(out=self.lhsD[0:32, :], in_=HI[0:32, :], mul=-1.0), R=["HI"], W=["lhsD"])
        f.op("act", lambda e: e.mul(out=self.lhsD[32:64, :], in_=LO[32:64, :], mul=-1.0), R=["LO"], W=["lhsD"])
        f.op("dve", lambda e: e.tensor_tensor(out=self.rhsD[64:96], in0=HI[64:96, :].unsqueeze(1).broadcast_to([32, 32, 128]),
                                              in1=bc(self.ohs[64:96], [32, 32, 128]), op=ALU.mult), R=["HI", "C"], W=["rhsD"])
        f.op("dve", lambda e: e.tensor_tensor(out=self.rhsD[96:128], in0=LO[96:128, :].unsqueeze(1).broadcast_to([32, 32, 128]),
                                              in1=bc(self.ohs[96:128], [32, 32, 128]), op=ALU.mult), R=["LO", "C"], W=["rhsD"])
        yield
        pc, pcres = self.pb()
        for g in range(4):
            f.op("pe", lambda e, g=g: e.matmul(pc[:, g * 128:(g + 1) * 128], lhsT=cv(16 + g), rhs=cv(20 + g), start=True, stop=True),
                 R=[BMr[g], CMr[g]], W=[pcres])
        f.op("act", lambda e: e.activation(out=self.CBt[:].rearrange("p g l -> p (g l)"), in_=pc[:, 0:512], func=AF.Copy), R=[pcres], W=["CBt"])
        f.op("dve", lambda e: e.memset(S[:, 224:228], 0.0), W=["Sq"])
        yield
        for g in range(4):
            pyd, pydres = self.pb()
            pyo, pyores = self.pb()
            f.op("pe", lambda e, g=g, pyo=pyo: e.matmul(pyo[:, 0:512], lhsT=cv(20 + g), rhs=self.hbf[:, g * 512:(g + 1) * 512], start=True, stop=True),
                 R=[CMr[g], "hbf"], W=[pyores])
            for jj in range(2):
                j = g * 2 + jj
                pD, pDres = self.pb()
                f.op("pe", lambda e, j=j, pD=pD: e.matmul(pD[:, 0:512], lhsT=self.lhsD[:], rhs=self.rhsD[:, 4 * j:4 * j + 4, :].rearrange("p a b -> p (a b)"),
                                                          start=True, stop=False), R=["lhsD", "rhsD"], W=[pDres])
                f.op("pe", lambda e, pD=pD: e.matmul(pD[:, 0:512], lhsT=self.identb, rhs=self.maskD, start=False, stop=True),
                     R=["CBc"], W=[pDres])
                E, eres = self.tf()
                f.op("act", lambda e, E=E, pD=pD: e.activation(out=E[:, 0:512], in_=pD[:, 0:512], func=AF.Exp), R=[pDres], W=[eres])
                Wp = self.Wp[jj]
                f.op("dve", lambda e, E=E, Wp=Wp, g=g: e.tensor_tensor(
                    out=Wp[:], in0=E[:, 0:512].rearrange("p (a l) -> p a l", a=4),
                    in1=self.CBt[:, g, :].unsqueeze(1).broadcast_to([128, 4, 128]), op=ALU.mult), R=[eres, "CBt"], W=[("Wp", jj)])
            yield
            for jj in range(2):
                j = g * 2 + jj
                Wp = self.Wp[jj]
                for h4 in range(4):
                    h = 4 * j + h4
                    f.op("pe", lambda e, Wp=Wp, h4=h4, h=h, pyd=pyd: e.matmul(pyd[:, (h % 8) * 64:(h % 8 + 1) * 64], lhsT=Wp[:, h4, :],
                                                                             rhs=self.xdt[:, h * 64:(h + 1) * 64], start=True, stop=True),
                         R=[("Wp", jj), "xdt"], W=[pydres])
            y1, y1res = self.tf()
            f.op("dve", lambda e, g=g, y1=y1, pyo=pyo: e.tensor_tensor(
                out=y1[:, 0:512].rearrange("p (h d) -> p h d", h=8), in0=pyo[:, 0:512].rearrange("p (h d) -> p h d", h=8),
                in1=bc(eacs[:, g * 8:(g + 1) * 8], [128, 8, 64]), op=ALU.mult), R=[pyores, "S"], W=[y1res])
            f.op("dve", lambda e, y1=y1, pyd=pyd: e.tensor_tensor(out=y1[:, 0:512], in0=y1[:, 0:512], in1=pyd[:, 0:512], op=ALU.add),
                 R=[pydres, y1res], W=[y1res])
            y3, y3res = self.tf()
            f.op("dve", lambda e, g=g, y3=y3: e.tensor_tensor(
                out=y3[:, 0:512].rearrange("p (h d) -> p h d", h=8), in0=self.xs_tok[:, g * 512:(g + 1) * 512].rearrange("p (h d) -> p h d", h=8),
                in1=bc(self.dsk[:, g * 8:(g + 1) * 8], [128, 8, 64]), op=ALU.mult), R=["xs_tok", "PR"], W=[y3res])
            f.op("dve", lambda e, y1=y1, y3=y3: e.tensor_tensor(out=y1[:, 0:512], in0=y1[:, 0:512], in1=y3[:, 0:512], op=ALU.add),
                 R=[y1res, y3res], W=[y1res])
            f.op("dve", lambda e, g=g, y1=y1: e.tensor_tensor(out=y1[:, 0:512], in0=y1[:, 0:512],
                                                              in1=self.zs[:, g * 512:(g + 1) * 512], op=ALU.mult),
                 R=[y1res, "zs"], W=[y1res])
            f.op("act", lambda e, g=g, y1=y1, y3=y3: e.activation(out=y3[:, 0:512], in_=y1[:, 0:512], func=AF.Square,
                                                                  accum_out=S[:, 224 + g:225 + g]), R=[y1res], W=[y3res, "Sq"])
            f.op("act", lambda e, g=g: e.activation(out=S[:, 228 + g:229 + g], in_=S[:, 224 + g:225 + g], func=AF.Sqrt, scale=1.0 / 512, bias=EPS),
                 R=["Sq"], W=["Sq"])
            f.op("dve", lambda e, g=g: e.reciprocal(out=S[:, 232 + g:233 + g], in_=S[:, 228 + g:229 + g]), R=["Sq"], W=["Sq"])
            pst, pstres = self.pb()
            f.op("pe", lambda e, g=g, pst=pst: e.matmul(pst[:, 0:512], lhsT=self.bm_tok[:, g * 128:(g + 1) * 128],
                                                        rhs=self.xw[:, g * 512:(g + 1) * 512], start=True, stop=True),
                 R=["bm_tok", "xw"], W=[pstres])
            f.op("dve", lambda e, g=g, y1=y1: e.tensor_scalar(out=self.s_tok[:, g * 512:(g + 1) * 512], in0=y1[:, 0:512],
                                                              scalar1=S[:, 232 + g:233 + g], scalar2=None, op0=ALU.mult), R=[y1res, "Sq"], W=["s_tok"])
            hg = self.hst[:, g * 512:(g + 1) * 512]
            f.op("dve", lambda e, g=g, hg=hg: e.tensor_tensor(out=hg.rearrange("p (h d) -> p h d", h=8), in0=hg.rearrange("p (h d) -> p h d", h=8),
                                                              in1=bc(cd[:, g * 8:(g + 1) * 8], [128, 8, 64]), op=ALU.mult),
                 R=["S", "hst"], W=["hst"])
            f.op("dve", lambda e, hg=hg, pst=pst: e.tensor_tensor(out=hg, in0=hg, in1=pst[:, 0:512], op=ALU.add), R=[pstres, "hst"], W=["hst"])
            f.op("pool", lambda e, g=g, hg=hg: e.tensor_copy(out=self.hbf[:, g * 512:(g + 1) * 512], in_=hg), R=["hst"], W=["hbf"])
            yield
        for half in range(2):
            yield
            pt, pres = self.pb()
            pbf = pt.bitcast(BF16)
            for k in range(8):
                kc = half * 8 + k
                f.op("pe", lambda e, k=k, kc=kc, pbf=pbf: e.transpose(out=pbf[:, k * 128:(k + 1) * 128], in_=self.s_tok[:, kc * 128:(kc + 1) * 128],
                                                                      identity=self.identb), R=["s_tok", "CBc"], W=[pres])
            f.op("dve", lambda e, half=half, pbf=pbf: e.tensor_tensor(
                out=self.sT[:, half * 8:half * 8 + 8, 0:128], in0=pbf[:, 0:1024].rearrange("p (k t) -> p k t", k=8),
                in1=bc(self.ssdnT[:, half * 8:half * 8 + 8], [128, 8, 128]), op=ALU.mult), R=[pres, "PV"], W=["sT"])

    def merge_out(self, ntok, bs, xres, x_ap, aT, sT, mT, hT, mergedT, mres_of, h2T):
        f = self.f
        actT = self.actT
        for s in range(8):
            for bi in range(3):
                gk = ("A", (SA_G0, SA_G1, SA_G2)[bi] + s)
                uk = (("S", s), ("A", SA_USSD + s), ("M", s))[bi]
                (gw, gwr), (uw, uwr) = self.slabs([gk, uk])
                for mm in range(2):
                    m = s * 2 + mm
                    col0 = mm * 128
                    acc = self.macc[mm]
                    accres = ("macc", mm)
                    pg, pgres = self.pb()
                    self.fm_mm(pg[:, 0:ntok], pgres, gw, gwr, col0, hT, "hT", ntok)
                    sg, sgres = self.tf()
                    f.op("act", lambda e, sg=sg, pg=pg: e.activation(out=sg[:, 0:ntok], in_=pg[:, 0:ntok], func=AF.Sigmoid), R=[pgres], W=[sgres])
                    pu, pures = self.pb()
                    if bi == 0:
                        uav = uw[0:64, 0:4096].rearrange("p (h c) -> p h c", h=16)
                        for hd in range(16):
                            f.op("pe", lambda e, hd=hd, pu=pu, uav=uav, col0=col0: e.matmul(pu[:, 0:ntok], lhsT=uav[:, hd, col0:col0 + 128], rhs=aT[0:64, hd, 0:ntok],
                                                                                         start=(hd == 0), stop=(hd == 15)), R=[uwr, "aT"], W=[pures])
                        f.op("dve", lambda e, acc=acc, sg=sg, pu=pu: e.tensor_tensor(out=acc[:, 0:ntok], in0=sg[:, 0:ntok], in1=pu[:, 0:ntok], op=ALU.mult),
                             R=[sgres, pures], W=[accres])
                    else:
                        if bi == 1:
                            self.fm_mm(pu[:, 0:ntok], pures, uw, uwr, col0, sT, "sT", ntok)
                        else:
                            self.fm_mm(pu[:, 0:ntok], pures, uw, uwr, col0, mT, "mT", ntok, kcs=8)
                        f.op("dve", lambda e, sg=sg, pu=pu: e.tensor_tensor(out=sg[:, 0:ntok], in0=sg[:, 0:ntok], in1=pu[:, 0:ntok], op=ALU.mult),
                             R=[sgres, pures], W=[sgres])
                        if bi == 1:
                            f.op("dve", lambda e, acc=acc, sg=sg: e.tensor_tensor(out=acc[:, 0:ntok], in0=acc[:, 0:ntok], in1=sg[:, 0:ntok], op=ALU.add),
                                 R=[sgres, accres], W=[accres])
                        else:
                            f.op("dve", lambda e, acc=acc, sg=sg, m=m: e.tensor_tensor(out=mergedT(m), in0=acc[:, 0:ntok], in1=sg[:, 0:ntok], op=ALU.add),
                                 R=[sgres, accres], W=[mres_of(m)])
        MR = [mres_of(m) for m in range(16)]
        for s in range(8):
            wt, wres = self.slab("A", SA_OUT + s)
            wv = wt[:, 0:4096].rearrange("p (k c) -> p k c", k=16)
            pt, pres = self.pb()
            for kc in range(16):
                f.op("pe", lambda e, kc=kc, pt=pt, wv=wv: e.matmul(pt[0:bs, 0:256], lhsT=mergedT(kc)[:, 0:bs], rhs=wv[:, kc, :],
                                                                   start=(kc == 0), stop=(kc == 15)), R=[wres, mres_of(kc)], W=[pres])
            xa = x_ap[:, s * 256:(s + 1) * 256]
            f.op("dve", lambda e, xa=xa, pt=pt: e.tensor_tensor(out=xa, in0=xa, in1=pt[0:bs, 0:256], op=ALU.add), R=[pres, xres], W=[xres])
        self.norm_T(x_ap, xres, bs, self.gffnT, h2T, "h2T", 0)

    def ffn_gen(self, ntok, bs, xres, x_ap, hT):
        f = self.f
        actT = self.actT
        for s in range(22):
            (gw, gwr), (uw, uwr) = self.slabs([("A", SA_GATE + s), ("A", SA_UP + s)])
            for mm in range(2):
                j = s * 2 + mm
                pg, pgres = self.pb()
                pu, pures = self.pb()
                self.fm_mm(pg[:, 0:ntok], pgres, gw, gwr, mm * 128, hT, "h2T", ntok)
                self.fm_mm(pu[:, 0:ntok], pures, uw, uwr, mm * 128, hT, "h2T", ntok)
                sg, sgres = self.tf()
                f.op("act", lambda e, sg=sg, pg=pg: e.activation(out=sg[:, 0:ntok], in_=pg[:, 0:ntok], func=AF.Silu), R=[pgres], W=[sgres])
                f.op("dve", lambda e, sg=sg, pu=pu, j=j: e.tensor_tensor(out=actT[:, j, 0:ntok], in0=sg[:, 0:ntok], in1=pu[:, 0:ntok], op=ALU.mult),
                     R=[sgres, pures], W=["actT"])
            yield
        for cg in range(8):
            pt, pres = self.pb()
            for qtr in range(4):
                wt, wres = self.slab("D", cg * 4 + qtr)
                wv = wt[:, 0:2816].rearrange("p (k c) -> p k c", k=11)
                for kc in range(11):
                    f.op("pe", lambda e, kc=kc, qtr=qtr, pt=pt, wv=wv: e.matmul(
                        pt[0:bs, 0:256], lhsT=actT[:, qtr * 11 + kc, 0:bs], rhs=wv[:, kc, :],
                        start=(qtr == 0 and kc == 0), stop=(qtr == 3 and kc == 10)), R=[wres, "actT"], W=[pres])
            xa = x_ap[:, cg * 256:(cg + 1) * 256]
            f.op("dve", lambda e, xa=xa, pt=pt: e.tensor_tensor(out=xa, in0=xa, in1=pt[0:bs, 0:256], op=ALU.add), R=[pres, xres], W=[xres])
            yield

    def order_in(self):
        o = [("A", SA_Q + i) for i in range(4)] + [("A", SA_K)] + [("A", SA_XBC + i) for i in range(12)]
        o += [("A", SA_QM + i) for i in range(4)] + [("A", SA_V)] + [("A", SA_Z + i) for i in range(8)] + [("T", 0)]
        return o

    def order_merge(self):
        o = []
        for s in range(8):
            o += [("A", SA_G0 + s), ("S", s), ("A", SA_G1 + s), ("A", SA_USSD + s), ("A", SA_G2 + s), ("M", s)]
        o += [("A", SA_OUT + i) for i in range(8)]
        return o

    def order_ffn(self):
        o = []
        for s in range(22):
            o += [("A", SA_GATE + s), ("A", SA_UP + s)]
        o += [("D", i) for i in range(32)]
        return o

    def phase_in(self, seq, st, par, last):
        f = self.f
        io = self.io
        t0 = st * NT
        xt = self.xtoks[par]
        xres = "x%d" % par
        f.dma("pool", xt[:], io["xp"][seq, t0:t0 + 128, :], W=[xres])
        self.norm_T(xt[:], xres, 128, self.gmixT, self.hT, "hT", 0)
        last3 = None
        k32 = None
        L = self.lastst
        if last:
            last3 = self.l3t[:, 0:96].rearrange("p (c j) -> p c j", c=24)
            k32t = L[:, 96:96 + 256].rearrange("p (a t) -> p a t", a=2)
            k32 = lambda pair: k32t[:, pair, :]
        f.op("pool", lambda e: e.tensor_copy(out=self.xbcT[:, :, 0:3], in_=self.hist[:]), R=["hist"], W=["xbch"])
        self.drain(self.in_proj_fm(self.hT, NT, self.qT, lambda pair: self.kT[:, pair, 128:256],
                                   lambda c: self.xbcT[:, c, 3:3 + NT], self.qmT, last3=last3, k32=k32, parts=("q", "k", "xbc")))
        self.interleave(self.phase_in_b(last), self.conv_gen(), 1, 1, None, None)

    def phase_in_b(self, last):
        f = self.f
        L = self.lastst
        yield from self.in_proj_fm(self.hT, NT, self.qT, None, None, self.qmT, parts=("qm",))
        wt, wres = self.slab("A", SA_V)
        pt, pres = self.pb()
        self.tm_mm(pt[:, 0:256], pres, wt, wres, 256, self.hT, "hT", 0, 128)
        f.op("act", lambda e, pt=pt: e.activation(out=self.vtok[:, 1, :], in_=pt[:, 0:256], func=AF.Copy), R=[pres], W=["vtok"])
        if last:
            f.op("dve", lambda e, pt=pt: e.tensor_copy(out=L[:, 352:608], in_=pt[:, 0:256]), R=[pres], W=["lastst"])
        yield
        for s in range(8):
            wt, wres = self.slab("A", SA_Z + s)
            pt, pres = self.pb()
            self.tm_mm(pt[:, 0:256], pres, wt, wres, 256, self.hT, "hT", 0, 128)
            f.op("act", lambda e, pt=pt, s=s: e.activation(out=self.zs[:, s * 256:(s + 1) * 256], in_=pt[:, 0:256], func=AF.Silu),
                 R=[pres], W=["zs"])
            yield
        wt, wres = self.slab("T", 0)
        pt, pres = self.pb()
        self.tm_mm(pt[:, 0:32], pres, wt, wres, 32, self.hT, "hT", 0, 128, cw=32)
        f.op("act", lambda e, pt=pt: e.activation(out=self.dtraw[:], in_=pt[:, 0:32], func=AF.Copy), R=[pres], W=["dtraw"])

    def conv_gen(self):
        f = self.f

        def save_hist(c):
            f.op("pool", lambda e, c=c: e.tensor_copy(out=self.hist[:, c, :], in_=self.xbcT[:, c, NT:NT + 3]), R=[("xbcT", c)], W=["hist"])
        yield from self.conv_fm(lambda c, j: self.xbcT[:, c, j:j + NT], NT, lambda c: self.xbcT[:, c, 3:3 + NT], lambda c: ("xbcT", c), save_hist=save_hist)

    def phase_mix(self, first):
        f = self.f

        has_prev = not first
        yield from self.swa_heads(
            q_rhs=lambda h: self.qT[64 * (h % 2):64 * (h % 2) + 64, (h // 2) * 4:(h // 2) * 4 + 4, 0:128],
            kprev=(lambda h: self.kT[64 * (h % 2):64 * (h % 2) + 64, h // 2, 0:128]) if has_prev else None,
            kcur=lambda h: self.kT[64 * (h % 2):64 * (h % 2) + 64, h // 2, 128:256],
            vprev=lambda h: self.vtok[:, 0, h * 64:(h + 1) * 64],
            vcur=lambda h: self.vtok[:, 1, h * 64:(h + 1) * 64],
            nq=128, kcn=128, Dmp=self.Dmp, Dmc=self.Dmd,
            out=lambda h: self.aT[0:64, h * 4:h * 4 + 4, 0:128],
            R_in=["qT", "kT", "vtok"], W_out=["aT"])
        f.op("pool", lambda e: e.tensor_copy(out=self.kT[:, :, 0:128], in_=self.kT[:, :, 128:256]), R=["kT"], W=["kT"])
        f.op("pool", lambda e: e.tensor_copy(out=self.vtok[:, 0, :], in_=self.vtok[:, 1, :]), R=["vtok"], W=["vtok"])
        yield from self.ssd_block()
        yield from self.mem_heads(q_rhs=lambda c: self.qmT[:, c, 0:NT], KTm=self.KTm, ktres="KTm", Vm=self.Vm, vres="Vm", nq=NT,
                                  out=lambda c: self.mT[:, c, 0:NT], R_in=["qmT"], W_out=["mT"])

    def phase_merge(self, par):
        self.merge_out(NT, 128, "x%d" % par, self.xtoks[par][:], self.aT, self.sT, self.mT, self.hT,
                       lambda m: self.xbcT[:, m, 3:3 + NT], lambda m: ("xbcT", m), self.h2T)

    @staticmethod
    def drain(gen):
        for _ in gen:
            pass

    def interleave(self, ga, gb, na=1, nb=1, ca="F", cb="M"):
        da = db = False
        while not (da and db):
            for _ in range(na):
                if not da:
                    self.ctx = ca
                    try:
                        next(ga)
                    except StopIteration:
                        da = True
            for _ in range(nb):
                if not db:
                    self.ctx = cb
                    try:
                        next(gb)
                    except StopIteration:
                        db = True
        self.ctx = None

    def run_sequence(self, seq, n_st):
        f = self.f
        io = self.io
        nfull = SEQ // NT
        is_last = lambda st: (st == nfull - 1) or (self.force_last and st == n_st - 1)
        self.worder = self.order_in() + self.order_merge()
        self.wpos = 0
        self.phase_in(seq, 0, 0, is_last(0))
        self.drain(self.phase_mix(True))
        for st in range(n_st):
            par = st % 2
            nxt = st + 1 < n_st
            self.worder = self.order_merge() + (self.order_in() if nxt else []) + self.order_ffn() + self.order_merge()
            self.wpos = 0
            self.phase_merge(par)
            ffn = self.ffn_gen(NT, 128, "x%d" % par, self.xtoks[par][:], self.h2T)
            if nxt:
                self.phase_in(seq, st + 1, 1 - par, is_last(st + 1))
                if self.pipeline:
                    self.interleave(ffn, self.phase_mix(False), self.il[0], self.il[1])
                else:
                    self.drain(ffn)
                    self.drain(self.phase_mix(False))
            else:
                self.drain(ffn)
            f.dma("pool", io["yp"][seq, st * NT:st * NT + 128, :], self.xtoks[par][:], R=["x%d" % par])
            if is_last(st):
                self.seq_last_outputs(seq)

    def seq_last_outputs(self, seq):
        f = self.f
        io = self.io
        L = self.lastst
        k32t = L[:, 96:96 + 256].rearrange("p (a t) -> p a t", a=2)
        if 'pkpv' in _SKIP:
            return
        ptk, pkres = self.pb()
        for pair in range(2):
            f.op("pe", lambda e, pair=pair, ptk=ptk: e.transpose(out=ptk[:, pair * 128:(pair + 1) * 128], in_=k32t[:, pair, :], identity=self.identf),
                 R=["lastst", "C"], W=[pkres])
        f.op("act", lambda e, ptk=ptk: e.activation(out=L[:, 608:864], in_=ptk[:, 0:256], func=AF.Copy), R=[pkres], W=["lastst"])
        f.dma("pool", io["pk"][seq], L[:, 608:864], R=["lastst"])
        f.dma("pool", io["pv"][seq], L[:, 352:608], R=["lastst"])
        if 'pconv' in _SKIP:
            return
        l3t = self.l3t
        pc3 = self.stage[0:4, 0:3072]
        for q6 in range(6):
            pt, pres = self.pb()
            for k in range(4):
                c = q6 * 4 + k
                f.op("pe", lambda e, k=k, c=c, pt=pt: e.transpose(out=pt[:, k * 128:(k + 1) * 128], in_=l3t[:, c * 4:c * 4 + 128], identity=self.identf),
                     R=["lastst", "C"], W=[pres])
            f.op("act", lambda e, q6=q6, pt=pt: e.activation(out=pc3[:, q6 * 512:(q6 + 1) * 512], in_=pt[0:4, 0:512], func=AF.Copy),
                 R=[pres], W=self.STAGE)
        f.dma("pool", io["pconv"][seq], pc3[1:4, :], R=self.STAGE)

    def seq_end(self, seq):
        if "seq_end" in _SKIP:
            return
        f = self.f
        io = self.io
        stage = self.stage
        SR = self.STAGE
        for q4 in range(4):
            pt, pres = self.pb()
            for k in range(4):
                c = q4 * 4 + k
                f.op("pe", lambda e, k=k, c=c, pt=pt: e.transpose(out=pt[:, k * 128:(k + 1) * 128], in_=self.hst[:, c * 128:(c + 1) * 128],
                                                                  identity=self.identf), R=["hst", "C"], W=[pres])
            f.op("act", lambda e, q4=q4, pt=pt: e.activation(out=stage[:, q4 * 512:(q4 + 1) * 512], in_=pt[:, 0:512], func=AF.Copy),
                 R=[pres], W=SR)
        f.dma("pool", io["pssm"][seq].rearrange("(c q) n -> q c n", q=128), stage[:, 0:2048].rearrange("p (c n) -> p c n", c=16), R=SR)

    def sample_group(self):
        f = self.f
        io = self.io
        NS = NSMP
        L = self.lastst
        stage = self.stage
        SR = self.STAGE
        x16 = self.xtok[0:NS, :]
        self.worder = self.order_in() + self.order_merge()
        self.wpos = 0
        f.dma("pool", io["sk"][:, 0:127, :], io["csk"][:, 1:128, :])
        f.dma("pool", io["sv"][:, 0:127, :], io["csv"][:, 1:128, :])
        f.dma("pool", io["sconv_o"][:, 0:2, :], io["sconv"].rearrange("(b j) c -> b j c", j=3)[:, 1:3, :])
        f.dma("pool", x16, io["xs"], W=["x0"])
        self.norm_T(x16, "x0", NS, self.gmixT, self.hT, "hT", 0)
        cbuf = self.cbuf
        f.dma("pool", stage[0:48, 0:3072], io["sconv"], W=SR)
        for q6 in range(6):
            pt, pres = self.pb()
            for k in range(4):
                c = q6 * 4 + k
                f.op("pe", lambda e, k=k, c=c, pt=pt: e.transpose(out=pt[:, k * 48:(k + 1) * 48], in_=stage[0:48, c * 128:(c + 1) * 128],
                                                                  identity=self.identf[0:48, 0:48]), R=SR + ["C"], W=[pres])
            for k in range(4):
                c = q6 * 4 + k
                f.op("act", lambda e, k=k, c=c, pt=pt: e.activation(out=cbuf[:, c, 0:3, :], in_=pt[:, k * 48:(k + 1) * 48].rearrange("p (b j) -> p j b", j=3),
                                                                    func=AF.Copy), R=[pres], W=["xbch"])
        k32s = L[:, 96:96 + 256].rearrange("p (a t) -> p a t", a=2)
        f.op("pool", lambda e: e.memset(L[:, 96:352], 0.0), W=["lastst"])
        xbctok = self.hst
        def xbc_hook(s, wt, wres):
            pt, pres = self.pb()
            self.tm_mm(pt[0:NS, 0:256], pres, wt, wres, 256, self.hT, "hT", 0, NS)
            if s < 8:
                dst = self.hst[0:NS, s * 256:(s + 1) * 256]
            else:
                dst = self.hbf32[0:NS, (s - 8) * 256:(s - 7) * 256]
            f.op("act", lambda e, pt=pt, dst=dst: e.activation(out=dst, in_=pt[0:NS, 0:256], func=AF.Copy), R=[pres], W=["hst" if s < 8 else "bcs"])
        self.xbc_hook = xbc_hook
        self.drain(self.in_proj_fm(self.hT, NS, self.qT, lambda pair: self.kT[:, pair, 128:128 + NS],
                                   lambda c: cbuf[:, c, 3, :], self.qmT, last3=None, k32=lambda pair: k32s[:, pair, 0:NS]))
        self.xbc_hook = None
        f.dma("pool", io["sconv_o"][:, 2, 0:2048], self.hst[0:NS, :], R=["hst"])
        f.dma("pool", io["sconv_o"][:, 2, 2048:3072], self.hbf32[0:NS, :], R=["bcs"])
        pt, pres = self.pb()
        for pair in range(2):
            f.op("pe", lambda e, pair=pair, pt=pt: e.transpose(out=pt[:, pair * 128:(pair + 1) * 128], in_=k32s[:, pair, :], identity=self.identf),
                 R=["lastst", "C"], W=[pres])
        f.op("act", lambda e, pt=pt: e.activation(out=L[0:NS, 608:864], in_=pt[0:NS, 0:256], func=AF.Copy), R=[pres], W=["lastst"])
        f.dma("pool", io["sk"][:, 127, :], L[0:NS, 608:864], R=["lastst"])
        wt, wres = self.slab("A", SA_V)
        pt, pres = self.pb()
        self.tm_mm(pt[0:NS, 0:256], pres, wt, wres, 256, self.hT, "hT", 0, NS)
        f.op("act", lambda e, pt=pt: e.activation(out=self.vtok[0:NS, 1, :], in_=pt[0:NS, 0:256], func=AF.Copy), R=[pres], W=["vtok"])
        f.op("dve", lambda e, pt=pt: e.tensor_copy(out=L[0:NS, 352:608], in_=pt[0:NS, 0:256]), R=[pres], W=["lastst"])
        f.dma("pool", io["sv"][:, 127, :], L[0:NS, 352:608], R=["lastst"])
        zT = self.zs[:, 0:16 * NS].rearrange("p (c t) -> p c t", c=16)
        for s in range(8):
            wt, wres = self.slab("A", SA_Z + s)
            for mm in range(2):
                pt, pres = self.pb()
                self.fm_mm(pt[:, 0:NS], pres, wt, wres, mm * 128, self.hT, "hT", NS)
                f.op("act", lambda e, pt=pt, c=s * 2 + mm: e.activation(out=zT[:, c, :], in_=pt[:, 0:NS], func=AF.Silu), R=[pres], W=["zs"])
        wt, wres = self.slab("T", 0)
        pt, pres = self.pb()
        self.tm_mm(pt[0:NS, 0:32], pres, wt, wres, 32, self.hT, "hT", 0, NS, cw=32)
        f.op("act", lambda e, pt=pt: e.activation(out=self.dtraw[0:NS, :], in_=pt[0:NS, 0:32], func=AF.Copy), R=[pres], W=["dtraw"])
        cvs = self.cvs
        self.drain(self.conv_fm(lambda c, j: cbuf[:, c, j, :], NS, lambda c: cvs[:, c, :], lambda c: ("xbcT", c)))
        CVS = [("xbcT", c) for c in range(24)]
        for b in range(NS):
            ck, ckres = self.tf()
            f.dma("pool", ck[:, 0:256], io["csk"][b], W=[ckres])
            f.dma("pool", ck[:, 256:512], io["csv"][b], W=[ckres])
            cb16, cbres = self.tb()
            f.op("pool", lambda e, ck=ck, cb16=cb16: e.tensor_copy(out=cb16[:, 0:512], in_=ck[:, 0:512]), R=[ckres], W=[cbres])
            pt, pres = self.pb()
            pbf = pt.bitcast(BF16)
            for pair in range(2):
                f.op("pe", lambda e, pair=pair, pbf=pbf, cb16=cb16: e.transpose(out=pbf[:, pair * 128:(pair + 1) * 128], in_=cb16[:, pair * 128:(pair + 1) * 128],
                                                                                identity=self.identb), R=[cbres, "CBc"], W=[pres])
            kTc, kTres = self.tb()
            f.op("act", lambda e, pbf=pbf, kTc=kTc: e.activation(out=kTc[:, 0:256], in_=pbf[:, 0:256], func=AF.Copy), R=[pres], W=[kTres])
            self.drain(self.swa_heads(
                q_rhs=lambda h, b=b: self.qT[64 * (h % 2):64 * (h % 2) + 64, (h // 2) * 4:(h // 2) * 4 + 4, b:b + 1],
                kprev=lambda h, kTc=kTc: kTc[64 * (h % 2):64 * (h % 2) + 64, (h // 2) * 128:(h // 2 + 1) * 128],
                kcur=lambda h: self.kT[64 * (h % 2):64 * (h % 2) + 64, h // 2, 128:128 + NS],
                vprev=lambda h, cb16=cb16: cb16[:, 256 + h * 64:256 + (h + 1) * 64],
                vcur=lambda h: self.vtok[0:NS, 1, h * 64:(h + 1) * 64],
                nq=1, kcn=NS, Dmp=self.Dmp[:, 0:1], Dmc=self.dm16[0:NS, b:b + 1],
                out=lambda h, b=b: self.aT[0:64, h * 4:h * 4 + 4, b:b + 1],
                R_in=["qT", "kT", "vtok", kTres, cbres], W_out=["aT"]))
            kst = stage[:, 0:2048]
            vst = stage[:, 2048:4096]
            f.dma("pool", kst.rearrange("p (m c) -> p m c", m=2), io["cmk"][b].rearrange("(m p) c -> p m c", p=128), W=SR)
            f.dma("pool", vst.rearrange("p (m c) -> p m c", m=2), io["cmv"][b].rearrange("(m p) c -> p m c", p=128), W=SR)
            f.op("pool", lambda e: e.tensor_copy(out=self.Vm[:].rearrange("p m c -> p (m c)"), in_=vst), R=SR, W=["Vm"])
            f.op("dve", lambda e: e.tensor_copy(out=self.xn[:], in_=kst), R=SR, W=["xn"])
            self.kmem_T(self.xn, "xn", self.KTm, "KTm")
            self.drain(self.mem_heads(q_rhs=lambda c, b=b: self.qmT[:, c, b:b + 1], KTm=self.KTm, ktres="KTm", Vm=self.Vm, vres="Vm", nq=1,
                                      out=lambda c, b=b: self.mT[:, c, b:b + 1], R_in=["qmT"], W_out=["mT"]))
        S = self.ssd_s
        dt = S[0:NS, 32:64]
        f.op("dve", lambda e: e.tensor_tensor(out=S[0:NS, 0:32], in0=self.dtraw[0:NS, :], in1=self.dtb[0:NS, :], op=ALU.add), R=["dtraw", "PR"], W=["S"])
        f.op("act", lambda e: e.activation(out=S[0:NS, 0:32], in_=S[0:NS, 0:32], func=AF.Exp), R=["S"], W=["S"])
        f.op("act", lambda e: e.activation(out=dt, in_=S[0:NS, 0:32], func=AF.Ln, bias=1.0), R=["S"], W=["S"])
        f.op("dve", lambda e: e.tensor_tensor(out=S[0:NS, 64:96], in0=dt, in1=self.a_bc[0:NS, :], op=ALU.mult), R=["S", "SM"], W=["S"])
        f.op("act", lambda e: e.activation(out=S[0:NS, 96:128], in_=S[0:NS, 64:96], func=AF.Exp), R=["S"], W=["S"])
        ex = stage[0:NS, :]
        f.op("dve", lambda e: e.tensor_copy(out=ex[:, 0:2048].rearrange("p (h d) -> p h d", h=32), in_=bc(S[0:NS, 96:128], [NS, 32, 64])), R=["S"] + SR, W=SR)
        f.op("dve", lambda e: e.tensor_copy(out=ex[:, 2048:4096].rearrange("p (h d) -> p h d", h=32), in_=bc(dt, [NS, 32, 64])), R=["S"] + SR, W=SR)
        cdT, dtT = self.cdT, self.dtT
        for which, dst in ((0, cdT), (1, dtT)):
            pt, pres = self.pb()
            for c in range(16):
                f.op("pe", lambda e, c=c, which=which, pt=pt: e.transpose(out=pt[:, c * NS:(c + 1) * NS], in_=ex[:, which * 2048 + c * 128: which * 2048 + (c + 1) * 128],
                                                                          identity=self.identf[0:NS, 0:NS]), R=SR + ["C"], W=[pres])
            f.op("act", lambda e, pt=pt, dst=dst: e.activation(out=dst[:].rearrange("p c t -> p (c t)"), in_=pt[:, 0:16 * NS], func=AF.Copy), R=[pres], W=["cdT"])
        xdtT = self.xdtT
        f.op("dve", lambda e: e.tensor_tensor(out=xdtT[:], in0=cvs[:, 0:16, :], in1=dtT[:], op=ALU.mult), R=CVS + ["cdT"], W=["xdtT"])
        bcs = self.bcs
        bpad = self.hst[:, 0:1024].rearrange("p (c t) -> p c t", c=8)
        f.op("pool", lambda e: e.memset(self.hst[:, 0:1024], 0.0), W=["hst"])
        f.op("dve", lambda e: e.tensor_copy(out=bpad[:, :, 0:NS], in_=cvs[:, 16:24, :]), R=CVS + ["hst"], W=["hst"])
        for half in range(2):
            pt, pres = self.pb()
            for g in range(4):
                f.op("pe", lambda e, g=g, half=half, pt=pt: e.transpose(out=pt[:, g * 128:(g + 1) * 128], in_=bpad[:, half * 4 + g, :], identity=self.identf),
                     R=["hst", "C"], W=[pres])
            f.op("act", lambda e, half=half, pt=pt: e.activation(out=bcs[0:NS, half * 512:(half + 1) * 512], in_=pt[0:NS, 0:512], func=AF.Copy), R=[pres], W=["bcs"])
        oh16 = self.rhsD[:].rearrange("p a b -> p (a b)").bitcast(F32)[0:NS, :].rearrange("p (a b) -> p a b", a=NS)
        f.op("dve", lambda e: e.tensor_copy(out=oh16[:], in_=bc(self.identf[0:NS, 0:NS], [NS, NS, 128])), R=["C"], W=["rhsD"])
        yT = self.yT
        U = self.hst
        for b in range(NS):
            H = stage[:, (b % 2) * 2048:(b % 2 + 1) * 2048]
            Hres = ["xs_tok", "xdt"] if b % 2 == 0 else ["xw", "s_tok"]
            H3 = H.rearrange("p (c n) -> p c n", c=16)
            f.dma("pool", H3, io["sssm"][b].rearrange("(c q) n -> q c n", q=128), W=Hres)
            pbm, pbmres = self.pb()
            pcm, pcmres = self.pb()
            f.op("pe", lambda e, b=b, pbm=pbm: e.matmul(pbm[:, 0:512], lhsT=oh16[0:NS, b, :], rhs=bcs[0:NS, 0:512], start=True, stop=True), R=["rhsD", "bcs"], W=[pbmres])
            f.op("pe", lambda e, b=b, pcm=pcm: e.matmul(pcm[:, 0:512], lhsT=oh16[0:NS, b, :], rhs=bcs[0:NS, 512:1024], start=True, stop=True), R=["rhsD", "bcs"], W=[pcmres])
            f.op("dve", lambda e, b=b, H3=H3: e.tensor_tensor(out=H3, in0=H3, in1=bc(cdT[:, :, b], [128, 16, 128]), op=ALU.mult), R=Hres + ["cdT"], W=Hres)
            U4 = U[:].rearrange("p (g r n) -> p g r n", g=4, r=4)
            f.op("dve", lambda e, b=b, pbm=pbm, U4=U4: e.tensor_tensor(
                out=U4, in0=pbm[:, 0:512].rearrange("p (g n) -> p g n", g=4).unsqueeze(2).broadcast_to([128, 4, 4, 128]),
                in1=xdtT[:, :, b].rearrange("p (g r) -> p g r", g=4).unsqueeze(3).broadcast_to([128, 4, 4, 128]), op=ALU.mult),
                R=[pbmres, "xdtT", "hst"], W=["hst"])
            f.op("dve", lambda e, H=H: e.tensor_tensor(out=H, in0=H, in1=U[:], op=ALU.add), R=Hres + ["hst"], W=Hres)
            f.dma("pool", io["sssm_o"][b].rearrange("(c q) n -> q c n", q=128), H3, R=Hres)
            f.op("dve", lambda e, pcm=pcm, U4=U4, H=H: e.tensor_tensor(
                out=U4, in0=H.rearrange("p (g r n) -> p g r n", g=4, r=4),
                in1=pcm[:, 0:512].rearrange("p (g n) -> p g n", g=4).unsqueeze(2).broadcast_to([128, 4, 4, 128]), op=ALU.mult),
                R=Hres + [pcmres, "hst"], W=["hst"])
            f.op("dve", lambda e, b=b: e.tensor_reduce(out=yT[:, :, b], in_=U[:].rearrange("p (c n) -> p c n", c=16), op=ALU.add, axis=AX.X),
                 R=["hst"], W=["cdT"])
        y2 = self.y2
        f.op("dve", lambda e: e.tensor_tensor(out=y2[:], in0=cvs[:, 0:16, :], in1=bc(self.dskT, [128, 16, NS]), op=ALU.mult), R=CVS + ["PV"], W=["cdT"])
        f.op("dve", lambda e: e.tensor_tensor(out=y2[:], in0=y2[:], in1=yT[:], op=ALU.add), R=["cdT"], W=["cdT"])
        f.op("dve", lambda e: e.tensor_tensor(out=y2[:], in0=y2[:], in1=zT, op=ALU.mult), R=["cdT", "zs"], W=["cdT"])
        sq, sqres = self.tb()
        f.op("act", lambda e, sq=sq: e.activation(out=sq[:, 0:16 * NS], in_=y2[:].rearrange("p c t -> p (c t)"), func=AF.Square), R=["cdT"], W=[sqres])
        p2, p2res = self.pb()
        for g in range(4):
            for r in range(4):
                c = g * 4 + r
                f.op("pe", lambda e, g=g, r=r, c=c, sq=sq, p2=p2: e.matmul(p2[:, g * NS:(g + 1) * NS], lhsT=self.onesb, rhs=sq[:, c * NS:(c + 1) * NS],
                                                                          start=(r == 0), stop=(r == 3)), R=[sqres, "CBc"], W=[p2res])
        rr, rres = self.rsqrt_bc(p2[:, 0:4 * NS], p2res, 4 * NS, 1.0 / 512)
        f.op("dve", lambda e, rr=rr: e.tensor_tensor(
            out=y2[:].rearrange("p (g r) t -> p g r t", g=4), in0=y2[:].rearrange("p (g r) t -> p g r t", g=4),
            in1=rr.rearrange("p (g t) -> p g t", g=4).unsqueeze(2).broadcast_to([128, 4, 4, NS]), op=ALU.mult), R=["cdT", rres], W=["cdT"])
        f.op("dve", lambda e: e.tensor_tensor(out=self.sT[:, :, 0:NS], in0=y2[:], in1=bc(self.ssdnT, [128, 16, NS]), op=ALU.mult), R=["cdT", "PV"], W=["sT"])
        self.worder = self.order_merge() + self.order_ffn()
        self.wpos = 0
        self.merge_out(NS, NS, "x0", x16, self.aT, self.sT, self.mT, self.hT,
                       lambda m: self.xbcT[:, m, 3:3 + NS], lambda m: ("xbcT", m), self.h2T)
        self.drain(self.ffn_gen(NS, NS, "x0", x16, self.h2T))
        f.dma("pool", io["ys"], x16, R=["x0"])


def slabify(W, cw, kcs):
    K, N = W.shape
    assert K == kcs * 128 and N % cw == 0
    a = W.reshape(kcs, 128, N // cw, cw).transpose(2, 1, 0, 3)
    return np.ascontiguousarray(a).reshape(N // cw, 128, kcs * cw)


def host_prep(inp):
    w_in = inp["w_in"][0]
    q = w_in[:, 0:1024].reshape(2048, 2, 2, 4, 64).transpose(0, 1, 3, 2, 4).reshape(2048, 1024)
    parts = [slabify(q, CW, 16), slabify(w_in[:, 1024:1280], CW, 16), slabify(w_in[:, 1280:1536], CW, 16),
             slabify(w_in[:, 1536:3584], CW, 16), slabify(w_in[:, 3584:6656], CW, 16), slabify(w_in[:, 6688:7712], CW, 16),
             slabify(w_in[:, 7712:13856], CW, 16), slabify(inp["w_mem_kv"][0], CW, 16), slabify(inp["w_up_ssd"][0], CW, 16),
             slabify(inp["w_out"][0], CW, 16), slabify(inp["w_gate"][0], CW, 16), slabify(inp["w_up"][0], CW, 16)]
    WA = np.concatenate(parts, 0)
    assert WA.shape[0] == NA
    WM = slabify(inp["w_up_mem"][0], CW, 8)
    ws = inp["w_up_swa"][0]
    WS = np.ascontiguousarray(ws.reshape(16, 64, 8, CW).transpose(2, 1, 0, 3)).reshape(8, 64, 16 * CW)
    wd = inp["w_down"][0]
    WD = np.ascontiguousarray(wd.reshape(4, 11, 128, 8, CW).transpose(3, 0, 2, 1, 4)).reshape(32, 128, 11 * CW)
    WT = slabify(w_in[:, 6656:6688], 32, 16)[0]
    ar = np.arange(128)
    c128 = np.zeros((128, 8 * 128 + 512 + 32), np.float32)
    c128[:, 0:128] = np.eye(128)
    c128[:, 128:256] = (ar[:, None] <= ar[None, :])
    c128[:, 256:384] = (ar[:, None] > ar[None, :])
    c128[:, 384:512] = 1.0
    k_, q_ = ar[:, None], ar[None, :]
    c128[:, 512:640] = np.where(q_ >= k_, q_ - k_, 20000.0)
    c128[:, 640:768] = np.where(q_ <= k_, q_ + 128 - k_, 20000.0)
    c128[:, 768:896] = (ar[:, None] // 64 == ar[None, :] // 64)
    c128[:, 1024:1536] = np.tile(np.where(ar[None, :] < ar[:, None], -30000.0, 0.0), (1, 4))
    c128[:, 1536:1568] = (ar[:, None] % 32 == np.arange(32)[None, :])
    pvec = np.zeros((128, 512), np.float32)
    T16 = lambda v: np.ascontiguousarray(v.reshape(16, 128).T)
    pvec[:, 0:16] = T16(inp["norm_mix"][0])
    pvec[:, 16:32] = T16(inp["norm_ffn"][0])
    pvec[:, 32:48] = T16(inp["norm_mem"][0])
    pvec[:, 48:64] = T16(inp["ssd_norm"][0])
    pvec[:, 64:160] = inp["conv_w"][0].reshape(4, 24, 128).transpose(2, 1, 0).reshape(128, 96)
    pvec[:, 160:184] = inp["conv_b"][0].reshape(24, 128).T
    pvec[:, 184] = np.tile(inp["q_norm_swa"][0], 2)
    pvec[:, 185] = np.tile(inp["k_norm_swa"][0], 2)
    pvec[:, 186:188] = inp["q_norm_mem"][0].reshape(2, 128).T
    pvec[:, 188:204] = inp["swa_sinks"][0][None, :]
    pvec[:, 204:220] = np.repeat(inp["d_skip"][0], 64).reshape(16, 128).T
    pvec[0:16, 220:236] = np.where(np.eye(16) > 0, 0.0, 20000.0)
    prow = np.zeros((128, 352), np.float32)
    prow[:, 0:32] = inp["dt_bias"][0][None, :]
    prow[:, 32:64] = inp["a_log"][0][None, :]
    prow[:, 64:96] = inp["d_skip"][0][None, :]
    prow[:, 96:352] = inp["k_norm_mem"][0][None, :]
    shared = dict(WA=WA, WM=WM, WS=WS, WD=WD, WT=np.ascontiguousarray(WT), c128=c128, pvec=pvec, prow=prow)
    in_maps = []
    for c in range(NCORES):
        m = dict(shared)
        m["xp"] = np.ascontiguousarray(inp["x_prompt"][2 * c:2 * c + 2])
        m["memp"] = np.ascontiguousarray(inp["mem_prompt"][2 * c:2 * c + 2])
        sl = slice(16 * c, 16 * c + 16)
        m["xs"] = np.ascontiguousarray(inp["x_sample"][sl, 0])
        m["csk"] = np.ascontiguousarray(inp["cache_swa_k"][0, sl]).reshape(16, 128, 256)
        m["csv"] = np.ascontiguousarray(inp["cache_swa_v"][0, sl]).reshape(16, 128, 256)
        m["cmk"] = np.ascontiguousarray(inp["cache_mem_k"][0, sl]).reshape(16, 256, 1024)
        m["cmv"] = np.ascontiguousarray(inp["cache_mem_v"][0, sl]).reshape(16, 256, 1024)
        m["sssm"] = np.ascontiguousarray(inp["state_ssm"][0, sl]).reshape(16, 2048, 128)
        m["sconv"] = np.ascontiguousarray(inp["state_conv"][0, sl]).reshape(48, 3072)
        in_maps.append(m)
    return in_maps


_CACHE = {}


def run(inputs, do_samples=True, n_st=None, dbg=False):
    inp = {k: np.asarray(v) for k, v in inputs.items()}
    b = Builder(do_samples=do_samples, n_st=n_st, dbg=dbg)
    nc = b.build()
    in_maps = host_prep(inp)
    if not do_samples:
        for m in in_maps:
            for k in ("xs", "csk", "csv", "cmk", "cmv", "sssm", "sconv"):
                m.pop(k)
    res = run_bass_kernel_spmd(nc, in_maps, core_ids=list(range(NCORES)))
    R = res.results
    cat = lambda k: np.concatenate([r[k] for r in R], 0)
    yp = cat("yp")
    ys = cat("ys").reshape(128, 1, D)
    outs = (yp, ys,
            cat("pk").reshape(1, 16, 128, 4, 64), cat("pv").reshape(1, 16, 128, 4, 64),
            cat("pmk").reshape(1, 16, 256, 4, 256), cat("pmv").reshape(1, 16, 256, 4, 256),
            cat("pssm").reshape(1, 16, 32, 64, 128), cat("pconv").reshape(1, 16, 3, 3072),
            cat("sk").reshape(1, 128, 128, 4, 64), cat("sv").reshape(1, 128, 128, 4, 64),
            cat("sssm_o").reshape(1, 128, 32, 64, 128), cat("sconv_o").reshape(1, 128, 3, 3072))
    outs = tuple(np.ascontiguousarray(o, dtype=np.float32) for o in outs)
    if dbg:
        return outs, {k: [r["dbg_" + k] for r in R] for k in b.dbg_outs}
    return outs


def kernel(**inputs):
    return run(inputs, do_samples=True)
```

```python
import contextlib
import os
import numpy as np
_SKIP = set(os.environ.get('KSKIP', '').split(','))
import concourse.bass as bass
import concourse.mybir as mybir
from concourse.bass_utils import run_bass_kernel_spmd

F32 = mybir.dt.float32
BF16 = mybir.dt.bfloat16
AF = mybir.ActivationFunctionType
ALU = mybir.AluOpType
AX = mybir.AxisListType

NCORES = 8
D = 2048
SEQ = 2048
NSEQ = 2
NSMP = 16
NT = 128
BLK = 128
KC = 16
DFF = 5632
EPS = 1e-6
CW = 256
ENGS = ("pe", "act", "dve", "pool", "sp")

SA_Q, SA_K, SA_V, SA_Z, SA_XBC, SA_QM, SA_G0, SA_G1, SA_G2, SA_MEM, SA_USSD, SA_OUT, SA_GATE, SA_UP = (
    0, 4, 5, 6, 14, 26, 30, 38, 46, 54, 62, 70, 78, 100)
NA = 122
SLOPES = [2.0 ** (-8.0 * (h + 1) / 16.0) for h in range(16)]


class FW:
    def __init__(self, nc, n_dma_sems=48):
        self.nc = nc
        self.es = contextlib.ExitStack()
        self.sem = {e: self.es.enter_context(nc.semaphore("s_" + e)) for e in ENGS}
        self.dsem = [self.es.enter_context(nc.semaphore("d_%d" % i)) for i in range(n_dma_sems)]
        self.dtot = [0] * n_dma_sems
        self.dnext = {}
        self.dpool = {'pool': (0, n_dma_sems // 2), 'sp': (n_dma_sems // 2, n_dma_sems)}
        self.n = {e: 0 for e in ENGS}
        self.waited = {e: {} for e in ENGS}
        self.stream = {e: [] for e in ENGS}
        self.last_w = {}
        self.readers = {}

    def sbuf(self, name, shape, dtype):
        return self.es.enter_context(self.nc.sbuf_tensor(name, list(shape), dtype))

    def psum(self, name, shape, dtype):
        return self.es.enter_context(self.nc.psum_tensor(name, list(shape), dtype))

    def _deps(self, R, W):
        deps = set()
        for r in R:
            w = self.last_w.get(r)
            if w is not None:
                deps.add(w)
            if isinstance(r, tuple) and r[0] == "ps":
                rd = self.readers.get(r)
                if rd:
                    for k, v in rd.items():
                        deps.add((k, v))
        for r in W:
            w = self.last_w.get(r)
            if w is not None:
                deps.add(w)
            rd = self.readers.get(r)
            if rd:
                for k, v in rd.items():
                    deps.add((k, v))
        return deps

    def _waits(self, eng, deps):
        need = {}
        for k, v in deps:
            if k == "pe" and eng == "pe":
                continue
            if v > need.get(k, 0):
                need[k] = v
        out = []
        wd = self.waited[eng]
        for k, v in need.items():
            if wd.get(k, 0) >= v:
                continue
            wd[k] = v
            s = self.sem[k] if isinstance(k, str) else self.dsem[k[1]]
            out.append((s, v))
        return out

    def _record(self, my, R, W):
        for r in R:
            self.readers.setdefault(r, {})[my[0]] = my[1]
        for r in W:
            self.last_w[r] = my
            self.readers[r] = {}

    def op(self, eng, fn, R=(), W=()):
        waits = self._waits(eng, self._deps(R, W))
        self.n[eng] += 1
        my = (eng, self.n[eng])
        self.stream[eng].append((waits, fn, self.sem[eng], 1))
        self._record(my, R, W)

    def dma(self, q, out, in_, R=(), W=(), **kw):
        deps = self._deps(R, W)
        lo, hi = self.dpool[q]
        i = self.dnext.get(q, lo)
        self.dnext[q] = lo + (i + 1 - lo) % (hi - lo)
        if self.dtot[i] > 0:
            deps.add((("d", i), self.dtot[i]))
        waits = self._waits(q, deps)
        self.dtot[i] += 16
        my = (("d", i), self.dtot[i])
        nonctg = kw.pop("nonctg", False)
        nc = self.nc

        def fn(e):
            if nonctg:
                with nc.allow_non_contiguous_dma(reason="tiny strided store"):
                    return e.dma_start(out=out, in_=in_, **kw)
            return e.dma_start(out=out, in_=in_, **kw)
        self.stream[q].append((waits, fn, self.dsem[i], 16))
        self._record(my, R, W)

    def emit(self):
        nc = self.nc
        waits = [(self.dsem[i], t) for i, t in enumerate(self.dtot) if t > 0]
        waits += [(self.sem[e], self.n[e]) for e in ENGS if e != "sp" and self.n[e] > 0]
        self.stream["sp"].append((waits, None, None, 0))
        with nc.Block() as block:
            def replay(name, e):
                for waits, fn, s, inc in self.stream[name]:
                    for (ws, wv) in waits:
                        e.wait_ge(ws, wv)
                    if fn is not None:
                        fn(e).then_inc(s, inc)

            @block.tensor
            def _(e):
                replay("pe", e)

            @block.scalar
            def _(e):
                replay("act", e)

            @block.vector
            def _(e):
                replay("dve", e)

            @block.gpsimd
            def _(e):
                replay("pool", e)

            @block.sync
            def _(e):
                replay("sp", e)
        self.es.close()


def bc(ap, shape):
    return ap.unsqueeze(2).broadcast_to(list(shape))


class Builder:
    def __init__(self, do_samples=True, n_st=None, dbg=False, nseq=NSEQ, stop=None, force_last=False):
        self.force_last = force_last
        self.pipeline = os.environ.get('KPIPE', '1') == '1'
        self.il = tuple(int(v) for v in os.environ.get('KIL', '1,1').split(','))
        self.nseq = nseq
        self.stop = stop
        self.do_samples = do_samples
        self.n_st = n_st
        self.dbg_on = dbg
        self.nc = bass.Bass("TRN2", target_bir_lowering=False)
        self.f = FW(self.nc)
        self.ins = {}
        self.outs = {}
        self.dbg_outs = {}

    def din(self, name, shape, dtype=F32):
        self.ins[name] = self.nc.dram_tensor(name, list(shape), dtype, kind="ExternalInput").ap()
        return self.ins[name]

    def dout(self, name, shape):
        self.outs[name] = self.nc.dram_tensor(name, list(shape), F32, kind="ExternalOutput").ap()
        return self.outs[name]

    def dscr(self, name, shape, dtype):
        return self.nc.dram_tensor(name, list(shape), dtype, kind="Internal").ap()

    def dump(self, name, ap, shape, R):
        if not self.dbg_on:
            return
        o = self.nc.dram_tensor("dbg_" + name, list(shape), F32, kind="ExternalOutput").ap()
        self.dbg_outs[name] = o
        self.f.dma("pool", o, ap, R=R)

    def pb(self):
        if self.ctx == 'F':
            i = self.pnF
            self.pnF = (self.pnF + 1) % 4
        elif self.ctx == 'M':
            i = 4 + self.pnM
            self.pnM = (self.pnM + 1) % 4
        else:
            i = self.pnext
            self.pnext = (self.pnext + 1) % 8
        return self.ps[i], ("ps", i)

    def slabs(self, keys):
        order = self.worder
        live = set(keys)
        pos = None
        try:
            pos = order.index(keys[-1], self.wpos)
            self.wpos = pos
        except ValueError:
            pass
        upcoming = order[pos + 1: pos + 1 + self.NBUF] if pos is not None else []
        protect = set(live)
        for k in keys:
            if k not in self.wloaded:
                self._issue(k, protect)
        for nk in upcoming:
            if nk in self.wloaded:
                protect.add(nk)
                continue
            if not self._issue(nk, protect):
                break
            protect.add(nk)
        return [(self.wbuf[self.wloaded[k]], ("wbuf", self.wloaded[k])) for k in keys]

    def slab(self, kind, idx):
        return self.slabs([(kind, idx)])[0]

    def _issue(self, key, protect):
        kind, idx = key
        held = {b: k for k, b in self.wloaded.items()}
        b = None
        for i in range(self.NBUF):
            cand = (self.wnext + i) % self.NBUF
            if held.get(cand) not in protect or cand not in held:
                b = cand
                break
        if b is None:
            return False
        self.wnext = (b + 1) % self.NBUF
        if b in held:
            del self.wloaded[held[b]]
        src, n, parts = self.wsrc(kind, idx)
        if key not in self.cast_done:
            self.cast_done.add(key)
            fsrc = self._wf32[kind] if kind == "T" else self._wf32[kind][idx]
            self._cast(src, fsrc, n, ("wscr", kind, idx))
        self.f.dma("sp", self.wbuf[b][0:parts, 0:n], src, R=[("wscr", kind, idx)], W=[("wbuf", b)])
        self.wloaded[key] = b
        return True

    def wsrc(self, kind, idx):
        if kind == "A":
            return self.WAb[idx], 4096, 128
        if kind == "M":
            return self.WMb[idx], 2048, 128
        if kind == "S":
            return self.WSb[idx], 4096, 64
        if kind == "D":
            return self.WDb[idx], 2816, 128
        if kind == "T":
            return self.WTb, 512, 128
        raise ValueError(kind)

    def build(self):
        nc, f = self.nc, self.f
        xp = self.din("xp", [NSEQ, SEQ, D])
        memp = self.din("memp", [NSEQ, 256, D])
        WAf = self.din("WA", [NA, 128, 4096])
        WMf = self.din("WM", [8, 128, 2048])
        WSf = self.din("WS", [8, 64, 4096])
        WDf = self.din("WD", [32, 128, 2816])
        WTf = self.din("WT", [128, 512])
        c128 = self.din("c128", [128, 8 * 128 + 512 + 32])
        pvec = self.din("pvec", [128, 512])
        prow = self.din("prow", [128, 32 * 3 + 256])
        if self.do_samples:
            xs_in = self.din("xs", [NSMP, D])
            csk = self.din("csk", [NSMP, 128, 256])
            csv = self.din("csv", [NSMP, 128, 256])
            cmk = self.din("cmk", [NSMP, 256, 1024])
            cmv = self.din("cmv", [NSMP, 256, 1024])
            sssm = self.din("sssm", [NSMP, 2048, 128])
            sconv = self.din("sconv", [NSMP * 3, 3072])
        yp = self.dout("yp", [NSEQ, SEQ, D])
        pk = self.dout("pk", [NSEQ, 128, 256])
        pv = self.dout("pv", [NSEQ, 128, 256])
        pmk = self.dout("pmk", [NSEQ, 256, 1024])
        pmv = self.dout("pmv", [NSEQ, 256, 1024])
        pssm = self.dout("pssm", [NSEQ, 2048, 128])
        pconv = self.dout("pconv", [NSEQ, 3, 3072])
        ys = self.dout("ys", [NSMP, D])
        sk = self.dout("sk", [NSMP, 128, 256])
        sv = self.dout("sv", [NSMP, 128, 256])
        sssm_o = self.dout("sssm_o", [NSMP, 2048, 128])
        sconv_o = self.dout("sconv_o", [NSMP, 3, 3072])
        self.io = {**self.ins, **self.outs}
        self.WAb = self.dscr("WAb", [NA, 128, 4096], BF16)
        self.WMb = self.dscr("WMb", [8, 128, 2048], BF16)
        self.WSb = self.dscr("WSb", [8, 64, 4096], BF16)
        self.WDb = self.dscr("WDb", [32, 128, 2816], BF16)
        self.WTb = self.dscr("WTb", [128, 512], BF16)

        def cast(dst, src, n, res):
            if n > 2048:
                assert n % 2048 == 0 or n == 2816
                a = n // 2048 if n % 2048 == 0 else 2
                f.dma("pool", dst.rearrange("p (a b) -> p a b", a=a), src.rearrange("p (a b) -> p a b", a=a), W=[res])
            else:
                f.dma("pool", dst, src, W=[res])
        self._cast = cast
        self._wf32 = {"A": WAf, "M": WMf, "S": WSf, "D": WDf, "T": WTf}
        self.cast_done = set()

        self.ps = [f.psum("ps%d" % i, [128, 512], F32) for i in range(8)]
        self.pnext = 0
        self.pnF = self.pnM = self.tfF = self.tfM = 0
        self.ctx = None
        self.NBUF = int(os.environ.get("KNBUF", "6"))
        self.wbuf = [f.sbuf("wbuf%d" % i, [128, 4096], BF16) for i in range(self.NBUF)]
        self.wnext = 0
        self.wloaded = {}
        self.worder = []
        self.wpos = 0

        C = f.sbuf("c128t", [128, 8 * 128 + 512 + 32], F32)
        f.dma("sp", C[:], c128, W=["C"])
        self.identf = C[:, 0:128]
        self.Tm = C[:, 128:256]
        self.Um = C[:, 256:384]
        self.onesf = C[:, 384:512]
        self.Dmd = C[:, 512:640]
        self.Dmp = C[:, 640:768]
        self.ohs = C[:, 1536:1568]
        PV = f.sbuf("pvect", [128, 512], F32)
        f.dma("sp", PV[:], pvec, W=["PV"])
        PR = f.sbuf("prowt", [128, 352], F32)
        f.dma("sp", PR[:], prow, W=["PR"])
        self.PV, self.PR = PV, PR
        self.gmixT = PV[:, 0:16]
        self.gffnT = PV[:, 16:32]
        self.gmemT = PV[:, 32:48]
        self.ssdnT = PV[:, 48:64]
        self.cwT = PV[:, 64:160]
        self.cbT = PV[:, 160:184]
        self.gq = PV[:, 184:185]
        self.gk = PV[:, 185:186]
        self.gqm = PV[:, 186:188]
        self.sink_raw = PV[:, 188:204]
        self.dskT = PV[:, 204:220]
        self.dm16 = PV[:, 220:236]
        self.dtb = PR[:, 0:32]
        self.alog = PR[:, 32:64]
        self.dsk = PR[:, 64:96]
        self.gkm = PR[:, 96:352]
        CB = f.sbuf("cbf", [128, 128 * 3 + 512], BF16)
        f.op("dve", lambda e: e.tensor_copy(out=CB[:, 0:128], in_=C[:, 0:128]), R=["C"], W=["CBc"])
        f.op("dve", lambda e: e.tensor_copy(out=CB[:, 128:256], in_=C[:, 384:512]), R=["C"], W=["CBc"])
        f.op("dve", lambda e: e.tensor_copy(out=CB[:, 256:384], in_=C[:, 768:896]), R=["C"], W=["CBc"])
        f.op("dve", lambda e: e.tensor_copy(out=CB[:, 384:896], in_=C[:, 1024:1536]), R=["C"], W=["CBc"])
        self.identb = CB[:, 0:128]
        self.onesb = CB[:, 128:256]
        self.blockones = CB[:, 256:384]
        self.maskD = CB[:, 384:896]
        SM = f.sbuf("smallp", [128, 64], F32)
        self.SM = SM
        f.op("act", lambda e: e.mul(out=SM[:, 0:1], in_=PV[:, 184:185], mul=0.125), R=["PV"], W=["SM"])
        f.op("act", lambda e: e.activation(out=SM[:, 1:17], in_=PV[:, 188:204], func=AF.Exp), R=["PV"], W=["SM"])
        f.op("act", lambda e: e.activation(out=SM[:, 17:49], in_=PR[:, 32:64], func=AF.Exp), R=["PR"], W=["SM"])
        f.op("act", lambda e: e.mul(out=SM[:, 17:49], in_=SM[:, 17:49], mul=-1.0), R=["SM"], W=["SM"])
        self.gq8 = SM[:, 0:1]
        self.esink = SM[:, 1:17]
        self.a_bc = SM[:, 17:49]

        self.hT = f.sbuf("hT", [128, KC, NT], BF16)
        self.xtoks = [f.sbuf("xtok%d" % i, [128, D], F32) for i in range(2)]
        self.xtok = self.xtoks[0]
        self.h2T = f.sbuf("h2T", [128, KC, NT], BF16)
        self.xn = f.sbuf("xn", [128, D], BF16)
        self.st1 = f.sbuf("st1", [128, 8], F32)
        self.qT = f.sbuf("qT", [128, 8, NT], BF16)
        self.kT = f.sbuf("kT", [128, 2, 2 * 128], BF16)
        self.vtok = f.sbuf("vtok", [128, 2, 256], BF16)
        self.zs = f.sbuf("zs", [128, D], BF16)
        self.xbcT = f.sbuf("xbcT", [128, 24, 3 + NT], BF16)
        self.hist = f.sbuf("hist", [128, 24, 3], BF16)
        self.qmT = f.sbuf("qmT", [128, 8, NT], BF16)
        self.aT = f.sbuf("aT", [64, 16, NT], BF16)
        self.sT = f.sbuf("sT", [128, KC, NT], BF16)
        self.mT = f.sbuf("mT", [128, 8, NT], BF16)
        self.actT = f.sbuf("actT", [128, 44, NT], BF16)
        self.hst = f.sbuf("hst", [128, D], F32)
        self.hbf = f.sbuf("hbf", [128, D], BF16)
        self.KTm = f.sbuf("KTm", [128, 8, 256], BF16)
        self.Vm = f.sbuf("Vm", [128, 2, 1024], BF16)
        self.lastst = f.sbuf("lastst", [128, 96 + 256 + 256 + 256], F32)
        self.dtraw = f.sbuf("dtraw", [128, 32], F32)
        self.l3t = f.sbuf("l3t", [128, 224], F32)
        f.op("pool", lambda e: e.memset(self.l3t[:], 0.0), W=["lastst"])
        self.macc = [f.sbuf("macc%d" % i, [128, NT], F32) for i in range(2)]
        self.tmpf = [f.sbuf("tmpf%d" % i, [128, 512], F32) for i in range(5)]
        self.tmpb = [f.sbuf("tmpb%d" % i, [128, 512], BF16) for i in range(4)]
        self.tfn = 0
        self.tbn = 0
        self.ssdbig = f.sbuf("ssdbig", [128, 4 * D], BF16)
        self.xs_tok = self.ssdbig[:, 0:D]
        self.xdt = self.ssdbig[:, D:2 * D]
        self.xw = self.ssdbig[:, 2 * D:3 * D]
        self.s_tok = self.ssdbig[:, 3 * D:4 * D]
        self.stage = self.ssdbig[:].bitcast(F32)
        self.STAGE = ["xs_tok", "xdt", "xw", "s_tok"]
        self.bm_tok = f.sbuf("bm_tok", [128, 512], BF16)
        self.ssd_s = f.sbuf("ssd_s", [128, 256], F32)
        self.HI = f.sbuf("HI", [128, 128], BF16)
        self.LO = f.sbuf("LO", [128, 128], BF16)
        self.lhsD = f.sbuf("lhsD", [128, 128], BF16)
        self.rhsD = f.sbuf("rhsD", [128, 32, 128], BF16)
        self.dA4 = f.sbuf("dA4", [128, 4, 32], F32)
        self.CBt = f.sbuf("CBt", [128, 4, 128], F32)
        self.Wp = [f.sbuf("Wp%d" % i, [128, 4, 128], BF16) for i in range(2)]
        if self.do_samples:
            self.cbuf = f.sbuf("cbuf", [128, 24, 4, NSMP], BF16)
            self.cvs = f.sbuf("cvs", [128, 24, NSMP], F32)
            self.cdT = f.sbuf("cdT", [128, 16, NSMP], F32)
            self.dtT = f.sbuf("dtT", [128, 16, NSMP], F32)
            self.xdtT = f.sbuf("xdtT", [128, 16, NSMP], F32)
            self.bcs = f.sbuf("bcs", [NSMP, 1024], F32)
            self.hbf32 = self.bcs
            self.yT = self.dtT
            self.y2 = self.cdT
        f.op("dve", lambda e: e.memset(self.lhsD[64:128, :], 1.0), W=["lhsD"])
        f.op("dve", lambda e: e.tensor_copy(out=self.rhsD[0:64], in_=bc(self.ohs[0:64], [64, 32, 128])), R=["C"], W=["rhsD"])

        n_st = SEQ // NT if self.n_st is None else self.n_st
        for seq in range(self.nseq):
            self.seq_start(seq)
            if self.stop == 'seq_start' or n_st == 0:
                continue
            self.run_sequence(seq, n_st)
            if n_st == SEQ // NT or self.force_last:
                self.seq_end(seq)
        if self.do_samples:
            self.sample_group()
        f.emit()
        return nc

    def tf(self):
        if self.ctx == 'F':
            i = 0
        elif self.ctx == 'M':
            i = 1 + self.tfM
            self.tfM = (self.tfM + 1) % 4
        else:
            i = self.tfn
            self.tfn = (self.tfn + 1) % len(self.tmpf)
        return self.tmpf[i], ("tmpf", i)

    def tb(self):
        i = self.tbn
        self.tbn = (self.tbn + 1) % len(self.tmpb)
        return self.tmpb[i], ("tmpb", i)

    def super_order(self):
        o = []
        o += [("A", SA_Q + i) for i in range(4)] + [("A", SA_K)] + [("A", SA_XBC + i) for i in range(12)]
        o += [("A", SA_QM + i) for i in range(4)] + [("A", SA_V)] + [("A", SA_Z + i) for i in range(8)] + [("T", 0)]
        for s in range(8):
            o += [("A", SA_G0 + s), ("S", s), ("A", SA_G1 + s), ("A", SA_USSD + s), ("A", SA_G2 + s), ("M", s)]
        o += [("A", SA_OUT + i) for i in range(8)]
        for s in range(22):
            o += [("A", SA_GATE + s), ("A", SA_UP + s)]
        o += [("D", i) for i in range(16)]
        return o

    def fm_mm(self, ps_ap, psres, wt, wres, col0, actT, actres, ntok, kcs=KC, M=128):
        wv = wt[:, 0:kcs * CW].rearrange("p (k c) -> p k c", k=kcs)
        for kc in range(kcs):
            self.f.op("pe", lambda e, kc=kc: e.matmul(ps_ap, lhsT=wv[:, kc, col0:col0 + M], rhs=actT[:, kc, 0:ntok],
                                                       start=(kc == 0), stop=(kc == kcs - 1)),
                      R=[wres, actres], W=[psres])

    def tm_mm(self, ps_ap, psres, wt, wres, ncols, actT, actres, t0, bs, kcs=KC, cw=CW, col0=0):
        wv = wt[:, 0:kcs * cw].rearrange("p (k c) -> p k c", k=kcs)
        for kc in range(kcs):
            self.f.op("pe", lambda e, kc=kc: e.matmul(ps_ap, lhsT=actT[:, kc, t0:t0 + bs], rhs=wv[:, kc, col0:col0 + ncols],
                                                       start=(kc == 0), stop=(kc == kcs - 1)),
                      R=[wres, actres], W=[psres])

    def norm_T(self, x_ap, xres, bs, gT, outT, outres, c0):
        f = self.f
        st1, xn = self.st1, self.xn
        f.op("dve", lambda e: e.memset(st1[0:bs, 0:1], 0.0), W=["st1"])
        f.op("act", lambda e: e.activation(out=xn[0:bs, :], in_=x_ap, func=AF.Square, accum_out=st1[0:bs, 0:1]),
             R=[xres], W=["xn", "st1"])
        f.op("act", lambda e: e.activation(out=st1[0:bs, 1:2], in_=st1[0:bs, 0:1], func=AF.Sqrt, scale=1.0 / D, bias=EPS),
             R=["st1"], W=["st1"])
        f.op("dve", lambda e: e.reciprocal(out=st1[0:bs, 2:3], in_=st1[0:bs, 1:2]), R=["st1"], W=["st1"])
        f.op("dve", lambda e: e.tensor_scalar(out=xn[0:bs, :], in0=x_ap, scalar1=st1[0:bs, 2:3], scalar2=None, op0=ALU.mult),
             R=[xres, "st1"], W=["xn"])
        for half in range(2):
            pt, pres = self.pb()
            pbf = pt.bitcast(BF16)
            for k in range(8):
                kc = half * 8 + k
                f.op("pe", lambda e, k=k, kc=kc, pbf=pbf: e.transpose(out=pbf[:, k * bs:(k + 1) * bs], in_=xn[0:bs, kc * 128:(kc + 1) * 128],
                                                                      identity=self.identb[0:bs, 0:bs]),
                     R=["xn", "CBc"], W=[pres])
            f.op("dve", lambda e, half=half, pbf=pbf: e.tensor_tensor(
                out=outT[:, half * 8:half * 8 + 8, c0:c0 + bs],
                in0=pbf[:, 0:8 * bs].rearrange("p (k t) -> p k t", k=8),
                in1=bc(gT[:, half * 8:half * 8 + 8], [128, 8, bs]), op=ALU.mult),
                R=[pres, "PV"], W=[outres])

    def rsqrt_bc(self, ps2, ps2res, n, scale):
        f = self.f
        t1, r1 = self.tf()
        f.op("act", lambda e: e.activation(out=t1[:, 0:n], in_=ps2, func=AF.Sqrt, scale=scale, bias=EPS), R=[ps2res], W=[r1])
        f.op("dve", lambda e: e.reciprocal(out=t1[:, 0:n], in_=t1[:, 0:n]), R=[r1], W=[r1])
        return t1[:, 0:n], r1

    def seq_start(self, seq):
        f = self.f
        io = self.io
        f.op("pool", lambda e: e.memset(self.hst[:], 0.0), W=["hst"])
        f.op("pool", lambda e: e.memset(self.hbf[:], 0.0), W=["hbf"])
        f.op("pool", lambda e: e.memset(self.hist[:], 0.0), W=["hist"])
        self.worder = [("A", SA_MEM + i) for i in range(8)]
        self.wpos = 0
        stage = self.stage
        SR = self.STAGE
        kst = stage[:, 0:2048].rearrange("p (m c) -> p m c", m=2)
        vst = stage[:, 2048:4096].rearrange("p (m c) -> p m c", m=2)
        for mt in range(2):
            xt = self.xtok
            f.dma("pool", xt[:], io["memp"][seq, mt * 128:(mt + 1) * 128, :], W=["x0"])
            self.norm_T(xt[:], "x0", 128, self.gmemT, self.hT, "hT", 0)
            for s in range(8):
                wt, wres = self.slab("A", SA_MEM + s)
                pt, pres = self.pb()
                self.tm_mm(pt[:, 0:256], pres, wt, wres, 256, self.hT, "hT", 0, 128)
                if s < 4:
                    hm = s
                    st1 = self.st1
                    jk, jkres = self.tb()
                    f.op("dve", lambda e: e.memset(st1[:, 4:5], 0.0), W=["st1b"])
                    f.op("act", lambda e, pt=pt, jk=jk: e.activation(out=jk[:, 0:256], in_=pt[:, 0:256], func=AF.Square,
                                                                     accum_out=st1[:, 4:5]), R=[pres], W=[jkres, "st1b"])
                    f.op("act", lambda e: e.activation(out=st1[:, 5:6], in_=st1[:, 4:5], func=AF.Sqrt, scale=1.0 / 256, bias=EPS),
                         R=["st1b"], W=["st1b"])
                    f.op("dve", lambda e: e.reciprocal(out=st1[:, 6:7], in_=st1[:, 5:6]), R=["st1b"], W=["st1b"])
                    f.op("dve", lambda e, pt=pt, mt=mt, hm=hm: e.scalar_tensor_tensor(
                        out=kst[:, mt, hm * 256:(hm + 1) * 256], in0=pt[:, 0:256], scalar=st1[:, 6:7], in1=self.gkm,
                        op0=ALU.mult, op1=ALU.mult), R=[pres, "st1b", "PR"], W=SR)
                else:
                    hm = s - 4
                    f.op("act", lambda e, pt=pt, mt=mt, hm=hm: e.activation(out=vst[:, mt, hm * 256:(hm + 1) * 256], in_=pt[:, 0:256],
                                                                           func=AF.Copy), R=[pres], W=SR)
            self.worder = [("A", SA_MEM + i) for i in range(8)]
            self.wpos = 0
        f.dma("pool", io["pmk"][seq].rearrange("(m p) c -> p m c", p=128), kst, R=SR)
        f.dma("pool", io["pmv"][seq].rearrange("(m p) c -> p m c", p=128), vst, R=SR)
        f.op("pool", lambda e: e.tensor_copy(out=self.Vm[:], in_=vst), R=SR, W=["Vm"])
        kb = self.xn
        f.op("dve", lambda e: e.tensor_copy(out=kb[:], in_=stage[:, 0:2048]), R=SR, W=["xn"])
        self.kmem_T(kb, "xn", self.KTm, "KTm")

    def kmem_T(self, kb, kres, KTm, ktres):
        f = self.f
        kbv = kb[:, 0:2048].rearrange("p (m c) -> p m c", m=2)
        for mt in range(2):
            pt, pres = self.pb()
            pbf = pt.bitcast(BF16)
            for c in range(8):
                f.op("pe", lambda e, c=c, mt=mt, pbf=pbf: e.transpose(out=pbf[:, c * 128:(c + 1) * 128], in_=kbv[:, mt, c * 128:(c + 1) * 128],
                                                                      identity=self.identb), R=[kres, "CBc"], W=[pres])
            f.op("act", lambda e, mt=mt, pbf=pbf: e.activation(out=KTm[:, :, mt * 128:(mt + 1) * 128],
                                                               in_=pbf[:, 0:1024].rearrange("p (c t) -> p c t", c=8), func=AF.Copy),
                 R=[pres], W=[ktres])

    def qk_evac(self, pt, pres, n, gcol, out_ap, outres, out32=None, out32res=None):
        f = self.f
        sq, sqres = self.tb()
        f.op("act", lambda e: e.activation(out=sq[:, 0:n], in_=pt[:, 0:n], func=AF.Square), R=[pres], W=[sqres])
        p2, p2res = self.pb()
        f.op("pe", lambda e: e.matmul(p2[:, 0:n], lhsT=self.blockones, rhs=sq[:, 0:n], start=True, stop=True),
             R=[sqres, "CBc"], W=[p2res])
        rr, rres = self.rsqrt_bc(p2[:, 0:n], p2res, n, 1.0 / 64)
        f.op("dve", lambda e: e.scalar_tensor_tensor(out=out_ap, in0=pt[:, 0:n], scalar=gcol, in1=rr, op0=ALU.mult, op1=ALU.mult),
             R=[pres, rres, "PV", "SM"], W=[outres])
        if out32 is not None:
            f.op("dve", lambda e: e.scalar_tensor_tensor(out=out32, in0=pt[:, 0:n], scalar=gcol, in1=rr, op0=ALU.mult, op1=ALU.mult),
                 R=[pres, rres, "PV", "SM"], W=[out32res])

    def in_proj_fm(self, hT, ntok, qT, kT_out, xbc_out, qmT, last3=None, k32=None, parts=("q", "k", "xbc", "qm")):
        f = self.f
        pend = []

        def flush(keep=0):
            while len(pend) > keep:
                pend.pop(0)()
        if "q" in parts:
            for s in range(4):
                wt, wres = self.slab("A", SA_Q + s)
                for mm in range(2):
                    c = s * 2 + mm
                    pt, pres = self.pb()
                    self.fm_mm(pt[:, 0:ntok], pres, wt, wres, mm * 128, hT, "hT", ntok)
                    flush(0)
                    pend.append(lambda pt=pt, pres=pres, c=c: self.qk_evac(pt, pres, ntok, self.gq8, qT[:, c, 0:ntok], "qT"))
                yield
        if "k" in parts:
            wt, wres = self.slab("A", SA_K)
            for pair in range(2):
                pt, pres = self.pb()
                self.fm_mm(pt[:, 0:ntok], pres, wt, wres, pair * 128, hT, "hT", ntok)
                flush(0)
                if k32 is not None:
                    pend.append(lambda pt=pt, pres=pres, pair=pair: self.qk_evac(pt, pres, ntok, self.gk, kT_out(pair), "kT", out32=k32(pair), out32res="lastst"))
                else:
                    pend.append(lambda pt=pt, pres=pres, pair=pair: self.qk_evac(pt, pres, ntok, self.gk, kT_out(pair), "kT"))
            yield
        if "xbc" in parts:
            for s in range(12):
                wt, wres = self.slab("A", SA_XBC + s)
                if getattr(self, "xbc_hook", None) is not None:
                    self.xbc_hook(s, wt, wres)
                for mm in range(2):
                    c = s * 2 + mm
                    pt, pres = self.pb()
                    self.fm_mm(pt[:, 0:ntok], pres, wt, wres, mm * 128, hT, "hT", ntok)
                    flush(0)
                    f.op("act", lambda e, pt=pt, c=c: e.activation(out=xbc_out(c), in_=pt[:, 0:ntok], func=AF.Copy), R=[pres], W=[("xbcT", c)])
                    if last3 is not None:
                        f.op("dve", lambda e, pt=pt, c=c: e.tensor_copy(out=last3[:, c, :], in_=pt[:, ntok - 4:ntok]), R=[pres], W=["lastst"])
                yield
        flush(0)
        if "qm" in parts:
            for hm in range(4):
                wt, wres = self.slab("A", SA_QM + hm)
                pa, ares = self.pb()
                pbk, bres = self.pb()
                self.fm_mm(pa[:, 0:ntok], ares, wt, wres, 0, hT, "hT", ntok)
                self.fm_mm(pbk[:, 0:ntok], bres, wt, wres, 128, hT, "hT", ntok)
                flush(0)

                def qm_evac(pa=pa, ares=ares, pbk=pbk, bres=bres, hm=hm):
                    sq, sqres = self.tb()
                    f.op("act", lambda e: e.activation(out=sq[:, 0:ntok], in_=pa[:, 0:ntok], func=AF.Square), R=[ares], W=[sqres])
                    f.op("act", lambda e: e.activation(out=sq[:, 256:256 + ntok], in_=pbk[:, 0:ntok], func=AF.Square), R=[bres], W=[sqres])
                    p2, p2res = self.pb()
                    f.op("pe", lambda e: e.matmul(p2[:, 0:ntok], lhsT=self.onesb, rhs=sq[:, 0:ntok], start=True, stop=False),
                         R=[sqres, "CBc"], W=[p2res])
                    f.op("pe", lambda e: e.matmul(p2[:, 0:ntok], lhsT=self.onesb, rhs=sq[:, 256:256 + ntok], start=False, stop=True),
                         R=[sqres, "CBc"], W=[p2res])
                    rr, rres = self.rsqrt_bc(p2[:, 0:ntok], p2res, ntok, 1.0 / 256)
                    for dc, (pp, ppres) in enumerate(((pa, ares), (pbk, bres))):
                        f.op("dve", lambda e, pp=pp, dc=dc: e.scalar_tensor_tensor(
                            out=qmT[:, hm * 2 + dc, 0:ntok], in0=pp[:, 0:ntok], scalar=self.gqm[:, dc:dc + 1], in1=rr,
                            op0=ALU.mult, op1=ALU.mult), R=[ppres, rres, "PV"], W=["qmT"])
                pend.append(qm_evac)
                yield
        flush(0)

    def swa_heads(self, q_rhs, kprev, kcur, vprev, vcur, nq, kcn, Dmp, Dmc, out, R_in, W_out):
        f = self.f
        n4 = 4 * nq
        for h in range(4):
            PTs = []
            for which in (0, 1):
                if which == 0 and kprev is None:
                    continue
                kk = kprev(h) if which == 0 else kcur(h)
                kn = 128 if which == 0 else kcn
                Dm = Dmp if which == 0 else Dmc
                pt, pres = self.pb()
                f.op("pe", lambda e, pt=pt, kk=kk, kn=kn, h=h: e.matmul(pt[0:kn, 0:n4], lhsT=kk, rhs=q_rhs(h), start=True, stop=True),
                     R=R_in, W=[pres])
                tt, tres = self.tf()
                for g in range(4):
                    sl = -SLOPES[h * 4 + g]
                    f.op("dve", lambda e, pt=pt, tt=tt, g=g, sl=sl, kn=kn, Dm=Dm: e.scalar_tensor_tensor(
                        out=tt[0:kn, g * nq:(g + 1) * nq], in0=Dm, scalar=sl, in1=pt[0:kn, g * nq:(g + 1) * nq],
                        op0=ALU.mult, op1=ALU.add), R=[pres, "C", "PV"], W=[tres])
                PT, ptres = self.tb()
                f.op("act", lambda e, PT=PT, tt=tt, kn=kn: e.activation(out=PT[0:kn, 0:n4], in_=tt[0:kn, 0:n4], func=AF.Exp),
                     R=[tres], W=[ptres])
                PTs.append((PT, ptres, kn, which))
            yield
            po, pores = self.pb()
            pd, pdres = self.pb()
            nP = len(PTs)
            for i, (PT, ptres, kn, which) in enumerate(PTs):
                vv = vprev(h) if which == 0 else vcur(h)
                f.op("pe", lambda e, PT=PT, kn=kn, vv=vv, i=i, po=po: e.matmul(po[0:64, 0:n4], lhsT=vv, rhs=PT[0:kn, 0:n4],
                                                                              start=(i == 0), stop=(i == len(PTs) - 1)),
                     R=R_in + [ptres], W=[pores])
            for i, (PT, ptres, kn, which) in enumerate(PTs):
                f.op("pe", lambda e, PT=PT, kn=kn, i=i, pd=pd: e.matmul(pd[0:64, 0:n4], lhsT=self.onesb[0:kn, 0:64], rhs=PT[0:kn, 0:n4],
                                                                       start=(i == 0), stop=(i == nP - 1)),
                     R=["CBc", ptres], W=[pdres])
            dn, dnres = self.tf()
            f.op("dve", lambda e, h=h, dn=dn, pd=pd: e.tensor_tensor(
                out=dn[0:64, 0:n4].rearrange("p (g q) -> p g q", g=4), in0=pd[0:64, 0:n4].rearrange("p (g q) -> p g q", g=4),
                in1=bc(self.esink[0:64, h * 4:h * 4 + 4], [64, 4, nq]), op=ALU.add), R=[pdres, "SM"], W=[dnres])
            f.op("dve", lambda e, dn=dn: e.reciprocal(out=dn[0:64, 0:n4], in_=dn[0:64, 0:n4]), R=[dnres], W=[dnres])
            f.op("dve", lambda e, h=h, dn=dn, po=po: e.tensor_tensor(
                out=out(h), in0=po[0:64, 0:n4].rearrange("p (g q) -> p g q", g=4),
                in1=dn[0:64, 0:n4].rearrange("p (g q) -> p g q", g=4), op=ALU.mult), R=[pores, dnres], W=W_out)
            yield

    def mem_heads(self, q_rhs, KTm, ktres, Vm, vres, nq, out, R_in, W_out):
        f = self.f
        for hm in range(4):
            pt, pres = self.pb()
            for mt in range(2):
                for dc in range(2):
                    f.op("pe", lambda e, mt=mt, dc=dc, hm=hm, pt=pt: e.matmul(
                        pt[:, mt * nq:(mt + 1) * nq], lhsT=KTm[:, hm * 2 + dc, mt * 128:(mt + 1) * 128], rhs=q_rhs(hm * 2 + dc),
                        start=(dc == 0), stop=(dc == 1)), R=R_in + [ktres], W=[pres])
            PT, ptres = self.tb()
            f.op("act", lambda e, PT=PT, pt=pt: e.activation(out=PT[:, 0:2 * nq], in_=pt[:, 0:2 * nq], func=AF.Exp, scale=1.0 / 16),
                 R=[pres], W=[ptres])
            yield
            pd, pdres = self.pb()
            for mt in range(2):
                f.op("pe", lambda e, mt=mt, PT=PT, pd=pd: e.matmul(pd[:, 0:nq], lhsT=self.onesb, rhs=PT[:, mt * nq:(mt + 1) * nq],
                                                                   start=(mt == 0), stop=(mt == 1)), R=["CBc", ptres], W=[pdres])
            dn, dnres = self.tf()
            f.op("dve", lambda e, dn=dn, pd=pd: e.reciprocal(out=dn[:, 0:nq], in_=pd[:, 0:nq]), R=[pdres], W=[dnres])
            po, pores = self.pb()
            for dc in range(2):
                for mt in range(2):
                    f.op("pe", lambda e, mt=mt, dc=dc, hm=hm, PT=PT, po=po: e.matmul(
                        po[:, dc * nq:(dc + 1) * nq], lhsT=Vm[:, mt, hm * 256 + dc * 128: hm * 256 + (dc + 1) * 128],
                        rhs=PT[:, mt * nq:(mt + 1) * nq], start=(mt == 0), stop=(mt == 1)), R=[vres, ptres], W=[pores])
            for dc in range(2):
                f.op("dve", lambda e, dc=dc, hm=hm, dn=dn, po=po: e.tensor_tensor(out=out(hm * 2 + dc), in0=po[:, dc * nq:(dc + 1) * nq],
                                                                                 in1=dn[:, 0:nq], op=ALU.mult), R=[pores, dnres], W=W_out)
            yield

    def conv_fm(self, tap, ntok, out, res_of, save_hist=None):
        f = self.f
        for c in range(24):
            acc, ares = self.tf()
            f.op("dve", lambda e, c=c, acc=acc: e.tensor_scalar(out=acc[:, 0:ntok], in0=tap(c, 0), scalar1=self.cwT[:, c * 4:c * 4 + 1],
                                                                scalar2=self.cbT[:, c:c + 1], op0=ALU.mult, op1=ALU.add),
                 R=[res_of(c), "PV", "xbch"], W=[ares])
            for j in range(1, 4):
                f.op("dve", lambda e, c=c, j=j, acc=acc: e.scalar_tensor_tensor(
                    out=acc[:, 0:ntok], in0=tap(c, j), scalar=self.cwT[:, c * 4 + j:c * 4 + j + 1], in1=acc[:, 0:ntok],
                    op0=ALU.mult, op1=ALU.add), R=[res_of(c), "PV", ares, "xbch"], W=[ares])
            if save_hist is not None:
                save_hist(c)
            f.op("act", lambda e, c=c, acc=acc: e.activation(out=out(c), in_=acc[:, 0:ntok], func=AF.Silu), R=[ares], W=[res_of(c)])
            if c % 2 == 1:
                yield

    def ssd_block(self):
        f = self.f
        S = self.ssd_s
        cv = lambda c: self.xbcT[:, c, 3:3 + 128]
        XS = [("xbcT", c) for c in range(16)]
        BMr = [("xbcT", 16 + g) for g in range(4)]
        CMr = [("xbcT", 20 + g) for g in range(4)]
        for half in range(2):
            pt, pres = self.pb()
            pbf = pt.bitcast(BF16)
            for k in range(8):
                f.op("pe", lambda e, k=k, half=half, pbf=pbf: e.transpose(out=pbf[:, k * 128:(k + 1) * 128], in_=cv(half * 8 + k),
                                                                          identity=self.identb), R=[("xbcT", half * 8 + k), "CBc"], W=[pres])
            f.op("act", lambda e, half=half, pbf=pbf: e.activation(out=self.xs_tok[:, half * 1024:(half + 1) * 1024], in_=pbf[:, 0:1024],
                                                                   func=AF.Copy), R=[pres], W=["xs_tok"])
        pt, pres = self.pb()
        pbf = pt.bitcast(BF16)
        for g in range(4):
            f.op("pe", lambda e, g=g, pbf=pbf: e.transpose(out=pbf[:, g * 128:(g + 1) * 128], in_=cv(16 + g), identity=self.identb),
                 R=[BMr[g], "CBc"], W=[pres])
        f.op("act", lambda e, pbf=pbf: e.activation(out=self.bm_tok[:], in_=pbf[:, 0:512], func=AF.Copy), R=[pres], W=["bm_tok"])
        yield
        dt = S[:, 32:64]
        f.op("dve", lambda e: e.tensor_tensor(out=S[:, 0:32], in0=self.dtraw[:], in1=self.dtb, op=ALU.add), R=["dtraw", "PR"], W=["S"])
        f.op("act", lambda e: e.activation(out=S[:, 0:32], in_=S[:, 0:32], func=AF.Exp), R=["S"], W=["S"])
        f.op("act", lambda e: e.activation(out=dt, in_=S[:, 0:32], func=AF.Ln, bias=1.0), R=["S"], W=["S"])
        f.op("dve", lambda e: e.tensor_tensor(out=S[:, 64:96], in0=dt, in1=self.a_bc, op=ALU.mult), R=["S", "SM"], W=["S"])
        f.op("dve", lambda e: e.tensor_copy(out=self.dA4[:], in_=S[:, 64:96].unsqueeze(1).broadcast_to([128, 4, 32])), R=["S"], W=["dA4"])
        yield
        pa, pares = self.pb()
        for i, lh in enumerate((self.Tm, self.Um, self.onesf)):
            f.op("pe", lambda e, i=i, lh=lh: e.matmul(pa[:, i * 32:(i + 1) * 32], lhsT=lh, rhs=S[:, 64:96], start=True, stop=True),
                 R=["S", "C"], W=[pares])
        f.op("pe", lambda e: e.matmul(pa[:, 128:256], lhsT=self.dA4[:].rearrange("p a b -> p (a b)"), rhs=self.Tm, start=True, stop=True),
             R=["dA4", "C"], W=[pares])
        f.op("act", lambda e: e.activation(out=S[:, 96:192], in_=pa[:, 0:96], func=AF.Exp), R=[pares], W=["S"])
        eacs, dte, cd = S[:, 96:128], S[:, 128:160], S[:, 160:192]
        f.op("dve", lambda e: e.tensor_tensor(out=S[:, 192:224], in0=dt, in1=dte, op=ALU.mult), R=["S"], W=["S"])
        xs3 = self.xs_tok.rearrange("p (h d) -> p h d", h=32)
        f.op("dve", lambda e: e.tensor_tensor(out=self.xdt.rearrange("p (h d) -> p h d", h=32), in0=xs3, in1=bc(dt, [128, 32, 64]),
                                              op=ALU.mult), R=["xs_tok", "S"], W=["xdt"])
        f.op("dve", lambda e: e.tensor_tensor(out=self.xw.rearrange("p (h d) -> p h d", h=32), in0=xs3,
                                              in1=bc(S[:, 192:224], [128, 32, 64]), op=ALU.mult), R=["xs_tok", "S"], W=["xw"])
        HI, LO = self.HI, self.LO
        f.op("act", lambda e: e.activation(out=HI[:], in_=pa[:, 128:256], func=AF.Copy), R=[pares], W=["HI"])
        f.op("dve", lambda e: e.tensor_tensor(out=LO[:], in0=pa[:, 128:256], in1=HI[:], op=ALU.subtract), R=[pares, "HI"], W=["LO"])
        f.op("act", lambda e: e.mul(out=self.lhsD[0:32, :], in_=HI[0:32, :], mul=-1.0), R=["HI"], W=["lhsD"])
        f.op("act", lambda e: e.mul(out=self.lhsD[32:64, :], in_=LO[32:64, :], mul=-1.0), R=["LO"], W=["lhsD"])
        f.op("dve", lambda e: e.tensor_tensor(out=self.rhsD[64:96], in0=HI[64:96, :].unsqueeze(1).broadcast_to([32, 32, 128]),
                                              in1=bc(self.ohs[64:96], [32, 32, 128]), op=ALU.mult), R=["HI", "C"], W=["rhsD"])
        f.op("dve", lambda e: e.tensor_tensor(out=self.rhsD[96:128], in0=LO[96:128, :].unsqueeze(1).broadcast_to([32, 32, 128]),
                                              in1=bc(self.ohs[96:128], [32, 32, 128]), op=ALU.mult), R=["LO", "C"], W=["rhsD"])
        yield
        pc, pcres = self.pb()
        for g in range(4):
            f.op("pe", lambda e, g=g: e.matmul(pc[:, g * 128:(g + 1) * 128], lhsT=cv(16 + g), rhs=cv(20 + g), start=True, stop=True),
                 R=[BMr[g], CMr[g]], W=[pcres])
        f.op("act", lambda e: e.activation(out=self.CBt[:].rearrange("p g l -> p (g l)"), in_=pc[:, 0:512], func=AF.Copy), R=[pcres], W=["CBt"])
        f.op("dve", lambda e: e.memset(S[:, 224:228], 0.0), W=["Sq"])
        yield
        for g in range(4):
            pyd, pydres = self.pb()
            pyo, pyores = self.pb()
            f.op("pe", lambda e, g=g, pyo=pyo: e.matmul(pyo[:, 0:512], lhsT=cv(20 + g), rhs=self.hbf[:, g * 512:(g + 1) * 512], start=True, stop=True),
                 R=[CMr[g], "hbf"], W=[pyores])
            for jj in range(2):
                j = g * 2 + jj
                pD, pDres = self.pb()
                f.op("pe", lambda e, j=j, pD=pD: e.matmul(pD[:, 0:512], lhsT=self.lhsD[:], rhs=self.rhsD[:, 4 * j:4 * j + 4, :].rearrange("p a b -> p (a b)"),
                                                          start=True, stop=False), R=["lhsD", "rhsD"], W=[pDres])
                f.op("pe", lambda e, pD=pD: e.matmul(pD[:, 0:512], lhsT=self.identb, rhs=self.maskD, start=False, stop=True),
                     R=["CBc"], W=[pDres])
                E, eres = self.tf()
                f.op("act", lambda e, E=E, pD=pD: e.activation(out=E[:, 0:512], in_=pD[:, 0:512], func=AF.Exp), R=[pDres], W=[eres])
                Wp = self.Wp[jj]
                f.op("dve", lambda e, E=E, Wp=Wp, g=g: e.tensor_tensor(
                    out=Wp[:], in0=E[:, 0:512].rearrange("p (a l) -> p a l", a=4),
                    in1=self.CBt[:, g, :].unsqueeze(1).broadcast_to([128, 4, 128]), op=ALU.mult), R=[eres, "CBt"], W=[("Wp", jj)])
            yield
            for jj in range(2):
                j = g * 2 + jj
                Wp = self.Wp[jj]
                for h4 in range(4):
                    h = 4 * j + h4
                    f.op("pe", lambda e, Wp=Wp, h4=h4, h=h, pyd=pyd: e.matmul(pyd[:, (h % 8) * 64:(h % 8 + 1) * 64], lhsT=Wp[:, h4, :],
                                                                             rhs=self.xdt[:, h * 64:(h + 1) * 64], start=True, stop=True),
                         R=[("Wp", jj), "xdt"], W=[pydres])
            y1, y1res = self.tf()
            f.op("dve", lambda e, g=g, y1=y1, pyo=pyo: e.tensor_tensor(
                out=y1[:, 0:512].rearrange("p (h d) -> p h d", h=8), in0=pyo[:, 0:512].rearrange("p (h d) -> p h d", h=8),
                in1=bc(eacs[:, g * 8:(g + 1) * 8], [128, 8, 64]), op=ALU.mult), R=[pyores, "S"], W=[y1res])
            f.op("dve", lambda e, y1=y1, pyd=pyd: e.tensor_tensor(out=y1[:, 0:512], in0=y1[:, 0:512], in1=pyd[:, 0:512], op=ALU.add),
                 R=[pydres, y1res], W=[y1res])
            y3, y3res = self.tf()
            f.op("dve", lambda e, g=g, y3=y3: e.tensor_tensor(
                out=y3[:, 0:512].rearrange("p (h d) -> p h d", h=8), in0=self.xs_tok[:, g * 512:(g + 1) * 512].rearrange("p (h d) -> p h d", h=8),
                in1=bc(self.dsk[:, g * 8:(g + 1) * 8], [128, 8, 64]), op=ALU.mult), R=["xs_tok", "PR"], W=[y3res])
            f.op("dve", lambda e, y1=y1, y3=y3: e.tensor_tensor(out=y1[:, 0:512], in0=y1[:, 0:512], in1=y3[:, 0:512], op=ALU.add),
                 R=[y1res, y3res], W=[y1res])
            f.op("dve", lambda e, g=g, y1=y1: e.tensor_tensor(out=y1[:, 0:512], in0=y1[:, 0:512],
                                                              in1=self.zs[:, g * 512:(g + 1) * 512], op=ALU.mult),
                 R=[y1res, "zs"], W=[y1res])
            f.op("act", lambda e, g=g, y1=y1, y3=y3: e.activation(out=y3[:, 0:512], in_=y1[:, 0:512], func=AF.Square,
                                                                  accum_out=S[:, 224 + g:225 + g]), R=[y1res], W=[y3res, "Sq"])
            f.op("act", lambda e, g=g: e.activation(out=S[:, 228 + g:229 + g], in_=S[:, 224 + g:225 + g], func=AF.Sqrt, scale=1.0 / 512, bias=EPS),
                 R=["Sq"], W=["Sq"])
            f.op("dve", lambda e, g=g: e.reciprocal(out=S[:, 232 + g:233 + g], in_=S[:, 228 + g:229 + g]), R=["Sq"], W=["Sq"])
            pst, pstres = self.pb()
            f.op("pe", lambda e, g=g, pst=pst: e.matmul(pst[:, 0:512], lhsT=self.bm_tok[:, g * 128:(g + 1) * 128],
                                                        rhs=self.xw[:, g * 512:(g + 1) * 512], start=True, stop=True),
                 R=["bm_tok", "xw"], W=[pstres])
            f.op("dve", lambda e, g=g, y1=y1: e.tensor_scalar(out=self.s_tok[:, g * 512:(g + 1) * 512], in0=y1[:, 0:512],
                                                              scalar1=S[:, 232 + g:233 + g], scalar2=None, op0=ALU.mult), R=[y1res, "Sq"], W=["s_tok"])
            hg = self.hst[:, g * 512:(g + 1) * 512]
            f.op("dve", lambda e, g=g, hg=hg: e.tensor_tensor(out=hg.rearrange("p (h d) -> p h d", h=8), in0=hg.rearrange("p (h d) -> p h d", h=8),
                                                              in1=bc(cd[:, g * 8:(g + 1) * 8], [128, 8, 64]), op=ALU.mult),
                 R=["S", "hst"], W=["hst"])
            f.op("dve", lambda e, hg=hg, pst=pst: e.tensor_tensor(out=hg, in0=hg, in1=pst[:, 0:512], op=ALU.add), R=[pstres, "hst"], W=["hst"])
            f.op("pool", lambda e, g=g, hg=hg: e.tensor_copy(out=self.hbf[:, g * 512:(g + 1) * 512], in_=hg), R=["hst"], W=["hbf"])
            yield
        for half in range(2):
            yield
            pt, pres = self.pb()
            pbf = pt.bitcast(BF16)
            for k in range(8):
                kc = half * 8 + k
                f.op("pe", lambda e, k=k, kc=kc, pbf=pbf: e.transpose(out=pbf[:, k * 128:(k + 1) * 128], in_=self.s_tok[:, kc * 128:(kc + 1) * 128],
                                                                      identity=self.identb), R=["s_tok", "CBc"], W=[pres])
            f.op("dve", lambda e, half=half, pbf=pbf: e.tensor_tensor(
                out=self.sT[:, half * 8:half * 8 + 8, 0:128], in0=pbf[:, 0:1024].rearrange("p (k t) -> p k t", k=8),
                in1=bc(self.ssdnT[:, half * 8:half * 8 + 8], [128, 8, 128]), op=ALU.mult), R=[pres, "PV"], W=["sT"])

    def merge_out(self, ntok, bs, xres, x_ap, aT, sT, mT, hT, mergedT, mres_of, h2T):
        f = self.f
        actT = self.actT
        for s in range(8):
            for bi in range(3):
                gk = ("A", (SA_G0, SA_G1, SA_G2)[bi] + s)
                uk = (("S", s), ("A", SA_USSD + s), ("M", s))[bi]
                (gw, gwr), (uw, uwr) = self.slabs([gk, uk])
                for mm in range(2):
                    m = s * 2 + mm
                    col0 = mm * 128
                    acc = self.macc[mm]
                    accres = ("macc", mm)
                    pg, pgres = self.pb()
                    self.fm_mm(pg[:, 0:ntok], pgres, gw, gwr, col0, hT, "hT", ntok)
                    sg, sgres = self.tf()
                    f.op("act", lambda e, sg=sg, pg=pg: e.activation(out=sg[:, 0:ntok], in_=pg[:, 0:ntok], func=AF.Sigmoid), R=[pgres], W=[sgres])
                    pu, pures = self.pb()
                    if bi == 0:
                        uav = uw[0:64, 0:4096].rearrange("p (h c) -> p h c", h=16)
                        for hd in range(16):
                            f.op("pe", lambda e, hd=hd, pu=pu, uav=uav, col0=col0: e.matmul(pu[:, 0:ntok], lhsT=uav[:, hd, col0:col0 + 128], rhs=aT[0:64, hd, 0:ntok],
                                                                                         start=(hd == 0), stop=(hd == 15)), R=[uwr, "aT"], W=[pures])
                        f.op("dve", lambda e, acc=acc, sg=sg, pu=pu: e.tensor_tensor(out=acc[:, 0:ntok], in0=sg[:, 0:ntok], in1=pu[:, 0:ntok], op=ALU.mult),
                             R=[sgres, pures], W=[accres])
                    else:
                        if bi == 1:
                            self.fm_mm(pu[:, 0:ntok], pures, uw, uwr, col0, sT, "sT", ntok)
                        else:
                            self.fm_mm(pu[:, 0:ntok], pures, uw, uwr, col0, mT, "mT", ntok, kcs=8)
                        f.op("dve", lambda e, sg=sg, pu=pu: e.tensor_tensor(out=sg[:, 0:ntok], in0=sg[:, 0:ntok], in1=pu[:, 0:ntok], op=ALU.mult),
                             R=[sgres, pures], W=[sgres])
                        if bi == 1:
                            f.op("dve", lambda e, acc=acc, sg=sg: e.tensor_tensor(out=acc[:, 0:ntok], in0=acc[:, 0:ntok], in1=sg[:, 0:ntok], op=ALU.add),
                                 R=[sgres, accres], W=[accres])
                        else:
                            f.op("dve", lambda e, acc=acc, sg=sg, m=m: e.tensor_tensor(out=mergedT(m), in0=acc[:, 0:ntok], in1=sg[:, 0:ntok], op=ALU.add),
                                 R=[sgres, accres], W=[mres_of(m)])
        MR = [mres_of(m) for m in range(16)]
        for s in range(8):
            wt, wres = self.slab("A", SA_OUT + s)
            wv = wt[:, 0:4096].rearrange("p (k c) -> p k c", k=16)
            pt, pres = self.pb()
            for kc in range(16):
                f.op("pe", lambda e, kc=kc, pt=pt, wv=wv: e.matmul(pt[0:bs, 0:256], lhsT=mergedT(kc)[:, 0:bs], rhs=wv[:, kc, :],
                                                                   start=(kc == 0), stop=(kc == 15)), R=[wres, mres_of(kc)], W=[pres])
            xa = x_ap[:, s * 256:(s + 1) * 256]
            f.op("dve", lambda e, xa=xa, pt=pt: e.tensor_tensor(out=xa, in0=xa, in1=pt[0:bs, 0:256], op=ALU.add), R=[pres, xres], W=[xres])
        self.norm_T(x_ap, xres, bs, self.gffnT, h2T, "h2T", 0)

    def ffn_gen(self, ntok, bs, xres, x_ap, hT):
        f = self.f
        actT = self.actT
        for s in range(22):
            (gw, gwr), (uw, uwr) = self.slabs([("A", SA_GATE + s), ("A", SA_UP + s)])
            for mm in range(2):
                j = s * 2 + mm
                pg, pgres = self.pb()
                pu, pures = self.pb()
                self.fm_mm(pg[:, 0:ntok], pgres, gw, gwr, mm * 128, hT, "h2T", ntok)
                self.fm_mm(pu[:, 0:ntok], pures, uw, uwr, mm * 128, hT, "h2T", ntok)
                sg, sgres = self.tf()
                f.op("act", lambda e, sg=sg, pg=pg: e.activation(out=sg[:, 0:ntok], in_=pg[:, 0:ntok], func=AF.Silu), R=[pgres], W=[sgres])
                f.op("dve", lambda e, sg=sg, pu=pu, j=j: e.tensor_tensor(out=actT[:, j, 0:ntok], in0=sg[:, 0:ntok], in1=pu[:, 0:ntok], op=ALU.mult),
                     R=[sgres, pures], W=["actT"])
            yield
        for cg in range(8):
            pt, pres = self.pb()
            for qtr in range(4):
                wt, wres = self.slab("D", cg * 4 + qtr)
                wv = wt[:, 0:2816].rearrange("p (k c) -> p k c", k=11)
                for kc in range(11):
                    f.op("pe", lambda e, kc=kc, qtr=qtr, pt=pt, wv=wv: e.matmul(
                        pt[0:bs, 0:256], lhsT=actT[:, qtr * 11 + kc, 0:bs], rhs=wv[:, kc, :],
                        start=(qtr == 0 and kc == 0), stop=(qtr == 3 and kc == 10)), R=[wres, "actT"], W=[pres])
            xa = x_ap[:, cg * 256:(cg + 1) * 256]
            f.op("dve", lambda e, xa=xa, pt=pt: e.tensor_tensor(out=xa, in0=xa, in1=pt[0:bs, 0:256], op=ALU.add), R=[pres, xres], W=[xres])
            yield

    def order_in(self):
        o = [("A", SA_Q + i) for i in range(4)] + [("A", SA_K)] + [("A", SA_XBC + i) for i in range(12)]
        o += [("A", SA_QM + i) for i in range(4)] + [("A", SA_V)] + [("A", SA_Z + i) for i in range(8)] + [("T", 0)]
        return o

    def order_merge(self):
        o = []
        for s in range(8):
            o += [("A", SA_G0 + s), ("S", s), ("A", SA_G1 + s), ("A", SA_USSD + s), ("A", SA_G2 + s), ("M", s)]
        o += [("A", SA_OUT + i) for i in range(8)]
        return o

    def order_ffn(self):
        o = []
        for s in range(22):
            o += [("A", SA_GATE + s), ("A", SA_UP + s)]
        o += [("D", i) for i in range(32)]
        return o

    def phase_in(self, seq, st, par, last):
        f = self.f
        io = self.io
        t0 = st * NT
        xt = self.xtoks[par]
        xres = "x%d" % par
        f.dma("pool", xt[:], io["xp"][seq, t0:t0 + 128, :], W=[xres])
        self.norm_T(xt[:], xres, 128, self.gmixT, self.hT, "hT", 0)
        last3 = None
        k32 = None
        L = self.lastst
        if last:
            last3 = self.l3t[:, 0:96].rearrange("p (c j) -> p c j", c=24)
            k32t = L[:, 96:96 + 256].rearrange("p (a t) -> p a t", a=2)
            k32 = lambda pair: k32t[:, pair, :]
        f.op("pool", lambda e: e.tensor_copy(out=self.xbcT[:, :, 0:3], in_=self.hist[:]), R=["hist"], W=["xbch"])
        self.drain(self.in_proj_fm(self.hT, NT, self.qT, lambda pair: self.kT[:, pair, 128:256],
                                   lambda c: self.xbcT[:, c, 3:3 + NT], self.qmT, last3=last3, k32=k32, parts=("q", "k", "xbc")))
        self.interleave(self.phase_in_b(last), self.conv_gen(), 1, 1, None, None)

    def phase_in_b(self, last):
        f = self.f
        L = self.lastst
        yield from self.in_proj_fm(self.hT, NT, self.qT, None, None, self.qmT, parts=("qm",))
        wt, wres = self.slab("A", SA_V)
        pt, pres = self.pb()
        self.tm_mm(pt[:, 0:256], pres, wt, wres, 256, self.hT, "hT", 0, 128)
        f.op("act", lambda e, pt=pt: e.activation(out=self.vtok[:, 1, :], in_=pt[:, 0:256], func=AF.Copy), R=[pres], W=["vtok"])
        if last:
            f.op("dve", lambda e, pt=pt: e.tensor_copy(out=L[:, 352:608], in_=pt[:, 0:256]), R=[pres], W=["lastst"])
        yield
        for s in range(8):
            wt, wres = self.slab("A", SA_Z + s)
            pt, pres = self.pb()
            self.tm_mm(pt[:, 0:256], pres, wt, wres, 256, self.hT, "hT", 0, 128)
            f.op("act", lambda e, pt=pt, s=s: e.activation(out=self.zs[:, s * 256:(s + 1) * 256], in_=pt[:, 0:256], func=AF.Silu),
                 R=[pres], W=["zs"])
            yield
        wt, wres = self.slab("T", 0)
        pt, pres = self.pb()
        self.tm_mm(pt[:, 0:32], pres, wt, wres, 32, self.hT, "hT", 0, 128, cw=32)
        f.op("act", lambda e, pt=pt: e.activation(out=self.dtraw[:], in_=pt[:, 0:32], func=AF.Copy), R=[pres], W=["dtraw"])

    def conv_gen(self):
        f = self.f

        def save_hist(c):
            f.op("pool", lambda e, c=c: e.tensor_copy(out=self.hist[:, c, :], in_=self.xbcT[:, c, NT:NT + 3]), R=[("xbcT", c)], W=["hist"])
        yield from self.conv_fm(lambda c, j: self.xbcT[:, c, j:j + NT], NT, lambda c: self.xbcT[:, c, 3:3 + NT], lambda c: ("xbcT", c), save_hist=save_hist)

    def phase_mix(self, first):
        f = self.f

        has_prev = not first
        yield from self.swa_heads(
            q_rhs=lambda h: self.qT[64 * (h % 2):64 * (h % 2) + 64, (h // 2) * 4:(h // 2) * 4 + 4, 0:128],
            kprev=(lambda h: self.kT[64 * (h % 2):64 * (h % 2) + 64, h // 2, 0:128]) if has_prev else None,
            kcur=lambda h: self.kT[64 * (h % 2):64 * (h % 2) + 64, h // 2, 128:256],
            vprev=lambda h: self.vtok[:, 0, h * 64:(h + 1) * 64],
            vcur=lambda h: self.vtok[:, 1, h * 64:(h + 1) * 64],
            nq=128, kcn=128, Dmp=self.Dmp, Dmc=self.Dmd,
            out=lambda h: self.aT[0:64, h * 4:h * 4 + 4, 0:128],
            R_in=["qT", "kT", "vtok"], W_out=["aT"])
        f.op("pool", lambda e: e.tensor_copy(out=self.kT[:, :, 0:128], in_=self.kT[:, :, 128:256]), R=["kT"], W=["kT"])
        f.op("pool", lambda e: e.tensor_copy(out=self.vtok[:, 0, :], in_=self.vtok[:, 1, :]), R=["vtok"], W=["vtok"])
        yield from self.ssd_block()
        yield from self.mem_heads(q_rhs=lambda c: self.qmT[:, c, 0:NT], KTm=self.KTm, ktres="KTm", Vm=self.Vm, vres="Vm", nq=NT,
                                  out=lambda c: self.mT[:, c, 0:NT], R_in=["qmT"], W_out=["mT"])

    def phase_merge(self, par):
        self.merge_out(NT, 128, "x%d" % par, self.xtoks[par][:], self.aT, self.sT, self.mT, self.hT,
                       lambda m: self.xbcT[:, m, 3:3 + NT], lambda m: ("xbcT", m), self.h2T)

    @staticmethod
    def drain(gen):
        for _ in gen:
            pass

    def interleave(self, ga, gb, na=1, nb=1, ca="F", cb="M"):
        da = db = False
        while not (da and db):
            for _ in range(na):
                if not da:
                    self.ctx = ca
                    try:
                        next(ga)
                    except StopIteration:
                        da = True
            for _ in range(nb):
                if not db:
                    self.ctx = cb
                    try:
                        next(gb)
                    except StopIteration:
                        db = True
        self.ctx = None

    def run_sequence(self, seq, n_st):
        f = self.f
        io = self.io
        nfull = SEQ // NT
        is_last = lambda st: (st == nfull - 1) or (self.force_last and st == n_st - 1)
        self.worder = self.order_in() + self.order_merge()
        self.wpos = 0
        self.phase_in(seq, 0, 0, is_last(0))
        self.drain(self.phase_mix(True))
        for st in range(n_st):
            par = st % 2
            nxt = st + 1 < n_st
            self.worder = self.order_merge() + (self.order_in() if nxt else []) + self.order_ffn() + self.order_merge()
            self.wpos = 0
            self.phase_merge(par)
            ffn = self.ffn_gen(NT, 128, "x%d" % par, self.xtoks[par][:], self.h2T)
            if nxt:
                self.phase_in(seq, st + 1, 1 - par, is_last(st + 1))
                if self.pipeline:
                    self.interleave(ffn, self.phase_mix(False), self.il[0], self.il[1])
                else:
                    self.drain(ffn)
                    self.drain(self.phase_mix(False))
            else:
                self.drain(ffn)
            f.dma("pool", io["yp"][seq, st * NT:st * NT + 128, :], self.xtoks[par][:], R=["x%d" % par])
            if is_last(st):
                self.seq_last_outputs(seq)

    def seq_last_outputs(self, seq):
        f = self.f
        io = self.io
        L = self.lastst
        k32t = L[:, 96:96 + 256].rearrange("p (a t) -> p a t", a=2)
        if 'pkpv' in _SKIP:
            return
        ptk, pkres = self.pb()
        for pair in range(2):
            f.op("pe", lambda e, pair=pair, ptk=ptk: e.transpose(out=ptk[:, pair * 128:(pair + 1) * 128], in_=k32t[:, pair, :], identity=self.identf),
                 R=["lastst", "C"], W=[pkres])
        f.op("act", lambda e, ptk=ptk: e.activation(out=L[:, 608:864], in_=ptk[:, 0:256], func=AF.Copy), R=[pkres], W=["lastst"])
        f.dma("pool", io["pk"][seq], L[:, 608:864], R=["lastst"])
        f.dma("pool", io["pv"][seq], L[:, 352:608], R=["lastst"])
        if 'pconv' in _SKIP:
            return
        l3t = self.l3t
        pc3 = self.stage[0:4, 0:3072]
        for q6 in range(6):
            pt, pres = self.pb()
            for k in range(4):
                c = q6 * 4 + k
                f.op("pe", lambda e, k=k, c=c, pt=pt: e.transpose(out=pt[:, k * 128:(k + 1) * 128], in_=l3t[:, c * 4:c * 4 + 128], identity=self.identf),
                     R=["lastst", "C"], W=[pres])
            f.op("act", lambda e, q6=q6, pt=pt: e.activation(out=pc3[:, q6 * 512:(q6 + 1) * 512], in_=pt[0:4, 0:512], func=AF.Copy),
                 R=[pres], W=self.STAGE)
        f.dma("pool", io["pconv"][seq], pc3[1:4, :], R=self.STAGE)

    def seq_end(self, seq):
        if "seq_end" in _SKIP:
            return
        f = self.f
        io = self.io
        stage = self.stage
        SR = self.STAGE
        for q4 in range(4):
            pt, pres = self.pb()
            for k in range(4):
                c = q4 * 4 + k
                f.op("pe", lambda e, k=k, c=c, pt=pt: e.transpose(out=pt[:, k * 128:(k + 1) * 128], in_=self.hst[:, c * 128:(c + 1) * 128],
                                                                  identity=self.identf), R=["hst", "C"], W=[pres])
            f.op("act", lambda e, q4=q4, pt=pt: e.activation(out=stage[:, q4 * 512:(q4 + 1) * 512], in_=pt[:, 0:512], func=AF.Copy),
                 R=[pres], W=SR)
        f.dma("pool", io["pssm"][seq].rearrange("(c q) n -> q c n", q=128), stage[:, 0:2048].rearrange("p (c n) -> p c n", c=16), R=SR)

    def sample_group(self):
        f = self.f
        io = self.io
        NS = NSMP
        L = self.lastst
        stage = self.stage
        SR = self.STAGE
        x16 = self.xtok[0:NS, :]
        self.worder = self.order_in() + self.order_merge()
        self.wpos = 0
        f.dma("pool", io["sk"][:, 0:127, :], io["csk"][:, 1:128, :])
        f.dma("pool", io["sv"][:, 0:127, :], io["csv"][:, 1:128, :])
        f.dma("pool", io["sconv_o"][:, 0:2, :], io["sconv"].rearrange("(b j) c -> b j c", j=3)[:, 1:3, :])
        f.dma("pool", x16, io["xs"], W=["x0"])
        self.norm_T(x16, "x0", NS, self.gmixT, self.hT, "hT", 0)
        cbuf = self.cbuf
        f.dma("pool", stage[0:48, 0:3072], io["sconv"], W=SR)
        for q6 in range(6):
            pt, pres = self.pb()
            for k in range(4):
                c = q6 * 4 + k
                f.op("pe", lambda e, k=k, c=c, pt=pt: e.transpose(out=pt[:, k * 48:(k + 1) * 48], in_=stage[0:48, c * 128:(c + 1) * 128],
                                                                  identity=self.identf[0:48, 0:48]), R=SR + ["C"], W=[pres])
            for k in range(4):
                c = q6 * 4 + k
                f.op("act", lambda e, k=k, c=c, pt=pt: e.activation(out=cbuf[:, c, 0:3, :], in_=pt[:, k * 48:(k + 1) * 48].rearrange("p (b j) -> p j b", j=3),
                                                                    func=AF.Copy), R=[pres], W=["xbch"])
        k32s = L[:, 96:96 + 256].rearrange("p (a t) -> p a t", a=2)
        f.op("pool", lambda e: e.memset(L[:, 96:352], 0.0), W=["lastst"])
        xbctok = self.hst
        def xbc_hook(s, wt, wres):
            pt, pres = self.pb()
            self.tm_mm(pt[0:NS, 0:256], pres, wt, wres, 256, self.hT, "hT", 0, NS)
            if s < 8:
                dst = self.hst[0:NS, s * 256:(s + 1) * 256]
            else:
                dst = self.hbf32[0:NS, (s - 8) * 256:(s - 7) * 256]
            f.op("act", lambda e, pt=pt, dst=dst: e.activation(out=dst, in_=pt[0:NS, 0:256], func=AF.Copy), R=[pres], W=["hst" if s < 8 else "bcs"])
        self.xbc_hook = xbc_hook
        self.drain(self.in_proj_fm(self.hT, NS, self.qT, lambda pair: self.kT[:, pair, 128:128 + NS],
                                   lambda c: cbuf[:, c, 3, :], self.qmT, last3=None, k32=lambda pair: k32s[:, pair, 0:NS]))
        self.xbc_hook = None
        f.dma("pool", io["sconv_o"][:, 2, 0:2048], self.hst[0:NS, :], R=["hst"])
        f.dma("pool", io["sconv_o"][:, 2, 2048:3072], self.hbf32[0:NS, :], R=["bcs"])
        pt, pres = self.pb()
        for pair in range(2):
            f.op("pe", lambda e, pair=pair, pt=pt: e.transpose(out=pt[:, pair * 128:(pair + 1) * 128], in_=k32s[:, pair, :], identity=self.identf),
                 R=["lastst", "C"], W=[pres])
        f.op("act", lambda e, pt=pt: e.activation(out=L[0:NS, 608:864], in_=pt[0:NS, 0:256], func=AF.Copy), R=[pres], W=["lastst"])
        f.dma("pool", io["sk"][:, 127, :], L[0:NS, 608:864], R=["lastst"])
        wt, wres = self.slab("A", SA_V)
        pt, pres = self.pb()
        self.tm_mm(pt[0:NS, 0:256], pres, wt, wres, 256, self.hT, "hT", 0, NS)
        f.op("act", lambda e, pt=pt: e.activation(out=self.vtok[0:NS, 1, :], in_=pt[0:NS, 0:256], func=AF.Copy), R=[pres], W=["vtok"])
        f.op("dve", lambda e, pt=pt: e.tensor_copy(out=L[0:NS, 352:608], in_=pt[0:NS, 0:256]), R=[pres], W=["lastst"])
        f.dma("pool", io["sv"][:, 127, :], L[0:NS, 352:608], R=["lastst"])
        zT = self.zs[:, 0:16 * NS].rearrange("p (c t) -> p c t", c=16)
        for s in range(8):
            wt, wres = self.slab("A", SA_Z + s)
            for mm in range(2):
                pt, pres = self.pb()
                self.fm_mm(pt[:, 0:NS], pres, wt, wres, mm * 128, self.hT, "hT", NS)
                f.op("act", lambda e, pt=pt, c=s * 2 + mm: e.activation(out=zT[:, c, :], in_=pt[:, 0:NS], func=AF.Silu), R=[pres], W=["zs"])
        wt, wres = self.slab("T", 0)
        pt, pres = self.pb()
        self.tm_mm(pt[0:NS, 0:32], pres, wt, wres, 32, self.hT, "hT", 0, NS, cw=32)
        f.op("act", lambda e, pt=pt: e.activation(out=self.dtraw[0:NS, :], in_=pt[0:NS, 0:32], func=AF.Copy), R=[pres], W=["dtraw"])
        cvs = self.cvs
        self.drain(self.conv_fm(lambda c, j: cbuf[:, c, j, :], NS, lambda c: cvs[:, c, :], lambda c: ("xbcT", c)))
        CVS = [("xbcT", c) for c in range(24)]
        for b in range(NS):
            ck, ckres = self.tf()
            f.dma("pool", ck[:, 0:256], io["csk"][b], W=[ckres])
            f.dma("pool", ck[:, 256:512], io["csv"][b], W=[ckres])
            cb16, cbres = self.bm_tok, "bm_tok"
            f.op("pool", lambda e, ck=ck, cb16=cb16: e.tensor_copy(out=cb16[:, 0:512], in_=ck[:, 0:512]), R=[ckres], W=[cbres])
            pt, pres = self.pb()
            pbf = pt.bitcast(BF16)
            for pair in range(2):
                f.op("pe", lambda e, pair=pair, pbf=pbf, cb16=cb16: e.transpose(out=pbf[:, pair * 128:(pair + 1) * 128], in_=cb16[:, pair * 128:(pair + 1) * 128],
                                                                                identity=self.identb), R=[cbres, "CBc"], W=[pres])
            kTc, kTres = self.Wp[0][:].rearrange("p a b -> p (a b)"), ("Wp", 0)
            f.op("act", lambda e, pbf=pbf, kTc=kTc: e.activation(out=kTc[:, 0:256], in_=pbf[:, 0:256], func=AF.Copy), R=[pres], W=[kTres])
            self.drain(self.swa_heads(
                q_rhs=lambda h, b=b: self.qT[64 * (h % 2):64 * (h % 2) + 64, (h // 2) * 4:(h // 2) * 4 + 4, b:b + 1],
                kprev=lambda h, kTc=kTc: kTc[64 * (h % 2):64 * (h % 2) + 64, (h // 2) * 128:(h // 2 + 1) * 128],
                kcur=lambda h: self.kT[64 * (h % 2):64 * (h % 2) + 64, h // 2, 128:128 + NS],
                vprev=lambda h, cb16=cb16: cb16[:, 256 + h * 64:256 + (h + 1) * 64],
                vcur=lambda h: self.vtok[0:NS, 1, h * 64:(h + 1) * 64],
                nq=1, kcn=NS, Dmp=self.Dmp[:, 0:1], Dmc=self.dm16[0:NS, b:b + 1],
                out=lambda h, b=b: self.aT[0:64, h * 4:h * 4 + 4, b:b + 1],
                R_in=["qT", "kT", "vtok", kTres, cbres], W_out=["aT"]))
            kst = stage[:, 0:2048]
            vst = stage[:, 2048:4096]
            f.dma("pool", kst.rearrange("p (m c) -> p m c", m=2), io["cmk"][b].rearrange("(m p) c -> p m c", p=128), W=SR)
            f.dma("pool", vst.rearrange("p (m c) -> p m c", m=2), io["cmv"][b].rearrange("(m p) c -> p m c", p=128), W=SR)
            f.op("pool", lambda e: e.tensor_copy(out=self.Vm[:].rearrange("p m c -> p (m c)"), in_=vst), R=SR, W=["Vm"])
            f.op("dve", lambda e: e.tensor_copy(out=self.xn[:], in_=kst), R=SR, W=["xn"])
            self.kmem_T(self.xn, "xn", self.KTm, "KTm")
            self.drain(self.mem_heads(q_rhs=lambda c, b=b: self.qmT[:, c, b:b + 1], KTm=self.KTm, ktres="KTm", Vm=self.Vm, vres="Vm", nq=1,
                                      out=lambda c, b=b: self.mT[:, c, b:b + 1], R_in=["qmT"], W_out=["mT"]))
        S = self.ssd_s
        dt = S[0:NS, 32:64]
        f.op("dve", lambda e: e.tensor_tensor(out=S[0:NS, 0:32], in0=self.dtraw[0:NS, :], in1=self.dtb[0:NS, :], op=ALU.add), R=["dtraw", "PR"], W=["S"])
        f.op("act", lambda e: e.activation(out=S[0:NS, 0:32], in_=S[0:NS, 0:32], func=AF.Exp), R=["S"], W=["S"])
        f.op("act", lambda e: e.activation(out=dt, in_=S[0:NS, 0:32], func=AF.Ln, bias=1.0), R=["S"], W=["S"])
        f.op("dve", lambda e: e.tensor_tensor(out=S[0:NS, 64:96], in0=dt, in1=self.a_bc[0:NS, :], op=ALU.mult), R=["S", "SM"], W=["S"])
        f.op("act", lambda e: e.activation(out=S[0:NS, 96:128], in_=S[0:NS, 64:96], func=AF.Exp), R=["S"], W=["S"])
        ex = stage[0:NS, :]
        f.op("dve", lambda e: e.tensor_copy(out=ex[:, 0:2048].rearrange("p (h d) -> p h d", h=32), in_=bc(S[0:NS, 96:128], [NS, 32, 64])), R=["S"] + SR, W=SR)
        f.op("dve", lambda e: e.tensor_copy(out=ex[:, 2048:4096].rearrange("p (h d) -> p h d", h=32), in_=bc(dt, [NS, 32, 64])), R=["S"] + SR, W=SR)
        cdT, dtT = self.cdT, self.dtT
        for which, dst in ((0, cdT), (1, dtT)):
            pt, pres = self.pb()
            for c in range(16):
                f.op("pe", lambda e, c=c, which=which, pt=pt: e.transpose(out=pt[:, c * NS:(c + 1) * NS], in_=ex[:, which * 2048 + c * 128: which * 2048 + (c + 1) * 128],
                                                                          identity=self.identf[0:NS, 0:NS]), R=SR + ["C"], W=[pres])
            f.op("act", lambda e, pt=pt, dst=dst: e.activation(out=dst[:].rearrange("p c t -> p (c t)"), in_=pt[:, 0:16 * NS], func=AF.Copy), R=[pres], W=["cdT"])
        xdtT = self.xdtT
        f.op("dve", lambda e: e.tensor_tensor(out=xdtT[:], in0=cvs[:, 0:16, :], in1=dtT[:], op=ALU.mult), R=CVS + ["cdT"], W=["xdtT"])
        bcs = self.bcs
        bpad = self.hst[:, 0:1024].rearrange("p (c t) -> p c t", c=8)
        f.op("pool", lambda e: e.memset(self.hst[:, 0:1024], 0.0), W=["hst"])
        f.op("dve", lambda e: e.tensor_copy(out=bpad[:, :, 0:NS], in_=cvs[:, 16:24, :]), R=CVS + ["hst"], W=["hst"])
        for half in range(2):
            pt, pres = self.pb()
            for g in range(4):
                f.op("pe", lambda e, g=g, half=half, pt=pt: e.transpose(out=pt[:, g * 128:(g + 1) * 128], in_=bpad[:, half * 4 + g, :], identity=self.identf),
                     R=["hst", "C"], W=[pres])
            f.op("act", lambda e, half=half, pt=pt: e.activation(out=bcs[0:NS, half * 512:(half + 1) * 512], in_=pt[0:NS, 0:512], func=AF.Copy), R=[pres], W=["bcs"])
        oh16 = self.rhsD[:].rearrange("p a b -> p (a b)").bitcast(F32)[0:NS, :].rearrange("p (a b) -> p a b", a=NS)
        f.op("dve", lambda e: e.tensor_copy(out=oh16[:], in_=bc(self.identf[0:NS, 0:NS], [NS, NS, 128])), R=["C"], W=["rhsD"])
        yT = self.yT
        U = self.hst
        for b in range(NS):
            H = stage[:, (b % 2) * 2048:(b % 2 + 1) * 2048]
            Hres = ["xs_tok", "xdt"] if b % 2 == 0 else ["xw", "s_tok"]
            H3 = H.rearrange("p (c n) -> p c n", c=16)
            f.dma("pool", H3, io["sssm"][b].rearrange("(c q) n -> q c n", q=128), W=Hres)
            pbm, pbmres = self.pb()
            pcm, pcmres = self.pb()
            f.op("pe", lambda e, b=b, pbm=pbm: e.matmul(pbm[:, 0:512], lhsT=oh16[0:NS, b, :], rhs=bcs[0:NS, 0:512], start=True, stop=True), R=["rhsD", "bcs"], W=[pbmres])
            f.op("pe", lambda e, b=b, pcm=pcm: e.matmul(pcm[:, 0:512], lhsT=oh16[0:NS, b, :], rhs=bcs[0:NS, 512:1024], start=True, stop=True), R=["rhsD", "bcs"], W=[pcmres])
            f.op("dve", lambda e, b=b, H3=H3: e.tensor_tensor(out=H3, in0=H3, in1=bc(cdT[:, :, b], [128, 16, 128]), op=ALU.mult), R=Hres + ["cdT"], W=Hres)
            U4 = U[:].rearrange("p (g r n) -> p g r n", g=4, r=4)
            f.op("dve", lambda e, b=b, pbm=pbm, U4=U4: e.tensor_tensor(
                out=U4, in0=pbm[:, 0:512].rearrange("p (g n) -> p g n", g=4).unsqueeze(2).broadcast_to([128, 4, 4, 128]),
                in1=xdtT[:, :, b].rearrange("p (g r) -> p g r", g=4).unsqueeze(3).broadcast_to([128, 4, 4, 128]), op=ALU.mult),
                R=[pbmres, "xdtT", "hst"], W=["hst"])
            f.op("dve", lambda e, H=H: e.tensor_tensor(out=H, in0=H, in1=U[:], op=ALU.add), R=Hres + ["hst"], W=Hres)
            f.dma("pool", io["sssm_o"][b].rearrange("(c q) n -> q c n", q=128), H3, R=Hres)
            f.op("dve", lambda e, pcm=pcm, U4=U4, H=H: e.tensor_tensor(
                out=U4, in0=H.rearrange("p (g r n) -> p g r n", g=4, r=4),
                in1=pcm[:, 0:512].rearrange("p (g n) -> p g n", g=4).unsqueeze(2).broadcast_to([128, 4, 4, 128]), op=ALU.mult),
                R=Hres + [pcmres, "hst"], W=["hst"])
            f.op("dve", lambda e, b=b: e.tensor_reduce(out=yT[:, :, b], in_=U[:].rearrange("p (c n) -> p c n", c=16), op=ALU.add, axis=AX.X),
                 R=["hst"], W=["cdT"])
        y2 = self.y2
        f.op("dve", lambda e: e.tensor_tensor(out=y2[:], in0=cvs[:, 0:16, :], in1=bc(self.dskT, [128, 16, NS]), op=ALU.mult), R=CVS + ["PV"], W=["cdT"])
        f.op("dve", lambda e: e.tensor_tensor(out=y2[:], in0=y2[:], in1=yT[:], op=ALU.add), R=["cdT"], W=["cdT"])
        f.op("dve", lambda e: e.tensor_tensor(out=y2[:], in0=y2[:], in1=zT, op=ALU.mult), R=["cdT", "zs"], W=["cdT"])
        sq, sqres = self.tb()
        f.op("act", lambda e, sq=sq: e.activation(out=sq[:, 0:16 * NS], in_=y2[:].rearrange("p c t -> p (c t)"), func=AF.Square), R=["cdT"], W=[sqres])
        p2, p2res = self.pb()
        for g in range(4):
            for r in range(4):
                c = g * 4 + r
                f.op("pe", lambda e, g=g, r=r, c=c, sq=sq, p2=p2: e.matmul(p2[:, g * NS:(g + 1) * NS], lhsT=self.onesb, rhs=sq[:, c * NS:(c + 1) * NS],
                                                                          start=(r == 0), stop=(r == 3)), R=[sqres, "CBc"], W=[p2res])
        rr, rres = self.rsqrt_bc(p2[:, 0:4 * NS], p2res, 4 * NS, 1.0 / 512)
        f.op("dve", lambda e, rr=rr: e.tensor_tensor(
            out=y2[:].rearrange("p (g r) t -> p g r t", g=4), in0=y2[:].rearrange("p (g r) t -> p g r t", g=4),
            in1=rr.rearrange("p (g t) -> p g t", g=4).unsqueeze(2).broadcast_to([128, 4, 4, NS]), op=ALU.mult), R=["cdT", rres], W=["cdT"])
        f.op("dve", lambda e: e.tensor_tensor(out=self.sT[:, :, 0:NS], in0=y2[:], in1=bc(self.ssdnT, [128, 16, NS]), op=ALU.mult), R=["cdT", "PV"], W=["sT"])
        self.worder = self.order_merge() + self.order_ffn()
        self.wpos = 0
        self.merge_out(NS, NS, "x0", x16, self.aT, self.sT, self.mT, self.hT,
                       lambda m: self.xbcT[:, m, 3:3 + NS], lambda m: ("xbcT", m), self.h2T)
        self.drain(self.ffn_gen(NS, NS, "x0", x16, self.h2T))
        f.dma("pool", io["ys"], x16, R=["x0"])


def slabify(W, cw, kcs):
    K, N = W.shape
    assert K == kcs * 128 and N % cw == 0
    a = W.reshape(kcs, 128, N // cw, cw).transpose(2, 1, 0, 3)
    return np.ascontiguousarray(a).reshape(N // cw, 128, kcs * cw)


def host_prep(inp):
    w_in = inp["w_in"][0]
    q = w_in[:, 0:1024].reshape(2048, 2, 2, 4, 64).transpose(0, 1, 3, 2, 4).reshape(2048, 1024)
    parts = [slabify(q, CW, 16), slabify(w_in[:, 1024:1280], CW, 16), slabify(w_in[:, 1280:1536], CW, 16),
             slabify(w_in[:, 1536:3584], CW, 16), slabify(w_in[:, 3584:6656], CW, 16), slabify(w_in[:, 6688:7712], CW, 16),
             slabify(w_in[:, 7712:13856], CW, 16), slabify(inp["w_mem_kv"][0], CW, 16), slabify(inp["w_up_ssd"][0], CW, 16),
             slabify(inp["w_out"][0], CW, 16), slabify(inp["w_gate"][0], CW, 16), slabify(inp["w_up"][0], CW, 16)]
    WA = np.concatenate(parts, 0)
    assert WA.shape[0] == NA
    WM = slabify(inp["w_up_mem"][0], CW, 8)
    ws = inp["w_up_swa"][0]
    WS = np.ascontiguousarray(ws.reshape(16, 64, 8, CW).transpose(2, 1, 0, 3)).reshape(8, 64, 16 * CW)
    wd = inp["w_down"][0]
    WD = np.ascontiguousarray(wd.reshape(4, 11, 128, 8, CW).transpose(3, 0, 2, 1, 4)).reshape(32, 128, 11 * CW)
    WT = slabify(w_in[:, 6656:6688], 32, 16)[0]
    ar = np.arange(128)
    c128 = np.zeros((128, 8 * 128 + 512 + 32), np.float32)
    c128[:, 0:128] = np.eye(128)
    c128[:, 128:256] = (ar[:, None] <= ar[None, :])
    c128[:, 256:384] = (ar[:, None] > ar[None, :])
    c128[:, 384:512] = 1.0
    k_, q_ = ar[:, None], ar[None, :]
    c128[:, 512:640] = np.where(q_ >= k_, q_ - k_, 20000.0)
    c128[:, 640:768] = np.where(q_ <= k_, q_ + 128 - k_, 20000.0)
    c128[:, 768:896] = (ar[:, None] // 64 == ar[None, :] // 64)
    c128[:, 1024:1536] = np.tile(np.where(ar[None, :] < ar[:, None], -30000.0, 0.0), (1, 4))
    c128[:, 1536:1568] = (ar[:, None] % 32 == np.arange(32)[None, :])
    pvec = np.zeros((128, 512), np.float32)
    T16 = lambda v: np.ascontiguousarray(v.reshape(16, 128).T)
    pvec[:, 0:16] = T16(inp["norm_mix"][0])
    pvec[:, 16:32] = T16(inp["norm_ffn"][0])
    pvec[:, 32:48] = T16(inp["norm_mem"][0])
    pvec[:, 48:64] = T16(inp["ssd_norm"][0])
    pvec[:, 64:160] = inp["conv_w"][0].reshape(4, 24, 128).transpose(2, 1, 0).reshape(128, 96)
    pvec[:, 160:184] = inp["conv_b"][0].reshape(24, 128).T
    pvec[:, 184] = np.tile(inp["q_norm_swa"][0], 2)
    pvec[:, 185] = np.tile(inp["k_norm_swa"][0], 2)
    pvec[:, 186:188] = inp["q_norm_mem"][0].reshape(2, 128).T
    pvec[:, 188:204] = inp["swa_sinks"][0][None, :]
    pvec[:, 204:220] = np.repeat(inp["d_skip"][0], 64).reshape(16, 128).T
    pvec[0:16, 220:236] = np.where(np.eye(16) > 0, 0.0, 20000.0)
    prow = np.zeros((128, 352), np.float32)
    prow[:, 0:32] = inp["dt_bias"][0][None, :]
    prow[:, 32:64] = inp["a_log"][0][None, :]
    prow[:, 64:96] = inp["d_skip"][0][None, :]
    prow[:, 96:352] = inp["k_norm_mem"][0][None, :]
    shared = dict(WA=WA, WM=WM, WS=WS, WD=WD, WT=np.ascontiguousarray(WT), c128=c128, pvec=pvec, prow=prow)
    in_maps = []
    for c in range(NCORES):
        m = dict(shared)
        m["xp"] = np.ascontiguousarray(inp["x_prompt"][2 * c:2 * c + 2])
        m["memp"] = np.ascontiguousarray(inp["mem_prompt"][2 * c:2 * c + 2])
        sl = slice(16 * c, 16 * c + 16)
        m["xs"] = np.ascontiguousarray(inp["x_sample"][sl, 0])
        m["csk"] = np.ascontiguousarray(inp["cache_swa_k"][0, sl]).reshape(16, 128, 256)
        m["csv"] = np.ascontiguousarray(inp["cache_swa_v"][0, sl]).reshape(16, 128, 256)
        m["cmk"] = np.ascontiguousarray(inp["cache_mem_k"][0, sl]).reshape(16, 256, 1024)
        m["cmv"] = np.ascontiguousarray(inp["cache_mem_v"][0, sl]).reshape(16, 256, 1024)
        m["sssm"] = np.ascontiguousarray(inp["state_ssm"][0, sl]).reshape(16, 2048, 128)
        m["sconv"] = np.ascontiguousarray(inp["state_conv"][0, sl]).reshape(48, 3072)
        in_maps.append(m)
    return in_maps


_CACHE = {}


def run(inputs, do_samples=True, n_st=None, dbg=False):
    inp = {k: np.asarray(v) for k, v in inputs.items()}
    b = Builder(do_samples=do_samples, n_st=n_st, dbg=dbg)
    nc = b.build()
    in_maps = host_prep(inp)
    if not do_samples:
        for m in in_maps:
            for k in ("xs", "csk", "csv", "cmk", "cmv", "sssm", "sconv"):
                m.pop(k)
    res = run_bass_kernel_spmd(nc, in_maps, core_ids=list(range(NCORES)))
    R = res.results
    cat = lambda k: np.concatenate([r[k] for r in R], 0)
    yp = cat("yp")
    ys = cat("ys").reshape(128, 1, D)
    outs = (yp, ys,
            cat("pk").reshape(1, 16, 128, 4, 64), cat("pv").reshape(1, 16, 128, 4, 64),
            cat("pmk").reshape(1, 16, 256, 4, 256), cat("pmv").reshape(1, 16, 256, 4, 256),
            cat("pssm").reshape(1, 16, 32, 64, 128), cat("pconv").reshape(1, 16, 3, 3072),
            cat("sk").reshape(1, 128, 128, 4, 64), cat("sv").reshape(1, 128, 128, 4, 64),
            cat("sssm_o").reshape(1, 128, 32, 64, 128), cat("sconv_o").reshape(1, 128, 3, 3072))
    outs = tuple(np.ascontiguousarray(o, dtype=np.float32) for o in outs)
    if dbg:
        return outs, {k: [r["dbg_" + k] for r in R] for k in b.dbg_outs}
    return outs


def kernel(**inputs):
    return run(inputs, do_samples=True)
```
